# Optimizing a Trainium2 kernel written in Bass

```python
import math
import jax, jax.numpy as jnp
from jax import lax
import numpy as np

D_MODEL = 1024
BATCH = 2
SEQ = 8192
DEPTH = 1

PLE_DIM = 256
EPS = 1e-6
GLA_HEADS = 4
GLA_DK = D_MODEL // 2 // GLA_HEADS
GLA_DV = D_MODEL // GLA_HEADS
GLA_RANK = 16
GLA_TAU = 16.0
GLA_CHUNK = 64
GLA_QK_W = GLA_HEADS * GLA_DK
GLA_V_W = GLA_HEADS * GLA_DV
ATT_GROUPS = ((128, 1), (512, 4), (2048, 16))
ATT_HEADS_PER_GROUP = 4
ATT_HEAD_DIM = 128
N_ATT_GROUPS = len(ATT_GROUPS)
N_ATT_HEADS = N_ATT_GROUPS * ATT_HEADS_PER_GROUP
ATT_W = ATT_HEADS_PER_GROUP * ATT_HEAD_DIM
ATT_BLOCK = 128
REL_BUCKETS = 32
REL_MAX_DIST = 2048
D_FF = 4 * D_MODEL
NEG_INF = -1e30

SPLIT_SIZES = (GLA_QK_W, GLA_QK_W, GLA_V_W, GLA_V_W, GLA_RANK,
               N_ATT_GROUPS * 3 * ATT_W, 2 * D_MODEL)
IN_COLS = sum(SPLIT_SIZES)

kernel_name = "hybrid_gla_dilated_attn_gated_block"


def rmsnorm(x, g):
    xf = x.astype(jnp.float32)
    y = xf * lax.rsqrt(jnp.mean(xf * xf, axis=-1, keepdims=True) + EPS)
    return (y * g.astype(jnp.float32)).astype(x.dtype)


def _split_cols(h, sizes):
    offsets = np.cumsum(np.array(sizes))[:-1].tolist()
    return jnp.split(h, offsets, axis=-1)


def _t5_causal_bucket(n):
    max_exact = REL_BUCKETS // 2
    nf = np.maximum(n, 1).astype(np.float32)
    large = max_exact + (np.log(nf / max_exact) / np.log(REL_MAX_DIST / max_exact)
                         * (REL_BUCKETS - max_exact)).astype(np.int32)
    large = np.minimum(large, REL_BUCKETS - 1)
    return np.where(n < max_exact, n, large).astype(np.int32)


def _group_bias(rel_bias, g, dil):
    qi = np.arange(ATT_BLOCK)[:, None]
    kj = np.arange(2 * ATT_BLOCK)[None, :]
    dist = np.maximum(qi + ATT_BLOCK - kj, 0) * dil
    bucket = _t5_causal_bucket(dist)
    tab = rel_bias[:, g * ATT_HEADS_PER_GROUP:(g + 1) * ATT_HEADS_PER_GROUP]
    return jnp.transpose(tab[bucket], (2, 0, 1)).astype(jnp.float32)


def gla_mixer(q, k, v, log_a):
    B, S, H, dk = q.shape
    dv = v.shape[-1]
    C = GLA_CHUNK
    N = S // C

    def chunk(t):
        return t.astype(jnp.float32).reshape(B, N, C, H, t.shape[-1]).transpose(0, 3, 1, 2, 4)

    q, k, v, g = chunk(q), chunk(k), chunk(v), chunk(log_a)
    b = jnp.cumsum(g, axis=3)
    b_last = b[:, :, :, -1:, :]
    q_dec = q * (dk ** -0.5) * jnp.exp(b)
    k_in = k * jnp.exp(-b)
    k_out = k * jnp.exp(b_last - b)
    causal = np.tril(np.ones((C, C), dtype=bool))
    attn = jnp.where(causal, jnp.einsum('bhncd,bhnsd->bhncs', q_dec, k_in), 0.0)
    o_intra = jnp.einsum('bhncs,bhnsv->bhncv', attn, v)
    upd = jnp.einsum('bhncd,bhncv->nbhdv', k_out, v)
    decay = jnp.exp(b_last[:, :, :, 0, :]).transpose(2, 0, 1, 3)

    def step(state, inp):
        dec, u = inp
        return dec[..., None] * state + u, state

    _, states = lax.scan(step, jnp.zeros((B, H, dk, dv), jnp.float32), (decay, upd))
    o_inter = jnp.einsum('bhncd,nbhdv->bhncv', q_dec, states)
    return (o_intra + o_inter).transpose(0, 2, 3, 1, 4).reshape(B, S, H, dv)


def dilated_attention(q, k, v, bias, dil, win_steps):
    B, S, H, hd = q.shape
    BLK = ATT_BLOCK
    L = S // dil
    nb = -(-L // BLK)
    Lp = nb * BLK
    Z = B * dil

    def sub(t):
        return t.reshape(B, L, dil, H, hd).transpose(0, 2, 3, 1, 4).reshape(Z, H, L, hd)

    qs = jnp.pad(sub(q), ((0, 0), (0, 0), (0, Lp - L), (0, 0))).reshape(Z, H, nb, BLK, hd)

    def kv_blocks(t):
        t = jnp.pad(sub(t), ((0, 0), (0, 0), (BLK, Lp - L), (0, 0))).reshape(Z, H, nb + 1, BLK, hd)
        return jnp.concatenate([t[:, :, :-1], t[:, :, 1:]], axis=3)

    kb, vb = kv_blocks(k), kv_blocks(v)
    qi = np.arange(BLK)[:, None]
    kj = np.arange(2 * BLK)[None, :]
    delta = qi + BLK - kj
    band = (delta >= 0) & (delta <= win_steps)
    valid = band[None] & ((np.arange(nb)[:, None, None] > 0) | (kj >= BLK)[None])
    logits = jnp.einsum('zhnqd,zhnkd->zhnqk', qs, kb).astype(jnp.float32) * (hd ** -0.5)
    logits = jnp.where(valid, logits + bias[:, None], NEG_INF)
    m = jnp.max(logits, axis=-1, keepdims=True)
    pexp = jnp.exp(logits - m)
    s = jnp.sum(pexp, axis=-1, keepdims=True)
    o = jnp.einsum('zhnqk,zhnkd->zhnqd', pexp, vb.astype(jnp.float32)) / s
    lse = (m + jnp.log(s))[..., 0]
    o = o.reshape(B, dil, H, Lp, hd)[:, :, :, :L].transpose(0, 3, 1, 2, 4).reshape(B, S, H, hd)
    lse = lse.reshape(B, dil, H, Lp)[..., :L].transpose(0, 3, 1, 2).reshape(B, S, H)
    return o, lse


def setup_inputs(seed: int = 0) -> dict:
    key = jax.random.key(seed)
    ks = jax.random.split(key, 20)

    def nrm(k, shape, scale):
        return jax.random.normal(k, shape, jnp.float32) * scale

    return {
        "x": nrm(ks[0], (BATCH, SEQ, D_MODEL), 1.0),
        "p": nrm(ks[1], (DEPTH, BATCH, SEQ, PLE_DIM), 1.0),
        "ln1": 1.0 + nrm(ks[2], (DEPTH, D_MODEL), 0.02),
        "w_in": nrm(ks[3], (DEPTH, D_MODEL, IN_COLS), D_MODEL ** -0.5),
        "w_a2": nrm(ks[4], (DEPTH, GLA_RANK, GLA_QK_W), GLA_RANK ** -0.5),
        "b_a": nrm(ks[5], (DEPTH, GLA_QK_W), 0.1),
        "gla_gn": 1.0 + nrm(ks[6], (DEPTH, GLA_V_W), 0.02),
        "w_o_gla": nrm(ks[7], (DEPTH, GLA_V_W, D_MODEL), GLA_V_W ** -0.5),
        "w_o_attn": nrm(ks[8], (DEPTH, ATT_W, D_MODEL), ATT_W ** -0.5),
        "w_out": nrm(ks[9], (DEPTH, D_MODEL, D_MODEL), D_MODEL ** -0.5),
        "ln2": 1.0 + nrm(ks[10], (DEPTH, D_MODEL), 0.02),
        "w_mlp1": nrm(ks[11], (DEPTH, D_MODEL, D_FF), D_MODEL ** -0.5),
        "w_mlp2": nrm(ks[12], (DEPTH, D_FF, D_MODEL), D_FF ** -0.5),
        "ln3": 1.0 + nrm(ks[13], (DEPTH, D_MODEL), 0.02),
        "w_pp": nrm(ks[14], (DEPTH, PLE_DIM, D_MODEL), PLE_DIM ** -0.5),
        "w_pg": nrm(ks[15], (DEPTH, D_MODEL, D_MODEL), D_MODEL ** -0.5),
        "rel_bias": nrm(ks[16], (REL_BUCKETS, N_ATT_HEADS), 0.5),
        "ln_f": 1.0 + nrm(ks[17], (D_MODEL,), 0.02),
    }


def reference(x, p, ln1, w_in, w_a2, b_a, gla_gn, w_o_gla, w_o_attn, w_out,
              ln2, w_mlp1, w_mlp2, ln3, w_pp, w_pg, rel_bias, ln_f):
    B, S, _ = x.shape
    for i in range(DEPTH):
        h = rmsnorm(x, ln1[i])
        hq, hk, hv, hr, ha, hatt, hgate = _split_cols(h @ w_in[i], SPLIT_SIZES)

        log_a = jax.nn.log_sigmoid((ha @ w_a2[i] + b_a[i]).astype(jnp.float32)) / GLA_TAU
        o = gla_mixer(hq.reshape(B, S, GLA_HEADS, GLA_DK),
                      hk.reshape(B, S, GLA_HEADS, GLA_DK),
                      hv.reshape(B, S, GLA_HEADS, GLA_DV),
                      log_a.reshape(B, S, GLA_HEADS, GLA_DK))
        o = o * lax.rsqrt(jnp.mean(o * o, axis=-1, keepdims=True) + EPS)
        o = o.reshape(B, S, GLA_V_W) * gla_gn[i].astype(jnp.float32) * jax.nn.silu(hr.astype(jnp.float32))
        y_gla = o.astype(x.dtype) @ w_o_gla[i]

        hatt = hatt.reshape(B, S, N_ATT_GROUPS, 3, ATT_HEADS_PER_GROUP, ATT_HEAD_DIM)
        outs, lses = [], []
        for g, (win, dil) in enumerate(ATT_GROUPS):
            o_g, lse_g = dilated_attention(hatt[:, :, g, 0], hatt[:, :, g, 1], hatt[:, :, g, 2],
                                           _group_bias(rel_bias, g, dil), dil, win // dil)
            outs.append(o_g)
            lses.append(lse_g)
        wts = jax.nn.softmax(jnp.stack(lses, axis=0), axis=0)
        o_att = jnp.sum(wts[..., None] * jnp.stack(outs, axis=0), axis=0).reshape(B, S, ATT_W)
        y_att = o_att.astype(x.dtype) @ w_o_attn[i]

        g_gla, g_att = jnp.split(hgate, 2, axis=-1)
        mix = (jax.nn.sigmoid(g_gla) * y_gla + jax.nn.sigmoid(g_att) * y_att) @ w_out[i]
        x = x + mix

        h2 = rmsnorm(x, ln2[i])
        x = x + jnp.square(jax.nn.relu(h2 @ w_mlp1[i])) @ w_mlp2[i]

        h3 = rmsnorm(x, ln3[i])
        x = x + jax.nn.sigmoid(h3 @ w_pg[i]) * (p[i] @ w_pp[i])
    return rmsnorm(x, ln_f)
```

```python
import contextlib
import numpy as np
import concourse.bass as bass
import concourse.mybir as mybir
from concourse.bass_utils import run_bass_kernel_spmd

F32 = mybir.dt.float32
BF16 = mybir.dt.bfloat16
ALU = mybir.AluOpType
AF = mybir.ActivationFunctionType

SAFE_SAME = True
EPS = 1e-6
NCORES = 8
SEG = 2048
NEGM = -30000.0


class Buf:
    __slots__ = ("w", "r")

    def __init__(self):
        self.w = None
        self.r = []


class Prog:
    ENGS = ("pe", "act", "dve", "pool", "sp")
    WINDOW = 40
    LAT = 300.0
    LAT_DMA = 200.0

    def __init__(self, nc, stack):
        self.nc = nc
        self.stack = stack
        self.ops = []
        self.phase = 0
        self.sems = {}
        self.all_dsems = []
        self.final = []
        for e in ("pe", "act", "dve", "pool"):
            self.sems[e] = stack.enter_context(nc.semaphore("s_" + e))

    def dma_sem(self, name):
        s = self.stack.enter_context(self.nc.semaphore("d_" + name))
        d = [s, 0]
        self.all_dsems.append(d)
        return d

    def op(self, eng, fn, reads=(), writes=(), dsem=None, cost=500.0, fin=None):
        idx = len(self.ops)
        deps = set()
        for b in reads:
            if b.w is not None:
                deps.add(b.w)
        for b in writes:
            if b.w is not None:
                deps.add(b.w)
            deps.update(b.r)
        self.ops.append([eng, fn, sorted(deps), dsem, cost, self.phase, cost if fin is None else fin])
        for b in reads:
            b.r.append(idx)
        for b in writes:
            b.w = idx
            b.r = []
        return idx

    def barrier(self):
        self.phase += 1

    def final_wait(self, eng, toks):
        self.final.append((eng, list(toks)))

    def schedule(self):
        ops = self.ops
        order = {e: [] for e in self.ENGS}
        finish = {}
        tnow = 0.0
        for ph in range(self.phase + 1):
            pend = {e: [] for e in self.ENGS}
            for i, o in enumerate(ops):
                if o[5] == ph:
                    pend[o[0]].append(i)
            tfree = {e: tnow for e in self.ENGS}
            remaining = sum(len(v) for v in pend.values())
            cand = {e: None for e in self.ENGS}
            dirty = set(self.ENGS)
            while remaining:
                for e in list(dirty):
                    best = None
                    lst = pend[e]
                    for i in lst[:self.WINDOW]:
                        o = ops[i]
                        st = tfree[e]
                        ok = True
                        for d in o[2]:
                            f = finish.get(d)
                            if f is None:
                                ok = False
                                break
                            lat = self.LAT_DMA if ops[d][3] is not None else (0.0 if ops[d][0] == e else self.LAT)
                            if f + lat > st:
                                st = f + lat
                        if ok and (best is None or (st, i) < best):
                            best = (st, i)
                    cand[e] = best
                dirty.clear()
                pick = None
                for e in self.ENGS:
                    c = cand[e]
                    if c is not None and (pick is None or c < pick[0]):
                        pick = (c, e)
                assert pick is not None, "scheduler stuck"
                (st, i), e = pick
                o = ops[i]
                tfree[e] = st + o[4]
                finish[i] = st + o[6]
                pend[e].remove(i)
                order[e].append(i)
                remaining -= 1
                dirty.update(self.ENGS)
            tnow = max(tfree.values())
            for e in self.ENGS:
                order[e].append(None)
        self.est_total = tnow
        return order

    def lower(self):
        order = self.schedule()
        ops = self.ops
        tok = {}
        cnt = {e: 0 for e in ("pe", "act", "dve", "pool")}
        bar_cnt = []
        nph = self.phase + 1
        pos = {e: 0 for e in self.ENGS}
        dcount = {id(d): 0 for d in self.all_dsems}
        bar_state = []
        for ph in range(nph):
            for e in self.ENGS:
                lst = order[e]
                while lst[pos[e]] is not None:
                    i = lst[pos[e]]
                    o = ops[i]
                    if o[3] is not None:
                        dcount[id(o[3])] += 16
                        tok[i] = (o[3][0], dcount[id(o[3])], e, True)
                    else:
                        cnt[e] += 1
                        tok[i] = (self.sems[e], cnt[e], e, False)
                    pos[e] += 1
                pos[e] += 1
            bar_state.append((dict(cnt), dict(dcount)))
        streams = {e: [] for e in self.ENGS}
        for e in self.ENGS:
            waited = {}
            ph = 0
            for i in order[e]:
                if i is None:
                    c, dc = bar_state[ph]
                    waits = []
                    for e2 in ("pe", "act", "dve", "pool"):
                        if e2 != e and c[e2] > waited.get(id(self.sems[e2]), 0):
                            waits.append((self.sems[e2], c[e2]))
                            waited[id(self.sems[e2])] = c[e2]
                    for d in self.all_dsems:
                        v = dc[id(d)]
                        if v > waited.get(id(d[0]), 0):
                            waits.append((d[0], v))
                            waited[id(d[0])] = v
                    if waits and ph < nph - 1:
                        streams[e].append((waits, None, None))
                    ph += 1
                    continue
                o = ops[i]
                waits = {}
                for d in o[2]:
                    s, v, e2, isdma = tok[d]
                    if e2 == e and not isdma:
                        if e in ("pe", "sp") or not SAFE_SAME:
                            continue
                    k = id(s)
                    if waited.get(k, 0) >= v:
                        continue
                    if k not in waits or waits[k][1] < v:
                        waits[k] = (s, v)
                for k, (s, v) in waits.items():
                    waited[k] = v
                t = tok[i]
                streams[e].append((list(waits.values()), o[1], (t[0], 16 if t[3] else 1)))
        for eng, toks in self.final:
            streams[eng].append(([(tok[t][0], tok[t][1]) for t in toks], None, None))
        self.streams = streams

    def run(self, block):
        self.lower()

        def play(name):
            def _f(e):
                for waits, fn, inc in self.streams[name]:
                    for s, v in waits:
                        e.wait_ge(s, v)
                    if fn is None:
                        continue
                    ins = fn(e)
                    if inc is not None:
                        ins.then_inc(inc[0], inc[1])
            return _f
        block.tensor(play("pe"))
        block.scalar(play("act"))
        block.vector(play("dve"))
        block.gpsimd(play("pool"))
        block.sync(play("sp"))


class Slot:
    def __init__(self, ap, sem=None):
        self.ap = ap
        self.buf = Buf()
        self.sem = sem


class Rot:
    def __init__(self, slots):
        self.slots = slots
        self.i = 0

    def next(self):
        s = self.slots[self.i % len(self.slots)]
        self.i += 1
        return s


def build_program():
    nc = bass.Bass("TRN2", target_bir_lowering=False)

    def din(name, shape):
        return nc.dram_tensor(name, shape, F32, kind="ExternalInput").ap()

    xe = din("xe", [8192, 1024])
    pin = din("p", [SEG, 256])
    w_in = din("w_in", [1024, 9744])
    wa2_d = din("w_a2aug", [32, 512])
    wog_d = din("w_o_gla", [1024, 1024])
    woa_d = din("w_o_attn", [512, 1024])
    wout_d = din("w_out", [1024, 1024])
    w1_d = din("w_mlp1", [1024, 4096])
    w2_d = din("w_mlp2", [4096, 1024])
    wpp_d = din("w_pp", [256, 1024])
    wpg_d = din("w_pg", [1024, 1024])
    cols_d = din("cols", [128, 32])
    lnf_d = din("ln_f", [1024])
    biasm_d = din("biasm", [128, 3, 2, 512])
    hoff_d = din("hoff", [128, 1])
    cmat_d = din("cmat", [128, 6, 128])
    sel2_d = din("sel2", [2, 2, 128])
    y = nc.dram_tensor("y", [SEG, 1024], F32, kind="ExternalOutput").ap()

    with contextlib.ExitStack() as st:
        P = Prog(nc, st)
        ARENA_BYTES = 200 * 1024
        arena = st.enter_context(nc.sbuf_tensor("arena", [128, ARENA_BYTES // 2], BF16))
        psb = [st.enter_context(nc.psum_tensor("psb%d" % i, [128, 512], F32)) for i in range(7)]
        ptr_t = st.enter_context(nc.psum_tensor("ptr", [128, 8, 128], BF16))
        block = st.enter_context(nc.Block())

        def carve(off, shape, dt):
            n = 1
            for s in shape[1:]:
                n *= s
            es = 2 if dt == BF16 else 4
            assert off % 4 == 0 and off + n * es <= ARENA_BYTES, (off, shape)
            a = arena[0:shape[0], off // 2: off // 2 + n * es // 2]
            if dt == F32:
                a = a.bitcast(F32)
            if len(shape) == 3:
                a = a.rearrange("p (a b) -> p a b", a=shape[1])
            elif len(shape) == 4:
                a = a.rearrange("p (a b c) -> p a b c", a=shape[1], b=shape[2])
            return a

        class Mem:
            def __init__(self, base, limit):
                self.off = base
                self.limit = limit

            def alloc(self, shape, dt):
                n = 1
                for s in shape[1:]:
                    n *= s
                es = 2 if dt == BF16 else 4
                a = carve(self.off, shape, dt)
                self.off += (n * es + 63) // 64 * 64
                assert self.off <= self.limit, (self.off, self.limit)
                return a

        KB = 1024

        def sl(start, n, step):
            return slice(start, start + step * (n - 1) + 1, step)

        def fsz(ap):
            n = 1
            for x in ap.shape[1:]:
                n *= x
            return n

        def PE(mms, reads, writes):
            cost = 0.0
            for m in mms:
                n = max(fsz(m[2]), 64)
                c = n / 2.0 + 30.0
                if m[1].dtype == F32:
                    c *= 4
                cost += c

            def fn(e):
                ins = None
                for m in mms:
                    kw = dict(start=m[3], stop=m[4])
                    if len(m) > 5 and m[5]:
                        kw["skip_group_check"] = True
                    ins = e.matmul(m[0], lhsT=m[1], rhs=m[2], **kw)
                return ins
            return P.op("pe", fn, reads, writes, cost=cost)

        def ACT(out, in_, func, reads, writes, **kw):
            return P.op("act", lambda e: e.activation(out=out, in_=in_, func=func, **kw), reads, writes,
                        cost=260.0 + 0.85 * fsz(out))

        def ENG(eng, method, reads, writes, **kw):
            o = kw.get("out", kw.get("ap"))
            n = fsz(o)
            cost = (130.0 + 1.0 * n) if eng == "dve" else (300.0 + 1.8 * n)
            return P.op(eng, lambda e: getattr(e, method)(**kw), reads, writes, cost=cost)

        def DVE(method, reads, writes, **kw):
            return ENG("dve", method, reads, writes, **kw)

        def POOL(method, reads, writes, **kw):
            return ENG("pool", method, reads, writes, **kw)

        def DMA(out, in_, reads, writes, dsem):
            nbytes = out.shape[0] * fsz(out) * 4
            return P.op("sp", lambda e: e.dma_start(out=out, in_=in_), reads, writes, dsem=dsem, cost=120.0, fin=2000.0 + nbytes / 150.0)

        psrot = Rot([Slot(t[:]) for t in psb])
        ptr = Slot(ptr_t[:])

        def PS():
            return psrot.next()

        G = Mem(0, 28 * KB)
        cmat = G.alloc([128, 6, 128], F32)
        identb = G.alloc([128, 128], BF16)
        onesel2 = G.alloc([128, 2, 2], BF16)
        sel2 = G.alloc([2, 2, 128], F32)
        colsT = G.alloc([128, 32], F32)
        hoff = G.alloc([128, 1], F32)
        mhalf4 = G.alloc([128, 4], F32)
        mhalf = mhalf4[:, 0:1]
        statv = G.alloc([128, 64], F32)
        junk = G.alloc([128, 1024], BF16)
        stage = Rot([Slot(G.alloc([128, 1024], F32), P.dma_sem("st%d" % i)) for i in range(2)])
        xts = Rot([Slot(G.alloc([128, 1024], F32), P.dma_sem("xt%d" % i)) for i in range(2)])
        hbs = Rot([Slot(G.alloc([128, 1024], BF16)) for i in range(2)])
        LT = cmat[:, 1, :]
        UT = cmat[:, 2, :]
        CM = cmat[:, 3, :]
        b_const = Buf()
        b_junk = Buf()
        stat_i = [0]

        def stat_col():
            c = stat_i[0] % 64
            stat_i[0] += 1
            return statv[:, c:c + 1], Buf()

        dc = P.dma_sem("const")
        DMA(cmat, cmat_d, [], [b_const], dc)
        DMA(sel2, sel2_d, [], [b_const], dc)
        DMA(colsT, cols_d, [], [b_const], dc)
        DMA(hoff, hoff_d, [], [b_const], dc)
        DVE("tensor_copy", [b_const], [b_const], out=identb, in_=cmat[:, 0, :])
        DVE("tensor_copy", [b_const], [b_const], out=onesel2, in_=cmat[:, 4, 0:4].rearrange("p (a b) -> p a b", a=2))
        POOL("memset", [], [b_const], ap=mhalf4, constant=-0.5)

        def col(i, kc):
            return colsT[:, i * 8 + kc: i * 8 + kc + 1]

        def load_w(dst_fn, dram, row0, nk, col0, ncols, scale_i, dst_bufs):
            for kc in range(nk):
                for c0 in range(0, ncols, 1024):
                    cn = min(1024, ncols - c0)
                    s = stage.next()
                    DMA(s.ap[:, 0:cn], dram[row0 + kc * 128: row0 + (kc + 1) * 128, col0 + c0: col0 + c0 + cn],
                        [], [s.buf], s.sem)
                    sc = col(scale_i, kc) if scale_i is not None else 1.0
                    POOL("tensor_scalar", [s.buf, b_const], [dst_bufs[kc]], out=dst_fn(kc, c0, cn), in0=s.ap[:, 0:cn],
                         scalar1=sc, scalar2=1.0, op0=ALU.mult, op1=ALU.mult)

        def rstd_of(src_ap, src_bufs, n):
            ss, bss = stat_col()
            ACT(junk[:, 0:n], src_ap, AF.Square, src_bufs, [b_junk, bss], accum_out=ss)
            vv, bvv = stat_col()
            POOL("tensor_scalar", [bss], [bvv], out=vv, in0=ss, scalar1=1.0 / n, scalar2=EPS, op0=ALU.mult, op1=ALU.add)
            rs, brs = stat_col()
            POOL("tensor_tensor", [bvv, b_const], [brs], out=rs, in0=vv, in1=mhalf, op=ALU.pow)
            return rs, brs

        def norm_to_hT(src_ap, src_bufs, hT_out, hT_bufs):
            rs, brs = rstd_of(src_ap, src_bufs, 1024)
            hb = hbs.next()
            DVE("tensor_scalar", list(src_bufs) + [brs], [hb.buf], out=hb.ap, in0=src_ap, scalar1=rs, scalar2=None,
                op0=ALU.mult)

            def fn(e):
                ins = None
                for kc in range(8):
                    ins = e.transpose(ptr.ap[:, kc, :], hb.ap[:, kc * 128:(kc + 1) * 128], identb)
                return ins
            P.op("pe", fn, [hb.buf, b_const], [ptr.buf], cost=650.0)
            ACT(hT_out, ptr.ap, AF.Copy, [ptr.buf], hT_bufs)

        def load_x(row0):
            s = xts.next()
            DMA(s.ap, xe[row0:row0 + 128, :], [], [s.buf], s.sem)
            return s

        OhT = carve(28 * KB, [128, 4, 2048], BF16)
        b_OhT = Buf()
        ogT = carve(44 * KB, [128, 8, 2048], BF16)
        b_ogT = [Buf() for _ in range(16)]
        mixT = carve(76 * KB, [128, 8, 2048], BF16)
        b_mixT = [Buf() for _ in range(4)]
        x1 = carve(108 * KB, [128, 16, 1024], F32)
        b_x1 = [Buf() for _ in range(16)]
        h2T = carve(28 * KB, [128, 8, 2048], BF16)
        b_h2T = [Buf() for _ in range(16)]

        hTh = carve(44 * KB, [128, 8, 2048], BF16)
        hTo = carve(76 * KB, [128, 8, 2048], BF16)
        b_hT = [Buf() for _ in range(32)]
        for t in range(32):
            s = load_x(4096 + t * 128)
            dst = hTh if t < 16 else hTo
            tt = t % 16
            norm_to_hT(s.ap, [s.buf], dst[:, :, tt * 128:(tt + 1) * 128], [b_hT[t]])
        b_hTall = b_hT

        MA = Mem(108 * KB, 200 * KB)
        NT = MA.alloc([128, 2, 2048], F32)
        ST = MA.alloc([2, 2048], F32)
        biasT = MA.alloc([128, 3, 512], F32)
        expbN = MA.alloc([128, 3, 512], F32)
        expbH = MA.alloc([128, 3, 512], F32)
        wA = [MA.alloc([128, 8, 3, 256], BF16) for _ in range(2)]
        b_wA = [[Buf() for _ in range(8)] for _ in range(2)]
        KTs = Rot([Slot(MA.alloc([128, 2, 512], BF16)) for _ in range(3)])
        QTs = Rot([Slot(MA.alloc([128, 2, 512], BF16)) for _ in range(2)])
        Vs = Rot([Slot(MA.alloc([128, 4, 256], BF16)) for _ in range(3)])
        Efs = Rot([Slot(MA.alloc([128, 512], F32)) for _ in range(2)])
        ETs = Rot([Slot(MA.alloc([128, 512], BF16)) for _ in range(2)])
        b_bias = Buf()
        b_exp = Buf()
        scale_att = 128.0 ** -0.5
        dbias = P.dma_sem("bias")
        wslot = 0
        b_NT = Buf()
        b_ST = Buf()
        for hp in range(2):
            DMA(biasT, biasm_d[:, :, hp, :], [], [b_bias], dbias)
            ACT(expbN, biasT, AF.Exp, [b_bias], [b_exp])
            ACT(expbH, biasT, AF.Exp, [b_bias], [b_exp])
            b4 = biasT.rearrange("p g (h j q) -> p g h j q", h=2, j=2)
            e4 = expbH.rearrange("p g (h j q) -> p g h j q", h=2, j=2)
            for g in range(3):
                ACT(e4[:, g, :, 0, :], b4[:, g, :, 0, :], AF.Exp, [b_bias, b_const], [b_exp], bias=hoff)
            DVE("memset", [], [b_NT], ap=NT, constant=0.0)
            DVE("memset", [], [b_ST], ap=ST, constant=0.0)
            for g in range(3):
                dil = (1, 4, 16)[g]
                nbk = 16 // dil
                wa = wA[wslot % 2]
                bwa = b_wA[wslot % 2]
                wslot += 1
                base = 3088 + g * 1536
                for kc in range(8):
                    s = stage.next()
                    src = w_in[kc * 128:(kc + 1) * 128, base:base + 1536].rearrange("p (c x) -> p c x", c=3)[:, :, hp * 256:(hp + 1) * 256]
                    sv = s.ap[:, 0:768].rearrange("p (c x) -> p c x", c=3)
                    DMA(sv, src, [], [s.buf], s.sem)
                    POOL("tensor_scalar", [s.buf, b_const], [bwa[kc]], out=wa[:, kc, :, :], in0=sv,
                         scalar1=col(0, kc), scalar2=1.0, op0=ALU.mult, op1=ALU.mult)
                b_NTg = Buf()
                b_STg = Buf()
                for r in range(dil):
                    segs = [("h", SEG - dil * 128 + r, 1)] + [("o", r + dil * 128 * n0, min(4, nbk - n0)) for n0 in range(0, nbk, 4)]
                    prev = None
                    for si, (src, start, nb) in enumerate(segs):
                        hTs = hTh if src == "h" else hTo
                        hb_ = b_hT[0:16] if src == "h" else b_hT[16:32]
                        cols = sl(start, 128 * nb, dil)
                        kt = KTs.next()
                        for hh in range(2):
                            ps = PS()
                            PE([(ps.ap[:, 0:nb * 128], wa[:, kc, 1, hh * 128:(hh + 1) * 128], hTs[:, kc, cols], kc == 0, kc == 7)
                                for kc in range(8)], hb_ + bwa, [ps.buf])
                            ACT(kt.ap[:, hh, 0:nb * 128], ps.ap[:, 0:nb * 128], AF.Copy, [ps.buf], [kt.buf])
                        qt = None
                        if src == "o":
                            qt = QTs.next()
                            for hh in range(2):
                                ps = PS()
                                PE([(ps.ap[:, 0:nb * 128], wa[:, kc, 0, hh * 128:(hh + 1) * 128], hTs[:, kc, cols], kc == 0, kc == 7)
                                    for kc in range(8)], hb_ + bwa, [ps.buf])
                                DVE("tensor_copy", [ps.buf], [qt.buf], out=qt.ap[:, hh, 0:nb * 128], in_=ps.ap[:, 0:nb * 128])
                        vs = Vs.next()
                        for b in range(nb):
                            bc = sl(start + dil * 128 * b, 128, dil)
                            ps = PS()
                            PE([(ps.ap[:, 0:256], hTs[:, kc, bc], wa[:, kc, 2, :], kc == 0, kc == 7) for kc in range(8)],
                               hb_ + bwa, [ps.buf])
                            if b % 2 == 0:
                                ACT(vs.ap[:, b, :], ps.ap[:, 0:256], AF.Copy, [ps.buf], [vs.buf])
                            else:
                                DVE("tensor_copy", [ps.buf], [vs.buf], out=vs.ap[:, b, :], in_=ps.ap[:, 0:256])
                        if src == "o":
                            for b in range(nb):
                                if b == 0:
                                    pk, pv, pb = prev
                                else:
                                    pk, pv, pb = kt, vs, b - 1
                                first = (si == 1 and b == 0)
                                blk = [(pk, pv, pb), (kt, vs, b)]
                                sp_ = PS()
                                s4 = sp_.ap.rearrange("p (h j q) -> p h j q", h=2, j=2)
                                PE([(s4[:, hh, j, :], blk[j][0].ap[:, hh, blk[j][2] * 128:(blk[j][2] + 1) * 128],
                                     qt.ap[:, hh, b * 128:(b + 1) * 128], True, True) for hh in range(2) for j in range(2)],
                                   [pk.buf, kt.buf, qt.buf], [sp_.buf])
                                ef = Efs.next()
                                ACT(ef.ap, sp_.ap, AF.Exp, [sp_.buf], [ef.buf], scale=scale_att)
                                et = ETs.next()
                                DVE("tensor_tensor", [ef.buf, b_exp], [et.buf], out=et.ap, in0=ef.ap,
                                    in1=(expbH if first else expbN)[:, g, :], op=ALU.mult)
                                e4t = et.ap.rearrange("p (h j q) -> p h j q", h=2, j=2)
                                np_ = PS()
                                mms = []
                                for hh in range(2):
                                    for j in range(2):
                                        mms.append((np_.ap[:, hh * 128:(hh + 1) * 128],
                                                    blk[j][1].ap[:, blk[j][2], hh * 128:(hh + 1) * 128], e4t[:, hh, j, :], j == 0, j == 1))
                                k = 0
                                for hh in range(2):
                                    for j in range(2):
                                        mms.append((np_.ap[0:2, 256:384], onesel2[:, hh, :], e4t[:, hh, j, :], k == 0, k == 3))
                                        k += 1
                                PE(mms, [pv.buf, vs.buf, et.buf, b_const], [np_.buf])
                                nat = sl(start + dil * 128 * b, 128, dil)
                                fresh = Buf()
                                DVE("tensor_tensor", [np_.buf, b_NT], [fresh], out=NT[:, :, nat], in0=NT[:, :, nat],
                                    in1=np_.ap[:, 0:256].rearrange("p (h q) -> p h q", h=2), op=ALU.add)
                                b_NTg.w = fresh.w
                                fresh2 = Buf()
                                DVE("tensor_tensor", [np_.buf, b_ST], [fresh2], out=ST[:, nat], in0=ST[:, nat],
                                    in1=np_.ap[0:2, 256:384], op=ALU.add)
                                b_STg.w = fresh2.w
                        prev = (kt, vs, nb - 1)
                b_NT = b_NTg
                b_ST = b_STg
            DVE("reciprocal", [b_ST], [b_ST], out=ST, in_=ST)
            for hh in range(2):
                for tb in range(4):
                    ps = PS()
                    PE([(ps.ap, sel2[:, hh, :], ST[:, tb * 512:(tb + 1) * 512], True, True)], [b_ST, b_const], [ps.buf])
                    DVE("tensor_tensor", [ps.buf, b_NT], [b_OhT], out=OhT[:, hp * 2 + hh, tb * 512:(tb + 1) * 512],
                        in0=NT[:, hh, tb * 512:(tb + 1) * 512], in1=ps.ap, op=ALU.mult)
        P.barrier()

        MG = Mem(76 * KB, 200 * KB)
        wG = MG.alloc([128, 8, 3088], BF16)
        b_wG = [Buf() for _ in range(8)]
        wa2 = MG.alloc([32, 512], BF16)
        b_wa2 = Buf()
        CM4 = MG.alloc([128, 4, 128], F32)
        hTt = [Slot(MG.alloc([128, 8, 128], BF16)) for _ in range(2)]
        haT = [Slot(MG.alloc([32, 128], BF16)) for _ in range(2)]
        ezs = [Slot(MG.alloc([128, 512], F32)) for _ in range(2)]
        sps = [Slot(MG.alloc([128, 512], F32)) for _ in range(2)]
        wex = [Slot(MG.alloc([128, 512], F32)) for _ in range(2)]
        kouts = [Slot(MG.alloc([128, 512], BF16)) for _ in range(2)]
        vbs = [Slot(MG.alloc([128, 1024], BF16)) for _ in range(2)]
        e1s = [Slot(MG.alloc([128, 4, 128], F32)) for _ in range(2)]
        e2s = [Slot(MG.alloc([128, 4, 128], F32)) for _ in range(2)]
        qdA = [Slot(MG.alloc([128, 4, 128], BF16)) for _ in range(2)]
        qdB = [Slot(MG.alloc([128, 4, 128], BF16)) for _ in range(2)]
        kinT = [Slot(MG.alloc([128, 4, 128], BF16)) for _ in range(2)]
        attnT = [Slot(MG.alloc([128, 4, 128], BF16)) for _ in range(2)]
        ers = [Slot(MG.alloc([128, 1024], F32)) for _ in range(2)]
        sil = [Slot(MG.alloc([128, 1024], F32)) for _ in range(2)]
        Sst = MG.alloc([128, 4, 256], F32)
        SbA = MG.alloc([128, 4, 256], BF16)
        SbB = MG.alloc([128, 4, 256], BF16)
        ogb = Slot(MG.alloc([128, 1024], BF16))
        ssh = MG.alloc([128, 8], F32)
        b_S = [Buf() for _ in range(4)]
        b_SbA = [Buf() for _ in range(4)]
        b_SbB = [Buf() for _ in range(4)]

        load_w(lambda kc, c0, cn: wG[:, kc, c0:c0 + cn], w_in, 0, 8, 0, 3088, 0, b_wG)
        s = stage.next()
        DMA(s.ap[0:32, 0:512], wa2_d, [], [s.buf], s.sem)
        POOL("tensor_copy", [s.buf], [b_wa2], out=wa2, in_=s.ap[0:32, 0:512])
        for hh in range(4):
            DVE("tensor_copy", [b_const], [b_const], out=CM4[:, hh, :], in_=CM)
        for i in range(2):
            POOL("memset", [], [haT[i].buf], ap=haT[i].ap, constant=1.0)
            POOL("memset", [], [qdA[i].buf], ap=qdA[i].ap, constant=0.0)
            POOL("memset", [], [qdB[i].buf], ap=qdB[i].ap, constant=0.0)
        DVE("memset", [], b_S, ap=Sst, constant=0.0)
        POOL("memset", [], b_SbA, ap=SbA, constant=0.0)

        gl = {}

        def gla_stage1(t):
            own = t >= 48
            i = t % 2
            xs = load_x(t * 128)
            ht = hTt[i]
            norm_to_hT(xs.ap, [xs.buf], ht.ap, [ht.buf])
            rb = [ht.buf] + b_wG
            kps = PS()
            PE([(kps.ap, ht.ap[:, kc, :], wG[:, kc, 512:1024], kc == 0, kc == 7) for kc in range(8)], rb, [kps.buf])
            hps = PS()
            PE([(hps.ap[0:16, 0:128], wG[:, kc, 3072:3088], ht.ap[:, kc, :], kc == 0, kc == 7) for kc in range(8)], rb, [hps.buf])
            ACT(haT[i].ap[0:16, :], hps.ap[0:16, 0:128], AF.Copy, [hps.buf], [haT[i].buf])
            zps = PS()
            PE([(zps.ap, haT[i].ap, wa2, True, True)], [haT[i].buf, b_wa2], [zps.buf])
            ACT(ezs[i].ap, zps.ap, AF.Exp, [zps.buf], [ezs[i].buf], scale=-1.0)
            ACT(sps[i].ap, ezs[i].ap, AF.Ln, [ezs[i].buf], [sps[i].buf], bias=1.0)
            vp = [PS(), PS()]
            for h2 in range(2):
                PE([(vp[h2].ap, ht.ap[:, kc, :], wG[:, kc, 1024 + h2 * 512:1536 + h2 * 512], kc == 0, kc == 7) for kc in range(8)],
                   rb, [vp[h2].buf])
            ACT(vbs[i].ap[:, 0:512], vp[0].ap, AF.Copy, [vp[0].buf], [vbs[i].buf])
            DVE("tensor_copy", [vp[1].buf], [vbs[i].buf], out=vbs[i].ap[:, 512:1024], in_=vp[1].ap)
            dps = PS()
            PE([(dps.ap, UT, sps[i].ap, True, True)], [sps[i].buf, b_const], [dps.buf])
            ACT(wex[i].ap, dps.ap, AF.Exp, [dps.buf], [wex[i].buf])
            DVE("tensor_tensor", [kps.buf, wex[i].buf], [kouts[i].buf], out=kouts[i].ap, in0=kps.ap, in1=wex[i].ap, op=ALU.mult)
            nps = PS()
            n4 = nps.ap.rearrange("p (h q) -> p h q", h=4)
            PE([(n4[:, hh, :], sps[i].ap[:, hh * 128:(hh + 1) * 128], LT, True, True) for hh in range(4)],
               [sps[i].buf, b_const], [nps.buf])
            ACT(e1s[i].ap, n4, AF.Exp, [nps.buf], [e1s[i].buf], scale=-1.0)
            if not own:
                return
            ACT(e2s[i].ap, n4, AF.Exp, [nps.buf], [e2s[i].buf])
            qps = PS()
            q4 = qps.ap.rearrange("p (h q) -> p h q", h=4)
            PE([(q4[:, hh, :], wG[:, kc, hh * 128:(hh + 1) * 128], ht.ap[:, kc, :], kc == 0, kc == 7)
                for hh in range(4) for kc in range(8)], rb, [qps.buf])
            DVE("scalar_tensor_tensor", [qps.buf, e1s[i].buf], [qdA[i].buf], out=qdA[i].ap[:, :, 0:64], in0=q4[:, :, 0:64],
                scalar=128.0 ** -0.5, in1=e1s[i].ap[:, :, 0:64], op0=ALU.mult, op1=ALU.mult)
            DVE("scalar_tensor_tensor", [qps.buf, e1s[i].buf], [qdB[i].buf], out=qdB[i].ap[:, :, 64:128], in0=q4[:, :, 64:128],
                scalar=128.0 ** -0.5, in1=e1s[i].ap[:, :, 64:128], op0=ALU.mult, op1=ALU.mult)
            ktp = PS()
            k4 = ktp.ap.rearrange("p (h q) -> p h q", h=4)
            PE([(k4[:, hh, :], wG[:, kc, 512 + hh * 128:512 + (hh + 1) * 128], ht.ap[:, kc, :], kc == 0, kc == 7)
                for hh in range(4) for kc in range(8)], rb, [ktp.buf])
            DVE("tensor_tensor", [ktp.buf, e2s[i].buf], [kinT[i].buf], out=kinT[i].ap, in0=k4, in1=e2s[i].ap, op=ALU.mult)
            aps = PS()
            a4 = aps.ap.rearrange("p (h q) -> p h q", h=4)
            mms = []
            for hh in range(4):
                mms.append((a4[:, hh, 0:64], kinT[i].ap[:, hh, :], qdA[i].ap[:, hh, 0:64], True, True))
                mms.append((a4[:, hh, 64:128], kinT[i].ap[:, hh, :], qdB[i].ap[:, hh, 64:128], True, True))
            PE(mms, [kinT[i].buf, qdA[i].buf, qdB[i].buf], [aps.buf])
            DVE("tensor_tensor", [aps.buf, b_const], [attnT[i].buf], out=attnT[i].ap, in0=a4, in1=CM4, op=ALU.mult)
            rp = [PS(), PS()]
            for h2 in range(2):
                PE([(rp[h2].ap, ht.ap[:, kc, :], wG[:, kc, 2048 + h2 * 512:2560 + h2 * 512], kc == 0, kc == 7) for kc in range(8)],
                   rb, [rp[h2].buf])
                ACT(ers[i].ap[:, h2 * 512:(h2 + 1) * 512], rp[h2].ap, AF.Exp, [rp[h2].buf], [ers[i].buf], scale=-1.0)
            ACT(ers[i].ap, ers[i].ap, AF.Ln, [ers[i].buf], [ers[i].buf], bias=1.0)
            ACT(ers[i].ap, ers[i].ap, AF.Exp, [ers[i].buf], [ers[i].buf], scale=-1.0)
            for h2 in range(2):
                DVE("tensor_tensor", [rp[h2].buf, ers[i].buf], [sil[i].buf], out=sil[i].ap[:, h2 * 512:(h2 + 1) * 512],
                    in0=rp[h2].ap, in1=ers[i].ap[:, h2 * 512:(h2 + 1) * 512], op=ALU.mult)

        def gla_stage2(t):
            own = t >= 48
            i = t % 2
            tt = t - 48
            last_prefix = (t == 47)
            ko = kouts[i]
            vb = vbs[i]
            e1 = e1s[i]
            if own:
                oP = [PS(), PS()]
                for pr in range(2):
                    mms = []
                    for hq in range(2):
                        hh = pr * 2 + hq
                        mms.append((oP[pr].ap[:, hq * 256:(hq + 1) * 256], attnT[i].ap[:, hh, :], vb.ap[:, hh * 256:(hh + 1) * 256],
                                    hq == 0, False, True))
                    for hq in range(2):
                        hh = pr * 2 + hq
                        mms.append((oP[pr].ap[:, hq * 256:(hq + 1) * 256], qdA[i].ap[:, hh, :], SbA[:, hh, :], False, False, True))
                    PE(mms, [attnT[i].buf, vb.buf, qdA[i].buf] + b_SbA[pr * 2:pr * 2 + 2], [oP[pr].buf])
            for c in range(2):
                uP = [PS(), PS()]
                for pr in range(2):
                    PE([(uP[pr].ap[:, hq * 256:(hq + 1) * 256], ko.ap[c * 64:(c + 1) * 64, (pr * 2 + hq) * 128:(pr * 2 + hq + 1) * 128],
                         vb.ap[c * 64:(c + 1) * 64, (pr * 2 + hq) * 256:(pr * 2 + hq + 1) * 256], True, True) for hq in range(2)],
                       [ko.buf, vb.buf], [uP[pr].buf])
                for hh in range(4):
                    pr, hq = hh // 2, hh % 2
                    DVE("scalar_tensor_tensor", [uP[pr].buf, e1.buf, b_S[hh]], [b_S[hh]], out=Sst[:, hh, :], in0=Sst[:, hh, :],
                        scalar=e1.ap[:, hh, c * 64 + 63:c * 64 + 64], in1=uP[pr].ap[:, hq * 256:(hq + 1) * 256],
                        op0=ALU.mult, op1=ALU.add)
                    if c == 0 and own:
                        ACT(SbB[:, hh, :], Sst[:, hh, :], AF.Copy, [b_S[hh]], [b_SbB[hh]])
                    if c == 1 and (own or last_prefix):
                        ACT(SbA[:, hh, :], Sst[:, hh, :], AF.Copy, [b_S[hh]], [b_SbA[hh]])
                if c == 0 and own:
                    for pr in range(2):
                        PE([(oP[pr].ap[:, hq * 256:(hq + 1) * 256], qdB[i].ap[:, pr * 2 + hq, :], SbB[:, pr * 2 + hq, :],
                             False, hq == 1, True) for hq in range(2)],
                           [qdB[i].buf] + b_SbB[pr * 2:pr * 2 + 2], [oP[pr].buf])
            if not own:
                return
            bssh = Buf()
            for hh in range(4):
                pr, hq = hh // 2, hh % 2
                ACT(junk[:, 0:256], oP[pr].ap[:, hq * 256:(hq + 1) * 256], AF.Square, [oP[pr].buf], [b_junk, bssh],
                    accum_out=ssh[:, hh:hh + 1])
            POOL("tensor_scalar", [bssh], [bssh], out=ssh[:, 4:8], in0=ssh[:, 0:4], scalar1=1.0 / 256, scalar2=EPS,
                 op0=ALU.mult, op1=ALU.add)
            POOL("tensor_tensor", [bssh, b_const], [bssh], out=ssh[:, 4:8], in0=ssh[:, 4:8], in1=mhalf4,
                 op=ALU.pow)
            for hh in range(4):
                pr, hq = hh // 2, hh % 2
                DVE("scalar_tensor_tensor", [oP[pr].buf, bssh, sil[i].buf], [ogb.buf], out=ogb.ap[:, hh * 256:(hh + 1) * 256],
                    in0=oP[pr].ap[:, hq * 256:(hq + 1) * 256], scalar=ssh[:, 4 + hh:5 + hh], in1=sil[i].ap[:, hh * 256:(hh + 1) * 256],
                    op0=ALU.mult, op1=ALU.mult)

            def fn(e):
                ins = None
                for kc in range(8):
                    ins = e.transpose(ptr.ap[:, kc, :], ogb.ap[:, kc * 128:(kc + 1) * 128], identb)
                return ins
            P.op("pe", fn, [ogb.buf, b_const], [ptr.buf], cost=650.0)
            ACT(ogT[:, :, tt * 128:(tt + 1) * 128], ptr.ap, AF.Copy, [ptr.buf], [b_ogT[tt]])

        gla_stage1(0)
        for t in range(64):
            if t + 1 < 64:
                gla_stage1(t + 1)
            gla_stage2(t)
        P.barrier()

        hTo2 = carve(108 * KB, [128, 8, 2048], BF16)
        b_hTo2 = [Buf() for _ in range(16)]
        for t in range(16):
            s = load_x(6144 + t * 128)
            norm_to_hT(s.ap, [s.buf], hTo2[:, :, t * 128:(t + 1) * 128], [b_hTo2[t]])
        MM = Mem(140 * KB, 200 * KB)
        wog = MM.alloc([128, 8, 512], BF16)
        woa = MM.alloc([128, 4, 512], BF16)
        wgA = MM.alloc([128, 8, 512], BF16)
        wgB = MM.alloc([128, 8, 512], BF16)
        b_wog = [Buf() for _ in range(8)]
        b_woa = [Buf() for _ in range(4)]
        b_wgA = [Buf() for _ in range(8)]
        b_wgB = [Buf() for _ in range(8)]
        sgs = Rot([Slot(MM.alloc([128, 512], F32)) for _ in range(4)])
        tms = Rot([Slot(MM.alloc([128, 512], F32)) for _ in range(2)])
        for fo in range(2):
            load_w(lambda kc, c0, cn: wog[:, kc, c0:c0 + cn], wog_d, 0, 8, fo * 512, 512, 3, b_wog)
            load_w(lambda kc, c0, cn: woa[:, kc, c0:c0 + cn], woa_d, 0, 4, fo * 512, 512, None, b_woa)
            load_w(lambda kc, c0, cn: wgA[:, kc, c0:c0 + cn], w_in, 0, 8, 7696 + fo * 512, 512, 0, b_wgA)
            load_w(lambda kc, c0, cn: wgB[:, kc, c0:c0 + cn], w_in, 0, 8, 8720 + fo * 512, 512, 0, b_wgB)
            for tb in range(4):
                tk = slice(tb * 512, (tb + 1) * 512)
                for fc in range(4):
                    fs = slice(fc * 128, (fc + 1) * 128)
                    ga = PS()
                    PE([(ga.ap, wgA[:, kc, fs], hTo2[:, kc, tk], kc == 0, kc == 7) for kc in range(8)],
                       b_wgA + b_hTo2[tb * 4:tb * 4 + 4], [ga.buf])
                    sa = sgs.next()
                    ACT(sa.ap, ga.ap, AF.Sigmoid, [ga.buf], [sa.buf])
                    gb = PS()
                    PE([(gb.ap, wgB[:, kc, fs], hTo2[:, kc, tk], kc == 0, kc == 7) for kc in range(8)],
                       b_wgB + b_hTo2[tb * 4:tb * 4 + 4], [gb.buf])
                    sb_ = sgs.next()
                    ACT(sb_.ap, gb.ap, AF.Sigmoid, [gb.buf], [sb_.buf])
                    yg = PS()
                    PE([(yg.ap, wog[:, kc, fs], ogT[:, kc, tk], kc == 0, kc == 7) for kc in range(8)],
                       b_wog + b_ogT[tb * 4:tb * 4 + 4], [yg.buf])
                    t1 = tms.next()
                    DVE("tensor_tensor", [yg.buf, sa.buf], [t1.buf], out=t1.ap, in0=yg.ap, in1=sa.ap, op=ALU.mult)
                    ya = PS()
                    PE([(ya.ap, woa[:, hh, fs], OhT[:, hh, tk], hh == 0, hh == 3) for hh in range(4)],
                       b_woa + [b_OhT], [ya.buf])
                    t2 = tms.next()
                    DVE("tensor_tensor", [ya.buf, sb_.buf], [t2.buf], out=t2.ap, in0=ya.ap, in1=sb_.ap, op=ALU.mult)
                    DVE("tensor_tensor", [t1.buf, t2.buf], [b_mixT[tb]], out=mixT[:, fo * 4 + fc, tk], in0=t1.ap, in1=t2.ap, op=ALU.add)
        P.barrier()

        wout = carve(60 * KB, [128, 8, 1024], BF16)
        b_wout = [Buf() for _ in range(8)]
        load_w(lambda kc, c0, cn: wout[:, kc, c0:c0 + cn], wout_d, 0, 8, 0, 1024, None, b_wout)
        for t in range(16):
            s = load_x(6144 + t * 128)
            for h2 in range(2):
                ps = PS()
                PE([(ps.ap, mixT[:, kc, t * 128:(t + 1) * 128], wout[:, kc, h2 * 512:(h2 + 1) * 512], kc == 0, kc == 7) for kc in range(8)],
                   b_mixT + b_wout, [ps.buf])
                DVE("tensor_tensor", [ps.buf, s.buf], [b_x1[t]], out=x1[:, t, h2 * 512:(h2 + 1) * 512], in0=ps.ap,
                    in1=s.ap[:, h2 * 512:(h2 + 1) * 512], op=ALU.add)
            norm_to_hT(x1[:, t, :], [b_x1[t]], h2T[:, :, t * 128:(t + 1) * 128], [b_h2T[t]])
        P.barrier()

        MF = Mem(60 * KB, 108 * KB)
        w1c = [MF.alloc([128, 8, 512], BF16) for _ in range(2)]
        w2c = [MF.alloc([128, 4, 1024], BF16) for _ in range(2)]
        b_w1c = [[Buf() for _ in range(8)] for _ in range(2)]
        b_w2c = [[Buf() for _ in range(4)] for _ in range(2)]
        uTs = Rot([Slot(MF.alloc([128, 4, 512], BF16)) for _ in range(2)])
        rls = Rot([Slot(MF.alloc([128, 512], F32)) for _ in range(2)])
        for ffg in range(8):
            wi = ffg % 2
            w1 = w1c[wi]
            w2 = w2c[wi]
            load_w(lambda kc, c0, cn: w1[:, kc, c0:c0 + cn], w1_d, 0, 8, ffg * 512, 512, 1, b_w1c[wi])
            load_w(lambda kc, c0, cn: w2[:, kc, c0:c0 + cn], w2_d, ffg * 512, 4, 0, 1024, None, b_w2c[wi])
            for tb in range(4):
                tk = slice(tb * 512, (tb + 1) * 512)
                ut = uTs.next()
                for j in range(4):
                    ps = PS()
                    PE([(ps.ap, w1[:, kc, j * 128:(j + 1) * 128], h2T[:, kc, tk], kc == 0, kc == 7) for kc in range(8)],
                       b_w1c[wi] + b_h2T[tb * 4:tb * 4 + 4], [ps.buf])
                    rl = rls.next()
                    ACT(rl.ap, ps.ap, AF.Relu, [ps.buf], [rl.buf])
                    DVE("tensor_tensor", [rl.buf], [ut.buf], out=ut.ap[:, j, :], in0=rl.ap, in1=rl.ap, op=ALU.mult)
                for tt in range(4):
                    t = tb * 4 + tt
                    for h2 in range(2):
                        ps = PS()
                        PE([(ps.ap, ut.ap[:, j, tt * 128:(tt + 1) * 128], w2[:, j, h2 * 512:(h2 + 1) * 512], j == 0, j == 3) for j in range(4)],
                           [ut.buf] + b_w2c[wi], [ps.buf])
                        DVE("tensor_tensor", [ps.buf, b_x1[t]], [b_x1[t]], out=x1[:, t, h2 * 512:(h2 + 1) * 512],
                            in0=x1[:, t, h2 * 512:(h2 + 1) * 512], in1=ps.ap, op=ALU.add)
        P.barrier()

        MP = Mem(28 * KB, 108 * KB)
        wpg = MP.alloc([128, 8, 1024], BF16)
        wpp = MP.alloc([128, 2, 1024], BF16)
        lnfb = MP.alloc([128, 1024], F32)
        b_wpg = [Buf() for _ in range(8)]
        b_wpp = [Buf() for _ in range(2)]
        b_lnf = Buf()
        h3s = Rot([Slot(MP.alloc([128, 8, 128], BF16)) for _ in range(2)])
        pfs = Rot([Slot(MP.alloc([128, 256], F32), P.dma_sem("pf%d" % i)) for i in range(2)])
        pbs = Rot([Slot(MP.alloc([128, 256], BF16)) for _ in range(2)])
        pTs = Rot([Slot(MP.alloc([128, 2, 128], BF16)) for _ in range(2)])
        sg2 = Rot([Slot(MP.alloc([128, 1024], F32)) for _ in range(2)])
        osb = Rot([Slot(MP.alloc([128, 1024], F32), P.dma_sem("os%d" % i)) for i in range(2)])
        load_w(lambda kc, c0, cn: wpg[:, kc, c0:c0 + cn], wpg_d, 0, 8, 0, 1024, 2, b_wpg)
        load_w(lambda kc, c0, cn: wpp[:, kc, c0:c0 + cn], wpp_d, 0, 2, 0, 1024, None, b_wpp)
        dl = P.dma_sem("lnf")
        DMA(lnfb, lnf_d.partition_broadcast(128), [], [b_lnf], dl)
        out_toks = []
        for t in range(16):
            h3 = h3s.next()
            norm_to_hT(x1[:, t, :], [b_x1[t]], h3.ap, [h3.buf])
            pf = pfs.next()
            DMA(pf.ap, pin[t * 128:(t + 1) * 128, :], [], [pf.buf], pf.sem)
            pb = pbs.next()
            DVE("tensor_copy", [pf.buf], [pb.buf], out=pb.ap, in_=pf.ap)

            def fn(e, pb=pb):
                ins = None
                for c in range(2):
                    ins = e.transpose(ptr.ap[:, c, :], pb.ap[:, c * 128:(c + 1) * 128], identb)
                return ins
            P.op("pe", fn, [pb.buf, b_const], [ptr.buf], cost=200.0)
            pT = pTs.next()
            ACT(pT.ap, ptr.ap[:, 0:2, :], AF.Copy, [ptr.buf], [pT.buf])
            sg = sg2.next()
            for h2 in range(2):
                hs = slice(h2 * 512, (h2 + 1) * 512)
                gp = PS()
                PE([(gp.ap, h3.ap[:, kc, :], wpg[:, kc, hs], kc == 0, kc == 7) for kc in range(8)], [h3.buf] + b_wpg, [gp.buf])
                ACT(sg.ap[:, hs], gp.ap, AF.Sigmoid, [gp.buf], [sg.buf])
                pp = PS()
                PE([(pp.ap, pT.ap[:, c, :], wpp[:, c, hs], c == 0, c == 1) for c in range(2)], [pT.buf] + b_wpp, [pp.buf])
                DVE("tensor_tensor", [pp.buf, sg.buf], [sg.buf], out=sg.ap[:, hs], in0=sg.ap[:, hs], in1=pp.ap, op=ALU.mult)
            DVE("tensor_tensor", [sg.buf, b_x1[t]], [b_x1[t]], out=x1[:, t, :], in0=x1[:, t, :], in1=sg.ap, op=ALU.add)
            rs, brs = rstd_of(x1[:, t, :], [b_x1[t]], 1024)
            ob = osb.next()
            DVE("scalar_tensor_tensor", [b_x1[t], brs, b_lnf], [ob.buf], out=ob.ap, in0=x1[:, t, :], scalar=rs, in1=lnfb,
                op0=ALU.mult, op1=ALU.mult)
            out_toks.append(DMA(y[t * 128:(t + 1) * 128, :], ob.ap, [ob.buf], [], ob.sem))
        P.final_wait("sp", out_toks[-2:])
        P.run(block)
    return nc


def _t5_bucket(n):
    max_exact = 16
    nf = np.maximum(n, 1).astype(np.float32)
    large = max_exact + (np.log(nf / max_exact) / np.log(2048 / max_exact) * (32 - max_exact)).astype(np.int32)
    large = np.minimum(large, 31)
    return np.where(n < max_exact, n, large).astype(np.int32)


def _const_mats():
    m = np.arange(128)[:, None]
    t = np.arange(128)[None, :]
    same = (m // 64) == (t // 64)
    cm = np.zeros((128, 6, 128), np.float32)
    cm[:, 0, :] = np.eye(128)
    cm[:, 1, :] = np.where(same & (m <= t), 1.0 / 16, 0.0)
    cm[:, 2, :] = np.where(same & (m > t), -1.0 / 16, 0.0)
    cm[:, 3, :] = np.where(same & (m <= t), 1.0, 0.0)
    cm[:, 4, 0:4] = np.array([1, 0, 0, 1], np.float32)[None, :]
    sel2 = np.zeros((2, 2, 128), np.float32)
    sel2[0, 0, :] = 1.0
    sel2[1, 1, :] = 1.0
    return cm, sel2


def _bias_layout(rel_bias):
    k = np.arange(128)[:, None, None]
    j = np.arange(2)[None, :, None]
    q = np.arange(128)[None, None, :]
    delta = q - k + 128 * (1 - j)
    valid = (delta >= 0) & (delta <= 128)
    out = np.full((128, 3, 2, 2, 2, 128), NEGM, np.float32)
    for g, dil in enumerate((1, 4, 16)):
        bucket = _t5_bucket(np.maximum(delta, 0) * dil)
        for hp in range(2):
            for hh in range(2):
                tab = rel_bias[:, g * 4 + hp * 2 + hh]
                vals = tab[bucket]
                out[:, g, hp, hh] = np.where(valid, vals, NEGM)
    return out.reshape(128, 3, 2, 512)


_PROG = None


def kernel(x, p, ln1, w_in, w_a2, b_a, gla_gn, w_o_gla, w_o_attn, w_out, ln2, w_mlp1, w_mlp2, ln3, w_pp, w_pg,
           rel_bias, ln_f):
    global _PROG
    f = lambda a: np.ascontiguousarray(np.asarray(a, dtype=np.float32))
    x = f(x); p = f(p)
    cm, sel2 = _const_mats()
    cols = np.stack([f(ln1)[0], f(ln2)[0], f(ln3)[0], f(gla_gn)[0]]).reshape(4, 8, 128).transpose(2, 0, 1).reshape(128, 32)
    wa2aug = np.zeros((32, 512), np.float32)
    wa2aug[0:16] = f(w_a2)[0]
    wa2aug[16] = f(b_a)[0]
    shared = {
        "w_in": f(w_in)[0], "w_a2aug": wa2aug, "w_o_gla": f(w_o_gla)[0], "w_o_attn": f(w_o_attn)[0],
        "w_out": f(w_out)[0], "w_mlp1": f(w_mlp1)[0], "w_mlp2": f(w_mlp2)[0], "w_pp": f(w_pp)[0], "w_pg": f(w_pg)[0],
        "cols": np.ascontiguousarray(cols), "ln_f": f(ln_f), "biasm": _bias_layout(f(rel_bias)),
        "cmat": cm, "sel2": sel2,
    }
    in_maps = []
    for c in range(NCORES):
        b, j = c // 4, c % 4
        xe = np.zeros((8192, 1024), np.float32)
        n = SEG * (j + 1)
        xe[8192 - n:] = x[b, 0:n]
        m = dict(shared)
        m["xe"] = xe
        m["p"] = np.ascontiguousarray(p[0, b, j * SEG:(j + 1) * SEG])
        m["hoff"] = np.full((128, 1), NEGM if j == 0 else 0.0, np.float32)
        in_maps.append(m)
    if _PROG is None:
        _PROG = build_program()
    res = run_bass_kernel_spmd(_PROG, in_maps, core_ids=list(range(NCORES)))
    out = np.zeros((2, 8192, 1024), np.float32)
    for c in range(NCORES):
        b, j = c // 4, c % 4
        out[b, j * SEG:(j + 1) * SEG] = res.results[c]["y"]
    return out
```

```python
import contextlib
import numpy as np
import concourse.bass as bass
import concourse.mybir as mybir
from concourse.bass_utils import run_bass_kernel_spmd

F32 = mybir.dt.float32
BF16 = mybir.dt.bfloat16
ALU = mybir.AluOpType
AF = mybir.ActivationFunctionType

SAFE_SAME = True
EPS = 1e-6
NCORES = 8
SEG = 2048
NEGM = -30000.0


class Buf:
    __slots__ = ("w", "r")

    def __init__(self):
        self.w = None
        self.r = []


class Prog:
    ENGS = ("pe", "act", "dve", "pool", "sp")
    WINDOW = 40
    LAT = 300.0
    LAT_DMA = 200.0

    def __init__(self, nc, stack):
        self.nc = nc
        self.stack = stack
        self.ops = []
        self.phase = 0
        self.sems = {}
        self.all_dsems = []
        self.final = []
        for e in ("pe", "act", "dve", "pool"):
            self.sems[e] = stack.enter_context(nc.semaphore("s_" + e))

    def dma_sem(self, name):
        s = self.stack.enter_context(self.nc.semaphore("d_" + name))
        d = [s, 0]
        self.all_dsems.append(d)
        return d

    def op(self, eng, fn, reads=(), writes=(), dsem=None, cost=500.0, fin=None):
        idx = len(self.ops)
        deps = set()
        for b in reads:
            if b.w is not None:
                deps.add(b.w)
        for b in writes:
            if b.w is not None:
                deps.add(b.w)
            deps.update(b.r)
        self.ops.append([eng, fn, sorted(deps), dsem, cost, self.phase, cost if fin is None else fin])
        for b in reads:
            b.r.append(idx)
        for b in writes:
            b.w = idx
            b.r = []
        return idx

    def barrier(self):
        self.phase += 1

    def final_wait(self, eng, toks):
        self.final.append((eng, list(toks)))

    def schedule(self):
        ops = self.ops
        order = {e: [] for e in self.ENGS}
        finish = {}
        tnow = 0.0
        for ph in range(self.phase + 1):
            pend = {e: [] for e in self.ENGS}
            for i, o in enumerate(ops):
                if o[5] == ph:
                    pend[o[0]].append(i)
            tfree = {e: tnow for e in self.ENGS}
            remaining = sum(len(v) for v in pend.values())
            cand = {e: None for e in self.ENGS}
            dirty = set(self.ENGS)
            while remaining:
                for e in list(dirty):
                    best = None
                    lst = pend[e]
                    for i in lst[:self.WINDOW]:
                        o = ops[i]
                        st = tfree[e]
                        ok = True
                        for d in o[2]:
                            f = finish.get(d)
                            if f is None:
                                ok = False
                                break
                            lat = self.LAT_DMA if ops[d][3] is not None else (0.0 if ops[d][0] == e else self.LAT)
                            if f + lat > st:
                                st = f + lat
                        if ok and (best is None or (st, i) < best):
                            best = (st, i)
                    cand[e] = best
                dirty.clear()
                pick = None
                for e in self.ENGS:
                    c = cand[e]
                    if c is not None and (pick is None or c < pick[0]):
                        pick = (c, e)
                assert pick is not None, "scheduler stuck"
                (st, i), e = pick
                o = ops[i]
                tfree[e] = st + o[4]
                finish[i] = st + o[6]
                pend[e].remove(i)
                order[e].append(i)
                remaining -= 1
                dirty.update(self.ENGS)
            tnow = max(tfree.values())
            self.phase_ends = getattr(self, 'phase_ends', []) + [tnow]
            for e in self.ENGS:
                order[e].append(None)
        self.est_total = tnow
        return order

    def lower(self):
        order = self.schedule()
        ops = self.ops
        tok = {}
        cnt = {e: 0 for e in ("pe", "act", "dve", "pool")}
        bar_cnt = []
        nph = self.phase + 1
        pos = {e: 0 for e in self.ENGS}
        dcount = {id(d): 0 for d in self.all_dsems}
        bar_state = []
        for ph in range(nph):
            for e in self.ENGS:
                lst = order[e]
                while lst[pos[e]] is not None:
                    i = lst[pos[e]]
                    o = ops[i]
                    if o[3] is not None:
                        dcount[id(o[3])] += 16
                        tok[i] = (o[3][0], dcount[id(o[3])], e, True)
                    else:
                        cnt[e] += 1
                        tok[i] = (self.sems[e], cnt[e], e, False)
                    pos[e] += 1
                pos[e] += 1
            bar_state.append((dict(cnt), dict(dcount)))
        streams = {e: [] for e in self.ENGS}
        for e in self.ENGS:
            waited = {}
            ph = 0
            for i in order[e]:
                if i is None:
                    c, dc = bar_state[ph]
                    waits = []
                    for e2 in ("pe", "act", "dve", "pool"):
                        if e2 != e and c[e2] > waited.get(id(self.sems[e2]), 0):
                            waits.append((self.sems[e2], c[e2]))
                            waited[id(self.sems[e2])] = c[e2]
                    for d in self.all_dsems:
                        v = dc[id(d)]
                        if v > waited.get(id(d[0]), 0):
                            waits.append((d[0], v))
                            waited[id(d[0])] = v
                    if waits and ph < nph - 1:
                        streams[e].append((waits, None, None))
                    ph += 1
                    continue
                o = ops[i]
                waits = {}
                for d in o[2]:
                    s, v, e2, isdma = tok[d]
                    if e2 == e and not isdma:
                        if e in ("pe", "sp") or not SAFE_SAME:
                            continue
                    k = id(s)
                    if waited.get(k, 0) >= v:
                        continue
                    if k not in waits or waits[k][1] < v:
                        waits[k] = (s, v)
                for k, (s, v) in waits.items():
                    waited[k] = v
                t = tok[i]
                streams[e].append((list(waits.values()), o[1], (t[0], 16 if t[3] else 1)))
        for eng, toks in self.final:
            streams[eng].append(([(tok[t][0], tok[t][1]) for t in toks], None, None))
        self.streams = streams

    def run(self, block):
        self.lower()

        def play(name):
            def _f(e):
                for waits, fn, inc in self.streams[name]:
                    for s, v in waits:
                        e.wait_ge(s, v)
                    if fn is None:
                        continue
                    ins = fn(e)
                    if inc is not None:
                        ins.then_inc(inc[0], inc[1])
            return _f
        block.tensor(play("pe"))
        block.scalar(play("act"))
        block.vector(play("dve"))
        block.gpsimd(play("pool"))
        block.sync(play("sp"))


class Slot:
    def __init__(self, ap, sem=None):
        self.ap = ap
        self.buf = Buf()
        self.sem = sem


class Rot:
    def __init__(self, slots):
        self.slots = slots
        self.i = 0

    def next(self):
        s = self.slots[self.i % len(self.slots)]
        self.i += 1
        return s


def build_program():
    nc = bass.Bass("TRN2", target_bir_lowering=False)

    def din(name, shape):
        return nc.dram_tensor(name, shape, F32, kind="ExternalInput").ap()

    xe = din("xe", [8192, 1024])
    pin = din("p", [SEG, 256])
    w_in = din("w_in", [1024, 9744])
    wa2_d = din("w_a2aug", [32, 512])
    wog_d = din("w_o_gla", [1024, 1024])
    woa_d = din("w_o_attn", [512, 1024])
    wout_d = din("w_out", [1024, 1024])
    w1_d = din("w_mlp1", [1024, 4096])
    w2_d = din("w_mlp2", [4096, 1024])
    wpp_d = din("w_pp", [256, 1024])
    wpg_d = din("w_pg", [1024, 1024])
    cols_d = din("cols", [128, 32])
    lnf_d = din("ln_f", [1024])
    biasm_d = din("biasm", [128, 3, 2, 512])
    hoff_d = din("hoff", [128, 1])
    cmat_d = din("cmat", [128, 6, 128])
    sel4_d = din("sel4", [4, 4, 128])
    y = nc.dram_tensor("y", [SEG, 1024], F32, kind="ExternalOutput").ap()

    with contextlib.ExitStack() as st:
        P = Prog(nc, st)
        ARENA_BYTES = 200 * 1024
        arena = st.enter_context(nc.sbuf_tensor("arena", [128, ARENA_BYTES // 2], BF16))
        psb = [st.enter_context(nc.psum_tensor("psb%d" % i, [128, 512], F32)) for i in range(7)]
        ptr_t = st.enter_context(nc.psum_tensor("ptr", [128, 8, 128], BF16))
        block = st.enter_context(nc.Block())

        def carve(off, shape, dt):
            n = 1
            for s in shape[1:]:
                n *= s
            es = 2 if dt == BF16 else 4
            assert off % 4 == 0 and off + n * es <= ARENA_BYTES, (off, shape)
            a = arena[0:shape[0], off // 2: off // 2 + n * es // 2]
            if dt == F32:
                a = a.bitcast(F32)
            if len(shape) == 3:
                a = a.rearrange("p (a b) -> p a b", a=shape[1])
            elif len(shape) == 4:
                a = a.rearrange("p (a b c) -> p a b c", a=shape[1], b=shape[2])
            return a

        class Mem:
            def __init__(self, base, limit):
                self.off = base
                self.limit = limit

            def alloc(self, shape, dt):
                n = 1
                for s in shape[1:]:
                    n *= s
                es = 2 if dt == BF16 else 4
                a = carve(self.off, shape, dt)
                self.off += (n * es + 63) // 64 * 64
                assert self.off <= self.limit, (self.off, self.limit)
                return a

        KB = 1024

        def sl(start, n, step):
            return slice(start, start + step * (n - 1) + 1, step)

        def fsz(ap):
            n = 1
            for x in ap.shape[1:]:
                n *= x
            return n

        def PE(mms, reads, writes):
            cost = 0.0
            for m in mms:
                n = max(fsz(m[2]), 64)
                c = n / 2.0 + 30.0
                if m[1].dtype == F32:
                    c *= 4
                cost += c

            def fn(e):
                ins = None
                for m in mms:
                    kw = dict(start=m[3], stop=m[4])
                    if len(m) > 5 and m[5]:
                        kw["skip_group_check"] = True
                    ins = e.matmul(m[0], lhsT=m[1], rhs=m[2], **kw)
                return ins
            return P.op("pe", fn, reads, writes, cost=cost)

        def ACT(out, in_, func, reads, writes, **kw):
            return P.op("act", lambda e: e.activation(out=out, in_=in_, func=func, **kw), reads, writes,
                        cost=260.0 + 0.85 * fsz(out))

        def ENG(eng, method, reads, writes, **kw):
            o = kw.get("out", kw.get("ap"))
            n = fsz(o)
            cost = (130.0 + 1.0 * n) if eng == "dve" else (300.0 + 1.8 * n)
            return P.op(eng, lambda e: getattr(e, method)(**kw), reads, writes, cost=cost)

        def DVE(method, reads, writes, **kw):
            return ENG("dve", method, reads, writes, **kw)

        def POOL(method, reads, writes, **kw):
            return ENG("pool", method, reads, writes, **kw)

        def DMA(out, in_, reads, writes, dsem):
            nbytes = out.shape[0] * fsz(out) * 4
            return P.op("sp", lambda e: e.dma_start(out=out, in_=in_), reads, writes, dsem=dsem, cost=120.0, fin=2000.0 + nbytes / 150.0)

        psrot = Rot([Slot(t[:]) for t in psb])
        ptr = Slot(ptr_t[:])

        def PS():
            return psrot.next()

        G = Mem(0, 28 * KB)
        cmat = G.alloc([128, 6, 128], F32)
        identb = G.alloc([128, 128], BF16)
        onesel4 = G.alloc([128, 4, 4], BF16)
        sel4 = G.alloc([4, 4, 128], F32)
        colsT = G.alloc([128, 32], F32)
        hoff = G.alloc([128, 1], F32)
        mhalf4 = G.alloc([128, 4], F32)
        mhalf = mhalf4[:, 0:1]
        statv = G.alloc([128, 64], F32)
        junk = G.alloc([128, 1024], BF16)
        stage = Rot([Slot(G.alloc([128, 1024], F32), P.dma_sem("st%d" % i)) for i in range(2)])
        xts = Rot([Slot(G.alloc([128, 1024], F32), P.dma_sem("xt%d" % i)) for i in range(2)])
        hbs = Rot([Slot(G.alloc([128, 1024], BF16)) for i in range(2)])
        LT = cmat[:, 1, :]
        UT = cmat[:, 2, :]
        CM = cmat[:, 3, :]
        b_const = Buf()
        b_junk = Buf()
        stat_i = [0]

        def stat_col():
            c = stat_i[0] % 64
            stat_i[0] += 1
            return statv[:, c:c + 1], Buf()

        dc = P.dma_sem("const")
        DMA(cmat, cmat_d, [], [b_const], dc)
        DMA(sel4, sel4_d, [], [b_const], dc)
        DMA(colsT, cols_d, [], [b_const], dc)
        DMA(hoff, hoff_d, [], [b_const], dc)
        DVE("tensor_copy", [b_const], [b_const], out=identb, in_=cmat[:, 0, :])
        DVE("tensor_copy", [b_const], [b_const], out=onesel4, in_=cmat[:, 4, 0:16].rearrange("p (a b) -> p a b", a=4))
        POOL("memset", [], [b_const], ap=mhalf4, constant=-0.5)

        def col(i, kc):
            return colsT[:, i * 8 + kc: i * 8 + kc + 1]

        def load_w(dst_fn, dram, row0, nk, col0, ncols, scale_i, dst_bufs):
            for kc in range(nk):
                for c0 in range(0, ncols, 1024):
                    cn = min(1024, ncols - c0)
                    s = stage.next()
                    DMA(s.ap[:, 0:cn], dram[row0 + kc * 128: row0 + (kc + 1) * 128, col0 + c0: col0 + c0 + cn],
                        [], [s.buf], s.sem)
                    sc = col(scale_i, kc) if scale_i is not None else 1.0
                    POOL("tensor_scalar", [s.buf, b_const], [dst_bufs[kc]], out=dst_fn(kc, c0, cn), in0=s.ap[:, 0:cn],
                         scalar1=sc, scalar2=1.0, op0=ALU.mult, op1=ALU.mult)

        def rstd_of(src_ap, src_bufs, n, scr_ap, scr_buf):
            ss, bss = stat_col()
            ACT(scr_ap, src_ap, AF.Square, src_bufs, [scr_buf, bss], accum_out=ss)
            vv, bvv = stat_col()
            POOL("tensor_scalar", [bss], [bvv], out=vv, in0=ss, scalar1=1.0 / n, scalar2=EPS, op0=ALU.mult, op1=ALU.add)
            rs, brs = stat_col()
            POOL("tensor_tensor", [bvv, b_const], [brs], out=rs, in0=vv, in1=mhalf, op=ALU.pow)
            return rs, brs

        def norm_to_hT(src_ap, src_bufs, hT_out, hT_bufs):
            hb = hbs.next()
            rs, brs = rstd_of(src_ap, src_bufs, 1024, junk, b_junk)
            DVE("tensor_scalar", list(src_bufs) + [brs], [hb.buf], out=hb.ap, in0=src_ap, scalar1=rs, scalar2=None,
                op0=ALU.mult)

            def fn(e):
                ins = None
                for kc in range(8):
                    ins = e.transpose(ptr.ap[:, kc, :], hb.ap[:, kc * 128:(kc + 1) * 128], identb)
                return ins
            P.op("pe", fn, [hb.buf, b_const], [ptr.buf], cost=650.0)
            ACT(hT_out, ptr.ap, AF.Copy, [ptr.buf], hT_bufs)

        def load_x(row0, step=1):
            s = xts.next()
            DMA(s.ap, xe[sl(row0, 128, step), :], [], [s.buf], s.sem)
            return s

        OhT = carve(28 * KB, [128, 4, 2048], BF16)
        b_OhT = Buf()
        ogT = carve(44 * KB, [128, 8, 2048], BF16)
        b_ogT = [Buf() for _ in range(16)]
        mixT = carve(76 * KB, [128, 8, 2048], BF16)
        b_mixT = [Buf() for _ in range(4)]
        x1 = carve(108 * KB, [128, 16, 1024], F32)
        b_x1 = [Buf() for _ in range(16)]
        h2T = carve(28 * KB, [128, 8, 2048], BF16)
        b_h2T = [Buf() for _ in range(16)]

        hTh = carve(44 * KB, [128, 8, 2048], BF16)
        hTo = carve(76 * KB, [128, 8, 2048], BF16)
        b_hTh = [Buf() for _ in range(16)]
        b_hTo = [Buf() for _ in range(16)]
        MA = Mem(108 * KB, 200 * KB)
        NT = MA.alloc([128, 4, 2048], F32)
        ST = MA.alloc([4, 2048], F32)
        biasT = carve(28 * KB, [128, 512], F32)
        expbN = carve(30 * KB, [128, 512], F32)
        expbH = carve(32 * KB, [128, 512], F32)
        wA = [MA.alloc([128, 8, 3, 256], BF16) for _ in range(2)]
        b_wA = [[Buf() for _ in range(8)] for _ in range(2)]
        KTs = Rot([Slot(MA.alloc([128, 2, 512], BF16)) for _ in range(3)])
        QTs = Rot([Slot(MA.alloc([128, 2, 512], BF16)) for _ in range(2)])
        Vs = Rot([Slot(MA.alloc([128, 4, 256], BF16)) for _ in range(3)])
        Efs = Rot([Slot(MA.alloc([128, 512], F32)) for _ in range(2)])
        ETs = Rot([Slot(MA.alloc([128, 512], BF16)) for _ in range(2)])
        b_bias = Buf()
        b_exp = Buf()
        scale_att = 128.0 ** -0.5
        dbias = P.dma_sem("bias")
        b_NTq = [[Buf() for _ in range(4)] for _ in range(2)]
        b_STq = [Buf() for _ in range(4)]
        DVE("memset", [], [b for l in b_NTq for b in l], ap=NT, constant=0.0)
        DVE("memset", [], b_STq, ap=ST, constant=0.0)
        wslot = 0

        def proj_seg(hTs, hbufs, t0, nb, wa, bwa, want_q):
            cols = slice(t0 * 128, (t0 + nb) * 128)
            hb_ = hbufs[t0:t0 + nb]
            kt = KTs.next()
            for hh in range(2):
                ps = PS()
                PE([(ps.ap[:, 0:nb * 128], wa[:, kc, 1, hh * 128:(hh + 1) * 128], hTs[:, kc, cols], kc == 0, kc == 7)
                    for kc in range(8)], hb_ + bwa, [ps.buf])
                ACT(kt.ap[:, hh, 0:nb * 128], ps.ap[:, 0:nb * 128], AF.Copy, [ps.buf], [kt.buf])
            qt = None
            if want_q:
                qt = QTs.next()
                for hh in range(2):
                    ps = PS()
                    PE([(ps.ap[:, 0:nb * 128], wa[:, kc, 0, hh * 128:(hh + 1) * 128], hTs[:, kc, cols], kc == 0, kc == 7)
                        for kc in range(8)], hb_ + bwa, [ps.buf])
                    DVE("tensor_copy", [ps.buf], [qt.buf], out=qt.ap[:, hh, 0:nb * 128], in_=ps.ap[:, 0:nb * 128])
            vs = Vs.next()
            for b in range(nb):
                bc = slice((t0 + b) * 128, (t0 + b + 1) * 128)
                ps = PS()
                PE([(ps.ap[:, 0:256], hTs[:, kc, bc], wa[:, kc, 2, :], kc == 0, kc == 7) for kc in range(8)],
                   [hbufs[t0 + b]] + bwa, [ps.buf])
                if b % 2 == 0:
                    ACT(vs.ap[:, b, :], ps.ap[:, 0:256], AF.Copy, [ps.buf], [vs.buf])
                else:
                    DVE("tensor_copy", [ps.buf], [vs.buf], out=vs.ap[:, b, :], in_=ps.ap[:, 0:256])
            return kt, qt, vs

        def attend(g, hp, prevb, curb, qt, qb, first, nat, quarters):
            blk = [prevb, curb]
            sp_ = PS()
            s4 = sp_.ap.rearrange("p (h j q) -> p h j q", h=2, j=2)
            PE([(s4[:, hh, j, :], blk[j][0].ap[:, hh, blk[j][2] * 128:(blk[j][2] + 1) * 128],
                 qt.ap[:, hh, qb * 128:(qb + 1) * 128], True, True) for hh in range(2) for j in range(2)],
               [prevb[0].buf, curb[0].buf, qt.buf], [sp_.buf])
            ef = Efs.next()
            ACT(ef.ap, sp_.ap, AF.Exp, [sp_.buf], [ef.buf], scale=scale_att)
            et = ETs.next()
            DVE("tensor_tensor", [ef.buf, b_exp], [et.buf], out=et.ap, in0=ef.ap,
                in1=(expbH if first else expbN), op=ALU.mult)
            e4t = et.ap.rearrange("p (h j q) -> p h j q", h=2, j=2)
            np_ = PS()
            mms = []
            for hh in range(2):
                for j in range(2):
                    mms.append((np_.ap[:, hh * 128:(hh + 1) * 128],
                                blk[j][1].ap[:, blk[j][2], hh * 128:(hh + 1) * 128], e4t[:, hh, j, :], j == 0, j == 1))
            k = 0
            for hh in range(2):
                for j in range(2):
                    mms.append((np_.ap[0:4, 256:384], onesel4[:, hp * 2 + hh, :], e4t[:, hh, j, :], k == 0, k == 3))
                    k += 1
            PE(mms, [prevb[1].buf, curb[1].buf, et.buf, b_const], [np_.buf])
            nb_ = [b_NTq[hp][q] for q in quarters]
            DVE("tensor_tensor", [np_.buf] + nb_, nb_, out=NT[:, hp * 2:hp * 2 + 2, nat], in0=NT[:, hp * 2:hp * 2 + 2, nat],
                in1=np_.ap[:, 0:256].rearrange("p (h q) -> p h q", h=2), op=ALU.add)
            sb_ = [b_STq[q] for q in quarters]
            DVE("tensor_tensor", [np_.buf] + sb_, sb_, out=ST[:, nat], in0=ST[:, nat],
                in1=np_.ap[0:4, 256:384], op=ALU.add)

        for g in range(3):
            dil = (1, 4, 16)[g]
            nbk = 16 // dil
            for k in range(dil):
                s = load_x(6144 - 128 * dil + k, dil)
                norm_to_hT(s.ap, [s.buf], hTh[:, :, k * 128:(k + 1) * 128], [b_hTh[k]])
            for k in range(16):
                r, n = divmod(k, nbk)
                s = load_x(6144 + r + dil * 128 * n, dil)
                norm_to_hT(s.ap, [s.buf], hTo[:, :, k * 128:(k + 1) * 128], [b_hTo[k]])
            for hp in range(2):
                DMA(biasT, biasm_d[:, g, hp, :], [], [b_bias], dbias)
                ACT(expbN, biasT, AF.Exp, [b_bias], [b_exp])
                ACT(expbH, biasT, AF.Exp, [b_bias], [b_exp])
                b4 = biasT.rearrange("p (h j q) -> p h j q", h=2, j=2)
                e4 = expbH.rearrange("p (h j q) -> p h j q", h=2, j=2)
                ACT(e4[:, :, 0, :], b4[:, :, 0, :], AF.Exp, [b_bias, b_const], [b_exp], bias=hoff)
                wa = wA[wslot % 2]
                bwa = b_wA[wslot % 2]
                wslot += 1
                base = 3088 + g * 1536
                for kc in range(8):
                    s = stage.next()
                    src = w_in[kc * 128:(kc + 1) * 128, base:base + 1536].rearrange("p (c x) -> p c x", c=3)[:, :, hp * 256:(hp + 1) * 256]
                    sv = s.ap[:, 0:768].rearrange("p (c x) -> p c x", c=3)
                    DMA(sv, src, [], [s.buf], s.sem)
                    POOL("tensor_scalar", [s.buf, b_const], [bwa[kc]], out=wa[:, kc, :, :], in0=sv,
                         scalar1=col(0, kc), scalar2=1.0, op0=ALU.mult, op1=ALU.mult)
                if g < 2:
                    for r in range(dil):
                        kt_p, _, vs_p = proj_seg(hTh, b_hTh, r, 1, wa, bwa, False)
                        prev = (kt_p, vs_p, 0)
                        for n0 in range(0, nbk, 4):
                            nb = min(4, nbk - n0)
                            kt, qt, vs = proj_seg(hTo, b_hTo, r * nbk + n0, nb, wa, bwa, True)
                            for b in range(nb):
                                cur = (kt, vs, b)
                                n = n0 + b
                                tok0 = r + dil * 128 * n
                                quarters = [tok0 // 512] if g == 0 else [n]
                                attend(g, hp, prev, cur, qt, b, (n == 0), sl(tok0, 128, dil), quarters)
                                prev = cur
                else:
                    for q4 in range(4):
                        kt_h, _, vs_h = proj_seg(hTh, b_hTh, q4 * 4, 4, wa, bwa, False)
                        kt, qt, vs = proj_seg(hTo, b_hTo, q4 * 4, 4, wa, bwa, True)
                        for b in range(4):
                            r = q4 * 4 + b
                            attend(g, hp, (kt_h, vs_h, b), (kt, vs, b), qt, b, True, sl(r, 128, 16), [0, 1, 2, 3])
        DVE("reciprocal", b_STq, b_STq, out=ST, in_=ST)
        for hh in range(4):
            for tb in range(4):
                ps = PS()
                PE([(ps.ap, sel4[:, hh, :], ST[:, tb * 512:(tb + 1) * 512], True, True)], b_STq + [b_const], [ps.buf])
                DVE("tensor_tensor", [ps.buf, b_NTq[hh // 2][tb]], [b_OhT], out=OhT[:, hh, tb * 512:(tb + 1) * 512],
                    in0=NT[:, hh, tb * 512:(tb + 1) * 512], in1=ps.ap, op=ALU.mult)
        P.barrier()

        MG = Mem(76 * KB, 200 * KB)
        wG = MG.alloc([128, 8, 3088], BF16)
        b_wG = [Buf() for _ in range(8)]
        wa2 = MG.alloc([32, 512], BF16)
        b_wa2 = Buf()
        CM4 = MG.alloc([128, 4, 128], F32)
        hTt = [Slot(MG.alloc([128, 8, 128], BF16)) for _ in range(2)]
        haT = [Slot(MG.alloc([32, 128], BF16)) for _ in range(2)]
        ezs = [Slot(MG.alloc([128, 512], F32)) for _ in range(2)]
        sps = [Slot(MG.alloc([128, 512], F32)) for _ in range(2)]
        wex = [Slot(MG.alloc([128, 512], F32)) for _ in range(2)]
        kouts = [Slot(MG.alloc([128, 512], BF16)) for _ in range(2)]
        vbs = [Slot(MG.alloc([128, 1024], BF16)) for _ in range(2)]
        e1s = [Slot(MG.alloc([128, 4, 128], F32)) for _ in range(2)]
        e2s = [Slot(MG.alloc([128, 4, 128], F32)) for _ in range(2)]
        qdA = [Slot(MG.alloc([128, 4, 128], BF16)) for _ in range(2)]
        qdB = [Slot(MG.alloc([128, 4, 128], BF16)) for _ in range(2)]
        kinT = [Slot(MG.alloc([128, 4, 128], BF16)) for _ in range(2)]
        attnT = [Slot(MG.alloc([128, 4, 128], BF16)) for _ in range(2)]
        ers = [Slot(MG.alloc([128, 1024], F32)) for _ in range(2)]
        sil = [Slot(MG.alloc([128, 1024], F32)) for _ in range(2)]
        Sst = MG.alloc([128, 4, 256], F32)
        SbA = MG.alloc([128, 4, 256], BF16)
        SbB = MG.alloc([128, 4, 256], BF16)
        ogb = Slot(MG.alloc([128, 1024], BF16))
        ssh = MG.alloc([128, 8], F32)
        b_S = [Buf() for _ in range(4)]
        b_SbA = [Buf() for _ in range(4)]
        b_SbB = [Buf() for _ in range(4)]

        load_w(lambda kc, c0, cn: wG[:, kc, c0:c0 + cn], w_in, 0, 8, 0, 3088, 0, b_wG)
        s = stage.next()
        DMA(s.ap[0:32, 0:512], wa2_d, [], [s.buf], s.sem)
        POOL("tensor_copy", [s.buf], [b_wa2], out=wa2, in_=s.ap[0:32, 0:512])
        for hh in range(4):
            DVE("tensor_copy", [b_const], [b_const], out=CM4[:, hh, :], in_=CM)
        for i in range(2):
            POOL("memset", [], [haT[i].buf], ap=haT[i].ap, constant=1.0)
            POOL("memset", [], [qdA[i].buf], ap=qdA[i].ap, constant=0.0)
            POOL("memset", [], [qdB[i].buf], ap=qdB[i].ap, constant=0.0)
        DVE("memset", [], b_S, ap=Sst, constant=0.0)
        POOL("memset", [], b_SbA, ap=SbA, constant=0.0)

        gl = {}

        def gla_stage1(t):
            own = t >= 48
            i = t % 2
            xs = load_x(t * 128)
            ht = hTt[i]
            norm_to_hT(xs.ap, [xs.buf], ht.ap, [ht.buf])
            rb = [ht.buf] + b_wG
            kps = PS()
            PE([(kps.ap, ht.ap[:, kc, :], wG[:, kc, 512:1024], kc == 0, kc == 7) for kc in range(8)], rb, [kps.buf])
            hps = PS()
            PE([(hps.ap[0:16, 0:128], wG[:, kc, 3072:3088], ht.ap[:, kc, :], kc == 0, kc == 7) for kc in range(8)], rb, [hps.buf])
            ACT(haT[i].ap[0:16, :], hps.ap[0:16, 0:128], AF.Copy, [hps.buf], [haT[i].buf])
            zps = PS()
            PE([(zps.ap, haT[i].ap, wa2, True, True)], [haT[i].buf, b_wa2], [zps.buf])
            ACT(ezs[i].ap, zps.ap, AF.Exp, [zps.buf], [ezs[i].buf], scale=-1.0)
            ACT(sps[i].ap, ezs[i].ap, AF.Ln, [ezs[i].buf], [sps[i].buf], bias=1.0)
            vp = [PS(), PS()]
            for h2 in range(2):
                PE([(vp[h2].ap, ht.ap[:, kc, :], wG[:, kc, 1024 + h2 * 512:1536 + h2 * 512], kc == 0, kc == 7) for kc in range(8)],
                   rb, [vp[h2].buf])
            ACT(vbs[i].ap[:, 0:512], vp[0].ap, AF.Copy, [vp[0].buf], [vbs[i].buf])
            DVE("tensor_copy", [vp[1].buf], [vbs[i].buf], out=vbs[i].ap[:, 512:1024], in_=vp[1].ap)
            dps = PS()
            PE([(dps.ap, UT, sps[i].ap, True, True)], [sps[i].buf, b_const], [dps.buf])
            ACT(wex[i].ap, dps.ap, AF.Exp, [dps.buf], [wex[i].buf])
            DVE("tensor_tensor", [kps.buf, wex[i].buf], [kouts[i].buf], out=kouts[i].ap, in0=kps.ap, in1=wex[i].ap, op=ALU.mult)
            nps = PS()
            n4 = nps.ap.rearrange("p (h q) -> p h q", h=4)
            PE([(n4[:, hh, :], sps[i].ap[:, hh * 128:(hh + 1) * 128], LT, True, True) for hh in range(4)],
               [sps[i].buf, b_const], [nps.buf])
            ACT(e1s[i].ap, n4, AF.Exp, [nps.buf], [e1s[i].buf], scale=-1.0)
            if not own:
                return
            ACT(e2s[i].ap, n4, AF.Exp, [nps.buf], [e2s[i].buf])
            qps = PS()
            q4 = qps.ap.rearrange("p (h q) -> p h q", h=4)
            PE([(q4[:, hh, :], wG[:, kc, hh * 128:(hh + 1) * 128], ht.ap[:, kc, :], kc == 0, kc == 7)
                for hh in range(4) for kc in range(8)], rb, [qps.buf])
            DVE("scalar_tensor_tensor", [qps.buf, e1s[i].buf], [qdA[i].buf], out=qdA[i].ap[:, :, 0:64], in0=q4[:, :, 0:64],
                scalar=128.0 ** -0.5, in1=e1s[i].ap[:, :, 0:64], op0=ALU.mult, op1=ALU.mult)
            DVE("scalar_tensor_tensor", [qps.buf, e1s[i].buf], [qdB[i].buf], out=qdB[i].ap[:, :, 64:128], in0=q4[:, :, 64:128],
                scalar=128.0 ** -0.5, in1=e1s[i].ap[:, :, 64:128], op0=ALU.mult, op1=ALU.mult)
            ktp = PS()
            k4 = ktp.ap.rearrange("p (h q) -> p h q", h=4)
            PE([(k4[:, hh, :], wG[:, kc, 512 + hh * 128:512 + (hh + 1) * 128], ht.ap[:, kc, :], kc == 0, kc == 7)
                for hh in range(4) for kc in range(8)], rb, [ktp.buf])
            DVE("tensor_tensor", [ktp.buf, e2s[i].buf], [kinT[i].buf], out=kinT[i].ap, in0=k4, in1=e2s[i].ap, op=ALU.mult)
            aps = PS()
            a4 = aps.ap.rearrange("p (h q) -> p h q", h=4)
            mms = []
            for hh in range(4):
                mms.append((a4[:, hh, 0:64], kinT[i].ap[:, hh, :], qdA[i].ap[:, hh, 0:64], True, True))
                mms.append((a4[:, hh, 64:128], kinT[i].ap[:, hh, :], qdB[i].ap[:, hh, 64:128], True, True))
            PE(mms, [kinT[i].buf, qdA[i].buf, qdB[i].buf], [aps.buf])
            DVE("tensor_tensor", [aps.buf, b_const], [attnT[i].buf], out=attnT[i].ap, in0=a4, in1=CM4, op=ALU.mult)
            rp = [PS(), PS()]
            for h2 in range(2):
                PE([(rp[h2].ap, ht.ap[:, kc, :], wG[:, kc, 2048 + h2 * 512:2560 + h2 * 512], kc == 0, kc == 7) for kc in range(8)],
                   rb, [rp[h2].buf])
                ACT(ers[i].ap[:, h2 * 512:(h2 + 1) * 512], rp[h2].ap, AF.Exp, [rp[h2].buf], [ers[i].buf], scale=-1.0)
            ACT(ers[i].ap, ers[i].ap, AF.Ln, [ers[i].buf], [ers[i].buf], bias=1.0)
            ACT(ers[i].ap, ers[i].ap, AF.Exp, [ers[i].buf], [ers[i].buf], scale=-1.0)
            for h2 in range(2):
                DVE("tensor_tensor", [rp[h2].buf, ers[i].buf], [sil[i].buf], out=sil[i].ap[:, h2 * 512:(h2 + 1) * 512],
                    in0=rp[h2].ap, in1=ers[i].ap[:, h2 * 512:(h2 + 1) * 512], op=ALU.mult)

        def gla_stage2(t):
            own = t >= 48
            i = t % 2
            tt = t - 48
            last_prefix = (t == 47)
            ko = kouts[i]
            vb = vbs[i]
            e1 = e1s[i]
            if own:
                oP = [PS(), PS()]
                for pr in range(2):
                    mms = []
                    for hq in range(2):
                        hh = pr * 2 + hq
                        mms.append((oP[pr].ap[:, hq * 256:(hq + 1) * 256], attnT[i].ap[:, hh, :], vb.ap[:, hh * 256:(hh + 1) * 256],
                                    hq == 0, False, True))
                    for hq in range(2):
                        hh = pr * 2 + hq
                        mms.append((oP[pr].ap[:, hq * 256:(hq + 1) * 256], qdA[i].ap[:, hh, :], SbA[:, hh, :], False, False, True))
                    PE(mms, [attnT[i].buf, vb.buf, qdA[i].buf] + b_SbA[pr * 2:pr * 2 + 2], [oP[pr].buf])
            for c in range(2):
                uP = [PS(), PS()]
                for pr in range(2):
                    PE([(uP[pr].ap[:, hq * 256:(hq + 1) * 256], ko.ap[c * 64:(c + 1) * 64, (pr * 2 + hq) * 128:(pr * 2 + hq + 1) * 128],
                         vb.ap[c * 64:(c + 1) * 64, (pr * 2 + hq) * 256:(pr * 2 + hq + 1) * 256], True, True) for hq in range(2)],
                       [ko.buf, vb.buf], [uP[pr].buf])
                for hh in range(4):
                    pr, hq = hh // 2, hh % 2
                    DVE("scalar_tensor_tensor", [uP[pr].buf, e1.buf, b_S[hh]], [b_S[hh]], out=Sst[:, hh, :], in0=Sst[:, hh, :],
                        scalar=e1.ap[:, hh, c * 64 + 63:c * 64 + 64], in1=uP[pr].ap[:, hq * 256:(hq + 1) * 256],
                        op0=ALU.mult, op1=ALU.add)
                    if c == 0 and own:
                        ACT(SbB[:, hh, :], Sst[:, hh, :], AF.Copy, [b_S[hh]], [b_SbB[hh]])
                    if c == 1 and (own or last_prefix):
                        ACT(SbA[:, hh, :], Sst[:, hh, :], AF.Copy, [b_S[hh]], [b_SbA[hh]])
                if c == 0 and own:
                    for pr in range(2):
                        PE([(oP[pr].ap[:, hq * 256:(hq + 1) * 256], qdB[i].ap[:, pr * 2 + hq, :], SbB[:, pr * 2 + hq, :],
                             False, hq == 1, True) for hq in range(2)],
                           [qdB[i].buf] + b_SbB[pr * 2:pr * 2 + 2], [oP[pr].buf])
            if not own:
                return
            bssh = Buf()
            for hh in range(4):
                pr, hq = hh // 2, hh % 2
                ACT(junk[:, 0:256], oP[pr].ap[:, hq * 256:(hq + 1) * 256], AF.Square, [oP[pr].buf], [b_junk, bssh],
                    accum_out=ssh[:, hh:hh + 1])
            POOL("tensor_scalar", [bssh], [bssh], out=ssh[:, 4:8], in0=ssh[:, 0:4], scalar1=1.0 / 256, scalar2=EPS,
                 op0=ALU.mult, op1=ALU.add)
            POOL("tensor_tensor", [bssh, b_const], [bssh], out=ssh[:, 4:8], in0=ssh[:, 4:8], in1=mhalf4,
                 op=ALU.pow)
            for hh in range(4):
                pr, hq = hh // 2, hh % 2
                DVE("scalar_tensor_tensor", [oP[pr].buf, bssh, sil[i].buf], [ogb.buf], out=ogb.ap[:, hh * 256:(hh + 1) * 256],
                    in0=oP[pr].ap[:, hq * 256:(hq + 1) * 256], scalar=ssh[:, 4 + hh:5 + hh], in1=sil[i].ap[:, hh * 256:(hh + 1) * 256],
                    op0=ALU.mult, op1=ALU.mult)

            def fn(e):
                ins = None
                for kc in range(8):
                    ins = e.transpose(ptr.ap[:, kc, :], ogb.ap[:, kc * 128:(kc + 1) * 128], identb)
                return ins
            P.op("pe", fn, [ogb.buf, b_const], [ptr.buf], cost=650.0)
            ACT(ogT[:, :, tt * 128:(tt + 1) * 128], ptr.ap, AF.Copy, [ptr.buf], [b_ogT[tt]])

        gla_stage1(0)
        for t in range(64):
            if t + 1 < 64:
                gla_stage1(t + 1)
            gla_stage2(t)
        P.barrier()

        hTo2 = carve(108 * KB, [128, 8, 2048], BF16)
        b_hTo2 = [Buf() for _ in range(16)]
        for t in range(16):
            s = load_x(6144 + t * 128)
            norm_to_hT(s.ap, [s.buf], hTo2[:, :, t * 128:(t + 1) * 128], [b_hTo2[t]])
        MM = Mem(140 * KB, 200 * KB)
        wog = MM.alloc([128, 8, 512], BF16)
        woa = MM.alloc([128, 4, 512], BF16)
        wgA = MM.alloc([128, 8, 512], BF16)
        wgB = MM.alloc([128, 8, 512], BF16)
        b_wog = [Buf() for _ in range(8)]
        b_woa = [Buf() for _ in range(4)]
        b_wgA = [Buf() for _ in range(8)]
        b_wgB = [Buf() for _ in range(8)]
        sgs = Rot([Slot(MM.alloc([128, 512], F32)) for _ in range(4)])
        tms = Rot([Slot(MM.alloc([128, 512], F32)) for _ in range(2)])
        for fo in range(2):
            load_w(lambda kc, c0, cn: wog[:, kc, c0:c0 + cn], wog_d, 0, 8, fo * 512, 512, 3, b_wog)
            load_w(lambda kc, c0, cn: woa[:, kc, c0:c0 + cn], woa_d, 0, 4, fo * 512, 512, None, b_woa)
            load_w(lambda kc, c0, cn: wgA[:, kc, c0:c0 + cn], w_in, 0, 8, 7696 + fo * 512, 512, 0, b_wgA)
            load_w(lambda kc, c0, cn: wgB[:, kc, c0:c0 + cn], w_in, 0, 8, 8720 + fo * 512, 512, 0, b_wgB)
            for tb in range(4):
                tk = slice(tb * 512, (tb + 1) * 512)
                for fc in range(4):
                    fs = slice(fc * 128, (fc + 1) * 128)
                    ga = PS()
                    PE([(ga.ap, wgA[:, kc, fs], hTo2[:, kc, tk], kc == 0, kc == 7) for kc in range(8)],
                       b_wgA + b_hTo2[tb * 4:tb * 4 + 4], [ga.buf])
                    sa = sgs.next()
                    ACT(sa.ap, ga.ap, AF.Sigmoid, [ga.buf], [sa.buf])
                    gb = PS()
                    PE([(gb.ap, wgB[:, kc, fs], hTo2[:, kc, tk], kc == 0, kc == 7) for kc in range(8)],
                       b_wgB + b_hTo2[tb * 4:tb * 4 + 4], [gb.buf])
                    sb_ = sgs.next()
                    ACT(sb_.ap, gb.ap, AF.Sigmoid, [gb.buf], [sb_.buf])
                    yg = PS()
                    PE([(yg.ap, wog[:, kc, fs], ogT[:, kc, tk], kc == 0, kc == 7) for kc in range(8)],
                       b_wog + b_ogT[tb * 4:tb * 4 + 4], [yg.buf])
                    t1 = tms.next()
                    DVE("tensor_tensor", [yg.buf, sa.buf], [t1.buf], out=t1.ap, in0=yg.ap, in1=sa.ap, op=ALU.mult)
                    ya = PS()
                    PE([(ya.ap, woa[:, hh, fs], OhT[:, hh, tk], hh == 0, hh == 3) for hh in range(4)],
                       b_woa + [b_OhT], [ya.buf])
                    t2 = tms.next()
                    DVE("tensor_tensor", [ya.buf, sb_.buf], [t2.buf], out=t2.ap, in0=ya.ap, in1=sb_.ap, op=ALU.mult)
                    DVE("tensor_tensor", [t1.buf, t2.buf], [b_mixT[tb]], out=mixT[:, fo * 4 + fc, tk], in0=t1.ap, in1=t2.ap, op=ALU.add)
        P.barrier()

        wout = carve(60 * KB, [128, 8, 1024], BF16)
        b_wout = [Buf() for _ in range(8)]
        load_w(lambda kc, c0, cn: wout[:, kc, c0:c0 + cn], wout_d, 0, 8, 0, 1024, None, b_wout)
        for t in range(16):
            s = load_x(6144 + t * 128)
            for h2 in range(2):
                ps = PS()
                PE([(ps.ap, mixT[:, kc, t * 128:(t + 1) * 128], wout[:, kc, h2 * 512:(h2 + 1) * 512], kc == 0, kc == 7) for kc in range(8)],
                   b_mixT + b_wout, [ps.buf])
                DVE("tensor_tensor", [ps.buf, s.buf], [b_x1[t]], out=x1[:, t, h2 * 512:(h2 + 1) * 512], in0=ps.ap,
                    in1=s.ap[:, h2 * 512:(h2 + 1) * 512], op=ALU.add)
            norm_to_hT(x1[:, t, :], [b_x1[t]], h2T[:, :, t * 128:(t + 1) * 128], [b_h2T[t]])
        P.barrier()

        MF = Mem(60 * KB, 108 * KB)
        w1c = [MF.alloc([128, 8, 512], BF16) for _ in range(2)]
        w2c = [MF.alloc([128, 4, 1024], BF16) for _ in range(2)]
        b_w1c = [[Buf() for _ in range(8)] for _ in range(2)]
        b_w2c = [[Buf() for _ in range(4)] for _ in range(2)]
        uTs = Rot([Slot(MF.alloc([128, 4, 512], BF16)) for _ in range(2)])
        rls = Rot([Slot(MF.alloc([128, 512], F32)) for _ in range(2)])
        for ffg in range(8):
            wi = ffg % 2
            w1 = w1c[wi]
            w2 = w2c[wi]
            load_w(lambda kc, c0, cn: w1[:, kc, c0:c0 + cn], w1_d, 0, 8, ffg * 512, 512, 1, b_w1c[wi])
            load_w(lambda kc, c0, cn: w2[:, kc, c0:c0 + cn], w2_d, ffg * 512, 4, 0, 1024, None, b_w2c[wi])
            for tb in range(4):
                tk = slice(tb * 512, (tb + 1) * 512)
                ut = uTs.next()
                for j in range(4):
                    ps = PS()
                    PE([(ps.ap, w1[:, kc, j * 128:(j + 1) * 128], h2T[:, kc, tk], kc == 0, kc == 7) for kc in range(8)],
                       b_w1c[wi] + b_h2T[tb * 4:tb * 4 + 4], [ps.buf])
                    rl = rls.next()
                    ACT(rl.ap, ps.ap, AF.Relu, [ps.buf], [rl.buf])
                    DVE("tensor_tensor", [rl.buf], [ut.buf], out=ut.ap[:, j, :], in0=rl.ap, in1=rl.ap, op=ALU.mult)
                for tt in range(4):
                    t = tb * 4 + tt
                    for h2 in range(2):
                        ps = PS()
                        PE([(ps.ap, ut.ap[:, j, tt * 128:(tt + 1) * 128], w2[:, j, h2 * 512:(h2 + 1) * 512], j == 0, j == 3) for j in range(4)],
                           [ut.buf] + b_w2c[wi], [ps.buf])
                        DVE("tensor_tensor", [ps.buf, b_x1[t]], [b_x1[t]], out=x1[:, t, h2 * 512:(h2 + 1) * 512],
                            in0=x1[:, t, h2 * 512:(h2 + 1) * 512], in1=ps.ap, op=ALU.add)
        P.barrier()

        MP = Mem(28 * KB, 108 * KB)
        wpg = MP.alloc([128, 8, 1024], BF16)
        wpp = MP.alloc([128, 2, 1024], BF16)
        lnfb = MP.alloc([128, 1024], F32)
        junkP = MP.alloc([128, 1024], BF16)
        b_junkP = Buf()
        b_wpg = [Buf() for _ in range(8)]
        b_wpp = [Buf() for _ in range(2)]
        b_lnf = Buf()
        h3s = Rot([Slot(MP.alloc([128, 8, 128], BF16)) for _ in range(2)])
        pfs = Rot([Slot(MP.alloc([128, 256], F32), P.dma_sem("pf%d" % i)) for i in range(2)])
        pbs = Rot([Slot(MP.alloc([128, 256], BF16)) for _ in range(2)])
        pTs = Rot([Slot(MP.alloc([128, 2, 128], BF16)) for _ in range(2)])
        sg2 = Rot([Slot(MP.alloc([128, 1024], F32)) for _ in range(2)])
        osb = Rot([Slot(MP.alloc([128, 1024], F32), P.dma_sem("os%d" % i)) for i in range(2)])
        load_w(lambda kc, c0, cn: wpg[:, kc, c0:c0 + cn], wpg_d, 0, 8, 0, 1024, 2, b_wpg)
        load_w(lambda kc, c0, cn: wpp[:, kc, c0:c0 + cn], wpp_d, 0, 2, 0, 1024, None, b_wpp)
        dl = P.dma_sem("lnf")
        DMA(lnfb, lnf_d.partition_broadcast(128), [], [b_lnf], dl)
        out_toks = []
        for t in range(16):
            h3 = h3s.next()
            norm_to_hT(x1[:, t, :], [b_x1[t]], h3.ap, [h3.buf])
            pf = pfs.next()
            DMA(pf.ap, pin[t * 128:(t + 1) * 128, :], [], [pf.buf], pf.sem)
            pb = pbs.next()
            DVE("tensor_copy", [pf.buf], [pb.buf], out=pb.ap, in_=pf.ap)

            def fn(e, pb=pb):
                ins = None
                for c in range(2):
                    ins = e.transpose(ptr.ap[:, c, :], pb.ap[:, c * 128:(c + 1) * 128], identb)
                return ins
            P.op("pe", fn, [pb.buf, b_const], [ptr.buf], cost=200.0)
            pT = pTs.next()
            ACT(pT.ap, ptr.ap[:, 0:2, :], AF.Copy, [ptr.buf], [pT.buf])
            sg = sg2.next()
            for h2 in range(2):
                hs = slice(h2 * 512, (h2 + 1) * 512)
                gp = PS()
                PE([(gp.ap, h3.ap[:, kc, :], wpg[:, kc, hs], kc == 0, kc == 7) for kc in range(8)], [h3.buf] + b_wpg, [gp.buf])
                ACT(sg.ap[:, hs], gp.ap, AF.Sigmoid, [gp.buf], [sg.buf])
                pp = PS()
                PE([(pp.ap, pT.ap[:, c, :], wpp[:, c, hs], c == 0, c == 1) for c in range(2)], [pT.buf] + b_wpp, [pp.buf])
                DVE("tensor_tensor", [pp.buf, sg.buf], [sg.buf], out=sg.ap[:, hs], in0=sg.ap[:, hs], in1=pp.ap, op=ALU.mult)
            DVE("tensor_tensor", [sg.buf, b_x1[t]], [b_x1[t]], out=x1[:, t, :], in0=x1[:, t, :], in1=sg.ap, op=ALU.add)
            ob = osb.next()
            rs, brs = rstd_of(x1[:, t, :], [b_x1[t]], 1024, junkP, b_junkP)
            DVE("scalar_tensor_tensor", [b_x1[t], brs, b_lnf], [ob.buf], out=ob.ap, in0=x1[:, t, :], scalar=rs, in1=lnfb,
                op0=ALU.mult, op1=ALU.mult)
            out_toks.append(DMA(y[t * 128:(t + 1) * 128, :], ob.ap, [ob.buf], [], ob.sem))
        P.final_wait("sp", out_toks[-2:])
        P.run(block)
    return nc


def _t5_bucket(n):
    max_exact = 16
    nf = np.maximum(n, 1).astype(np.float32)
    large = max_exact + (np.log(nf / max_exact) / np.log(2048 / max_exact) * (32 - max_exact)).astype(np.int32)
    large = np.minimum(large, 31)
    return np.where(n < max_exact, n, large).astype(np.int32)


def _const_mats():
    m = np.arange(128)[:, None]
    t = np.arange(128)[None, :]
    same = (m // 64) == (t // 64)
    cm = np.zeros((128, 6, 128), np.float32)
    cm[:, 0, :] = np.eye(128)
    cm[:, 1, :] = np.where(same & (m <= t), 1.0 / 16, 0.0)
    cm[:, 2, :] = np.where(same & (m > t), -1.0 / 16, 0.0)
    cm[:, 3, :] = np.where(same & (m <= t), 1.0, 0.0)
    cm[:, 4, 0:16] = np.eye(4, dtype=np.float32).reshape(16)[None, :]
    sel4 = np.zeros((4, 4, 128), np.float32)
    for hh in range(4):
        sel4[hh, hh, :] = 1.0
    return cm, sel4


def _bias_layout(rel_bias):
    k = np.arange(128)[:, None, None]
    j = np.arange(2)[None, :, None]
    q = np.arange(128)[None, None, :]
    delta = q - k + 128 * (1 - j)
    valid = (delta >= 0) & (delta <= 128)
    out = np.full((128, 3, 2, 2, 2, 128), NEGM, np.float32)
    for g, dil in enumerate((1, 4, 16)):
        bucket = _t5_bucket(np.maximum(delta, 0) * dil)
        for hp in range(2):
            for hh in range(2):
                tab = rel_bias[:, g * 4 + hp * 2 + hh]
                vals = tab[bucket]
                out[:, g, hp, hh] = np.where(valid, vals, NEGM)
    return out.reshape(128, 3, 2, 512)


_PROG = None


def kernel(x, p, ln1, w_in, w_a2, b_a, gla_gn, w_o_gla, w_o_attn, w_out, ln2, w_mlp1, w_mlp2, ln3, w_pp, w_pg,
           rel_bias, ln_f):
    global _PROG
    f = lambda a: np.ascontiguousarray(np.asarray(a, dtype=np.float32))
    x = f(x); p = f(p)
    cm, sel4 = _const_mats()
    cols = np.stack([f(ln1)[0], f(ln2)[0], f(ln3)[0], f(gla_gn)[0]]).reshape(4, 8, 128).transpose(2, 0, 1).reshape(128, 32)
    wa2aug = np.zeros((32, 512), np.float32)
    wa2aug[0:16] = f(w_a2)[0]
    wa2aug[16] = f(b_a)[0]
    shared = {
        "w_in": f(w_in)[0], "w_a2aug": wa2aug, "w_o_gla": f(w_o_gla)[0], "w_o_attn": f(w_o_attn)[0],
        "w_out": f(w_out)[0], "w_mlp1": f(w_mlp1)[0], "w_mlp2": f(w_mlp2)[0], "w_pp": f(w_pp)[0], "w_pg": f(w_pg)[0],
        "cols": np.ascontiguousarray(cols), "ln_f": f(ln_f), "biasm": _bias_layout(f(rel_bias)),
        "cmat": cm, "sel4": sel4,
    }
    in_maps = []
    for c in range(NCORES):
        b, j = c // 4, c % 4
        xe = np.zeros((8192, 1024), np.float32)
        n = SEG * (j + 1)
        xe[8192 - n:] = x[b, 0:n]
        m = dict(shared)
        m["xe"] = xe
        m["p"] = np.ascontiguousarray(p[0, b, j * SEG:(j + 1) * SEG])
        m["hoff"] = np.full((128, 1), NEGM if j == 0 else 0.0, np.float32)
        in_maps.append(m)
    if _PROG is None:
        _PROG = build_program()
    res = run_bass_kernel_spmd(_PROG, in_maps, core_ids=list(range(NCORES)))
    out = np.zeros((2, 8192, 1024), np.float32)
    for c in range(NCORES):
        b, j = c // 4, c % 4
        out[b, j * SEG:(j + 1) * SEG] = res.results[c]["y"]
    return out
```

```python
import contextlib
import numpy as np
import concourse.bass as bass
import concourse.mybir as mybir
from concourse.bass_utils import run_bass_kernel_spmd

F32 = mybir.dt.float32
BF16 = mybir.dt.bfloat16
ALU = mybir.AluOpType
AF = mybir.ActivationFunctionType

SAFE_SAME = True
EPS = 1e-6
NCORES = 8
SEG = 2048
NEGM = -30000.0


class Buf:
    __slots__ = ("w", "r")

    def __init__(self):
        self.w = None
        self.r = []


class Prog:
    ENGS = ("pe", "act", "dve", "pool", "sp")
    WINDOW = 100
    LAT = 300.0
    SLACK = 500.0
    LAT_DMA = 200.0

    def __init__(self, nc, stack):
        self.nc = nc
        self.stack = stack
        self.ops = []
        self.phase = 0
        self.sems = {}
        self.all_dsems = []
        self.final = []
        for e in ("pe", "act", "dve", "pool"):
            self.sems[e] = stack.enter_context(nc.semaphore("s_" + e))

    def dma_sem(self, name):
        s = self.stack.enter_context(self.nc.semaphore("d_" + name))
        d = [s, 0]
        self.all_dsems.append(d)
        return d

    def op(self, eng, fn, reads=(), writes=(), dsem=None, cost=500.0, fin=None):
        idx = len(self.ops)
        deps = set()
        for b in reads:
            if b.w is not None:
                deps.add(b.w)
        for b in writes:
            if b.w is not None:
                deps.add(b.w)
            deps.update(b.r)
        self.ops.append([eng, fn, sorted(deps), dsem, cost, self.phase, cost if fin is None else fin])
        for b in reads:
            b.r.append(idx)
        for b in writes:
            b.w = idx
            b.r = []
        return idx

    def barrier(self):
        self.phase += 1

    def final_wait(self, eng, toks):
        self.final.append((eng, list(toks)))

    def schedule(self):
        ops = self.ops
        n = len(ops)
        succ = [[] for _ in range(n)]
        for i, o in enumerate(ops):
            for d in o[2]:
                if ops[d][5] == o[5]:
                    succ[d].append(i)
        bl = [0.0] * n
        for i in range(n - 1, -1, -1):
            m = 0.0
            for j in succ[i]:
                v = bl[j] + (0.0 if ops[j][0] == ops[i][0] else self.LAT)
                if v > m:
                    m = v
            bl[i] = ops[i][6] + m
        order = {e: [] for e in self.ENGS}
        finish = {}
        tnow = 0.0
        for ph in range(self.phase + 1):
            pend = {e: [] for e in self.ENGS}
            for i, o in enumerate(ops):
                if o[5] == ph:
                    pend[o[0]].append(i)
            tfree = {e: tnow for e in self.ENGS}
            remaining = sum(len(v) for v in pend.values())
            cand = {e: None for e in self.ENGS}
            dirty = set(self.ENGS)
            while remaining:
                for e in list(dirty):
                    cl = []
                    for i in pend[e][:self.WINDOW]:
                        o = ops[i]
                        st = tfree[e]
                        ok = True
                        for d in o[2]:
                            f = finish.get(d)
                            if f is None:
                                ok = False
                                break
                            lat = self.LAT_DMA if ops[d][3] is not None else (0.0 if ops[d][0] == e else self.LAT)
                            if f + lat > st:
                                st = f + lat
                        if ok:
                            cl.append((st, i))
                    if not cl:
                        cand[e] = None
                    else:
                        tmin = min(c[0] for c in cl)
                        lim = tmin + self.SLACK
                        best = None
                        for st, i in cl:
                            if st <= lim:
                                key = (-bl[i], i)
                                if best is None or key < best[0]:
                                    best = (key, st, i)
                        cand[e] = (best[1], best[2])
                dirty.clear()
                pick = None
                for e in self.ENGS:
                    c = cand[e]
                    if c is not None and (pick is None or c < pick[0]):
                        pick = (c, e)
                assert pick is not None, "scheduler stuck"
                (st, i), e = pick
                o = ops[i]
                tfree[e] = st + o[4]
                finish[i] = st + o[6]
                pend[e].remove(i)
                order[e].append(i)
                remaining -= 1
                dirty.update(self.ENGS)
            tnow = max(tfree.values())
            self.phase_ends = getattr(self, 'phase_ends', []) + [tnow]
            for e in self.ENGS:
                order[e].append(None)
        self.est_total = tnow
        return order

    def lower(self):
        order = self.schedule()
        ops = self.ops
        tok = {}
        cnt = {e: 0 for e in ("pe", "act", "dve", "pool")}
        bar_cnt = []
        nph = self.phase + 1
        pos = {e: 0 for e in self.ENGS}
        dcount = {id(d): 0 for d in self.all_dsems}
        bar_state = []
        for ph in range(nph):
            for e in self.ENGS:
                lst = order[e]
                while lst[pos[e]] is not None:
                    i = lst[pos[e]]
                    o = ops[i]
                    if o[3] is not None:
                        dcount[id(o[3])] += 16
                        tok[i] = (o[3][0], dcount[id(o[3])], e, True)
                    else:
                        cnt[e] += 1
                        tok[i] = (self.sems[e], cnt[e], e, False)
                    pos[e] += 1
                pos[e] += 1
            bar_state.append((dict(cnt), dict(dcount)))
        streams = {e: [] for e in self.ENGS}
        for e in self.ENGS:
            waited = {}
            ph = 0
            for i in order[e]:
                if i is None:
                    c, dc = bar_state[ph]
                    waits = []
                    for e2 in ("pe", "act", "dve", "pool"):
                        if e2 != e and c[e2] > waited.get(id(self.sems[e2]), 0):
                            waits.append((self.sems[e2], c[e2]))
                            waited[id(self.sems[e2])] = c[e2]
                    for d in self.all_dsems:
                        v = dc[id(d)]
                        if v > waited.get(id(d[0]), 0):
                            waits.append((d[0], v))
                            waited[id(d[0])] = v
                    if waits and ph < nph - 1:
                        streams[e].append((waits, None, None))
                    ph += 1
                    continue
                o = ops[i]
                waits = {}
                for d in o[2]:
                    s, v, e2, isdma = tok[d]
                    if e2 == e and not isdma:
                        if e in ("pe", "sp") or not SAFE_SAME:
                            continue
                    k = id(s)
                    if waited.get(k, 0) >= v:
                        continue
                    if k not in waits or waits[k][1] < v:
                        waits[k] = (s, v)
                for k, (s, v) in waits.items():
                    waited[k] = v
                t = tok[i]
                streams[e].append((list(waits.values()), o[1], (t[0], 16 if t[3] else 1)))
        for eng, toks in self.final:
            streams[eng].append(([(tok[t][0], tok[t][1]) for t in toks], None, None))
        self.streams = streams

    def run(self, block):
        self.lower()

        def play(name):
            def _f(e):
                for waits, fn, inc in self.streams[name]:
                    for s, v in waits:
                        e.wait_ge(s, v)
                    if fn is None:
                        continue
                    ins = fn(e)
                    if inc is not None:
                        ins.then_inc(inc[0], inc[1])
            return _f
        block.tensor(play("pe"))
        block.scalar(play("act"))
        block.vector(play("dve"))
        block.gpsimd(play("pool"))
        block.sync(play("sp"))


class Slot:
    def __init__(self, ap, sem=None):
        self.ap = ap
        self.buf = Buf()
        self.sem = sem


class Rot:
    def __init__(self, slots):
        self.slots = slots
        self.i = 0

    def next(self):
        s = self.slots[self.i % len(self.slots)]
        self.i += 1
        return s


def build_program():
    nc = bass.Bass("TRN2", target_bir_lowering=False)

    def din(name, shape):
        return nc.dram_tensor(name, shape, F32, kind="ExternalInput").ap()

    xe = din("xe", [8192, 1024])
    pin = din("p", [SEG, 256])
    w_in = din("w_in", [1024, 9744])
    wa2_d = din("w_a2aug", [32, 512])
    wog_d = din("w_o_gla", [1024, 1024])
    woa_d = din("w_o_attn", [512, 1024])
    wout_d = din("w_out", [1024, 1024])
    w1_d = din("w_mlp1", [1024, 4096])
    w2_d = din("w_mlp2", [4096, 1024])
    wpp_d = din("w_pp", [256, 1024])
    wpg_d = din("w_pg", [1024, 1024])
    cols_d = din("cols", [128, 32])
    lnf_d = din("ln_f", [1024])
    biasm_d = din("biasm", [128, 3, 2, 512])
    hoff_d = din("hoff", [128, 1])
    cmat_d = din("cmat", [128, 6, 128])
    sel4_d = din("sel4", [4, 4, 128])
    y = nc.dram_tensor("y", [SEG, 1024], F32, kind="ExternalOutput").ap()

    with contextlib.ExitStack() as st:
        P = Prog(nc, st)
        ARENA_BYTES = 200 * 1024
        arena = st.enter_context(nc.sbuf_tensor("arena", [128, ARENA_BYTES // 2], BF16))
        psb = [st.enter_context(nc.psum_tensor("psb%d" % i, [128, 512], F32)) for i in range(7)]
        ptr_t = st.enter_context(nc.psum_tensor("ptr", [128, 8, 128], BF16))
        block = st.enter_context(nc.Block())

        def carve(off, shape, dt):
            n = 1
            for s in shape[1:]:
                n *= s
            es = 2 if dt == BF16 else 4
            assert off % 4 == 0 and off + n * es <= ARENA_BYTES, (off, shape)
            a = arena[0:shape[0], off // 2: off // 2 + n * es // 2]
            if dt == F32:
                a = a.bitcast(F32)
            if len(shape) == 3:
                a = a.rearrange("p (a b) -> p a b", a=shape[1])
            elif len(shape) == 4:
                a = a.rearrange("p (a b c) -> p a b c", a=shape[1], b=shape[2])
            return a

        class Mem:
            def __init__(self, base, limit):
                self.off = base
                self.limit = limit

            def alloc(self, shape, dt):
                n = 1
                for s in shape[1:]:
                    n *= s
                es = 2 if dt == BF16 else 4
                a = carve(self.off, shape, dt)
                self.off += (n * es + 63) // 64 * 64
                assert self.off <= self.limit, (self.off, self.limit)
                return a

        KB = 1024

        def sl(start, n, step):
            return slice(start, start + step * (n - 1) + 1, step)

        def fsz(ap):
            n = 1
            for x in ap.shape[1:]:
                n *= x
            return n

        def PE(mms, reads, writes):
            cost = 0.0
            for m in mms:
                n = max(fsz(m[2]), 64)
                c = n / 2.4 + 25.0
                if m[1].dtype == F32:
                    c *= 4
                cost += c

            def fn(e):
                ins = None
                for m in mms:
                    kw = dict(start=m[3], stop=m[4])
                    if len(m) > 5 and m[5]:
                        kw["skip_group_check"] = True
                    ins = e.matmul(m[0], lhsT=m[1], rhs=m[2], **kw)
                return ins
            return P.op("pe", fn, reads, writes, cost=cost)

        def ACT(out, in_, func, reads, writes, **kw):
            return P.op("act", lambda e: e.activation(out=out, in_=in_, func=func, **kw), reads, writes,
                        cost=180.0 + 0.83 * fsz(out))

        def ENG(eng, method, reads, writes, **kw):
            o = kw.get("out", kw.get("ap"))
            n = fsz(o)
            cost = (100.0 + 1.15 * n) if eng == "dve" else (250.0 + 0.6 * n)
            return P.op(eng, lambda e: getattr(e, method)(**kw), reads, writes, cost=cost)

        def DVE(method, reads, writes, **kw):
            return ENG("dve", method, reads, writes, **kw)

        def POOL(method, reads, writes, **kw):
            return ENG("pool", method, reads, writes, **kw)

        def DMA(out, in_, reads, writes, dsem):
            nbytes = out.shape[0] * fsz(out) * 4
            return P.op("sp", lambda e: e.dma_start(out=out, in_=in_), reads, writes, dsem=dsem, cost=120.0, fin=2000.0 + nbytes / 150.0)

        psrot = Rot([Slot(t[:]) for t in psb])
        ptr = Slot(ptr_t[:])

        def PS():
            return psrot.next()

        G = Mem(0, 28 * KB)
        cmat = G.alloc([128, 6, 128], F32)
        identb = G.alloc([128, 128], BF16)
        onesel4 = G.alloc([128, 4, 4], BF16)
        sel4 = G.alloc([4, 4, 128], F32)
        colsT = G.alloc([128, 32], F32)
        hoff = G.alloc([128, 1], F32)
        mhalf4 = G.alloc([128, 4], F32)
        mhalf = mhalf4[:, 0:1]
        statv = G.alloc([128, 64], F32)
        junk = G.alloc([128, 1024], BF16)
        stage = Rot([Slot(G.alloc([128, 1024], F32), P.dma_sem("st%d" % i)) for i in range(2)])
        xts = Rot([Slot(G.alloc([128, 1024], F32), P.dma_sem("xt%d" % i)) for i in range(2)])
        hbs = Rot([Slot(G.alloc([128, 1024], BF16)) for i in range(2)])
        LT = cmat[:, 1, :]
        UT = cmat[:, 2, :]
        CM = cmat[:, 3, :]
        b_const = Buf()
        b_junk = Buf()
        stat_i = [0]

        b_stat = [Buf() for _ in range(64)]

        def stat_col():
            c = stat_i[0] % 64
            stat_i[0] += 1
            return statv[:, c:c + 1], b_stat[c]

        dc = P.dma_sem("const")
        DMA(cmat, cmat_d, [], [b_const], dc)
        DMA(sel4, sel4_d, [], [b_const], dc)
        DMA(colsT, cols_d, [], [b_const], dc)
        DMA(hoff, hoff_d, [], [b_const], dc)
        DVE("tensor_copy", [b_const], [b_const], out=identb, in_=cmat[:, 0, :])
        DVE("tensor_copy", [b_const], [b_const], out=onesel4, in_=cmat[:, 4, 0:16].rearrange("p (a b) -> p a b", a=4))
        POOL("memset", [], [b_const], ap=mhalf4, constant=-0.5)

        def col(i, kc):
            return colsT[:, i * 8 + kc: i * 8 + kc + 1]

        def load_w(dst_fn, dram, row0, nk, col0, ncols, scale_i, dst_bufs):
            for kc in range(nk):
                for c0 in range(0, ncols, 1024):
                    cn = min(1024, ncols - c0)
                    s = stage.next()
                    DMA(s.ap[:, 0:cn], dram[row0 + kc * 128: row0 + (kc + 1) * 128, col0 + c0: col0 + c0 + cn],
                        [], [s.buf], s.sem)
                    sc = col(scale_i, kc) if scale_i is not None else 1.0
                    POOL("tensor_scalar", [s.buf, b_const], [dst_bufs[kc]], out=dst_fn(kc, c0, cn), in0=s.ap[:, 0:cn],
                         scalar1=sc, scalar2=1.0, op0=ALU.mult, op1=ALU.mult)

        def rstd_of(src_ap, src_bufs, n, scr_ap, scr_buf):
            ss, bss = stat_col()
            ACT(scr_ap, src_ap, AF.Square, src_bufs, [scr_buf, bss], accum_out=ss)
            vv, bvv = stat_col()
            POOL("tensor_scalar", [bss], [bvv], out=vv, in0=ss, scalar1=1.0 / n, scalar2=EPS, op0=ALU.mult, op1=ALU.add)
            rs, brs = stat_col()
            POOL("tensor_tensor", [bvv, b_const], [brs], out=rs, in0=vv, in1=mhalf, op=ALU.pow)
            return rs, brs

        def norm_to_hT(src_ap, src_bufs, hT_out, hT_bufs):
            hb = hbs.next()
            rs, brs = rstd_of(src_ap, src_bufs, 1024, junk, b_junk)
            DVE("tensor_scalar", list(src_bufs) + [brs], [hb.buf], out=hb.ap, in0=src_ap, scalar1=rs, scalar2=None,
                op0=ALU.mult)

            def fn(e):
                ins = None
                for kc in range(8):
                    ins = e.transpose(ptr.ap[:, kc, :], hb.ap[:, kc * 128:(kc + 1) * 128], identb)
                return ins
            P.op("pe", fn, [hb.buf, b_const], [ptr.buf], cost=650.0)
            ACT(hT_out, ptr.ap, AF.Copy, [ptr.buf], hT_bufs)

        def load_x(row0, step=1):
            s = xts.next()
            DMA(s.ap, xe[sl(row0, 128, step), :], [], [s.buf], s.sem)
            return s

        OhT = carve(28 * KB, [128, 4, 2048], BF16)
        b_OhT = Buf()
        ogT = carve(44 * KB, [128, 8, 2048], BF16)
        b_ogT = [Buf() for _ in range(16)]
        mixT = carve(76 * KB, [128, 8, 2048], BF16)
        b_mixT = [Buf() for _ in range(4)]
        x1 = carve(108 * KB, [128, 16, 1024], F32)
        b_x1 = [Buf() for _ in range(16)]
        h2T = carve(28 * KB, [128, 8, 2048], BF16)
        b_h2T = [Buf() for _ in range(16)]

        hTh = carve(44 * KB, [128, 8, 2048], BF16)
        hTo = carve(76 * KB, [128, 8, 2048], BF16)
        b_hTh = [Buf() for _ in range(16)]
        b_hTo = [Buf() for _ in range(16)]
        MA = Mem(108 * KB, 200 * KB)
        NT = MA.alloc([128, 4, 2048], F32)
        ST = MA.alloc([4, 2048], F32)
        biasT = carve(28 * KB, [128, 512], F32)
        expbN = carve(30 * KB, [128, 512], F32)
        expbH = carve(32 * KB, [128, 512], F32)
        wA = [MA.alloc([128, 8, 3, 256], BF16) for _ in range(2)]
        b_wA = [[Buf() for _ in range(8)] for _ in range(2)]
        KTs = Rot([Slot(MA.alloc([128, 2, 512], BF16)) for _ in range(3)])
        QTs = Rot([Slot(MA.alloc([128, 2, 512], BF16)) for _ in range(2)])
        Vs = Rot([Slot(MA.alloc([128, 4, 256], BF16)) for _ in range(3)])
        Efs = Rot([Slot(MA.alloc([128, 512], F32)) for _ in range(2)])
        ETs = Rot([Slot(MA.alloc([128, 512], BF16)) for _ in range(2)])
        b_bias = Buf()
        b_exp = Buf()
        scale_att = 128.0 ** -0.5
        dbias = P.dma_sem("bias")
        b_NTq = [[Buf() for _ in range(4)] for _ in range(2)]
        b_STq = [Buf() for _ in range(4)]
        DVE("memset", [], [b for l in b_NTq for b in l], ap=NT, constant=0.0)
        DVE("memset", [], b_STq, ap=ST, constant=0.0)
        wslot = 0

        def proj_seg(hTs, hbufs, t0, nb, wa, bwa, want_q):
            cols = slice(t0 * 128, (t0 + nb) * 128)
            hb_ = hbufs[t0:t0 + nb]
            kt = KTs.next()
            for hh in range(2):
                ps = PS()
                PE([(ps.ap[:, 0:nb * 128], wa[:, kc, 1, hh * 128:(hh + 1) * 128], hTs[:, kc, cols], kc == 0, kc == 7)
                    for kc in range(8)], hb_ + bwa, [ps.buf])
                ACT(kt.ap[:, hh, 0:nb * 128], ps.ap[:, 0:nb * 128], AF.Copy, [ps.buf], [kt.buf])
            qt = None
            if want_q:
                qt = QTs.next()
                for hh in range(2):
                    ps = PS()
                    PE([(ps.ap[:, 0:nb * 128], wa[:, kc, 0, hh * 128:(hh + 1) * 128], hTs[:, kc, cols], kc == 0, kc == 7)
                        for kc in range(8)], hb_ + bwa, [ps.buf])
                    DVE("tensor_copy", [ps.buf], [qt.buf], out=qt.ap[:, hh, 0:nb * 128], in_=ps.ap[:, 0:nb * 128])
            vs = Vs.next()
            for b in range(nb):
                bc = slice((t0 + b) * 128, (t0 + b + 1) * 128)
                ps = PS()
                PE([(ps.ap[:, 0:256], hTs[:, kc, bc], wa[:, kc, 2, :], kc == 0, kc == 7) for kc in range(8)],
                   [hbufs[t0 + b]] + bwa, [ps.buf])
                if b % 2 == 0:
                    ACT(vs.ap[:, b, :], ps.ap[:, 0:256], AF.Copy, [ps.buf], [vs.buf])
                else:
                    DVE("tensor_copy", [ps.buf], [vs.buf], out=vs.ap[:, b, :], in_=ps.ap[:, 0:256])
            return kt, qt, vs

        def attend(g, hp, prevb, curb, qt, qb, first, nat, quarters):
            blk = [prevb, curb]
            sp_ = PS()
            s4 = sp_.ap.rearrange("p (h j q) -> p h j q", h=2, j=2)
            PE([(s4[:, hh, j, :], blk[j][0].ap[:, hh, blk[j][2] * 128:(blk[j][2] + 1) * 128],
                 qt.ap[:, hh, qb * 128:(qb + 1) * 128], True, True) for hh in range(2) for j in range(2)],
               [prevb[0].buf, curb[0].buf, qt.buf], [sp_.buf])
            ef = Efs.next()
            ACT(ef.ap, sp_.ap, AF.Exp, [sp_.buf], [ef.buf], scale=scale_att)
            et = ETs.next()
            DVE("tensor_tensor", [ef.buf, b_exp], [et.buf], out=et.ap, in0=ef.ap,
                in1=(expbH if first else expbN), op=ALU.mult)
            e4t = et.ap.rearrange("p (h j q) -> p h j q", h=2, j=2)
            np_ = PS()
            mms = []
            for hh in range(2):
                for j in range(2):
                    mms.append((np_.ap[:, hh * 128:(hh + 1) * 128],
                                blk[j][1].ap[:, blk[j][2], hh * 128:(hh + 1) * 128], e4t[:, hh, j, :], j == 0, j == 1))
            k = 0
            for hh in range(2):
                for j in range(2):
                    mms.append((np_.ap[0:4, 256:384], onesel4[:, hp * 2 + hh, :], e4t[:, hh, j, :], k == 0, k == 3))
                    k += 1
            PE(mms, [prevb[1].buf, curb[1].buf, et.buf, b_const], [np_.buf])
            nb_ = [b_NTq[hp][q] for q in quarters]
            DVE("tensor_tensor", [np_.buf] + nb_, nb_, out=NT[:, hp * 2:hp * 2 + 2, nat], in0=NT[:, hp * 2:hp * 2 + 2, nat],
                in1=np_.ap[:, 0:256].rearrange("p (h q) -> p h q", h=2), op=ALU.add)
            sb_ = [b_STq[q] for q in quarters]
            DVE("tensor_tensor", [np_.buf] + sb_, sb_, out=ST[:, nat], in0=ST[:, nat],
                in1=np_.ap[0:4, 256:384], op=ALU.add)

        for g in range(3):
            dil = (1, 4, 16)[g]
            nbk = 16 // dil
            for k in range(dil):
                s = load_x(6144 - 128 * dil + k, dil)
                norm_to_hT(s.ap, [s.buf], hTh[:, :, k * 128:(k + 1) * 128], [b_hTh[k]])
            for k in range(16):
                r, n = divmod(k, nbk)
                s = load_x(6144 + r + dil * 128 * n, dil)
                norm_to_hT(s.ap, [s.buf], hTo[:, :, k * 128:(k + 1) * 128], [b_hTo[k]])
            for hp in range(2):
                DMA(biasT, biasm_d[:, g, hp, :], [], [b_bias], dbias)
                ACT(expbN, biasT, AF.Exp, [b_bias], [b_exp])
                ACT(expbH, biasT, AF.Exp, [b_bias], [b_exp])
                b4 = biasT.rearrange("p (h j q) -> p h j q", h=2, j=2)
                e4 = expbH.rearrange("p (h j q) -> p h j q", h=2, j=2)
                ACT(e4[:, :, 0, :], b4[:, :, 0, :], AF.Exp, [b_bias, b_const], [b_exp], bias=hoff)
                wa = wA[wslot % 2]
                bwa = b_wA[wslot % 2]
                wslot += 1
                base = 3088 + g * 1536
                for kc in range(8):
                    s = stage.next()
                    src = w_in[kc * 128:(kc + 1) * 128, base:base + 1536].rearrange("p (c x) -> p c x", c=3)[:, :, hp * 256:(hp + 1) * 256]
                    sv = s.ap[:, 0:768].rearrange("p (c x) -> p c x", c=3)
                    DMA(sv, src, [], [s.buf], s.sem)
                    POOL("tensor_scalar", [s.buf, b_const], [bwa[kc]], out=wa[:, kc, :, :], in0=sv,
                         scalar1=col(0, kc), scalar2=1.0, op0=ALU.mult, op1=ALU.mult)
                if g < 2:
                    for r in range(dil):
                        kt_p, _, vs_p = proj_seg(hTh, b_hTh, r, 1, wa, bwa, False)
                        prev = (kt_p, vs_p, 0)
                        for n0 in range(0, nbk, 4):
                            nb = min(4, nbk - n0)
                            kt, qt, vs = proj_seg(hTo, b_hTo, r * nbk + n0, nb, wa, bwa, True)
                            for b in range(nb):
                                cur = (kt, vs, b)
                                n = n0 + b
                                tok0 = r + dil * 128 * n
                                quarters = [tok0 // 512] if g == 0 else [n]
                                attend(g, hp, prev, cur, qt, b, (n == 0), sl(tok0, 128, dil), quarters)
                                prev = cur
                else:
                    for q4 in range(4):
                        kt_h, _, vs_h = proj_seg(hTh, b_hTh, q4 * 4, 4, wa, bwa, False)
                        kt, qt, vs = proj_seg(hTo, b_hTo, q4 * 4, 4, wa, bwa, True)
                        for b in range(4):
                            r = q4 * 4 + b
                            attend(g, hp, (kt_h, vs_h, b), (kt, vs, b), qt, b, True, sl(r, 128, 16), [0, 1, 2, 3])
        DVE("reciprocal", b_STq, b_STq, out=ST, in_=ST)
        for hh in range(4):
            for tb in range(4):
                ps = PS()
                PE([(ps.ap, sel4[:, hh, :], ST[:, tb * 512:(tb + 1) * 512], True, True)], b_STq + [b_const], [ps.buf])
                DVE("tensor_tensor", [ps.buf, b_NTq[hh // 2][tb]], [b_OhT], out=OhT[:, hh, tb * 512:(tb + 1) * 512],
                    in0=NT[:, hh, tb * 512:(tb + 1) * 512], in1=ps.ap, op=ALU.mult)
        P.barrier()

        MG = Mem(76 * KB, 200 * KB)
        wG = MG.alloc([128, 8, 3088], BF16)
        b_wG = [Buf() for _ in range(8)]
        wa2 = MG.alloc([32, 512], BF16)
        b_wa2 = Buf()
        CM4 = MG.alloc([128, 4, 128], F32)
        hTt = [Slot(MG.alloc([128, 8, 128], BF16)) for _ in range(2)]
        haT = [Slot(MG.alloc([32, 128], BF16)) for _ in range(2)]
        ezs = [Slot(MG.alloc([128, 512], F32)) for _ in range(2)]
        sps = [Slot(MG.alloc([128, 512], F32)) for _ in range(2)]
        wex = [Slot(MG.alloc([128, 512], F32)) for _ in range(2)]
        kouts = [Slot(MG.alloc([128, 512], BF16)) for _ in range(2)]
        vbs = [Slot(MG.alloc([128, 1024], BF16)) for _ in range(2)]
        e1s = [Slot(MG.alloc([128, 4, 128], F32)) for _ in range(2)]
        e2s = [Slot(MG.alloc([128, 4, 128], F32)) for _ in range(2)]
        qdA = [Slot(MG.alloc([128, 4, 128], BF16)) for _ in range(2)]
        qdB = [Slot(MG.alloc([128, 4, 128], BF16)) for _ in range(2)]
        kinT = [Slot(MG.alloc([128, 4, 128], BF16)) for _ in range(2)]
        attnT = [Slot(MG.alloc([128, 4, 128], BF16)) for _ in range(2)]
        ers = [Slot(MG.alloc([128, 1024], F32)) for _ in range(2)]
        sil = [Slot(MG.alloc([128, 1024], F32)) for _ in range(2)]
        Sst = MG.alloc([128, 4, 256], F32)
        SbA = MG.alloc([128, 4, 256], BF16)
        SbB = MG.alloc([128, 4, 256], BF16)
        ogb = Slot(MG.alloc([128, 1024], BF16))
        ssh = MG.alloc([128, 8], F32)
        b_ssh = Buf()
        b_S = [Buf() for _ in range(4)]
        b_SbA = [Buf() for _ in range(4)]
        b_SbB = [Buf() for _ in range(4)]

        load_w(lambda kc, c0, cn: wG[:, kc, c0:c0 + cn], w_in, 0, 8, 0, 3088, 0, b_wG)
        s = stage.next()
        DMA(s.ap[0:32, 0:512], wa2_d, [], [s.buf], s.sem)
        POOL("tensor_copy", [s.buf], [b_wa2], out=wa2, in_=s.ap[0:32, 0:512])
        for hh in range(4):
            DVE("tensor_copy", [b_const], [b_const], out=CM4[:, hh, :], in_=CM)
        for i in range(2):
            POOL("memset", [], [haT[i].buf], ap=haT[i].ap, constant=1.0)
            POOL("memset", [], [qdA[i].buf], ap=qdA[i].ap, constant=0.0)
            POOL("memset", [], [qdB[i].buf], ap=qdB[i].ap, constant=0.0)
        DVE("memset", [], b_S, ap=Sst, constant=0.0)
        POOL("memset", [], b_SbA, ap=SbA, constant=0.0)

        gl = {}

        def gla_stage1(t):
            own = t >= 48
            i = t % 2
            xs = load_x(t * 128)
            ht = hTt[i]
            norm_to_hT(xs.ap, [xs.buf], ht.ap, [ht.buf])
            rb = [ht.buf] + b_wG
            kps = PS()
            PE([(kps.ap, ht.ap[:, kc, :], wG[:, kc, 512:1024], kc == 0, kc == 7) for kc in range(8)], rb, [kps.buf])
            hps = PS()
            PE([(hps.ap[0:16, 0:128], wG[:, kc, 3072:3088], ht.ap[:, kc, :], kc == 0, kc == 7) for kc in range(8)], rb, [hps.buf])
            ACT(haT[i].ap[0:16, :], hps.ap[0:16, 0:128], AF.Copy, [hps.buf], [haT[i].buf])
            zps = PS()
            PE([(zps.ap, haT[i].ap, wa2, True, True)], [haT[i].buf, b_wa2], [zps.buf])
            ACT(ezs[i].ap, zps.ap, AF.Exp, [zps.buf], [ezs[i].buf], scale=-1.0)
            ACT(sps[i].ap, ezs[i].ap, AF.Ln, [ezs[i].buf], [sps[i].buf], bias=1.0)
            vp = [PS(), PS()]
            for h2 in range(2):
                PE([(vp[h2].ap, ht.ap[:, kc, :], wG[:, kc, 1024 + h2 * 512:1536 + h2 * 512], kc == 0, kc == 7) for kc in range(8)],
                   rb, [vp[h2].buf])
            ACT(vbs[i].ap[:, 0:512], vp[0].ap, AF.Copy, [vp[0].buf], [vbs[i].buf])
            DVE("tensor_copy", [vp[1].buf], [vbs[i].buf], out=vbs[i].ap[:, 512:1024], in_=vp[1].ap)
            dps = PS()
            PE([(dps.ap, UT, sps[i].ap, True, True)], [sps[i].buf, b_const], [dps.buf])
            ACT(wex[i].ap, dps.ap, AF.Exp, [dps.buf], [wex[i].buf])
            DVE("tensor_tensor", [kps.buf, wex[i].buf], [kouts[i].buf], out=kouts[i].ap, in0=kps.ap, in1=wex[i].ap, op=ALU.mult)
            nps = PS()
            n4 = nps.ap.rearrange("p (h q) -> p h q", h=4)
            PE([(n4[:, hh, :], sps[i].ap[:, hh * 128:(hh + 1) * 128], LT, True, True) for hh in range(4)],
               [sps[i].buf, b_const], [nps.buf])
            ACT(e1s[i].ap, n4, AF.Exp, [nps.buf], [e1s[i].buf], scale=-1.0)
            if not own:
                return
            ACT(e2s[i].ap, n4, AF.Exp, [nps.buf], [e2s[i].buf])
            qps = PS()
            q4 = qps.ap.rearrange("p (h q) -> p h q", h=4)
            PE([(q4[:, hh, :], wG[:, kc, hh * 128:(hh + 1) * 128], ht.ap[:, kc, :], kc == 0, kc == 7)
                for hh in range(4) for kc in range(8)], rb, [qps.buf])
            DVE("scalar_tensor_tensor", [qps.buf, e1s[i].buf], [qdA[i].buf], out=qdA[i].ap[:, :, 0:64], in0=q4[:, :, 0:64],
                scalar=128.0 ** -0.5, in1=e1s[i].ap[:, :, 0:64], op0=ALU.mult, op1=ALU.mult)
            DVE("scalar_tensor_tensor", [qps.buf, e1s[i].buf], [qdB[i].buf], out=qdB[i].ap[:, :, 64:128], in0=q4[:, :, 64:128],
                scalar=128.0 ** -0.5, in1=e1s[i].ap[:, :, 64:128], op0=ALU.mult, op1=ALU.mult)
            ktp = PS()
            k4 = ktp.ap.rearrange("p (h q) -> p h q", h=4)
            PE([(k4[:, hh, :], wG[:, kc, 512 + hh * 128:512 + (hh + 1) * 128], ht.ap[:, kc, :], kc == 0, kc == 7)
                for hh in range(4) for kc in range(8)], rb, [ktp.buf])
            DVE("tensor_tensor", [ktp.buf, e2s[i].buf], [kinT[i].buf], out=kinT[i].ap, in0=k4, in1=e2s[i].ap, op=ALU.mult)
            aps = PS()
            a4 = aps.ap.rearrange("p (h q) -> p h q", h=4)
            mms = []
            for hh in range(4):
                mms.append((a4[:, hh, 0:64], kinT[i].ap[:, hh, :], qdA[i].ap[:, hh, 0:64], True, True))
                mms.append((a4[:, hh, 64:128], kinT[i].ap[:, hh, :], qdB[i].ap[:, hh, 64:128], True, True))
            PE(mms, [kinT[i].buf, qdA[i].buf, qdB[i].buf], [aps.buf])
            DVE("tensor_tensor", [aps.buf, b_const], [attnT[i].buf], out=attnT[i].ap, in0=a4, in1=CM4, op=ALU.mult)
            rp = [PS(), PS()]
            for h2 in range(2):
                PE([(rp[h2].ap, ht.ap[:, kc, :], wG[:, kc, 2048 + h2 * 512:2560 + h2 * 512], kc == 0, kc == 7) for kc in range(8)],
                   rb, [rp[h2].buf])
                ACT(ers[i].ap[:, h2 * 512:(h2 + 1) * 512], rp[h2].ap, AF.Exp, [rp[h2].buf], [ers[i].buf], scale=-1.0)
            ACT(ers[i].ap, ers[i].ap, AF.Ln, [ers[i].buf], [ers[i].buf], bias=1.0)
            ACT(ers[i].ap, ers[i].ap, AF.Exp, [ers[i].buf], [ers[i].buf], scale=-1.0)
            for h2 in range(2):
                DVE("tensor_tensor", [rp[h2].buf, ers[i].buf], [sil[i].buf], out=sil[i].ap[:, h2 * 512:(h2 + 1) * 512],
                    in0=rp[h2].ap, in1=ers[i].ap[:, h2 * 512:(h2 + 1) * 512], op=ALU.mult)

        def gla_stage2(t):
            own = t >= 48
            i = t % 2
            tt = t - 48
            last_prefix = (t == 47)
            ko = kouts[i]
            vb = vbs[i]
            e1 = e1s[i]
            if own:
                oP = [PS(), PS()]
                for pr in range(2):
                    mms = []
                    for hq in range(2):
                        hh = pr * 2 + hq
                        mms.append((oP[pr].ap[:, hq * 256:(hq + 1) * 256], attnT[i].ap[:, hh, :], vb.ap[:, hh * 256:(hh + 1) * 256],
                                    hq == 0, False, True))
                    for hq in range(2):
                        hh = pr * 2 + hq
                        mms.append((oP[pr].ap[:, hq * 256:(hq + 1) * 256], qdA[i].ap[:, hh, :], SbA[:, hh, :], False, False, True))
                    PE(mms, [attnT[i].buf, vb.buf, qdA[i].buf] + b_SbA[pr * 2:pr * 2 + 2], [oP[pr].buf])
            for c in range(2):
                uP = [PS(), PS()]
                for pr in range(2):
                    PE([(uP[pr].ap[:, hq * 256:(hq + 1) * 256], ko.ap[c * 64:(c + 1) * 64, (pr * 2 + hq) * 128:(pr * 2 + hq + 1) * 128],
                         vb.ap[c * 64:(c + 1) * 64, (pr * 2 + hq) * 256:(pr * 2 + hq + 1) * 256], True, True) for hq in range(2)],
                       [ko.buf, vb.buf], [uP[pr].buf])
                for hh in range(4):
                    pr, hq = hh // 2, hh % 2
                    DVE("scalar_tensor_tensor", [uP[pr].buf, e1.buf, b_S[hh]], [b_S[hh]], out=Sst[:, hh, :], in0=Sst[:, hh, :],
                        scalar=e1.ap[:, hh, c * 64 + 63:c * 64 + 64], in1=uP[pr].ap[:, hq * 256:(hq + 1) * 256],
                        op0=ALU.mult, op1=ALU.add)
                    if c == 0 and own:
                        ACT(SbB[:, hh, :], Sst[:, hh, :], AF.Copy, [b_S[hh]], [b_SbB[hh]])
                    if c == 1 and (own or last_prefix):
                        ACT(SbA[:, hh, :], Sst[:, hh, :], AF.Copy, [b_S[hh]], [b_SbA[hh]])
                if c == 0 and own:
                    for pr in range(2):
                        PE([(oP[pr].ap[:, hq * 256:(hq + 1) * 256], qdB[i].ap[:, pr * 2 + hq, :], SbB[:, pr * 2 + hq, :],
                             False, hq == 1, True) for hq in range(2)],
                           [qdB[i].buf] + b_SbB[pr * 2:pr * 2 + 2], [oP[pr].buf])
            if not own:
                return
            bssh = b_ssh
            for hh in range(4):
                pr, hq = hh // 2, hh % 2
                ACT(junk[:, 0:256], oP[pr].ap[:, hq * 256:(hq + 1) * 256], AF.Square, [oP[pr].buf], [b_junk, bssh],
                    accum_out=ssh[:, hh:hh + 1])
            POOL("tensor_scalar", [bssh], [bssh], out=ssh[:, 4:8], in0=ssh[:, 0:4], scalar1=1.0 / 256, scalar2=EPS,
                 op0=ALU.mult, op1=ALU.add)
            POOL("tensor_tensor", [bssh, b_const], [bssh], out=ssh[:, 4:8], in0=ssh[:, 4:8], in1=mhalf4,
                 op=ALU.pow)
            for hh in range(4):
                pr, hq = hh // 2, hh % 2
                DVE("scalar_tensor_tensor", [oP[pr].buf, bssh, sil[i].buf], [ogb.buf], out=ogb.ap[:, hh * 256:(hh + 1) * 256],
                    in0=oP[pr].ap[:, hq * 256:(hq + 1) * 256], scalar=ssh[:, 4 + hh:5 + hh], in1=sil[i].ap[:, hh * 256:(hh + 1) * 256],
                    op0=ALU.mult, op1=ALU.mult)

            def fn(e):
                ins = None
                for kc in range(8):
                    ins = e.transpose(ptr.ap[:, kc, :], ogb.ap[:, kc * 128:(kc + 1) * 128], identb)
                return ins
            P.op("pe", fn, [ogb.buf, b_const], [ptr.buf], cost=650.0)
            ACT(ogT[:, :, tt * 128:(tt + 1) * 128], ptr.ap, AF.Copy, [ptr.buf], [b_ogT[tt]])

        gla_stage1(0)
        for t in range(64):
            if t + 1 < 64:
                gla_stage1(t + 1)
            gla_stage2(t)
        P.barrier()

        hTo2 = carve(108 * KB, [128, 8, 2048], BF16)
        b_hTo2 = [Buf() for _ in range(16)]
        for t in range(16):
            s = load_x(6144 + t * 128)
            norm_to_hT(s.ap, [s.buf], hTo2[:, :, t * 128:(t + 1) * 128], [b_hTo2[t]])
        MM = Mem(140 * KB, 200 * KB)
        wog = MM.alloc([128, 8, 512], BF16)
        woa = MM.alloc([128, 4, 512], BF16)
        wgA = MM.alloc([128, 8, 512], BF16)
        wgB = MM.alloc([128, 8, 512], BF16)
        b_wog = [Buf() for _ in range(8)]
        b_woa = [Buf() for _ in range(4)]
        b_wgA = [Buf() for _ in range(8)]
        b_wgB = [Buf() for _ in range(8)]
        sgs = Rot([Slot(MM.alloc([128, 512], F32)) for _ in range(4)])
        tms = Rot([Slot(MM.alloc([128, 512], F32)) for _ in range(2)])
        for fo in range(2):
            load_w(lambda kc, c0, cn: wog[:, kc, c0:c0 + cn], wog_d, 0, 8, fo * 512, 512, 3, b_wog)
            load_w(lambda kc, c0, cn: woa[:, kc, c0:c0 + cn], woa_d, 0, 4, fo * 512, 512, None, b_woa)
            load_w(lambda kc, c0, cn: wgA[:, kc, c0:c0 + cn], w_in, 0, 8, 7696 + fo * 512, 512, 0, b_wgA)
            load_w(lambda kc, c0, cn: wgB[:, kc, c0:c0 + cn], w_in, 0, 8, 8720 + fo * 512, 512, 0, b_wgB)
            for tb in range(4):
                tk = slice(tb * 512, (tb + 1) * 512)
                for fc in range(4):
                    fs = slice(fc * 128, (fc + 1) * 128)
                    ga = PS()
                    PE([(ga.ap, wgA[:, kc, fs], hTo2[:, kc, tk], kc == 0, kc == 7) for kc in range(8)],
                       b_wgA + b_hTo2[tb * 4:tb * 4 + 4], [ga.buf])
                    sa = sgs.next()
                    ACT(sa.ap, ga.ap, AF.Sigmoid, [ga.buf], [sa.buf])
                    gb = PS()
                    PE([(gb.ap, wgB[:, kc, fs], hTo2[:, kc, tk], kc == 0, kc == 7) for kc in range(8)],
                       b_wgB + b_hTo2[tb * 4:tb * 4 + 4], [gb.buf])
                    sb_ = sgs.next()
                    ACT(sb_.ap, gb.ap, AF.Sigmoid, [gb.buf], [sb_.buf])
                    yg = PS()
                    PE([(yg.ap, wog[:, kc, fs], ogT[:, kc, tk], kc == 0, kc == 7) for kc in range(8)],
                       b_wog + b_ogT[tb * 4:tb * 4 + 4], [yg.buf])
                    t1 = tms.next()
                    DVE("tensor_tensor", [yg.buf, sa.buf], [t1.buf], out=t1.ap, in0=yg.ap, in1=sa.ap, op=ALU.mult)
                    ya = PS()
                    PE([(ya.ap, woa[:, hh, fs], OhT[:, hh, tk], hh == 0, hh == 3) for hh in range(4)],
                       b_woa + [b_OhT], [ya.buf])
                    t2 = tms.next()
                    DVE("tensor_tensor", [ya.buf, sb_.buf], [t2.buf], out=t2.ap, in0=ya.ap, in1=sb_.ap, op=ALU.mult)
                    DVE("tensor_tensor", [t1.buf, t2.buf], [b_mixT[tb]], out=mixT[:, fo * 4 + fc, tk], in0=t1.ap, in1=t2.ap, op=ALU.add)
        P.barrier()

        wout = carve(60 * KB, [128, 8, 1024], BF16)
        b_wout = [Buf() for _ in range(8)]
        load_w(lambda kc, c0, cn: wout[:, kc, c0:c0 + cn], wout_d, 0, 8, 0, 1024, None, b_wout)
        for t in range(16):
            s = load_x(6144 + t * 128)
            for h2 in range(2):
                ps = PS()
                PE([(ps.ap, mixT[:, kc, t * 128:(t + 1) * 128], wout[:, kc, h2 * 512:(h2 + 1) * 512], kc == 0, kc == 7) for kc in range(8)],
                   b_mixT + b_wout, [ps.buf])
                DVE("tensor_tensor", [ps.buf, s.buf], [b_x1[t]], out=x1[:, t, h2 * 512:(h2 + 1) * 512], in0=ps.ap,
                    in1=s.ap[:, h2 * 512:(h2 + 1) * 512], op=ALU.add)
            norm_to_hT(x1[:, t, :], [b_x1[t]], h2T[:, :, t * 128:(t + 1) * 128], [b_h2T[t]])
        P.barrier()

        MF = Mem(60 * KB, 108 * KB)
        w1c = [MF.alloc([128, 8, 512], BF16) for _ in range(2)]
        w2c = [MF.alloc([128, 4, 1024], BF16) for _ in range(2)]
        b_w1c = [[Buf() for _ in range(8)] for _ in range(2)]
        b_w2c = [[Buf() for _ in range(4)] for _ in range(2)]
        uTs = Rot([Slot(MF.alloc([128, 4, 512], BF16)) for _ in range(2)])
        rls = Rot([Slot(MF.alloc([128, 512], F32)) for _ in range(2)])
        for ffg in range(8):
            wi = ffg % 2
            w1 = w1c[wi]
            w2 = w2c[wi]
            load_w(lambda kc, c0, cn: w1[:, kc, c0:c0 + cn], w1_d, 0, 8, ffg * 512, 512, 1, b_w1c[wi])
            load_w(lambda kc, c0, cn: w2[:, kc, c0:c0 + cn], w2_d, ffg * 512, 4, 0, 1024, None, b_w2c[wi])
            for tb in range(4):
                tk = slice(tb * 512, (tb + 1) * 512)
                ut = uTs.next()
                for j in range(4):
                    ps = PS()
                    PE([(ps.ap, w1[:, kc, j * 128:(j + 1) * 128], h2T[:, kc, tk], kc == 0, kc == 7) for kc in range(8)],
                       b_w1c[wi] + b_h2T[tb * 4:tb * 4 + 4], [ps.buf])
                    rl = rls.next()
                    ACT(rl.ap, ps.ap, AF.Relu, [ps.buf], [rl.buf])
                    DVE("tensor_tensor", [rl.buf], [ut.buf], out=ut.ap[:, j, :], in0=rl.ap, in1=rl.ap, op=ALU.mult)
                for tt in range(4):
                    t = tb * 4 + tt
                    for h2 in range(2):
                        ps = PS()
                        PE([(ps.ap, ut.ap[:, j, tt * 128:(tt + 1) * 128], w2[:, j, h2 * 512:(h2 + 1) * 512], j == 0, j == 3) for j in range(4)],
                           [ut.buf] + b_w2c[wi], [ps.buf])
                        DVE("tensor_tensor", [ps.buf, b_x1[t]], [b_x1[t]], out=x1[:, t, h2 * 512:(h2 + 1) * 512],
                            in0=x1[:, t, h2 * 512:(h2 + 1) * 512], in1=ps.ap, op=ALU.add)
        P.barrier()

        MP = Mem(28 * KB, 108 * KB)
        wpg = MP.alloc([128, 8, 1024], BF16)
        wpp = MP.alloc([128, 2, 1024], BF16)
        lnfb = MP.alloc([128, 1024], F32)
        junkP = MP.alloc([128, 1024], BF16)
        b_junkP = Buf()
        b_wpg = [Buf() for _ in range(8)]
        b_wpp = [Buf() for _ in range(2)]
        b_lnf = Buf()
        h3s = Rot([Slot(MP.alloc([128, 8, 128], BF16)) for _ in range(2)])
        pfs = Rot([Slot(MP.alloc([128, 256], F32), P.dma_sem("pf%d" % i)) for i in range(2)])
        pbs = Rot([Slot(MP.alloc([128, 256], BF16)) for _ in range(2)])
        pTs = Rot([Slot(MP.alloc([128, 2, 128], BF16)) for _ in range(2)])
        sg2 = Rot([Slot(MP.alloc([128, 1024], F32)) for _ in range(2)])
        osb = Rot([Slot(MP.alloc([128, 1024], F32), P.dma_sem("os%d" % i)) for i in range(2)])
        load_w(lambda kc, c0, cn: wpg[:, kc, c0:c0 + cn], wpg_d, 0, 8, 0, 1024, 2, b_wpg)
        load_w(lambda kc, c0, cn: wpp[:, kc, c0:c0 + cn], wpp_d, 0, 2, 0, 1024, None, b_wpp)
        dl = P.dma_sem("lnf")
        DMA(lnfb, lnf_d.partition_broadcast(128), [], [b_lnf], dl)
        out_toks = []
        for t in range(16):
            h3 = h3s.next()
            norm_to_hT(x1[:, t, :], [b_x1[t]], h3.ap, [h3.buf])
            pf = pfs.next()
            DMA(pf.ap, pin[t * 128:(t + 1) * 128, :], [], [pf.buf], pf.sem)
            pb = pbs.next()
            DVE("tensor_copy", [pf.buf], [pb.buf], out=pb.ap, in_=pf.ap)

            def fn(e, pb=pb):
                ins = None
                for c in range(2):
                    ins = e.transpose(ptr.ap[:, c, :], pb.ap[:, c * 128:(c + 1) * 128], identb)
                return ins
            P.op("pe", fn, [pb.buf, b_const], [ptr.buf], cost=200.0)
            pT = pTs.next()
            ACT(pT.ap, ptr.ap[:, 0:2, :], AF.Copy, [ptr.buf], [pT.buf])
            sg = sg2.next()
            for h2 in range(2):
                hs = slice(h2 * 512, (h2 + 1) * 512)
                gp = PS()
                PE([(gp.ap, h3.ap[:, kc, :], wpg[:, kc, hs], kc == 0, kc == 7) for kc in range(8)], [h3.buf] + b_wpg, [gp.buf])
                ACT(sg.ap[:, hs], gp.ap, AF.Sigmoid, [gp.buf], [sg.buf])
                pp = PS()
                PE([(pp.ap, pT.ap[:, c, :], wpp[:, c, hs], c == 0, c == 1) for c in range(2)], [pT.buf] + b_wpp, [pp.buf])
                DVE("tensor_tensor", [pp.buf, sg.buf], [sg.buf], out=sg.ap[:, hs], in0=sg.ap[:, hs], in1=pp.ap, op=ALU.mult)
            DVE("tensor_tensor", [sg.buf, b_x1[t]], [b_x1[t]], out=x1[:, t, :], in0=x1[:, t, :], in1=sg.ap, op=ALU.add)
            ob = osb.next()
            rs, brs = rstd_of(x1[:, t, :], [b_x1[t]], 1024, junkP, b_junkP)
            DVE("scalar_tensor_tensor", [b_x1[t], brs, b_lnf], [ob.buf], out=ob.ap, in0=x1[:, t, :], scalar=rs, in1=lnfb,
                op0=ALU.mult, op1=ALU.mult)
            out_toks.append(DMA(y[t * 128:(t + 1) * 128, :], ob.ap, [ob.buf], [], ob.sem))
        P.final_wait("sp", out_toks[-2:])
        P.run(block)
    return nc


def _t5_bucket(n):
    max_exact = 16
    nf = np.maximum(n, 1).astype(np.float32)
    large = max_exact + (np.log(nf / max_exact) / np.log(2048 / max_exact) * (32 - max_exact)).astype(np.int32)
    large = np.minimum(large, 31)
    return np.where(n < max_exact, n, large).astype(np.int32)


def _const_mats():
    m = np.arange(128)[:, None]
    t = np.arange(128)[None, :]
    same = (m // 64) == (t // 64)
    cm = np.zeros((128, 6, 128), np.float32)
    cm[:, 0, :] = np.eye(128)
    cm[:, 1, :] = np.where(same & (m <= t), 1.0 / 16, 0.0)
    cm[:, 2, :] = np.where(same & (m > t), -1.0 / 16, 0.0)
    cm[:, 3, :] = np.where(same & (m <= t), 1.0, 0.0)
    cm[:, 4, 0:16] = np.eye(4, dtype=np.float32).reshape(16)[None, :]
    sel4 = np.zeros((4, 4, 128), np.float32)
    for hh in range(4):
        sel4[hh, hh, :] = 1.0
    return cm, sel4


def _bias_layout(rel_bias):
    k = np.arange(128)[:, None, None]
    j = np.arange(2)[None, :, None]
    q = np.arange(128)[None, None, :]
    delta = q - k + 128 * (1 - j)
    valid = (delta >= 0) & (delta <= 128)
    out = np.full((128, 3, 2, 2, 2, 128), NEGM, np.float32)
    for g, dil in enumerate((1, 4, 16)):
        bucket = _t5_bucket(np.maximum(delta, 0) * dil)
        for hp in range(2):
            for hh in range(2):
                tab = rel_bias[:, g * 4 + hp * 2 + hh]
                vals = tab[bucket]
                out[:, g, hp, hh] = np.where(valid, vals, NEGM)
    return out.reshape(128, 3, 2, 512)


_PROG = None


def kernel(x, p, ln1, w_in, w_a2, b_a, gla_gn, w_o_gla, w_o_attn, w_out, ln2, w_mlp1, w_mlp2, ln3, w_pp, w_pg,
           rel_bias, ln_f):
    global _PROG
    f = lambda a: np.ascontiguousarray(np.asarray(a, dtype=np.float32))
    x = f(x); p = f(p)
    cm, sel4 = _const_mats()
    cols = np.stack([f(ln1)[0], f(ln2)[0], f(ln3)[0], f(gla_gn)[0]]).reshape(4, 8, 128).transpose(2, 0, 1).reshape(128, 32)
    wa2aug = np.zeros((32, 512), np.float32)
    wa2aug[0:16] = f(w_a2)[0]
    wa2aug[16] = f(b_a)[0]
    shared = {
        "w_in": f(w_in)[0], "w_a2aug": wa2aug, "w_o_gla": f(w_o_gla)[0], "w_o_attn": f(w_o_attn)[0],
        "w_out": f(w_out)[0], "w_mlp1": f(w_mlp1)[0], "w_mlp2": f(w_mlp2)[0], "w_pp": f(w_pp)[0], "w_pg": f(w_pg)[0],
        "cols": np.ascontiguousarray(cols), "ln_f": f(ln_f), "biasm": _bias_layout(f(rel_bias)),
        "cmat": cm, "sel4": sel4,
    }
    in_maps = []
    for c in range(NCORES):
        b, j = c // 4, c % 4
        xe = np.zeros((8192, 1024), np.float32)
        n = SEG * (j + 1)
        xe[8192 - n:] = x[b, 0:n]
        m = dict(shared)
        m["xe"] = xe
        m["p"] = np.ascontiguousarray(p[0, b, j * SEG:(j + 1) * SEG])
        m["hoff"] = np.full((128, 1), NEGM if j == 0 else 0.0, np.float32)
        in_maps.append(m)
    if _PROG is None:
        _PROG = build_program()
    res = run_bass_kernel_spmd(_PROG, in_maps, core_ids=list(range(NCORES)))
    out = np.zeros((2, 8192, 1024), np.float32)
    for c in range(NCORES):
        b, j = c // 4, c % 4
        out[b, j * SEG:(j + 1) * SEG] = res.results[c]["y"]
    return out
```

```python
import contextlib
import numpy as np
import concourse.bass as bass
import concourse.mybir as mybir
from concourse.bass_utils import run_bass_kernel_spmd

F32 = mybir.dt.float32
BF16 = mybir.dt.bfloat16
ALU = mybir.AluOpType
AF = mybir.ActivationFunctionType

SAFE_SAME = True
EPS = 1e-6
NCORES = 8
SEG = 2048
NEGM = -30000.0


class Buf:
    __slots__ = ("w", "r")

    def __init__(self):
        self.w = None
        self.r = []


class Prog:
    ENGS = ("pe", "act", "dve", "pool", "sp")
    WINDOW = 100
    LAT = 300.0
    SLACK = 500.0
    LAT_DMA = 200.0

    def __init__(self, nc, stack):
        self.nc = nc
        self.stack = stack
        self.ops = []
        self.phase = 0
        self.sems = {}
        self.all_dsems = []
        self.final = []
        for e in ("pe", "act", "dve", "pool"):
            self.sems[e] = stack.enter_context(nc.semaphore("s_" + e))

    def dma_sem(self, name):
        s = self.stack.enter_context(self.nc.semaphore("d_" + name))
        d = [s, 0]
        self.all_dsems.append(d)
        return d

    def op(self, eng, fn, reads=(), writes=(), dsem=None, cost=500.0, fin=None):
        idx = len(self.ops)
        deps = set()
        for b in reads:
            if b.w is not None:
                deps.add(b.w)
        for b in writes:
            if b.w is not None:
                deps.add(b.w)
            deps.update(b.r)
        self.ops.append([eng, fn, sorted(deps), dsem, cost, self.phase, cost if fin is None else fin])
        for b in reads:
            b.r.append(idx)
        for b in writes:
            b.w = idx
            b.r = []
        return idx

    def barrier(self):
        self.phase += 1

    def final_wait(self, eng, toks):
        self.final.append((eng, list(toks)))

    def schedule(self):
        ops = self.ops
        n = len(ops)
        succ = [[] for _ in range(n)]
        for i, o in enumerate(ops):
            for d in o[2]:
                if ops[d][5] == o[5]:
                    succ[d].append(i)
        bl = [0.0] * n
        for i in range(n - 1, -1, -1):
            m = 0.0
            for j in succ[i]:
                v = bl[j] + (0.0 if ops[j][0] == ops[i][0] else self.LAT)
                if v > m:
                    m = v
            bl[i] = ops[i][6] + m
        order = {e: [] for e in self.ENGS}
        finish = {}
        tnow = 0.0
        for ph in range(self.phase + 1):
            pend = {e: [] for e in self.ENGS}
            for i, o in enumerate(ops):
                if o[5] == ph:
                    pend[o[0]].append(i)
            tfree = {e: tnow for e in self.ENGS}
            remaining = sum(len(v) for v in pend.values())
            cand = {e: None for e in self.ENGS}
            dirty = set(self.ENGS)
            while remaining:
                for e in list(dirty):
                    cl = []
                    for i in pend[e][:self.WINDOW]:
                        o = ops[i]
                        st = tfree[e]
                        ok = True
                        for d in o[2]:
                            f = finish.get(d)
                            if f is None:
                                ok = False
                                break
                            lat = self.LAT_DMA if ops[d][3] is not None else (0.0 if ops[d][0] == e else self.LAT)
                            if f + lat > st:
                                st = f + lat
                        if ok:
                            cl.append((st, i))
                    if not cl:
                        cand[e] = None
                    else:
                        tmin = min(c[0] for c in cl)
                        lim = tmin + self.SLACK
                        best = None
                        for st, i in cl:
                            if st <= lim:
                                key = (-bl[i], i)
                                if best is None or key < best[0]:
                                    best = (key, st, i)
                        cand[e] = (best[1], best[2])
                dirty.clear()
                pick = None
                for e in self.ENGS:
                    c = cand[e]
                    if c is not None and (pick is None or c < pick[0]):
                        pick = (c, e)
                assert pick is not None, "scheduler stuck"
                (st, i), e = pick
                o = ops[i]
                tfree[e] = st + o[4]
                finish[i] = st + o[6]
                pend[e].remove(i)
                order[e].append(i)
                remaining -= 1
                dirty.update(self.ENGS)
            tnow = max(tfree.values())
            self.phase_ends = getattr(self, 'phase_ends', []) + [tnow]
            for e in self.ENGS:
                order[e].append(None)
        self.est_total = tnow
        return order

    def lower(self):
        order = self.schedule()
        ops = self.ops
        tok = {}
        cnt = {e: 0 for e in ("pe", "act", "dve", "pool")}
        bar_cnt = []
        nph = self.phase + 1
        pos = {e: 0 for e in self.ENGS}
        dcount = {id(d): 0 for d in self.all_dsems}
        bar_state = []
        for ph in range(nph):
            for e in self.ENGS:
                lst = order[e]
                while lst[pos[e]] is not None:
                    i = lst[pos[e]]
                    o = ops[i]
                    if o[3] is not None:
                        dcount[id(o[3])] += 16
                        tok[i] = (o[3][0], dcount[id(o[3])], e, True)
                    else:
                        cnt[e] += 1
                        tok[i] = (self.sems[e], cnt[e], e, False)
                    pos[e] += 1
                pos[e] += 1
            bar_state.append((dict(cnt), dict(dcount)))
        streams = {e: [] for e in self.ENGS}
        for e in self.ENGS:
            waited = {}
            ph = 0
            for i in order[e]:
                if i is None:
                    c, dc = bar_state[ph]
                    waits = []
                    for e2 in ("pe", "act", "dve", "pool"):
                        if e2 != e and c[e2] > waited.get(id(self.sems[e2]), 0):
                            waits.append((self.sems[e2], c[e2]))
                            waited[id(self.sems[e2])] = c[e2]
                    for d in self.all_dsems:
                        v = dc[id(d)]
                        if v > waited.get(id(d[0]), 0):
                            waits.append((d[0], v))
                            waited[id(d[0])] = v
                    if waits and ph < nph - 1:
                        streams[e].append((waits, None, None))
                    ph += 1
                    continue
                o = ops[i]
                waits = {}
                for d in o[2]:
                    s, v, e2, isdma = tok[d]
                    if e2 == e and not isdma:
                        if e in ("pe", "sp") or not SAFE_SAME:
                            continue
                    k = id(s)
                    if waited.get(k, 0) >= v:
                        continue
                    if k not in waits or waits[k][1] < v:
                        waits[k] = (s, v)
                for k, (s, v) in waits.items():
                    waited[k] = v
                t = tok[i]
                streams[e].append((list(waits.values()), o[1], (t[0], 16 if t[3] else 1)))
        for eng, toks in self.final:
            streams[eng].append(([(tok[t][0], tok[t][1]) for t in toks], None, None))
        self.streams = streams

    def run(self, block):
        self.lower()

        def play(name):
            def _f(e):
                for waits, fn, inc in self.streams[name]:
                    for s, v in waits:
                        e.wait_ge(s, v)
                    if fn is None:
                        continue
                    ins = fn(e)
                    if inc is not None:
                        ins.then_inc(inc[0], inc[1])
            return _f
        block.tensor(play("pe"))
        block.scalar(play("act"))
        block.vector(play("dve"))
        block.gpsimd(play("pool"))
        block.sync(play("sp"))


class Slot:
    def __init__(self, ap, sem=None):
        self.ap = ap
        self.buf = Buf()
        self.sem = sem


class Rot:
    def __init__(self, slots):
        self.slots = slots
        self.i = 0

    def next(self):
        s = self.slots[self.i % len(self.slots)]
        self.i += 1
        return s


def build_program():
    nc = bass.Bass("TRN2", target_bir_lowering=False)

    def din(name, shape):
        return nc.dram_tensor(name, shape, F32, kind="ExternalInput").ap()

    xe = din("xe", [8192, 1024])
    pin = din("p", [SEG, 256])
    w_in = din("w_in", [1024, 9744])
    wa2_d = din("w_a2aug", [32, 512])
    wog_d = din("w_o_gla", [1024, 1024])
    woa_d = din("w_o_attn", [512, 1024])
    wout_d = din("w_out", [1024, 1024])
    w1_d = din("w_mlp1", [1024, 4096])
    w2_d = din("w_mlp2", [4096, 1024])
    wpp_d = din("w_pp", [256, 1024])
    wpg_d = din("w_pg", [1024, 1024])
    cols_d = din("cols", [128, 32])
    lnf_d = din("ln_f", [1024])
    biasm_d = din("biasm", [128, 3, 2, 512])
    hoff_d = din("hoff", [128, 1])
    cmat_d = din("cmat", [128, 6, 128])
    sel4_d = din("sel4", [4, 4, 128])
    y = nc.dram_tensor("y", [SEG, 1024], F32, kind="ExternalOutput").ap()

    with contextlib.ExitStack() as st:
        P = Prog(nc, st)
        ARENA_BYTES = 200 * 1024
        arena = st.enter_context(nc.sbuf_tensor("arena", [128, ARENA_BYTES // 2], BF16))
        psb = [st.enter_context(nc.psum_tensor("psb%d" % i, [128, 512], F32)) for i in range(7)]
        ptr_t = st.enter_context(nc.psum_tensor("ptr", [128, 8, 128], BF16))
        block = st.enter_context(nc.Block())

        def carve(off, shape, dt):
            n = 1
            for s in shape[1:]:
                n *= s
            es = 2 if dt == BF16 else 4
            assert off % 4 == 0 and off + n * es <= ARENA_BYTES, (off, shape)
            a = arena[0:shape[0], off // 2: off // 2 + n * es // 2]
            if dt == F32:
                a = a.bitcast(F32)
            if len(shape) == 3:
                a = a.rearrange("p (a b) -> p a b", a=shape[1])
            elif len(shape) == 4:
                a = a.rearrange("p (a b c) -> p a b c", a=shape[1], b=shape[2])
            return a

        class Mem:
            def __init__(self, base, limit):
                self.off = base
                self.limit = limit

            def alloc(self, shape, dt):
                n = 1
                for s in shape[1:]:
                    n *= s
                es = 2 if dt == BF16 else 4
                a = carve(self.off, shape, dt)
                self.off += (n * es + 63) // 64 * 64
                assert self.off <= self.limit, (self.off, self.limit)
                return a

        KB = 1024

        def sl(start, n, step):
            return slice(start, start + step * (n - 1) + 1, step)

        def fsz(ap):
            n = 1
            for x in ap.shape[1:]:
                n *= x
            return n

        def PE(mms, reads, writes):
            cost = 0.0
            for m in mms:
                n = max(fsz(m[2]), 64)
                c = n / 2.4 + 25.0
                if m[1].dtype == F32:
                    c *= 4
                cost += c

            def fn(e):
                ins = None
                for m in mms:
                    kw = dict(start=m[3], stop=m[4])
                    if len(m) > 5 and m[5]:
                        kw["skip_group_check"] = True
                    ins = e.matmul(m[0], lhsT=m[1], rhs=m[2], **kw)
                return ins
            return P.op("pe", fn, reads, writes, cost=cost)

        def ACT(out, in_, func, reads, writes, **kw):
            return P.op("act", lambda e: e.activation(out=out, in_=in_, func=func, **kw), reads, writes,
                        cost=180.0 + 0.83 * fsz(out))

        def ENG(eng, method, reads, writes, **kw):
            o = kw.get("out", kw.get("ap"))
            n = fsz(o)
            cost = (100.0 + 1.15 * n) if eng == "dve" else (250.0 + 0.6 * n)
            return P.op(eng, lambda e: getattr(e, method)(**kw), reads, writes, cost=cost)

        def DVE(method, reads, writes, **kw):
            return ENG("dve", method, reads, writes, **kw)

        def POOL(method, reads, writes, **kw):
            return ENG("pool", method, reads, writes, **kw)

        def DMA(out, in_, reads, writes, dsem):
            nbytes = out.shape[0] * fsz(out) * 4
            return P.op("sp", lambda e: e.dma_start(out=out, in_=in_), reads, writes, dsem=dsem, cost=120.0, fin=2000.0 + nbytes / 150.0)

        psrot = Rot([Slot(t[:]) for t in psb])
        ptr = Slot(ptr_t[:])

        def PS():
            return psrot.next()

        G = Mem(0, 28 * KB)
        cmat = G.alloc([128, 6, 128], F32)
        identb = G.alloc([128, 128], BF16)
        onesel4 = G.alloc([128, 4, 4], BF16)
        sel4 = G.alloc([4, 4, 128], F32)
        colsT = G.alloc([128, 32], F32)
        hoff = G.alloc([128, 1], F32)
        mhalf4 = G.alloc([128, 4], F32)
        mhalf = mhalf4[:, 0:1]
        statv = G.alloc([128, 64], F32)
        junk = G.alloc([128, 1024], BF16)
        stage = Rot([Slot(G.alloc([128, 1024], F32), P.dma_sem("st%d" % i)) for i in range(2)])
        xts = Rot([Slot(G.alloc([128, 1024], F32), P.dma_sem("xt%d" % i)) for i in range(2)])
        hbs = Rot([Slot(G.alloc([128, 1024], BF16)) for i in range(2)])
        LT = cmat[:, 1, :]
        UT = cmat[:, 2, :]
        CM = cmat[:, 3, :]
        b_const = Buf()
        b_junk = Buf()
        stat_i = [0]

        b_stat = [Buf() for _ in range(64)]

        def stat_col():
            c = stat_i[0] % 64
            stat_i[0] += 1
            return statv[:, c:c + 1], b_stat[c]

        dc = P.dma_sem("const")
        DMA(cmat, cmat_d, [], [b_const], dc)
        DMA(sel4, sel4_d, [], [b_const], dc)
        DMA(colsT, cols_d, [], [b_const], dc)
        DMA(hoff, hoff_d, [], [b_const], dc)
        DVE("tensor_copy", [b_const], [b_const], out=identb, in_=cmat[:, 0, :])
        DVE("tensor_copy", [b_const], [b_const], out=onesel4, in_=cmat[:, 4, 0:16].rearrange("p (a b) -> p a b", a=4))
        POOL("memset", [], [b_const], ap=mhalf4, constant=-0.5)

        def col(i, kc):
            return colsT[:, i * 8 + kc: i * 8 + kc + 1]

        cast_rr = [0]

        def cast_w(dst, src, sc, reads, writes):
            k = (cast_rr[0] % 2) * 2
            cast_rr[0] += 1
            if k == 0:
                POOL("tensor_scalar", reads, writes, out=dst, in0=src, scalar1=sc, scalar2=1.0, op0=ALU.mult, op1=ALU.mult)
            elif k == 1:
                ACT(dst, src, AF.Copy, reads, writes, scale=sc)
            else:
                DVE("tensor_scalar", reads, writes, out=dst, in0=src, scalar1=sc, scalar2=None, op0=ALU.mult)

        def load_w(dst_fn, dram, row0, nk, col0, ncols, scale_i, dst_bufs, stg=None):
            stg = stg or stage
            for kc in range(nk):
                for c0 in range(0, ncols, 1024):
                    cn = min(1024, ncols - c0)
                    s = stg.next()
                    DMA(s.ap[:, 0:cn], dram[row0 + kc * 128: row0 + (kc + 1) * 128, col0 + c0: col0 + c0 + cn],
                        [], [s.buf], s.sem)
                    sc = col(scale_i, kc) if scale_i is not None else 1.0
                    cast_w(dst_fn(kc, c0, cn), s.ap[:, 0:cn], sc, [s.buf, b_const], [dst_bufs[kc]])

        def rstd_of(src_ap, src_bufs, n, scr_ap, scr_buf):
            ss, bss = stat_col()
            ACT(scr_ap, src_ap, AF.Square, src_bufs, [scr_buf, bss], accum_out=ss)
            vv, bvv = stat_col()
            POOL("tensor_scalar", [bss], [bvv], out=vv, in0=ss, scalar1=1.0 / n, scalar2=EPS, op0=ALU.mult, op1=ALU.add)
            rs, brs = stat_col()
            POOL("tensor_tensor", [bvv, b_const], [brs], out=rs, in0=vv, in1=mhalf, op=ALU.pow)
            return rs, brs

        def norm_to_hT(src_ap, src_bufs, hT_out, hT_bufs):
            hb = hbs.next()
            rs, brs = rstd_of(src_ap, src_bufs, 1024, junk, b_junk)
            DVE("tensor_scalar", list(src_bufs) + [brs], [hb.buf], out=hb.ap, in0=src_ap, scalar1=rs, scalar2=None,
                op0=ALU.mult)

            def fn(e):
                ins = None
                for kc in range(8):
                    ins = e.transpose(ptr.ap[:, kc, :], hb.ap[:, kc * 128:(kc + 1) * 128], identb)
                return ins
            P.op("pe", fn, [hb.buf, b_const], [ptr.buf], cost=650.0)
            ACT(hT_out, ptr.ap, AF.Copy, [ptr.buf], hT_bufs)

        def load_x(row0, step=1):
            s = xts.next()
            DMA(s.ap, xe[sl(row0, 128, step), :], [], [s.buf], s.sem)
            return s

        OhT = carve(28 * KB, [128, 4, 2048], BF16)
        b_OhT = Buf()
        ogT = carve(44 * KB, [128, 8, 2048], BF16)
        b_ogT = [Buf() for _ in range(16)]
        mixT = carve(76 * KB, [128, 8, 2048], BF16)
        b_mixT = [Buf() for _ in range(4)]
        x1 = carve(108 * KB, [128, 16, 1024], F32)
        b_x1 = [Buf() for _ in range(16)]
        h2T = carve(28 * KB, [128, 8, 2048], BF16)
        b_h2T = [Buf() for _ in range(16)]

        hTh = carve(44 * KB, [128, 8, 2048], BF16)
        hTo = carve(76 * KB, [128, 8, 2048], BF16)
        b_hTh = [Buf() for _ in range(16)]
        b_hTo = [Buf() for _ in range(16)]
        MA = Mem(108 * KB, 200 * KB)
        NT = MA.alloc([128, 4, 2048], F32)
        ST = MA.alloc([4, 2048], F32)
        biasT = carve(28 * KB, [128, 512], F32)
        expbN = carve(30 * KB, [128, 512], F32)
        expbH = carve(32 * KB, [128, 512], F32)
        wA = [MA.alloc([128, 8, 3, 256], BF16) for _ in range(2)]
        b_wA = [[Buf() for _ in range(8)] for _ in range(2)]
        KTs = Rot([Slot(MA.alloc([128, 2, 512], BF16)) for _ in range(3)])
        QTs = Rot([Slot(MA.alloc([128, 2, 512], BF16)) for _ in range(2)])
        Vs = Rot([Slot(MA.alloc([128, 4, 256], BF16)) for _ in range(3)])
        Efs = Rot([Slot(MA.alloc([128, 512], F32)) for _ in range(2)])
        ETs = Rot([Slot(MA.alloc([128, 512], BF16)) for _ in range(2)])
        b_bias = Buf()
        b_exp = Buf()
        scale_att = 128.0 ** -0.5
        dbias = P.dma_sem("bias")
        b_NTq = [[Buf() for _ in range(4)] for _ in range(2)]
        b_STq = [Buf() for _ in range(4)]
        DVE("memset", [], [b for l in b_NTq for b in l], ap=NT, constant=0.0)
        DVE("memset", [], b_STq, ap=ST, constant=0.0)
        wslot = 0

        def proj_seg(hTs, hbufs, t0, nb, wa, bwa, want_q):
            cols = slice(t0 * 128, (t0 + nb) * 128)
            hb_ = hbufs[t0:t0 + nb]
            kt = KTs.next()
            for hh in range(2):
                ps = PS()
                PE([(ps.ap[:, 0:nb * 128], wa[:, kc, 1, hh * 128:(hh + 1) * 128], hTs[:, kc, cols], kc == 0, kc == 7)
                    for kc in range(8)], hb_ + bwa, [ps.buf])
                ACT(kt.ap[:, hh, 0:nb * 128], ps.ap[:, 0:nb * 128], AF.Copy, [ps.buf], [kt.buf])
            qt = None
            if want_q:
                qt = QTs.next()
                for hh in range(2):
                    ps = PS()
                    PE([(ps.ap[:, 0:nb * 128], wa[:, kc, 0, hh * 128:(hh + 1) * 128], hTs[:, kc, cols], kc == 0, kc == 7)
                        for kc in range(8)], hb_ + bwa, [ps.buf])
                    DVE("tensor_copy", [ps.buf], [qt.buf], out=qt.ap[:, hh, 0:nb * 128], in_=ps.ap[:, 0:nb * 128])
            vs = Vs.next()
            for b in range(nb):
                bc = slice((t0 + b) * 128, (t0 + b + 1) * 128)
                ps = PS()
                PE([(ps.ap[:, 0:256], hTs[:, kc, bc], wa[:, kc, 2, :], kc == 0, kc == 7) for kc in range(8)],
                   [hbufs[t0 + b]] + bwa, [ps.buf])
                if b % 2 == 0:
                    ACT(vs.ap[:, b, :], ps.ap[:, 0:256], AF.Copy, [ps.buf], [vs.buf])
                else:
                    DVE("tensor_copy", [ps.buf], [vs.buf], out=vs.ap[:, b, :], in_=ps.ap[:, 0:256])
            return kt, qt, vs

        def attend(g, hp, prevb, curb, qt, qb, first, nat, quarters):
            blk = [prevb, curb]
            sp_ = PS()
            s4 = sp_.ap.rearrange("p (h j q) -> p h j q", h=2, j=2)
            PE([(s4[:, hh, j, :], blk[j][0].ap[:, hh, blk[j][2] * 128:(blk[j][2] + 1) * 128],
                 qt.ap[:, hh, qb * 128:(qb + 1) * 128], True, True) for hh in range(2) for j in range(2)],
               [prevb[0].buf, curb[0].buf, qt.buf], [sp_.buf])
            ef = Efs.next()
            ACT(ef.ap, sp_.ap, AF.Exp, [sp_.buf], [ef.buf], scale=scale_att)
            et = ETs.next()
            DVE("tensor_tensor", [ef.buf, b_exp], [et.buf], out=et.ap, in0=ef.ap,
                in1=(expbH if first else expbN), op=ALU.mult)
            e4t = et.ap.rearrange("p (h j q) -> p h j q", h=2, j=2)
            np_ = PS()
            mms = []
            for hh in range(2):
                for j in range(2):
                    mms.append((np_.ap[:, hh * 128:(hh + 1) * 128],
                                blk[j][1].ap[:, blk[j][2], hh * 128:(hh + 1) * 128], e4t[:, hh, j, :], j == 0, j == 1))
            k = 0
            for hh in range(2):
                for j in range(2):
                    mms.append((np_.ap[0:4, 256:384], onesel4[:, hp * 2 + hh, :], e4t[:, hh, j, :], k == 0, k == 3))
                    k += 1
            PE(mms, [prevb[1].buf, curb[1].buf, et.buf, b_const], [np_.buf])
            nb_ = [b_NTq[hp][q] for q in quarters]
            DVE("tensor_tensor", [np_.buf] + nb_, nb_, out=NT[:, hp * 2:hp * 2 + 2, nat], in0=NT[:, hp * 2:hp * 2 + 2, nat],
                in1=np_.ap[:, 0:256].rearrange("p (h q) -> p h q", h=2), op=ALU.add)
            sb_ = [b_STq[q] for q in quarters]
            DVE("tensor_tensor", [np_.buf] + sb_, sb_, out=ST[:, nat], in0=ST[:, nat],
                in1=np_.ap[0:4, 256:384], op=ALU.add)

        for g in range(3):
            dil = (1, 4, 16)[g]
            nbk = 16 // dil
            for k in range(dil):
                s = load_x(6144 - 128 * dil + k, dil)
                norm_to_hT(s.ap, [s.buf], hTh[:, :, k * 128:(k + 1) * 128], [b_hTh[k]])
            for k in range(16):
                r, n = divmod(k, nbk)
                s = load_x(6144 + r + dil * 128 * n, dil)
                norm_to_hT(s.ap, [s.buf], hTo[:, :, k * 128:(k + 1) * 128], [b_hTo[k]])
            for hp in range(2):
                DMA(biasT, biasm_d[:, g, hp, :], [], [b_bias], dbias)
                ACT(expbN, biasT, AF.Exp, [b_bias], [b_exp])
                ACT(expbH, biasT, AF.Exp, [b_bias], [b_exp])
                b4 = biasT.rearrange("p (h j q) -> p h j q", h=2, j=2)
                e4 = expbH.rearrange("p (h j q) -> p h j q", h=2, j=2)
                ACT(e4[:, :, 0, :], b4[:, :, 0, :], AF.Exp, [b_bias, b_const], [b_exp], bias=hoff)
                wa = wA[wslot % 2]
                bwa = b_wA[wslot % 2]
                wslot += 1
                base = 3088 + g * 1536
                for kc in range(8):
                    s = stage.next()
                    src = w_in[kc * 128:(kc + 1) * 128, base:base + 1536].rearrange("p (c x) -> p c x", c=3)[:, :, hp * 256:(hp + 1) * 256]
                    sv = s.ap[:, 0:768].rearrange("p (c x) -> p c x", c=3)
                    DMA(sv, src, [], [s.buf], s.sem)
                    POOL("tensor_scalar", [s.buf, b_const], [bwa[kc]], out=wa[:, kc, :, :], in0=sv,
                         scalar1=col(0, kc), scalar2=1.0, op0=ALU.mult, op1=ALU.mult)
                if g < 2:
                    for r in range(dil):
                        kt_p, _, vs_p = proj_seg(hTh, b_hTh, r, 1, wa, bwa, False)
                        prev = (kt_p, vs_p, 0)
                        for n0 in range(0, nbk, 4):
                            nb = min(4, nbk - n0)
                            kt, qt, vs = proj_seg(hTo, b_hTo, r * nbk + n0, nb, wa, bwa, True)
                            for b in range(nb):
                                cur = (kt, vs, b)
                                n = n0 + b
                                tok0 = r + dil * 128 * n
                                quarters = [tok0 // 512] if g == 0 else [n]
                                attend(g, hp, prev, cur, qt, b, (n == 0), sl(tok0, 128, dil), quarters)
                                prev = cur
                else:
                    for q4 in range(4):
                        kt_h, _, vs_h = proj_seg(hTh, b_hTh, q4 * 4, 4, wa, bwa, False)
                        kt, qt, vs = proj_seg(hTo, b_hTo, q4 * 4, 4, wa, bwa, True)
                        for b in range(4):
                            r = q4 * 4 + b
                            attend(g, hp, (kt_h, vs_h, b), (kt, vs, b), qt, b, True, sl(r, 128, 16), [0, 1, 2, 3])
        DVE("reciprocal", b_STq, b_STq, out=ST, in_=ST)
        for hh in range(4):
            for tb in range(4):
                ps = PS()
                PE([(ps.ap, sel4[:, hh, :], ST[:, tb * 512:(tb + 1) * 512], True, True)], b_STq + [b_const], [ps.buf])
                DVE("tensor_tensor", [ps.buf, b_NTq[hh // 2][tb]], [b_OhT], out=OhT[:, hh, tb * 512:(tb + 1) * 512],
                    in0=NT[:, hh, tb * 512:(tb + 1) * 512], in1=ps.ap, op=ALU.mult)
        P.barrier()

        MG = Mem(76 * KB, 200 * KB)
        wG = MG.alloc([128, 8, 3088], BF16)
        b_wGq = [Buf() for _ in range(8)]
        b_wGk = [Buf() for _ in range(8)]
        b_wGv = [Buf() for _ in range(8)]
        b_wGr = [Buf() for _ in range(8)]
        b_wGa = [Buf() for _ in range(8)]
        wa2 = MG.alloc([32, 512], BF16)
        b_wa2 = Buf()
        CM4 = MG.alloc([128, 4, 128], F32)
        NS1 = 3
        NS2 = 3
        hTt = [Slot(MG.alloc([128, 8, 128], BF16)) for _ in range(NS1)]
        haT = [Slot(MG.alloc([32, 128], BF16)) for _ in range(NS1)]
        sps = [Slot(MG.alloc([128, 512], F32)) for _ in range(2)]
        wex = [Slot(MG.alloc([128, 512], F32)) for _ in range(2)]
        kouts = [Slot(MG.alloc([128, 512], BF16)) for _ in range(NS2)]
        vbs = [Slot(MG.alloc([128, 1024], BF16)) for _ in range(NS2)]
        e1s = [Slot(MG.alloc([128, 4, 128], F32)) for _ in range(NS2)]
        e2s = [Slot(MG.alloc([128, 4, 128], F32)) for _ in range(2)]
        qdA = [Slot(MG.alloc([128, 4, 128], BF16)) for _ in range(NS2)]
        qdB = [Slot(MG.alloc([128, 4, 128], BF16)) for _ in range(NS2)]
        kinT = [Slot(MG.alloc([128, 4, 128], BF16)) for _ in range(2)]
        attnT = [Slot(MG.alloc([128, 4, 128], BF16)) for _ in range(NS2)]
        sil = [Slot(MG.alloc([128, 1024], F32), P.dma_sem("sil%d" % i)) for i in range(NS2)]
        Sst = MG.alloc([128, 4, 256], F32)
        SbA = MG.alloc([128, 4, 256], BF16)
        SbB = MG.alloc([128, 4, 256], BF16)
        ogbs = Rot([Slot(MG.alloc([128, 1024], BF16)) for _ in range(2)])
        for o_ in ogbs.slots:
            o_.bufs = [Buf(), Buf()]
        ssh = MG.alloc([128, 32], F32)
        b_ssh = [Buf() for _ in range(4)]
        junkH = MG.alloc([128, 256], BF16)
        b_junkH = Buf()
        b_S = [Buf() for _ in range(4)]
        b_SbA = [Buf() for _ in range(4)]
        b_SbB = [Buf() for _ in range(4)]

        stgG = Rot(stage.slots + sil)
        load_w(lambda kc, c0, cn: wG[:, kc, 512 + c0:512 + c0 + cn], w_in, 0, 8, 512, 512, 0, b_wGk, stgG)
        load_w(lambda kc, c0, cn: wG[:, kc, 3072 + c0:3072 + c0 + cn], w_in, 0, 8, 3072, 16, 0, b_wGa, stgG)
        load_w(lambda kc, c0, cn: wG[:, kc, 1024 + c0:1024 + c0 + cn], w_in, 0, 8, 1024, 1024, 0, b_wGv, stgG)
        load_w(lambda kc, c0, cn: wG[:, kc, c0:c0 + cn], w_in, 0, 8, 0, 512, 0, b_wGq, stgG)
        load_w(lambda kc, c0, cn: wG[:, kc, 2048 + c0:2048 + c0 + cn], w_in, 0, 8, 2048, 1024, 0, b_wGr, stgG)
        s = stage.next()
        DMA(s.ap[0:32, 0:512], wa2_d, [], [s.buf], s.sem)
        POOL("tensor_copy", [s.buf], [b_wa2], out=wa2, in_=s.ap[0:32, 0:512])
        for hh in range(4):
            DVE("tensor_copy", [b_const], [b_const], out=CM4[:, hh, :], in_=CM)
        for i in range(NS1):
            POOL("memset", [], [haT[i].buf], ap=haT[i].ap, constant=1.0)
        for i in range(NS2):
            POOL("memset", [], [qdA[i].buf], ap=qdA[i].ap, constant=0.0)
            POOL("memset", [], [qdB[i].buf], ap=qdB[i].ap, constant=0.0)
        DVE("memset", [], b_S, ap=Sst, constant=0.0)
        POOL("memset", [], b_SbA, ap=SbA, constant=0.0)

        gl = {}

        def gla_stage1(t):
            own = t >= 48
            i1_ = t % NS1
            i = t % 2
            j = t % NS2
            xs = load_x(t * 128)
            ht = hTt[i1_]
            norm_to_hT(xs.ap, [xs.buf], ht.ap, [ht.buf])
            kps = PS()
            PE([(kps.ap, ht.ap[:, kc, :], wG[:, kc, 512:1024], kc == 0, kc == 7) for kc in range(8)], [ht.buf] + b_wGk, [kps.buf])
            hps = PS()
            PE([(hps.ap[0:16, 0:128], wG[:, kc, 3072:3088], ht.ap[:, kc, :], kc == 0, kc == 7) for kc in range(8)],
               [ht.buf] + b_wGa, [hps.buf])
            ACT(haT[i1_].ap[0:16, :], hps.ap[0:16, 0:128], AF.Copy, [hps.buf], [haT[i1_].buf])
            zps = PS()
            PE([(zps.ap, haT[i1_].ap, wa2, True, True)], [haT[i1_].buf, b_wa2], [zps.buf])
            ACT(sps[i].ap, zps.ap, AF.Exp, [zps.buf], [sps[i].buf], scale=-1.0)
            ACT(sps[i].ap, sps[i].ap, AF.Ln, [sps[i].buf], [sps[i].buf], bias=1.0)
            vp = [PS(), PS()]
            for h2 in range(2):
                PE([(vp[h2].ap, ht.ap[:, kc, :], wG[:, kc, 1024 + h2 * 512:1536 + h2 * 512], kc == 0, kc == 7) for kc in range(8)],
                   [ht.buf] + b_wGv, [vp[h2].buf])
            ACT(vbs[j].ap[:, 0:512], vp[0].ap, AF.Copy, [vp[0].buf], [vbs[j].buf])
            DVE("tensor_copy", [vp[1].buf], [vbs[j].buf], out=vbs[j].ap[:, 512:1024], in_=vp[1].ap)
            dps = PS()
            PE([(dps.ap, UT, sps[i].ap, True, True)], [sps[i].buf, b_const], [dps.buf])
            ACT(wex[i].ap, dps.ap, AF.Exp, [dps.buf], [wex[i].buf])
            DVE("tensor_tensor", [kps.buf, wex[i].buf], [kouts[j].buf], out=kouts[j].ap, in0=kps.ap, in1=wex[i].ap, op=ALU.mult)
            nps = PS()
            n4 = nps.ap.rearrange("p (h q) -> p h q", h=4)
            PE([(n4[:, hh, :], sps[i].ap[:, hh * 128:(hh + 1) * 128], LT, True, True) for hh in range(4)],
               [sps[i].buf, b_const], [nps.buf])
            ACT(e1s[j].ap, n4, AF.Exp, [nps.buf], [e1s[j].buf], scale=-1.0)
            if not own:
                return
            ACT(e2s[i].ap, n4, AF.Exp, [nps.buf], [e2s[i].buf])
            qps = PS()
            q4 = qps.ap.rearrange("p (h q) -> p h q", h=4)
            PE([(q4[:, hh, :], wG[:, kc, hh * 128:(hh + 1) * 128], ht.ap[:, kc, :], kc == 0, kc == 7)
                for hh in range(4) for kc in range(8)], [ht.buf] + b_wGq, [qps.buf])
            DVE("scalar_tensor_tensor", [qps.buf, e1s[j].buf], [qdA[j].buf], out=qdA[j].ap[:, :, 0:64], in0=q4[:, :, 0:64],
                scalar=128.0 ** -0.5, in1=e1s[j].ap[:, :, 0:64], op0=ALU.mult, op1=ALU.mult)
            DVE("scalar_tensor_tensor", [qps.buf, e1s[j].buf], [qdB[j].buf], out=qdB[j].ap[:, :, 64:128], in0=q4[:, :, 64:128],
                scalar=128.0 ** -0.5, in1=e1s[j].ap[:, :, 64:128], op0=ALU.mult, op1=ALU.mult)
            ktp = PS()
            k4 = ktp.ap.rearrange("p (h q) -> p h q", h=4)
            PE([(k4[:, hh, :], wG[:, kc, 512 + hh * 128:512 + (hh + 1) * 128], ht.ap[:, kc, :], kc == 0, kc == 7)
                for hh in range(4) for kc in range(8)], [ht.buf] + b_wGk, [ktp.buf])
            DVE("tensor_tensor", [ktp.buf, e2s[i].buf], [kinT[i].buf], out=kinT[i].ap, in0=k4, in1=e2s[i].ap, op=ALU.mult)
            aps = PS()
            a4 = aps.ap.rearrange("p (h q) -> p h q", h=4)
            mms = []
            for hh in range(4):
                mms.append((a4[:, hh, 0:64], kinT[i].ap[:, hh, :], qdA[j].ap[:, hh, 0:64], True, True))
                mms.append((a4[:, hh, 64:128], kinT[i].ap[:, hh, :], qdB[j].ap[:, hh, 64:128], True, True))
            PE(mms, [kinT[i].buf, qdA[j].buf, qdB[j].buf], [aps.buf])
            DVE("tensor_tensor", [aps.buf, b_const], [attnT[j].buf], out=attnT[j].ap, in0=a4, in1=CM4, op=ALU.mult)
            rp = [PS(), PS()]
            for h2 in range(2):
                hs = slice(h2 * 512, (h2 + 1) * 512)
                PE([(rp[h2].ap, ht.ap[:, kc, :], wG[:, kc, 2048 + h2 * 512:2560 + h2 * 512], kc == 0, kc == 7) for kc in range(8)],
                   [ht.buf] + b_wGr, [rp[h2].buf])
                ACT(sil[j].ap[:, hs], rp[h2].ap, AF.Exp, [rp[h2].buf], [sil[j].buf], scale=-1.0)
                ACT(sil[j].ap[:, hs], sil[j].ap[:, hs], AF.Ln, [sil[j].buf], [sil[j].buf], bias=1.0)
                ACT(sil[j].ap[:, hs], sil[j].ap[:, hs], AF.Exp, [sil[j].buf], [sil[j].buf], scale=-1.0)
                DVE("tensor_tensor", [rp[h2].buf, sil[j].buf], [sil[j].buf], out=sil[j].ap[:, hs],
                    in0=rp[h2].ap, in1=sil[j].ap[:, hs], op=ALU.mult)

        def gla_stage2(t):
            own = t >= 48
            j = t % NS2
            tt = t - 48
            last_prefix = (t == 47)
            ko = kouts[j]
            vb = vbs[j]
            e1 = e1s[j]
            if own:
                oP = [PS(), PS()]
                for pr in range(2):
                    mms = []
                    for hq in range(2):
                        hh = pr * 2 + hq
                        mms.append((oP[pr].ap[:, hq * 256:(hq + 1) * 256], attnT[j].ap[:, hh, :], vb.ap[:, hh * 256:(hh + 1) * 256],
                                    hq == 0, False, True))
                    for hq in range(2):
                        hh = pr * 2 + hq
                        mms.append((oP[pr].ap[:, hq * 256:(hq + 1) * 256], qdA[j].ap[:, hh, :], SbA[:, hh, :], False, False, True))
                    PE(mms, [attnT[j].buf, vb.buf, qdA[j].buf] + b_SbA[pr * 2:pr * 2 + 2], [oP[pr].buf])
            for c in range(2):
                uP = [PS(), PS()]
                for pr in range(2):
                    PE([(uP[pr].ap[:, hq * 256:(hq + 1) * 256], ko.ap[c * 64:(c + 1) * 64, (pr * 2 + hq) * 128:(pr * 2 + hq + 1) * 128],
                         vb.ap[c * 64:(c + 1) * 64, (pr * 2 + hq) * 256:(pr * 2 + hq + 1) * 256], True, True) for hq in range(2)],
                       [ko.buf, vb.buf], [uP[pr].buf])
                for hh in range(4):
                    pr, hq = hh // 2, hh % 2
                    DVE("scalar_tensor_tensor", [uP[pr].buf, e1.buf, b_S[hh]], [b_S[hh]], out=Sst[:, hh, :], in0=Sst[:, hh, :],
                        scalar=e1.ap[:, hh, c * 64 + 63:c * 64 + 64], in1=uP[pr].ap[:, hq * 256:(hq + 1) * 256],
                        op0=ALU.mult, op1=ALU.add)
                    if c == 0 and own:
                        ACT(SbB[:, hh, :], Sst[:, hh, :], AF.Copy, [b_S[hh]], [b_SbB[hh]])
                    if c == 1 and (own or last_prefix):
                        ACT(SbA[:, hh, :], Sst[:, hh, :], AF.Copy, [b_S[hh]], [b_SbA[hh]])
                if c == 0 and own:
                    for pr in range(2):
                        PE([(oP[pr].ap[:, hq * 256:(hq + 1) * 256], qdB[j].ap[:, pr * 2 + hq, :], SbB[:, pr * 2 + hq, :],
                             False, hq == 1, True) for hq in range(2)],
                           [qdB[j].buf] + b_SbB[pr * 2:pr * 2 + 2], [oP[pr].buf])
            if not own:
                return
            og = ogbs.next()
            for pr in range(2):
                bssh = b_ssh[(t % 2) * 2 + pr]
                sc = (t % 2) * 16 + pr * 8
                for hq in range(2):
                    ACT(junkH, oP[pr].ap[:, hq * 256:(hq + 1) * 256], AF.Square, [oP[pr].buf], [b_junkH, bssh],
                        accum_out=ssh[:, sc + hq:sc + hq + 1])
                POOL("tensor_scalar", [bssh], [bssh], out=ssh[:, sc + 2:sc + 4], in0=ssh[:, sc:sc + 2], scalar1=1.0 / 256, scalar2=EPS,
                     op0=ALU.mult, op1=ALU.add)
                POOL("tensor_tensor", [bssh, b_const], [bssh], out=ssh[:, sc + 2:sc + 4], in0=ssh[:, sc + 2:sc + 4], in1=mhalf4[:, 0:2],
                     op=ALU.pow)
                for hq in range(2):
                    hh = pr * 2 + hq
                    DVE("scalar_tensor_tensor", [oP[pr].buf, bssh, sil[j].buf], [og.bufs[pr]], out=og.ap[:, hh * 256:(hh + 1) * 256],
                        in0=oP[pr].ap[:, hq * 256:(hq + 1) * 256], scalar=ssh[:, sc + 2 + hq:sc + 3 + hq],
                        in1=sil[j].ap[:, hh * 256:(hh + 1) * 256], op0=ALU.mult, op1=ALU.mult)

            def fn(e, og=og):
                ins = None
                for kc in range(8):
                    ins = e.transpose(ptr.ap[:, kc, :], og.ap[:, kc * 128:(kc + 1) * 128], identb)
                return ins
            P.op("pe", fn, og.bufs + [b_const], [ptr.buf], cost=650.0)
            ACT(ogT[:, :, tt * 128:(tt + 1) * 128], ptr.ap, AF.Copy, [ptr.buf], [b_ogT[tt]])

        gla_stage1(0)
        for t in range(64):
            if t + 1 < 64:
                gla_stage1(t + 1)
            gla_stage2(t)
        P.barrier()

        hTo2 = carve(108 * KB, [128, 8, 2048], BF16)
        b_hTo2 = [Buf() for _ in range(16)]
        for t in range(16):
            s = load_x(6144 + t * 128)
            norm_to_hT(s.ap, [s.buf], hTo2[:, :, t * 128:(t + 1) * 128], [b_hTo2[t]])
        MM = Mem(140 * KB, 200 * KB)
        wog = MM.alloc([128, 8, 512], BF16)
        woa = MM.alloc([128, 4, 512], BF16)
        wgA = MM.alloc([128, 8, 512], BF16)
        wgB = MM.alloc([128, 8, 512], BF16)
        b_wog = [Buf() for _ in range(8)]
        b_woa = [Buf() for _ in range(4)]
        b_wgA = [Buf() for _ in range(8)]
        b_wgB = [Buf() for _ in range(8)]
        sgs = Rot([Slot(MM.alloc([128, 512], F32)) for _ in range(4)])
        tms = Rot([Slot(MM.alloc([128, 512], F32)) for _ in range(2)])
        for fo in range(2):
            load_w(lambda kc, c0, cn: wog[:, kc, c0:c0 + cn], wog_d, 0, 8, fo * 512, 512, 3, b_wog)
            load_w(lambda kc, c0, cn: woa[:, kc, c0:c0 + cn], woa_d, 0, 4, fo * 512, 512, None, b_woa)
            load_w(lambda kc, c0, cn: wgA[:, kc, c0:c0 + cn], w_in, 0, 8, 7696 + fo * 512, 512, 0, b_wgA)
            load_w(lambda kc, c0, cn: wgB[:, kc, c0:c0 + cn], w_in, 0, 8, 8720 + fo * 512, 512, 0, b_wgB)
            for tb in range(4):
                tk = slice(tb * 512, (tb + 1) * 512)
                for fc in range(4):
                    fs = slice(fc * 128, (fc + 1) * 128)
                    ga = PS()
                    PE([(ga.ap, wgA[:, kc, fs], hTo2[:, kc, tk], kc == 0, kc == 7) for kc in range(8)],
                       b_wgA + b_hTo2[tb * 4:tb * 4 + 4], [ga.buf])
                    sa = sgs.next()
                    ACT(sa.ap, ga.ap, AF.Sigmoid, [ga.buf], [sa.buf])
                    gb = PS()
                    PE([(gb.ap, wgB[:, kc, fs], hTo2[:, kc, tk], kc == 0, kc == 7) for kc in range(8)],
                       b_wgB + b_hTo2[tb * 4:tb * 4 + 4], [gb.buf])
                    sb_ = sgs.next()
                    ACT(sb_.ap, gb.ap, AF.Sigmoid, [gb.buf], [sb_.buf])
                    yg = PS()
                    PE([(yg.ap, wog[:, kc, fs], ogT[:, kc, tk], kc == 0, kc == 7) for kc in range(8)],
                       b_wog + b_ogT[tb * 4:tb * 4 + 4], [yg.buf])
                    t1 = tms.next()
                    DVE("tensor_tensor", [yg.buf, sa.buf], [t1.buf], out=t1.ap, in0=yg.ap, in1=sa.ap, op=ALU.mult)
                    ya = PS()
                    PE([(ya.ap, woa[:, hh, fs], OhT[:, hh, tk], hh == 0, hh == 3) for hh in range(4)],
                       b_woa + [b_OhT], [ya.buf])
                    t2 = tms.next()
                    DVE("tensor_tensor", [ya.buf, sb_.buf], [t2.buf], out=t2.ap, in0=ya.ap, in1=sb_.ap, op=ALU.mult)
                    DVE("tensor_tensor", [t1.buf, t2.buf], [b_mixT[tb]], out=mixT[:, fo * 4 + fc, tk], in0=t1.ap, in1=t2.ap, op=ALU.add)
        P.barrier()

        wout = carve(60 * KB, [128, 8, 1024], BF16)
        b_wout = [Buf() for _ in range(8)]
        load_w(lambda kc, c0, cn: wout[:, kc, c0:c0 + cn], wout_d, 0, 8, 0, 1024, None, b_wout)
        for t in range(16):
            s = load_x(6144 + t * 128)
            for h2 in range(2):
                ps = PS()
                PE([(ps.ap, mixT[:, kc, t * 128:(t + 1) * 128], wout[:, kc, h2 * 512:(h2 + 1) * 512], kc == 0, kc == 7) for kc in range(8)],
                   b_mixT + b_wout, [ps.buf])
                DVE("tensor_tensor", [ps.buf, s.buf], [b_x1[t]], out=x1[:, t, h2 * 512:(h2 + 1) * 512], in0=ps.ap,
                    in1=s.ap[:, h2 * 512:(h2 + 1) * 512], op=ALU.add)
            norm_to_hT(x1[:, t, :], [b_x1[t]], h2T[:, :, t * 128:(t + 1) * 128], [b_h2T[t]])
        P.barrier()

        MF = Mem(60 * KB, 108 * KB)
        w1c = [MF.alloc([128, 8, 512], BF16) for _ in range(2)]
        w2c = [MF.alloc([128, 4, 1024], BF16) for _ in range(2)]
        b_w1c = [[Buf() for _ in range(8)] for _ in range(2)]
        b_w2c = [[Buf() for _ in range(4)] for _ in range(2)]
        uTs = Rot([Slot(MF.alloc([128, 4, 512], BF16)) for _ in range(2)])
        rls = Rot([Slot(MF.alloc([128, 512], F32)) for _ in range(2)])
        for ffg in range(8):
            wi = ffg % 2
            w1 = w1c[wi]
            w2 = w2c[wi]
            load_w(lambda kc, c0, cn: w1[:, kc, c0:c0 + cn], w1_d, 0, 8, ffg * 512, 512, 1, b_w1c[wi])
            load_w(lambda kc, c0, cn: w2[:, kc, c0:c0 + cn], w2_d, ffg * 512, 4, 0, 1024, None, b_w2c[wi])
            for tb in range(4):
                tk = slice(tb * 512, (tb + 1) * 512)
                ut = uTs.next()
                for j in range(4):
                    ps = PS()
                    PE([(ps.ap, w1[:, kc, j * 128:(j + 1) * 128], h2T[:, kc, tk], kc == 0, kc == 7) for kc in range(8)],
                       b_w1c[wi] + b_h2T[tb * 4:tb * 4 + 4], [ps.buf])
                    rl = rls.next()
                    ACT(rl.ap, ps.ap, AF.Relu, [ps.buf], [rl.buf])
                    DVE("tensor_tensor", [rl.buf], [ut.buf], out=ut.ap[:, j, :], in0=rl.ap, in1=rl.ap, op=ALU.mult)
                for tt in range(4):
                    t = tb * 4 + tt
                    for h2 in range(2):
                        ps = PS()
                        PE([(ps.ap, ut.ap[:, j, tt * 128:(tt + 1) * 128], w2[:, j, h2 * 512:(h2 + 1) * 512], j == 0, j == 3) for j in range(4)],
                           [ut.buf] + b_w2c[wi], [ps.buf])
                        DVE("tensor_tensor", [ps.buf, b_x1[t]], [b_x1[t]], out=x1[:, t, h2 * 512:(h2 + 1) * 512],
                            in0=x1[:, t, h2 * 512:(h2 + 1) * 512], in1=ps.ap, op=ALU.add)
        P.barrier()

        MP = Mem(28 * KB, 108 * KB)
        wpg = MP.alloc([128, 8, 1024], BF16)
        wpp = MP.alloc([128, 2, 1024], BF16)
        lnfb = MP.alloc([128, 1024], F32)
        junkP = MP.alloc([128, 1024], BF16)
        b_junkP = Buf()
        b_wpg = [Buf() for _ in range(8)]
        b_wpp = [Buf() for _ in range(2)]
        b_lnf = Buf()
        h3s = Rot([Slot(MP.alloc([128, 8, 128], BF16)) for _ in range(2)])
        pfs = Rot([Slot(MP.alloc([128, 256], F32), P.dma_sem("pf%d" % i)) for i in range(2)])
        pbs = Rot([Slot(MP.alloc([128, 256], BF16)) for _ in range(2)])
        pTs = Rot([Slot(MP.alloc([128, 2, 128], BF16)) for _ in range(2)])
        sg2 = Rot([Slot(MP.alloc([128, 1024], F32)) for _ in range(2)])
        osb = Rot([Slot(MP.alloc([128, 1024], F32), P.dma_sem("os%d" % i)) for i in range(2)])
        load_w(lambda kc, c0, cn: wpg[:, kc, c0:c0 + cn], wpg_d, 0, 8, 0, 1024, 2, b_wpg)
        load_w(lambda kc, c0, cn: wpp[:, kc, c0:c0 + cn], wpp_d, 0, 2, 0, 1024, None, b_wpp)
        dl = P.dma_sem("lnf")
        DMA(lnfb, lnf_d.partition_broadcast(128), [], [b_lnf], dl)
        out_toks = []
        for t in range(16):
            h3 = h3s.next()
            norm_to_hT(x1[:, t, :], [b_x1[t]], h3.ap, [h3.buf])
            pf = pfs.next()
            DMA(pf.ap, pin[t * 128:(t + 1) * 128, :], [], [pf.buf], pf.sem)
            pb = pbs.next()
            DVE("tensor_copy", [pf.buf], [pb.buf], out=pb.ap, in_=pf.ap)

            def fn(e, pb=pb):
                ins = None
                for c in range(2):
                    ins = e.transpose(ptr.ap[:, c, :], pb.ap[:, c * 128:(c + 1) * 128], identb)
                return ins
            P.op("pe", fn, [pb.buf, b_const], [ptr.buf], cost=200.0)
            pT = pTs.next()
            ACT(pT.ap, ptr.ap[:, 0:2, :], AF.Copy, [ptr.buf], [pT.buf])
            sg = sg2.next()
            for h2 in range(2):
                hs = slice(h2 * 512, (h2 + 1) * 512)
                gp = PS()
                PE([(gp.ap, h3.ap[:, kc, :], wpg[:, kc, hs], kc == 0, kc == 7) for kc in range(8)], [h3.buf] + b_wpg, [gp.buf])
                ACT(sg.ap[:, hs], gp.ap, AF.Sigmoid, [gp.buf], [sg.buf])
                pp = PS()
                PE([(pp.ap, pT.ap[:, c, :], wpp[:, c, hs], c == 0, c == 1) for c in range(2)], [pT.buf] + b_wpp, [pp.buf])
                DVE("tensor_tensor", [pp.buf, sg.buf], [sg.buf], out=sg.ap[:, hs], in0=sg.ap[:, hs], in1=pp.ap, op=ALU.mult)
            DVE("tensor_tensor", [sg.buf, b_x1[t]], [b_x1[t]], out=x1[:, t, :], in0=x1[:, t, :], in1=sg.ap, op=ALU.add)
            ob = osb.next()
            rs, brs = rstd_of(x1[:, t, :], [b_x1[t]], 1024, junkP, b_junkP)
            DVE("scalar_tensor_tensor", [b_x1[t], brs, b_lnf], [ob.buf], out=ob.ap, in0=x1[:, t, :], scalar=rs, in1=lnfb,
                op0=ALU.mult, op1=ALU.mult)
            out_toks.append(DMA(y[t * 128:(t + 1) * 128, :], ob.ap, [ob.buf], [], ob.sem))
        P.final_wait("sp", out_toks[-2:])
        P.run(block)
    return nc


def _t5_bucket(n):
    max_exact = 16
    nf = np.maximum(n, 1).astype(np.float32)
    large = max_exact + (np.log(nf / max_exact) / np.log(2048 / max_exact) * (32 - max_exact)).astype(np.int32)
    large = np.minimum(large, 31)
    return np.where(n < max_exact, n, large).astype(np.int32)


def _const_mats():
    m = np.arange(128)[:, None]
    t = np.arange(128)[None, :]
    same = (m // 64) == (t // 64)
    cm = np.zeros((128, 6, 128), np.float32)
    cm[:, 0, :] = np.eye(128)
    cm[:, 1, :] = np.where(same & (m <= t), 1.0 / 16, 0.0)
    cm[:, 2, :] = np.where(same & (m > t), -1.0 / 16, 0.0)
    cm[:, 3, :] = np.where(same & (m <= t), 1.0, 0.0)
    cm[:, 4, 0:16] = np.eye(4, dtype=np.float32).reshape(16)[None, :]
    sel4 = np.zeros((4, 4, 128), np.float32)
    for hh in range(4):
        sel4[hh, hh, :] = 1.0
    return cm, sel4


def _bias_layout(rel_bias):
    k = np.arange(128)[:, None, None]
    j = np.arange(2)[None, :, None]
    q = np.arange(128)[None, None, :]
    delta = q - k + 128 * (1 - j)
    valid = (delta >= 0) & (delta <= 128)
    out = np.full((128, 3, 2, 2, 2, 128), NEGM, np.float32)
    for g, dil in enumerate((1, 4, 16)):
        bucket = _t5_bucket(np.maximum(delta, 0) * dil)
        for hp in range(2):
            for hh in range(2):
                tab = rel_bias[:, g * 4 + hp * 2 + hh]
                vals = tab[bucket]
                out[:, g, hp, hh] = np.where(valid, vals, NEGM)
    return out.reshape(128, 3, 2, 512)


_PROG = None


def kernel(x, p, ln1, w_in, w_a2, b_a, gla_gn, w_o_gla, w_o_attn, w_out, ln2, w_mlp1, w_mlp2, ln3, w_pp, w_pg,
           rel_bias, ln_f):
    global _PROG
    f = lambda a: np.ascontiguousarray(np.asarray(a, dtype=np.float32))
    x = f(x); p = f(p)
    cm, sel4 = _const_mats()
    cols = np.stack([f(ln1)[0], f(ln2)[0], f(ln3)[0], f(gla_gn)[0]]).reshape(4, 8, 128).transpose(2, 0, 1).reshape(128, 32)
    wa2aug = np.zeros((32, 512), np.float32)
    wa2aug[0:16] = f(w_a2)[0]
    wa2aug[16] = f(b_a)[0]
    shared = {
        "w_in": f(w_in)[0], "w_a2aug": wa2aug, "w_o_gla": f(w_o_gla)[0], "w_o_attn": f(w_o_attn)[0],
        "w_out": f(w_out)[0], "w_mlp1": f(w_mlp1)[0], "w_mlp2": f(w_mlp2)[0], "w_pp": f(w_pp)[0], "w_pg": f(w_pg)[0],
        "cols": np.ascontiguousarray(cols), "ln_f": f(ln_f), "biasm": _bias_layout(f(rel_bias)),
        "cmat": cm, "sel4": sel4,
    }
    in_maps = []
    for c in range(NCORES):
        b, j = c // 4, c % 4
        xe = np.zeros((8192, 1024), np.float32)
        n = SEG * (j + 1)
        xe[8192 - n:] = x[b, 0:n]
        m = dict(shared)
        m["xe"] = xe
        m["p"] = np.ascontiguousarray(p[0, b, j * SEG:(j + 1) * SEG])
        m["hoff"] = np.full((128, 1), NEGM if j == 0 else 0.0, np.float32)
        in_maps.append(m)
    if _PROG is None:
        _PROG = build_program()
    res = run_bass_kernel_spmd(_PROG, in_maps, core_ids=list(range(NCORES)))
    out = np.zeros((2, 8192, 1024), np.float32)
    for c in range(NCORES):
        b, j = c // 4, c % 4
        out[b, j * SEG:(j + 1) * SEG] = res.results[c]["y"]
    return out
```

```python
import contextlib
import numpy as np
import concourse.bass as bass
import concourse.mybir as mybir
from concourse.bass_utils import run_bass_kernel_spmd

F32 = mybir.dt.float32
BF16 = mybir.dt.bfloat16
ALU = mybir.AluOpType
AF = mybir.ActivationFunctionType

SAFE_SAME = True
EPS = 1e-6
NCORES = 8
SEG = 2048
NEGM = -30000.0


class Buf:
    __slots__ = ("w", "r")

    def __init__(self):
        self.w = None
        self.r = []


class Prog:
    ENGS = ("pe", "act", "dve", "pool", "sp")
    WINDOW = 100
    LAT = 300.0
    SLACK = 500.0
    LAT_DMA = 200.0

    def __init__(self, nc, stack):
        self.nc = nc
        self.stack = stack
        self.ops = []
        self.phase = 0
        self.sems = {}
        self.all_dsems = []
        self.final = []
        for e in ("pe", "act", "dve", "pool"):
            self.sems[e] = stack.enter_context(nc.semaphore("s_" + e))

    def dma_sem(self, name):
        s = self.stack.enter_context(self.nc.semaphore("d_" + name))
        d = [s, 0]
        self.all_dsems.append(d)
        return d

    def op(self, eng, fn, reads=(), writes=(), dsem=None, cost=500.0, fin=None):
        idx = len(self.ops)
        deps = set()
        for b in reads:
            if b.w is not None:
                deps.add(b.w)
        for b in writes:
            if b.w is not None:
                deps.add(b.w)
            deps.update(b.r)
        self.ops.append([eng, fn, sorted(deps), dsem, cost, self.phase, cost if fin is None else fin])
        for b in reads:
            b.r.append(idx)
        for b in writes:
            b.w = idx
            b.r = []
        return idx

    def barrier(self):
        self.phase += 1

    def final_wait(self, eng, toks):
        self.final.append((eng, list(toks)))

    def schedule(self):
        ops = self.ops
        n = len(ops)
        succ = [[] for _ in range(n)]
        for i, o in enumerate(ops):
            for d in o[2]:
                if ops[d][5] == o[5]:
                    succ[d].append(i)
        bl = [0.0] * n
        for i in range(n - 1, -1, -1):
            m = 0.0
            for j in succ[i]:
                v = bl[j] + (0.0 if ops[j][0] == ops[i][0] else self.LAT)
                if v > m:
                    m = v
            bl[i] = ops[i][6] + m
        order = {e: [] for e in self.ENGS}
        finish = {}
        tnow = 0.0
        for ph in range(self.phase + 1):
            pend = {e: [] for e in self.ENGS}
            for i, o in enumerate(ops):
                if o[5] == ph:
                    pend[o[0]].append(i)
            tfree = {e: tnow for e in self.ENGS}
            remaining = sum(len(v) for v in pend.values())
            cand = {e: None for e in self.ENGS}
            dirty = set(self.ENGS)
            while remaining:
                for e in list(dirty):
                    cl = []
                    for i in pend[e][:self.WINDOW]:
                        o = ops[i]
                        st = tfree[e]
                        ok = True
                        for d in o[2]:
                            f = finish.get(d)
                            if f is None:
                                ok = False
                                break
                            lat = self.LAT_DMA if ops[d][3] is not None else (0.0 if ops[d][0] == e else self.LAT)
                            if f + lat > st:
                                st = f + lat
                        if ok:
                            cl.append((st, i))
                    if not cl:
                        cand[e] = None
                    else:
                        tmin = min(c[0] for c in cl)
                        lim = tmin + self.SLACK
                        best = None
                        for st, i in cl:
                            if st <= lim:
                                key = (-bl[i], i)
                                if best is None or key < best[0]:
                                    best = (key, st, i)
                        cand[e] = (best[1], best[2])
                dirty.clear()
                pick = None
                for e in self.ENGS:
                    c = cand[e]
                    if c is not None and (pick is None or c < pick[0]):
                        pick = (c, e)
                assert pick is not None, "scheduler stuck"
                (st, i), e = pick
                o = ops[i]
                tfree[e] = st + o[4]
                finish[i] = st + o[6]
                pend[e].remove(i)
                order[e].append(i)
                remaining -= 1
                dirty.update(self.ENGS)
            tnow = max(tfree.values())
            self.phase_ends = getattr(self, 'phase_ends', []) + [tnow]
            for e in self.ENGS:
                order[e].append(None)
        self.est_total = tnow
        return order

    def lower(self):
        order = self.schedule()
        ops = self.ops
        tok = {}
        cnt = {e: 0 for e in ("pe", "act", "dve", "pool")}
        bar_cnt = []
        nph = self.phase + 1
        pos = {e: 0 for e in self.ENGS}
        dcount = {id(d): 0 for d in self.all_dsems}
        bar_state = []
        for ph in range(nph):
            for e in self.ENGS:
                lst = order[e]
                while lst[pos[e]] is not None:
                    i = lst[pos[e]]
                    o = ops[i]
                    if o[3] is not None:
                        dcount[id(o[3])] += 16
                        tok[i] = (o[3][0], dcount[id(o[3])], e, True)
                    else:
                        cnt[e] += 1
                        tok[i] = (self.sems[e], cnt[e], e, False)
                    pos[e] += 1
                pos[e] += 1
            bar_state.append((dict(cnt), dict(dcount)))
        streams = {e: [] for e in self.ENGS}
        for e in self.ENGS:
            waited = {}
            ph = 0
            for i in order[e]:
                if i is None:
                    c, dc = bar_state[ph]
                    waits = []
                    for e2 in ("pe", "act", "dve", "pool"):
                        if e2 != e and c[e2] > waited.get(id(self.sems[e2]), 0):
                            waits.append((self.sems[e2], c[e2]))
                            waited[id(self.sems[e2])] = c[e2]
                    for d in self.all_dsems:
                        v = dc[id(d)]
                        if v > waited.get(id(d[0]), 0):
                            waits.append((d[0], v))
                            waited[id(d[0])] = v
                    if waits and ph < nph - 1:
                        streams[e].append((waits, None, None))
                    ph += 1
                    continue
                o = ops[i]
                waits = {}
                for d in o[2]:
                    s, v, e2, isdma = tok[d]
                    if e2 == e and not isdma:
                        if e in ("pe", "sp") or not SAFE_SAME:
                            continue
                    k = id(s)
                    if waited.get(k, 0) >= v:
                        continue
                    if k not in waits or waits[k][1] < v:
                        waits[k] = (s, v)
                for k, (s, v) in waits.items():
                    waited[k] = v
                t = tok[i]
                streams[e].append((list(waits.values()), o[1], (t[0], 16 if t[3] else 1)))
        for eng, toks in self.final:
            streams[eng].append(([(tok[t][0], tok[t][1]) for t in toks], None, None))
        self.streams = streams

    def run(self, block):
        self.lower()

        def play(name):
            def _f(e):
                for waits, fn, inc in self.streams[name]:
                    for s, v in waits:
                        e.wait_ge(s, v)
                    if fn is None:
                        continue
                    ins = fn(e)
                    if inc is not None:
                        ins.then_inc(inc[0], inc[1])
            return _f
        block.tensor(play("pe"))
        block.scalar(play("act"))
        block.vector(play("dve"))
        block.gpsimd(play("pool"))
        block.sync(play("sp"))


class Slot:
    def __init__(self, ap, sem=None):
        self.ap = ap
        self.buf = Buf()
        self.sem = sem


class Rot:
    def __init__(self, slots):
        self.slots = slots
        self.i = 0

    def next(self):
        s = self.slots[self.i % len(self.slots)]
        self.i += 1
        return s


def build_program():
    nc = bass.Bass("TRN2", target_bir_lowering=False)

    def din(name, shape):
        return nc.dram_tensor(name, shape, F32, kind="ExternalInput").ap()

    xe = din("xe", [8192, 1024])
    pin = din("p", [SEG, 256])
    w_in = din("w_in", [1024, 9744])
    wa2_d = din("w_a2aug", [32, 512])
    wog_d = din("w_o_gla", [1024, 1024])
    woa_d = din("w_o_attn", [512, 1024])
    wout_d = din("w_out", [1024, 1024])
    w1_d = din("w_mlp1", [1024, 4096])
    w2_d = din("w_mlp2", [4096, 1024])
    wpp_d = din("w_pp", [256, 1024])
    wpg_d = din("w_pg", [1024, 1024])
    cols_d = din("cols", [128, 32])
    lnf_d = din("ln_f", [1024])
    biasm_d = din("biasm", [128, 3, 2, 512])
    hoff_d = din("hoff", [128, 1])
    cmat_d = din("cmat", [128, 6, 128])
    sel4_d = din("sel4", [4, 4, 128])
    y = nc.dram_tensor("y", [SEG, 1024], F32, kind="ExternalOutput").ap()

    with contextlib.ExitStack() as st:
        P = Prog(nc, st)
        ARENA_BYTES = 200 * 1024
        arena = st.enter_context(nc.sbuf_tensor("arena", [128, ARENA_BYTES // 2], BF16))
        psb = [st.enter_context(nc.psum_tensor("psb%d" % i, [128, 512], F32)) for i in range(7)]
        ptr_t = st.enter_context(nc.psum_tensor("ptr", [128, 8, 128], BF16))
        block = st.enter_context(nc.Block())

        def carve(off, shape, dt):
            n = 1
            for s in shape[1:]:
                n *= s
            es = 2 if dt == BF16 else 4
            assert off % 4 == 0 and off + n * es <= ARENA_BYTES, (off, shape)
            a = arena[0:shape[0], off // 2: off // 2 + n * es // 2]
            if dt == F32:
                a = a.bitcast(F32)
            if len(shape) == 3:
                a = a.rearrange("p (a b) -> p a b", a=shape[1])
            elif len(shape) == 4:
                a = a.rearrange("p (a b c) -> p a b c", a=shape[1], b=shape[2])
            return a

        class Mem:
            def __init__(self, base, limit):
                self.off = base
                self.limit = limit

            def alloc(self, shape, dt):
                n = 1
                for s in shape[1:]:
                    n *= s
                es = 2 if dt == BF16 else 4
                a = carve(self.off, shape, dt)
                self.off += (n * es + 63) // 64 * 64
                assert self.off <= self.limit, (self.off, self.limit)
                return a

        KB = 1024

        def sl(start, n, step):
            return slice(start, start + step * (n - 1) + 1, step)

        def fsz(ap):
            n = 1
            for x in ap.shape[1:]:
                n *= x
            return n

        def PE(mms, reads, writes):
            cost = 0.0
            for m in mms:
                n = max(fsz(m[2]), 64)
                c = n / 2.4 + 25.0
                if m[1].dtype == F32:
                    c *= 4
                cost += c

            def fn(e):
                ins = None
                for m in mms:
                    kw = dict(start=m[3], stop=m[4])
                    if len(m) > 5 and m[5]:
                        kw["skip_group_check"] = True
                    ins = e.matmul(m[0], lhsT=m[1], rhs=m[2], **kw)
                return ins
            return P.op("pe", fn, reads, writes, cost=cost)

        def ACT(out, in_, func, reads, writes, **kw):
            return P.op("act", lambda e: e.activation(out=out, in_=in_, func=func, **kw), reads, writes,
                        cost=180.0 + 0.83 * fsz(out))

        def ENG(eng, method, reads, writes, **kw):
            o = kw.get("out", kw.get("ap"))
            n = fsz(o)
            cost = (100.0 + 1.15 * n) if eng == "dve" else (250.0 + 0.6 * n)
            return P.op(eng, lambda e: getattr(e, method)(**kw), reads, writes, cost=cost)

        def DVE(method, reads, writes, **kw):
            return ENG("dve", method, reads, writes, **kw)

        def POOL(method, reads, writes, **kw):
            return ENG("pool", method, reads, writes, **kw)

        def DMA(out, in_, reads, writes, dsem):
            nbytes = out.shape[0] * fsz(out) * 4
            return P.op("sp", lambda e: e.dma_start(out=out, in_=in_), reads, writes, dsem=dsem, cost=120.0, fin=2000.0 + nbytes / 150.0)

        psrot = Rot([Slot(t[:]) for t in psb])
        ptr = Slot(ptr_t[:])

        def PS():
            return psrot.next()

        G = Mem(0, 28 * KB)
        cmat = G.alloc([128, 6, 128], F32)
        identb = G.alloc([128, 128], BF16)
        onesel4 = G.alloc([128, 4, 4], BF16)
        sel4 = G.alloc([4, 4, 128], F32)
        colsT = G.alloc([128, 32], F32)
        hoff = G.alloc([128, 1], F32)
        mhalf4 = G.alloc([128, 4], F32)
        mhalf = mhalf4[:, 0:1]
        statv = G.alloc([128, 64], F32)
        junk = G.alloc([128, 1024], BF16)
        stage = Rot([Slot(G.alloc([128, 1024], F32), P.dma_sem("st%d" % i)) for i in range(2)])
        xts = Rot([Slot(G.alloc([128, 1024], F32), P.dma_sem("xt%d" % i)) for i in range(2)])
        hbs = Rot([Slot(G.alloc([128, 1024], BF16)) for i in range(2)])
        LT = cmat[:, 1, :]
        UT = cmat[:, 2, :]
        CM = cmat[:, 3, :]
        b_const = Buf()
        b_junk = Buf()
        stat_i = [0]

        b_stat = [Buf() for _ in range(64)]

        def stat_col():
            c = stat_i[0] % 64
            stat_i[0] += 1
            return statv[:, c:c + 1], b_stat[c]

        dc = P.dma_sem("const")
        DMA(cmat, cmat_d, [], [b_const], dc)
        DMA(sel4, sel4_d, [], [b_const], dc)
        DMA(colsT, cols_d, [], [b_const], dc)
        DMA(hoff, hoff_d, [], [b_const], dc)
        DVE("tensor_copy", [b_const], [b_const], out=identb, in_=cmat[:, 0, :])
        DVE("tensor_copy", [b_const], [b_const], out=onesel4, in_=cmat[:, 4, 0:16].rearrange("p (a b) -> p a b", a=4))
        POOL("memset", [], [b_const], ap=mhalf4, constant=-0.5)

        def col(i, kc):
            return colsT[:, i * 8 + kc: i * 8 + kc + 1]

        cast_rr = [0]

        def cast_w(dst, src, sc, reads, writes):
            k = (cast_rr[0] % 2) * 2
            cast_rr[0] += 1
            if k == 0:
                POOL("tensor_scalar", reads, writes, out=dst, in0=src, scalar1=sc, scalar2=1.0, op0=ALU.mult, op1=ALU.mult)
            elif k == 1:
                ACT(dst, src, AF.Copy, reads, writes, scale=sc)
            else:
                DVE("tensor_scalar", reads, writes, out=dst, in0=src, scalar1=sc, scalar2=None, op0=ALU.mult)

        def load_w(dst_fn, dram, row0, nk, col0, ncols, scale_i, dst_bufs, stg=None):
            stg = stg or stage
            for kc in range(nk):
                for c0 in range(0, ncols, 1024):
                    cn = min(1024, ncols - c0)
                    s = stg.next()
                    DMA(s.ap[:, 0:cn], dram[row0 + kc * 128: row0 + (kc + 1) * 128, col0 + c0: col0 + c0 + cn],
                        [], [s.buf], s.sem)
                    sc = col(scale_i, kc) if scale_i is not None else 1.0
                    cast_w(dst_fn(kc, c0, cn), s.ap[:, 0:cn], sc, [s.buf, b_const], [dst_bufs[kc]])

        def rstd_of(src_ap, src_bufs, n, scr_ap, scr_buf):
            ss, bss = stat_col()
            ACT(scr_ap, src_ap, AF.Square, src_bufs, [scr_buf, bss], accum_out=ss)
            vv, bvv = stat_col()
            POOL("tensor_scalar", [bss], [bvv], out=vv, in0=ss, scalar1=1.0 / n, scalar2=EPS, op0=ALU.mult, op1=ALU.add)
            rs, brs = stat_col()
            POOL("tensor_tensor", [bvv, b_const], [brs], out=rs, in0=vv, in1=mhalf, op=ALU.pow)
            return rs, brs

        def norm_to_hT(src_ap, src_bufs, hT_out, hT_bufs):
            hb = hbs.next()
            rs, brs = rstd_of(src_ap, src_bufs, 1024, junk, b_junk)
            DVE("tensor_scalar", list(src_bufs) + [brs], [hb.buf], out=hb.ap, in0=src_ap, scalar1=rs, scalar2=None,
                op0=ALU.mult)

            def fn(e):
                ins = None
                for kc in range(8):
                    ins = e.transpose(ptr.ap[:, kc, :], hb.ap[:, kc * 128:(kc + 1) * 128], identb)
                return ins
            P.op("pe", fn, [hb.buf, b_const], [ptr.buf], cost=650.0)
            ACT(hT_out, ptr.ap, AF.Copy, [ptr.buf], hT_bufs)

        def load_x(row0, step=1):
            s = xts.next()
            DMA(s.ap, xe[sl(row0, 128, step), :], [], [s.buf], s.sem)
            return s

        OhT = carve(28 * KB, [128, 4, 2048], BF16)
        b_OhT = Buf()
        ogT = carve(44 * KB, [128, 8, 2048], BF16)
        b_ogT = [Buf() for _ in range(16)]
        mixT = carve(76 * KB, [128, 8, 2048], BF16)
        b_mixT = [Buf() for _ in range(4)]
        x1 = carve(108 * KB, [128, 16, 1024], F32)
        b_x1 = [Buf() for _ in range(16)]
        h2T = carve(28 * KB, [128, 8, 2048], BF16)
        b_h2T = [Buf() for _ in range(16)]

        hTh = carve(44 * KB, [128, 8, 2048], BF16)
        hTo = carve(76 * KB, [128, 8, 2048], BF16)
        b_hTh = [Buf() for _ in range(16)]
        b_hTo = [Buf() for _ in range(16)]
        MA = Mem(108 * KB, 200 * KB)
        NT = MA.alloc([128, 4, 2048], F32)
        ST = MA.alloc([4, 2048], F32)
        biasT = carve(28 * KB, [128, 512], F32)
        expbN = carve(30 * KB, [128, 512], F32)
        expbH = carve(32 * KB, [128, 512], F32)
        wA = [MA.alloc([128, 8, 3, 256], BF16) for _ in range(2)]
        b_wA = [[Buf() for _ in range(8)] for _ in range(2)]
        KTs = Rot([Slot(MA.alloc([128, 2, 512], BF16)) for _ in range(3)])
        QTs = Rot([Slot(MA.alloc([128, 2, 512], BF16)) for _ in range(2)])
        Vs = Rot([Slot(MA.alloc([128, 4, 256], BF16)) for _ in range(3)])
        Efs = Rot([Slot(MA.alloc([128, 512], F32)) for _ in range(2)])
        ETs = Rot([Slot(MA.alloc([128, 512], BF16)) for _ in range(2)])
        b_bias = Buf()
        b_exp = Buf()
        scale_att = 128.0 ** -0.5
        dbias = P.dma_sem("bias")
        b_NTq = [[Buf() for _ in range(4)] for _ in range(2)]
        b_STq = [Buf() for _ in range(4)]
        DVE("memset", [], [b for l in b_NTq for b in l], ap=NT, constant=0.0)
        DVE("memset", [], b_STq, ap=ST, constant=0.0)
        wslot = 0

        def proj_seg(hTs, hbufs, t0, nb, wa, bwa, want_q):
            cols = slice(t0 * 128, (t0 + nb) * 128)
            hb_ = hbufs[t0:t0 + nb]
            kt = KTs.next()
            for hh in range(2):
                ps = PS()
                PE([(ps.ap[:, 0:nb * 128], wa[:, kc, 1, hh * 128:(hh + 1) * 128], hTs[:, kc, cols], kc == 0, kc == 7)
                    for kc in range(8)], hb_ + bwa, [ps.buf])
                ACT(kt.ap[:, hh, 0:nb * 128], ps.ap[:, 0:nb * 128], AF.Copy, [ps.buf], [kt.buf])
            qt = None
            if want_q:
                qt = QTs.next()
                for hh in range(2):
                    ps = PS()
                    PE([(ps.ap[:, 0:nb * 128], wa[:, kc, 0, hh * 128:(hh + 1) * 128], hTs[:, kc, cols], kc == 0, kc == 7)
                        for kc in range(8)], hb_ + bwa, [ps.buf])
                    DVE("tensor_copy", [ps.buf], [qt.buf], out=qt.ap[:, hh, 0:nb * 128], in_=ps.ap[:, 0:nb * 128])
            vs = Vs.next()
            for b in range(nb):
                bc = slice((t0 + b) * 128, (t0 + b + 1) * 128)
                ps = PS()
                PE([(ps.ap[:, 0:256], hTs[:, kc, bc], wa[:, kc, 2, :], kc == 0, kc == 7) for kc in range(8)],
                   [hbufs[t0 + b]] + bwa, [ps.buf])
                if b % 2 == 0:
                    ACT(vs.ap[:, b, :], ps.ap[:, 0:256], AF.Copy, [ps.buf], [vs.buf])
                else:
                    DVE("tensor_copy", [ps.buf], [vs.buf], out=vs.ap[:, b, :], in_=ps.ap[:, 0:256])
            return kt, qt, vs

        def attend(g, hp, prevb, curb, qt, qb, first, nat, quarters):
            blk = [prevb, curb]
            sp_ = PS()
            s4 = sp_.ap.rearrange("p (h j q) -> p h j q", h=2, j=2)
            PE([(s4[:, hh, j, :], blk[j][0].ap[:, hh, blk[j][2] * 128:(blk[j][2] + 1) * 128],
                 qt.ap[:, hh, qb * 128:(qb + 1) * 128], True, True) for hh in range(2) for j in range(2)],
               [prevb[0].buf, curb[0].buf, qt.buf], [sp_.buf])
            ef = Efs.next()
            ACT(ef.ap, sp_.ap, AF.Exp, [sp_.buf], [ef.buf], scale=scale_att)
            et = ETs.next()
            DVE("tensor_tensor", [ef.buf, b_exp], [et.buf], out=et.ap, in0=ef.ap,
                in1=(expbH if first else expbN), op=ALU.mult)
            e4t = et.ap.rearrange("p (h j q) -> p h j q", h=2, j=2)
            np_ = PS()
            mms = []
            for hh in range(2):
                for j in range(2):
                    mms.append((np_.ap[:, hh * 128:(hh + 1) * 128],
                                blk[j][1].ap[:, blk[j][2], hh * 128:(hh + 1) * 128], e4t[:, hh, j, :], j == 0, j == 1))
            k = 0
            for hh in range(2):
                for j in range(2):
                    mms.append((np_.ap[0:4, 256:384], onesel4[:, hp * 2 + hh, :], e4t[:, hh, j, :], k == 0, k == 3))
                    k += 1
            PE(mms, [prevb[1].buf, curb[1].buf, et.buf, b_const], [np_.buf])
            nb_ = [b_NTq[hp][q] for q in quarters]
            DVE("tensor_tensor", [np_.buf] + nb_, nb_, out=NT[:, hp * 2:hp * 2 + 2, nat], in0=NT[:, hp * 2:hp * 2 + 2, nat],
                in1=np_.ap[:, 0:256].rearrange("p (h q) -> p h q", h=2), op=ALU.add)
            sb_ = [b_STq[q] for q in quarters]
            DVE("tensor_tensor", [np_.buf] + sb_, sb_, out=ST[:, nat], in0=ST[:, nat],
                in1=np_.ap[0:4, 256:384], op=ALU.add)

        for g in range(3):
            dil = (1, 4, 16)[g]
            nbk = 16 // dil
            for k in range(dil):
                s = load_x(6144 - 128 * dil + k, dil)
                norm_to_hT(s.ap, [s.buf], hTh[:, :, k * 128:(k + 1) * 128], [b_hTh[k]])
            for k in range(16):
                r, n = divmod(k, nbk)
                s = load_x(6144 + r + dil * 128 * n, dil)
                norm_to_hT(s.ap, [s.buf], hTo[:, :, k * 128:(k + 1) * 128], [b_hTo[k]])
            for hp in range(2):
                DMA(biasT, biasm_d[:, g, hp, :], [], [b_bias], dbias)
                ACT(expbN, biasT, AF.Exp, [b_bias], [b_exp])
                ACT(expbH, biasT, AF.Exp, [b_bias], [b_exp])
                b4 = biasT.rearrange("p (h j q) -> p h j q", h=2, j=2)
                e4 = expbH.rearrange("p (h j q) -> p h j q", h=2, j=2)
                ACT(e4[:, :, 0, :], b4[:, :, 0, :], AF.Exp, [b_bias, b_const], [b_exp], bias=hoff)
                wa = wA[wslot % 2]
                bwa = b_wA[wslot % 2]
                wslot += 1
                base = 3088 + g * 1536
                for kc in range(8):
                    s = stage.next()
                    src = w_in[kc * 128:(kc + 1) * 128, base:base + 1536].rearrange("p (c x) -> p c x", c=3)[:, :, hp * 256:(hp + 1) * 256]
                    sv = s.ap[:, 0:768].rearrange("p (c x) -> p c x", c=3)
                    DMA(sv, src, [], [s.buf], s.sem)
                    POOL("tensor_scalar", [s.buf, b_const], [bwa[kc]], out=wa[:, kc, :, :], in0=sv,
                         scalar1=col(0, kc), scalar2=1.0, op0=ALU.mult, op1=ALU.mult)
                if g < 2:
                    for r in range(dil):
                        kt_p, _, vs_p = proj_seg(hTh, b_hTh, r, 1, wa, bwa, False)
                        prev = (kt_p, vs_p, 0)
                        for n0 in range(0, nbk, 4):
                            nb = min(4, nbk - n0)
                            kt, qt, vs = proj_seg(hTo, b_hTo, r * nbk + n0, nb, wa, bwa, True)
                            for b in range(nb):
                                cur = (kt, vs, b)
                                n = n0 + b
                                tok0 = r + dil * 128 * n
                                quarters = [tok0 // 512] if g == 0 else [n]
                                attend(g, hp, prev, cur, qt, b, (n == 0), sl(tok0, 128, dil), quarters)
                                prev = cur
                else:
                    for q4 in range(4):
                        kt_h, _, vs_h = proj_seg(hTh, b_hTh, q4 * 4, 4, wa, bwa, False)
                        kt, qt, vs = proj_seg(hTo, b_hTo, q4 * 4, 4, wa, bwa, True)
                        for b in range(4):
                            r = q4 * 4 + b
                            attend(g, hp, (kt_h, vs_h, b), (kt, vs, b), qt, b, True, sl(r, 128, 16), [0, 1, 2, 3])
        DVE("reciprocal", b_STq, b_STq, out=ST, in_=ST)
        for hh in range(4):
            for tb in range(4):
                ps = PS()
                PE([(ps.ap, sel4[:, hh, :], ST[:, tb * 512:(tb + 1) * 512], True, True)], b_STq + [b_const], [ps.buf])
                DVE("tensor_tensor", [ps.buf, b_NTq[hh // 2][tb]], [b_OhT], out=OhT[:, hh, tb * 512:(tb + 1) * 512],
                    in0=NT[:, hh, tb * 512:(tb + 1) * 512], in1=ps.ap, op=ALU.mult)
        P.barrier()

        MG = Mem(76 * KB, 200 * KB)
        wG = MG.alloc([128, 8, 3088], BF16)
        b_wGq = [Buf() for _ in range(8)]
        b_wGk = [Buf() for _ in range(8)]
        b_wGv = [Buf() for _ in range(8)]
        b_wGr = [Buf() for _ in range(8)]
        b_wGa = [Buf() for _ in range(8)]
        wa2 = MG.alloc([32, 512], BF16)
        b_wa2 = Buf()
        CM4 = MG.alloc([128, 4, 128], F32)
        NS1 = 3
        NS2 = 3
        hTt = [Slot(MG.alloc([128, 8, 128], BF16)) for _ in range(NS1)]
        haT = [Slot(MG.alloc([32, 128], BF16)) for _ in range(NS1)]
        sps = [Slot(MG.alloc([128, 512], F32)) for _ in range(2)]
        wex = [Slot(MG.alloc([128, 512], F32)) for _ in range(2)]
        kouts = [Slot(MG.alloc([128, 512], BF16)) for _ in range(NS2)]
        vbs = [Slot(MG.alloc([128, 1024], BF16)) for _ in range(NS2)]
        e1s = [Slot(MG.alloc([128, 4, 128], F32)) for _ in range(NS2)]
        e2s = [Slot(MG.alloc([128, 4, 128], F32)) for _ in range(2)]
        qdA = [Slot(MG.alloc([128, 4, 128], BF16)) for _ in range(NS2)]
        qdB = [Slot(MG.alloc([128, 4, 128], BF16)) for _ in range(NS2)]
        kinT = [Slot(MG.alloc([128, 4, 128], BF16)) for _ in range(2)]
        attnT = [Slot(MG.alloc([128, 4, 128], BF16)) for _ in range(NS2)]
        sil = [Slot(MG.alloc([128, 1024], F32), P.dma_sem("sil%d" % i)) for i in range(NS2)]
        Sst = MG.alloc([128, 4, 256], F32)
        SbA = MG.alloc([128, 4, 256], BF16)
        SbB = MG.alloc([128, 4, 256], BF16)
        ogbs = Rot([Slot(MG.alloc([128, 1024], BF16)) for _ in range(2)])
        for o_ in ogbs.slots:
            o_.bufs = [Buf(), Buf()]
        ssh = MG.alloc([128, 32], F32)
        b_ssh = [Buf() for _ in range(4)]
        junkH = MG.alloc([128, 256], BF16)
        b_junkH = Buf()
        b_S = [Buf() for _ in range(4)]
        b_SbA = [Buf() for _ in range(4)]
        b_SbB = [Buf() for _ in range(4)]

        stgG = Rot(stage.slots + sil)
        load_w(lambda kc, c0, cn: wG[:, kc, 512 + c0:512 + c0 + cn], w_in, 0, 8, 512, 512, 0, b_wGk, stgG)
        load_w(lambda kc, c0, cn: wG[:, kc, 3072 + c0:3072 + c0 + cn], w_in, 0, 8, 3072, 16, 0, b_wGa, stgG)
        load_w(lambda kc, c0, cn: wG[:, kc, 1024 + c0:1024 + c0 + cn], w_in, 0, 8, 1024, 1024, 0, b_wGv, stgG)
        load_w(lambda kc, c0, cn: wG[:, kc, c0:c0 + cn], w_in, 0, 8, 0, 512, 0, b_wGq, stgG)
        load_w(lambda kc, c0, cn: wG[:, kc, 2048 + c0:2048 + c0 + cn], w_in, 0, 8, 2048, 1024, 0, b_wGr, stgG)
        s = stage.next()
        DMA(s.ap[0:32, 0:512], wa2_d, [], [s.buf], s.sem)
        POOL("tensor_copy", [s.buf], [b_wa2], out=wa2, in_=s.ap[0:32, 0:512])
        for hh in range(4):
            DVE("tensor_copy", [b_const], [b_const], out=CM4[:, hh, :], in_=CM)
        for i in range(NS1):
            POOL("memset", [], [haT[i].buf], ap=haT[i].ap, constant=1.0)
        for i in range(NS2):
            POOL("memset", [], [qdA[i].buf], ap=qdA[i].ap, constant=0.0)
            POOL("memset", [], [qdB[i].buf], ap=qdB[i].ap, constant=0.0)
        DVE("memset", [], b_S, ap=Sst, constant=0.0)
        POOL("memset", [], b_SbA, ap=SbA, constant=0.0)

        gl = {}

        def gla_stage1(t):
            own = t >= 48
            i1_ = t % NS1
            i = t % 2
            j = t % NS2
            xs = load_x(t * 128)
            ht = hTt[i1_]
            norm_to_hT(xs.ap, [xs.buf], ht.ap, [ht.buf])
            kps = PS()
            PE([(kps.ap, ht.ap[:, kc, :], wG[:, kc, 512:1024], kc == 0, kc == 7) for kc in range(8)], [ht.buf] + b_wGk, [kps.buf])
            hps = PS()
            PE([(hps.ap[0:16, 0:128], wG[:, kc, 3072:3088], ht.ap[:, kc, :], kc == 0, kc == 7) for kc in range(8)],
               [ht.buf] + b_wGa, [hps.buf])
            ACT(haT[i1_].ap[0:16, :], hps.ap[0:16, 0:128], AF.Copy, [hps.buf], [haT[i1_].buf])
            zps = PS()
            PE([(zps.ap, haT[i1_].ap, wa2, True, True)], [haT[i1_].buf, b_wa2], [zps.buf])
            ACT(sps[i].ap, zps.ap, AF.Exp, [zps.buf], [sps[i].buf], scale=-1.0)
            ACT(sps[i].ap, sps[i].ap, AF.Ln, [sps[i].buf], [sps[i].buf], bias=1.0)
            vp = [PS(), PS()]
            for h2 in range(2):
                PE([(vp[h2].ap, ht.ap[:, kc, :], wG[:, kc, 1024 + h2 * 512:1536 + h2 * 512], kc == 0, kc == 7) for kc in range(8)],
                   [ht.buf] + b_wGv, [vp[h2].buf])
            ACT(vbs[j].ap[:, 0:512], vp[0].ap, AF.Copy, [vp[0].buf], [vbs[j].buf])
            DVE("tensor_copy", [vp[1].buf], [vbs[j].buf], out=vbs[j].ap[:, 512:1024], in_=vp[1].ap)
            dps = PS()
            PE([(dps.ap, UT, sps[i].ap, True, True)], [sps[i].buf, b_const], [dps.buf])
            ACT(wex[i].ap, dps.ap, AF.Exp, [dps.buf], [wex[i].buf])
            DVE("tensor_tensor", [kps.buf, wex[i].buf], [kouts[j].buf], out=kouts[j].ap, in0=kps.ap, in1=wex[i].ap, op=ALU.mult)
            nps = PS()
            n4 = nps.ap.rearrange("p (h q) -> p h q", h=4)
            PE([(n4[:, hh, :], sps[i].ap[:, hh * 128:(hh + 1) * 128], LT, True, True) for hh in range(4)],
               [sps[i].buf, b_const], [nps.buf])
            ACT(e1s[j].ap, n4, AF.Exp, [nps.buf], [e1s[j].buf], scale=-1.0)
            if not own:
                return
            ACT(e2s[i].ap, n4, AF.Exp, [nps.buf], [e2s[i].buf])
            qps = PS()
            q4 = qps.ap.rearrange("p (h q) -> p h q", h=4)
            PE([(q4[:, hh, :], wG[:, kc, hh * 128:(hh + 1) * 128], ht.ap[:, kc, :], kc == 0, kc == 7)
                for hh in range(4) for kc in range(8)], [ht.buf] + b_wGq, [qps.buf])
            DVE("scalar_tensor_tensor", [qps.buf, e1s[j].buf], [qdA[j].buf], out=qdA[j].ap[:, :, 0:64], in0=q4[:, :, 0:64],
                scalar=128.0 ** -0.5, in1=e1s[j].ap[:, :, 0:64], op0=ALU.mult, op1=ALU.mult)
            DVE("scalar_tensor_tensor", [qps.buf, e1s[j].buf], [qdB[j].buf], out=qdB[j].ap[:, :, 64:128], in0=q4[:, :, 64:128],
                scalar=128.0 ** -0.5, in1=e1s[j].ap[:, :, 64:128], op0=ALU.mult, op1=ALU.mult)
            ktp = PS()
            k4 = ktp.ap.rearrange("p (h q) -> p h q", h=4)
            PE([(k4[:, hh, :], wG[:, kc, 512 + hh * 128:512 + (hh + 1) * 128], ht.ap[:, kc, :], kc == 0, kc == 7)
                for hh in range(4) for kc in range(8)], [ht.buf] + b_wGk, [ktp.buf])
            DVE("tensor_tensor", [ktp.buf, e2s[i].buf], [kinT[i].buf], out=kinT[i].ap, in0=k4, in1=e2s[i].ap, op=ALU.mult)
            aps = PS()
            a4 = aps.ap.rearrange("p (h q) -> p h q", h=4)
            mms = []
            for hh in range(4):
                mms.append((a4[:, hh, 0:64], kinT[i].ap[:, hh, :], qdA[j].ap[:, hh, 0:64], True, True))
                mms.append((a4[:, hh, 64:128], kinT[i].ap[:, hh, :], qdB[j].ap[:, hh, 64:128], True, True))
            PE(mms, [kinT[i].buf, qdA[j].buf, qdB[j].buf], [aps.buf])
            DVE("tensor_tensor", [aps.buf, b_const], [attnT[j].buf], out=attnT[j].ap, in0=a4, in1=CM4, op=ALU.mult)
            rp = [PS(), PS()]
            for h2 in range(2):
                hs = slice(h2 * 512, (h2 + 1) * 512)
                PE([(rp[h2].ap, ht.ap[:, kc, :], wG[:, kc, 2048 + h2 * 512:2560 + h2 * 512], kc == 0, kc == 7) for kc in range(8)],
                   [ht.buf] + b_wGr, [rp[h2].buf])
                ACT(sil[j].ap[:, hs], rp[h2].ap, AF.Exp, [rp[h2].buf], [sil[j].buf], scale=-1.0)
                ACT(sil[j].ap[:, hs], sil[j].ap[:, hs], AF.Ln, [sil[j].buf], [sil[j].buf], bias=1.0)
                ACT(sil[j].ap[:, hs], sil[j].ap[:, hs], AF.Exp, [sil[j].buf], [sil[j].buf], scale=-1.0)
                DVE("tensor_tensor", [rp[h2].buf, sil[j].buf], [sil[j].buf], out=sil[j].ap[:, hs],
                    in0=rp[h2].ap, in1=sil[j].ap[:, hs], op=ALU.mult)

        def gla_stage2(t):
            own = t >= 48
            j = t % NS2
            tt = t - 48
            last_prefix = (t == 47)
            ko = kouts[j]
            vb = vbs[j]
            e1 = e1s[j]
            if own:
                oP = [PS(), PS()]
                for pr in range(2):
                    mms = []
                    for hq in range(2):
                        hh = pr * 2 + hq
                        mms.append((oP[pr].ap[:, hq * 256:(hq + 1) * 256], attnT[j].ap[:, hh, :], vb.ap[:, hh * 256:(hh + 1) * 256],
                                    hq == 0, False, True))
                    for hq in range(2):
                        hh = pr * 2 + hq
                        mms.append((oP[pr].ap[:, hq * 256:(hq + 1) * 256], qdA[j].ap[:, hh, :], SbA[:, hh, :], False, False, True))
                    PE(mms, [attnT[j].buf, vb.buf, qdA[j].buf] + b_SbA[pr * 2:pr * 2 + 2], [oP[pr].buf])
            for c in range(2):
                uP = [PS(), PS()]
                for pr in range(2):
                    PE([(uP[pr].ap[:, hq * 256:(hq + 1) * 256], ko.ap[c * 64:(c + 1) * 64, (pr * 2 + hq) * 128:(pr * 2 + hq + 1) * 128],
                         vb.ap[c * 64:(c + 1) * 64, (pr * 2 + hq) * 256:(pr * 2 + hq + 1) * 256], True, True) for hq in range(2)],
                       [ko.buf, vb.buf], [uP[pr].buf])
                for hh in range(4):
                    pr, hq = hh // 2, hh % 2
                    DVE("scalar_tensor_tensor", [uP[pr].buf, e1.buf, b_S[hh]], [b_S[hh]], out=Sst[:, hh, :], in0=Sst[:, hh, :],
                        scalar=e1.ap[:, hh, c * 64 + 63:c * 64 + 64], in1=uP[pr].ap[:, hq * 256:(hq + 1) * 256],
                        op0=ALU.mult, op1=ALU.add)
                    if c == 0 and own:
                        ACT(SbB[:, hh, :], Sst[:, hh, :], AF.Copy, [b_S[hh]], [b_SbB[hh]])
                    if c == 1 and (own or last_prefix):
                        ACT(SbA[:, hh, :], Sst[:, hh, :], AF.Copy, [b_S[hh]], [b_SbA[hh]])
                if c == 0 and own:
                    for pr in range(2):
                        PE([(oP[pr].ap[:, hq * 256:(hq + 1) * 256], qdB[j].ap[:, pr * 2 + hq, :], SbB[:, pr * 2 + hq, :],
                             False, hq == 1, True) for hq in range(2)],
                           [qdB[j].buf] + b_SbB[pr * 2:pr * 2 + 2], [oP[pr].buf])
            if not own:
                return
            og = ogbs.next()
            for pr in range(2):
                bssh = b_ssh[(t % 2) * 2 + pr]
                sc = (t % 2) * 16 + pr * 8
                for hq in range(2):
                    ACT(junkH, oP[pr].ap[:, hq * 256:(hq + 1) * 256], AF.Square, [oP[pr].buf], [b_junkH, bssh],
                        accum_out=ssh[:, sc + hq:sc + hq + 1])
                POOL("tensor_scalar", [bssh], [bssh], out=ssh[:, sc + 2:sc + 4], in0=ssh[:, sc:sc + 2], scalar1=1.0 / 256, scalar2=EPS,
                     op0=ALU.mult, op1=ALU.add)
                POOL("tensor_tensor", [bssh, b_const], [bssh], out=ssh[:, sc + 2:sc + 4], in0=ssh[:, sc + 2:sc + 4], in1=mhalf4[:, 0:2],
                     op=ALU.pow)
                for hq in range(2):
                    hh = pr * 2 + hq
                    DVE("scalar_tensor_tensor", [oP[pr].buf, bssh, sil[j].buf], [og.bufs[pr]], out=og.ap[:, hh * 256:(hh + 1) * 256],
                        in0=oP[pr].ap[:, hq * 256:(hq + 1) * 256], scalar=ssh[:, sc + 2 + hq:sc + 3 + hq],
                        in1=sil[j].ap[:, hh * 256:(hh + 1) * 256], op0=ALU.mult, op1=ALU.mult)

            def fn(e, og=og):
                ins = None
                for kc in range(8):
                    ins = e.transpose(ptr.ap[:, kc, :], og.ap[:, kc * 128:(kc + 1) * 128], identb)
                return ins
            P.op("pe", fn, og.bufs + [b_const], [ptr.buf], cost=650.0)
            ACT(ogT[:, :, tt * 128:(tt + 1) * 128], ptr.ap, AF.Copy, [ptr.buf], [b_ogT[tt]])

        gla_stage1(0)
        for t in range(64):
            if t + 1 < 64:
                gla_stage1(t + 1)
            gla_stage2(t)
        P.barrier()

        hTo2 = carve(108 * KB, [128, 8, 2048], BF16)
        b_hTo2 = [Buf() for _ in range(16)]
        for t in range(16):
            s = load_x(6144 + t * 128)
            norm_to_hT(s.ap, [s.buf], hTo2[:, :, t * 128:(t + 1) * 128], [b_hTo2[t]])
        MM = Mem(140 * KB, 180 * KB)
        wsets = []
        for _ in range(2):
            wsets.append(dict(
                wog=MM.alloc([128, 8, 256], BF16), woa=MM.alloc([128, 4, 256], BF16),
                wgA=MM.alloc([128, 8, 256], BF16), wgB=MM.alloc([128, 8, 256], BF16),
                b_wog=[Buf() for _ in range(8)], b_woa=[Buf() for _ in range(4)],
                b_wgA=[Buf() for _ in range(8)], b_wgB=[Buf() for _ in range(8)]))
        sgs = Rot([Slot(MM.alloc([128, 512], F32)) for _ in range(4)])
        tms = Rot([Slot(MM.alloc([128, 512], F32)) for _ in range(2)])
        wout = carve(180 * KB, [128, 8, 1024], BF16)
        b_wout = [Buf() for _ in range(8)]
        for fo in range(4):
            ws = wsets[fo % 2]
            wog, woa, wgA, wgB = ws["wog"], ws["woa"], ws["wgA"], ws["wgB"]
            b_wog, b_woa, b_wgA, b_wgB = ws["b_wog"], ws["b_woa"], ws["b_wgA"], ws["b_wgB"]
            load_w(lambda kc, c0, cn: wgA[:, kc, c0:c0 + cn], w_in, 0, 8, 7696 + fo * 256, 256, 0, b_wgA)
            load_w(lambda kc, c0, cn: wgB[:, kc, c0:c0 + cn], w_in, 0, 8, 8720 + fo * 256, 256, 0, b_wgB)
            load_w(lambda kc, c0, cn: wog[:, kc, c0:c0 + cn], wog_d, 0, 8, fo * 256, 256, 3, b_wog)
            load_w(lambda kc, c0, cn: woa[:, kc, c0:c0 + cn], woa_d, 0, 4, fo * 256, 256, None, b_woa)
            if fo == 1:
                load_w(lambda kc, c0, cn: wout[:, kc, c0:c0 + cn], wout_d, 0, 8, 0, 1024, None, b_wout)
            for tb in range(4):
                tk = slice(tb * 512, (tb + 1) * 512)
                for fc in range(2):
                    fs = slice(fc * 128, (fc + 1) * 128)
                    ga = PS()
                    PE([(ga.ap, wgA[:, kc, fs], hTo2[:, kc, tk], kc == 0, kc == 7) for kc in range(8)],
                       b_wgA + b_hTo2[tb * 4:tb * 4 + 4], [ga.buf])
                    sa = sgs.next()
                    ACT(sa.ap, ga.ap, AF.Sigmoid, [ga.buf], [sa.buf])
                    gb = PS()
                    PE([(gb.ap, wgB[:, kc, fs], hTo2[:, kc, tk], kc == 0, kc == 7) for kc in range(8)],
                       b_wgB + b_hTo2[tb * 4:tb * 4 + 4], [gb.buf])
                    sb_ = sgs.next()
                    ACT(sb_.ap, gb.ap, AF.Sigmoid, [gb.buf], [sb_.buf])
                    yg = PS()
                    PE([(yg.ap, wog[:, kc, fs], ogT[:, kc, tk], kc == 0, kc == 7) for kc in range(8)],
                       b_wog + b_ogT[tb * 4:tb * 4 + 4], [yg.buf])
                    t1 = tms.next()
                    DVE("tensor_tensor", [yg.buf, sa.buf], [t1.buf], out=t1.ap, in0=yg.ap, in1=sa.ap, op=ALU.mult)
                    ya = PS()
                    PE([(ya.ap, woa[:, hh, fs], OhT[:, hh, tk], hh == 0, hh == 3) for hh in range(4)],
                       b_woa + [b_OhT], [ya.buf])
                    t2 = tms.next()
                    DVE("tensor_tensor", [ya.buf, sb_.buf], [t2.buf], out=t2.ap, in0=ya.ap, in1=sb_.ap, op=ALU.mult)
                    DVE("tensor_tensor", [t1.buf, t2.buf], [b_mixT[tb]], out=mixT[:, fo * 2 + fc, tk], in0=t1.ap, in1=t2.ap, op=ALU.add)
        P.barrier()

        w1c = [carve(60 * KB, [128, 8, 512], BF16), carve(76 * KB, [128, 8, 512], BF16)]
        w2c = [carve(68 * KB, [128, 4, 1024], BF16), carve(84 * KB, [128, 4, 1024], BF16)]
        b_w1c = [[Buf() for _ in range(8)] for _ in range(2)]
        b_w2c = [[Buf() for _ in range(4)] for _ in range(2)]

        def load_ff(ffg):
            wi = ffg % 2
            w1 = w1c[wi]
            w2 = w2c[wi]
            load_w(lambda kc, c0, cn: w1[:, kc, c0:c0 + cn], w1_d, 0, 8, ffg * 512, 512, 1, b_w1c[wi])
            load_w(lambda kc, c0, cn: w2[:, kc, c0:c0 + cn], w2_d, ffg * 512, 4, 0, 1024, None, b_w2c[wi])
        load_ff(0)
        for t in range(16):
            s = load_x(6144 + t * 128)
            for h2 in range(2):
                ps = PS()
                PE([(ps.ap, mixT[:, kc, t * 128:(t + 1) * 128], wout[:, kc, h2 * 512:(h2 + 1) * 512], kc == 0, kc == 7) for kc in range(8)],
                   b_mixT + b_wout, [ps.buf])
                DVE("tensor_tensor", [ps.buf, s.buf], [b_x1[t]], out=x1[:, t, h2 * 512:(h2 + 1) * 512], in0=ps.ap,
                    in1=s.ap[:, h2 * 512:(h2 + 1) * 512], op=ALU.add)
            norm_to_hT(x1[:, t, :], [b_x1[t]], h2T[:, :, t * 128:(t + 1) * 128], [b_h2T[t]])
        P.barrier()

        MF = Mem(92 * KB, 108 * KB)
        uTs = Rot([Slot(MF.alloc([128, 4, 512], BF16)) for _ in range(2)])
        rls = Rot([Slot(MF.alloc([128, 512], F32)) for _ in range(2)])
        wpg = carve(172 * KB, [128, 8, 1024], BF16)
        wpp = carve(188 * KB, [128, 2, 1024], BF16)
        lnfb = carve(192 * KB, [128, 1024], F32)
        b_wpg = [Buf() for _ in range(8)]
        b_wpp = [Buf() for _ in range(2)]
        b_lnf = Buf()
        for ffg in range(8):
            wi = ffg % 2
            w1 = w1c[wi]
            w2 = w2c[wi]
            if ffg > 0:
                load_ff(ffg)
            if ffg == 2:
                load_w(lambda kc, c0, cn: wpg[:, kc, c0:c0 + cn], wpg_d, 0, 8, 0, 1024, 2, b_wpg)
                load_w(lambda kc, c0, cn: wpp[:, kc, c0:c0 + cn], wpp_d, 0, 2, 0, 1024, None, b_wpp)
                dl = P.dma_sem("lnf")
                DMA(lnfb, lnf_d.partition_broadcast(128), [], [b_lnf], dl)
            for tb in range(4):
                tk = slice(tb * 512, (tb + 1) * 512)
                ut = uTs.next()
                for j in range(4):
                    ps = PS()
                    PE([(ps.ap, w1[:, kc, j * 128:(j + 1) * 128], h2T[:, kc, tk], kc == 0, kc == 7) for kc in range(8)],
                       b_w1c[wi] + b_h2T[tb * 4:tb * 4 + 4], [ps.buf])
                    rl = rls.next()
                    ACT(rl.ap, ps.ap, AF.Relu, [ps.buf], [rl.buf])
                    DVE("tensor_tensor", [rl.buf], [ut.buf], out=ut.ap[:, j, :], in0=rl.ap, in1=rl.ap, op=ALU.mult)
                for tt in range(4):
                    t = tb * 4 + tt
                    for h2 in range(2):
                        ps = PS()
                        PE([(ps.ap, ut.ap[:, j, tt * 128:(tt + 1) * 128], w2[:, j, h2 * 512:(h2 + 1) * 512], j == 0, j == 3) for j in range(4)],
                           [ut.buf] + b_w2c[wi], [ps.buf])
                        DVE("tensor_tensor", [ps.buf, b_x1[t]], [b_x1[t]], out=x1[:, t, h2 * 512:(h2 + 1) * 512],
                            in0=x1[:, t, h2 * 512:(h2 + 1) * 512], in1=ps.ap, op=ALU.add)
        P.barrier()

        MP = Mem(28 * KB, 108 * KB)
        junkP = MP.alloc([128, 1024], BF16)
        b_junkP = Buf()
        h3s = Rot([Slot(MP.alloc([128, 8, 128], BF16)) for _ in range(2)])
        pfs = Rot([Slot(MP.alloc([128, 256], F32), P.dma_sem("pf%d" % i)) for i in range(2)])
        pbs = Rot([Slot(MP.alloc([128, 256], BF16)) for _ in range(2)])
        pTs = Rot([Slot(MP.alloc([128, 2, 128], BF16)) for _ in range(2)])
        sg2 = Rot([Slot(MP.alloc([128, 1024], F32)) for _ in range(2)])
        osb = Rot([Slot(MP.alloc([128, 1024], F32), P.dma_sem("os%d" % i)) for i in range(2)])
        out_toks = []
        for t in range(16):
            h3 = h3s.next()
            norm_to_hT(x1[:, t, :], [b_x1[t]], h3.ap, [h3.buf])
            pf = pfs.next()
            DMA(pf.ap, pin[t * 128:(t + 1) * 128, :], [], [pf.buf], pf.sem)
            pb = pbs.next()
            DVE("tensor_copy", [pf.buf], [pb.buf], out=pb.ap, in_=pf.ap)

            def fn(e, pb=pb):
                ins = None
                for c in range(2):
                    ins = e.transpose(ptr.ap[:, c, :], pb.ap[:, c * 128:(c + 1) * 128], identb)
                return ins
            P.op("pe", fn, [pb.buf, b_const], [ptr.buf], cost=200.0)
            pT = pTs.next()
            ACT(pT.ap, ptr.ap[:, 0:2, :], AF.Copy, [ptr.buf], [pT.buf])
            sg = sg2.next()
            for h2 in range(2):
                hs = slice(h2 * 512, (h2 + 1) * 512)
                gp = PS()
                PE([(gp.ap, h3.ap[:, kc, :], wpg[:, kc, hs], kc == 0, kc == 7) for kc in range(8)], [h3.buf] + b_wpg, [gp.buf])
                ACT(sg.ap[:, hs], gp.ap, AF.Sigmoid, [gp.buf], [sg.buf])
                pp = PS()
                PE([(pp.ap, pT.ap[:, c, :], wpp[:, c, hs], c == 0, c == 1) for c in range(2)], [pT.buf] + b_wpp, [pp.buf])
                DVE("tensor_tensor", [pp.buf, sg.buf], [sg.buf], out=sg.ap[:, hs], in0=sg.ap[:, hs], in1=pp.ap, op=ALU.mult)
            DVE("tensor_tensor", [sg.buf, b_x1[t]], [b_x1[t]], out=x1[:, t, :], in0=x1[:, t, :], in1=sg.ap, op=ALU.add)
            ob = osb.next()
            rs, brs = rstd_of(x1[:, t, :], [b_x1[t]], 1024, junkP, b_junkP)
            DVE("scalar_tensor_tensor", [b_x1[t], brs, b_lnf], [ob.buf], out=ob.ap, in0=x1[:, t, :], scalar=rs, in1=lnfb,
                op0=ALU.mult, op1=ALU.mult)
            out_toks.append(DMA(y[t * 128:(t + 1) * 128, :], ob.ap, [ob.buf], [], ob.sem))
        P.final_wait("sp", out_toks[-2:])
        P.run(block)
    return nc


def _t5_bucket(n):
    max_exact = 16
    nf = np.maximum(n, 1).astype(np.float32)
    large = max_exact + (np.log(nf / max_exact) / np.log(2048 / max_exact) * (32 - max_exact)).astype(np.int32)
    large = np.minimum(large, 31)
    return np.where(n < max_exact, n, large).astype(np.int32)


def _const_mats():
    m = np.arange(128)[:, None]
    t = np.arange(128)[None, :]
    same = (m // 64) == (t // 64)
    cm = np.zeros((128, 6, 128), np.float32)
    cm[:, 0, :] = np.eye(128)
    cm[:, 1, :] = np.where(same & (m <= t), 1.0 / 16, 0.0)
    cm[:, 2, :] = np.where(same & (m > t), -1.0 / 16, 0.0)
    cm[:, 3, :] = np.where(same & (m <= t), 1.0, 0.0)
    cm[:, 4, 0:16] = np.eye(4, dtype=np.float32).reshape(16)[None, :]
    sel4 = np.zeros((4, 4, 128), np.float32)
    for hh in range(4):
        sel4[hh, hh, :] = 1.0
    return cm, sel4


def _bias_layout(rel_bias):
    k = np.arange(128)[:, None, None]
    j = np.arange(2)[None, :, None]
    q = np.arange(128)[None, None, :]
    delta = q - k + 128 * (1 - j)
    valid = (delta >= 0) & (delta <= 128)
    out = np.full((128, 3, 2, 2, 2, 128), NEGM, np.float32)
    for g, dil in enumerate((1, 4, 16)):
        bucket = _t5_bucket(np.maximum(delta, 0) * dil)
        for hp in range(2):
            for hh in range(2):
                tab = rel_bias[:, g * 4 + hp * 2 + hh]
                vals = tab[bucket]
                out[:, g, hp, hh] = np.where(valid, vals, NEGM)
    return out.reshape(128, 3, 2, 512)


_PROG = None


def kernel(x, p, ln1, w_in, w_a2, b_a, gla_gn, w_o_gla, w_o_attn, w_out, ln2, w_mlp1, w_mlp2, ln3, w_pp, w_pg,
           rel_bias, ln_f):
    global _PROG
    f = lambda a: np.ascontiguousarray(np.asarray(a, dtype=np.float32))
    x = f(x); p = f(p)
    cm, sel4 = _const_mats()
    cols = np.stack([f(ln1)[0], f(ln2)[0], f(ln3)[0], f(gla_gn)[0]]).reshape(4, 8, 128).transpose(2, 0, 1).reshape(128, 32)
    wa2aug = np.zeros((32, 512), np.float32)
    wa2aug[0:16] = f(w_a2)[0]
    wa2aug[16] = f(b_a)[0]
    shared = {
        "w_in": f(w_in)[0], "w_a2aug": wa2aug, "w_o_gla": f(w_o_gla)[0], "w_o_attn": f(w_o_attn)[0],
        "w_out": f(w_out)[0], "w_mlp1": f(w_mlp1)[0], "w_mlp2": f(w_mlp2)[0], "w_pp": f(w_pp)[0], "w_pg": f(w_pg)[0],
        "cols": np.ascontiguousarray(cols), "ln_f": f(ln_f), "biasm": _bias_layout(f(rel_bias)),
        "cmat": cm, "sel4": sel4,
    }
    in_maps = []
    for c in range(NCORES):
        b, j = c // 4, c % 4
        xe = np.zeros((8192, 1024), np.float32)
        n = SEG * (j + 1)
        xe[8192 - n:] = x[b, 0:n]
        m = dict(shared)
        m["xe"] = xe
        m["p"] = np.ascontiguousarray(p[0, b, j * SEG:(j + 1) * SEG])
        m["hoff"] = np.full((128, 1), NEGM if j == 0 else 0.0, np.float32)
        in_maps.append(m)
    if _PROG is None:
        _PROG = build_program()
    res = run_bass_kernel_spmd(_PROG, in_maps, core_ids=list(range(NCORES)))
    out = np.zeros((2, 8192, 1024), np.float32)
    for c in range(NCORES):
        b, j = c // 4, c % 4
        out[b, j * SEG:(j + 1) * SEG] = res.results[c]["y"]
    return out
```

```python
import contextlib
import numpy as np
import concourse.bass as bass
import concourse.mybir as mybir
from concourse.bass_utils import run_bass_kernel_spmd

F32 = mybir.dt.float32
BF16 = mybir.dt.bfloat16
ALU = mybir.AluOpType
AF = mybir.ActivationFunctionType

SAFE_SAME = True
EPS = 1e-6
NCORES = 8
SEG = 2048
NEGM = -30000.0


class Buf:
    __slots__ = ("w", "r")

    def __init__(self):
        self.w = None
        self.r = []


class Prog:
    ENGS = ("pe", "act", "dve", "pool", "sp")
    WINDOW = 100
    LAT = 300.0
    SLACK = 500.0
    LAT_DMA = 200.0

    def __init__(self, nc, stack):
        self.nc = nc
        self.stack = stack
        self.ops = []
        self.phase = 0
        self.sems = {}
        self.all_dsems = []
        self.final = []
        for e in ("pe", "act", "dve", "pool"):
            self.sems[e] = stack.enter_context(nc.semaphore("s_" + e))

    def dma_sem(self, name):
        s = self.stack.enter_context(self.nc.semaphore("d_" + name))
        d = [s, 0]
        self.all_dsems.append(d)
        return d

    def op(self, eng, fn, reads=(), writes=(), dsem=None, cost=500.0, fin=None):
        idx = len(self.ops)
        deps = set()
        for b in reads:
            if b.w is not None:
                deps.add(b.w)
        for b in writes:
            if b.w is not None:
                deps.add(b.w)
            deps.update(b.r)
        self.ops.append([eng, fn, sorted(deps), dsem, cost, self.phase, cost if fin is None else fin])
        for b in reads:
            b.r.append(idx)
        for b in writes:
            b.w = idx
            b.r = []
        return idx

    def barrier(self):
        self.phase += 1

    def final_wait(self, eng, toks):
        self.final.append((eng, list(toks)))

    def schedule(self):
        ops = self.ops
        n = len(ops)
        succ = [[] for _ in range(n)]
        for i, o in enumerate(ops):
            for d in o[2]:
                if ops[d][5] == o[5]:
                    succ[d].append(i)
        bl = [0.0] * n
        for i in range(n - 1, -1, -1):
            m = 0.0
            for j in succ[i]:
                v = bl[j] + (0.0 if ops[j][0] == ops[i][0] else self.LAT)
                if v > m:
                    m = v
            bl[i] = ops[i][6] + m
        order = {e: [] for e in self.ENGS}
        finish = {}
        tnow = 0.0
        for ph in range(self.phase + 1):
            pend = {e: [] for e in self.ENGS}
            for i, o in enumerate(ops):
                if o[5] == ph:
                    pend[o[0]].append(i)
            tfree = {e: tnow for e in self.ENGS}
            remaining = sum(len(v) for v in pend.values())
            cand = {e: None for e in self.ENGS}
            dirty = set(self.ENGS)
            while remaining:
                for e in list(dirty):
                    cl = []
                    for i in pend[e][:self.WINDOW]:
                        o = ops[i]
                        st = tfree[e]
                        ok = True
                        for d in o[2]:
                            f = finish.get(d)
                            if f is None:
                                ok = False
                                break
                            lat = self.LAT_DMA if ops[d][3] is not None else (0.0 if ops[d][0] == e else self.LAT)
                            if f + lat > st:
                                st = f + lat
                        if ok:
                            cl.append((st, i))
                    if not cl:
                        cand[e] = None
                    else:
                        tmin = min(c[0] for c in cl)
                        lim = tmin + self.SLACK
                        best = None
                        for st, i in cl:
                            if st <= lim:
                                key = (-bl[i], i)
                                if best is None or key < best[0]:
                                    best = (key, st, i)
                        cand[e] = (best[1], best[2])
                dirty.clear()
                pick = None
                for e in self.ENGS:
                    c = cand[e]
                    if c is not None and (pick is None or c < pick[0]):
                        pick = (c, e)
                assert pick is not None, "scheduler stuck"
                (st, i), e = pick
                o = ops[i]
                tfree[e] = st + o[4]
                finish[i] = st + o[6]
                pend[e].remove(i)
                order[e].append(i)
                remaining -= 1
                dirty.update(self.ENGS)
            tnow = max(tfree.values())
            self.phase_ends = getattr(self, 'phase_ends', []) + [tnow]
            for e in self.ENGS:
                order[e].append(None)
        self.est_total = tnow
        return order

    def lower(self):
        order = self.schedule()
        ops = self.ops
        tok = {}
        cnt = {e: 0 for e in ("pe", "act", "dve", "pool")}
        bar_cnt = []
        nph = self.phase + 1
        pos = {e: 0 for e in self.ENGS}
        dcount = {id(d): 0 for d in self.all_dsems}
        bar_state = []
        for ph in range(nph):
            for e in self.ENGS:
                lst = order[e]
                while lst[pos[e]] is not None:
                    i = lst[pos[e]]
                    o = ops[i]
                    if o[3] is not None:
                        dcount[id(o[3])] += 16
                        tok[i] = (o[3][0], dcount[id(o[3])], e, True)
                    else:
                        cnt[e] += 1
                        tok[i] = (self.sems[e], cnt[e], e, False)
                    pos[e] += 1
                pos[e] += 1
            bar_state.append((dict(cnt), dict(dcount)))
        streams = {e: [] for e in self.ENGS}
        for e in self.ENGS:
            waited = {}
            ph = 0
            for i in order[e]:
                if i is None:
                    c, dc = bar_state[ph]
                    waits = []
                    for e2 in ("pe", "act", "dve", "pool"):
                        if e2 != e and c[e2] > waited.get(id(self.sems[e2]), 0):
                            waits.append((self.sems[e2], c[e2]))
                            waited[id(self.sems[e2])] = c[e2]
                    for d in self.all_dsems:
                        v = dc[id(d)]
                        if v > waited.get(id(d[0]), 0):
                            waits.append((d[0], v))
                            waited[id(d[0])] = v
                    if waits and ph < nph - 1:
                        streams[e].append((waits, None, None))
                    ph += 1
                    continue
                o = ops[i]
                waits = {}
                for d in o[2]:
                    s, v, e2, isdma = tok[d]
                    if e2 == e and not isdma:
                        if e in ("pe", "sp") or not SAFE_SAME:
                            continue
                    k = id(s)
                    if waited.get(k, 0) >= v:
                        continue
                    if k not in waits or waits[k][1] < v:
                        waits[k] = (s, v)
                for k, (s, v) in waits.items():
                    waited[k] = v
                t = tok[i]
                streams[e].append((list(waits.values()), o[1], (t[0], 16 if t[3] else 1)))
        for eng, toks in self.final:
            streams[eng].append(([(tok[t][0], tok[t][1]) for t in toks], None, None))
        self.streams = streams

    def run(self, block):
        self.lower()

        def play(name):
            def _f(e):
                for waits, fn, inc in self.streams[name]:
                    for s, v in waits:
                        e.wait_ge(s, v)
                    if fn is None:
                        continue
                    ins = fn(e)
                    if inc is not None:
                        ins.then_inc(inc[0], inc[1])
            return _f
        block.tensor(play("pe"))
        block.scalar(play("act"))
        block.vector(play("dve"))
        block.gpsimd(play("pool"))
        block.sync(play("sp"))


class Slot:
    def __init__(self, ap, sem=None):
        self.ap = ap
        self.buf = Buf()
        self.sem = sem


class Rot:
    def __init__(self, slots):
        self.slots = slots
        self.i = 0

    def next(self):
        s = self.slots[self.i % len(self.slots)]
        self.i += 1
        return s


def build_program():
    nc = bass.Bass("TRN2", target_bir_lowering=False)

    def din(name, shape):
        return nc.dram_tensor(name, shape, F32, kind="ExternalInput").ap()

    xe = din("xe", [8192, 1024])
    pin = din("p", [SEG, 256])
    w_in = din("w_in", [1024, 9744])
    wa2_d = din("w_a2aug", [32, 512])
    wog_d = din("w_o_gla", [1024, 1024])
    woa_d = din("w_o_attn", [512, 1024])
    wout_d = din("w_out", [1024, 1024])
    w1_d = din("w_mlp1", [1024, 4096])
    w2_d = din("w_mlp2", [4096, 1024])
    wpp_d = din("w_pp", [256, 1024])
    wpg_d = din("w_pg", [1024, 1024])
    cols_d = din("cols", [128, 32])
    lnf_d = din("ln_f", [1024])
    biasm_d = din("biasm", [128, 3, 2, 512])
    hoff_d = din("hoff", [128, 1])
    cmat_d = din("cmat", [128, 6, 128])
    sel4_d = din("sel4", [4, 4, 128])
    y = nc.dram_tensor("y", [SEG, 1024], F32, kind="ExternalOutput").ap()

    with contextlib.ExitStack() as st:
        P = Prog(nc, st)
        ARENA_BYTES = 200 * 1024
        arena = st.enter_context(nc.sbuf_tensor("arena", [128, ARENA_BYTES // 2], BF16))
        psb = [st.enter_context(nc.psum_tensor("psb%d" % i, [128, 512], F32)) for i in range(7)]
        ptr_t = st.enter_context(nc.psum_tensor("ptr", [128, 8, 128], BF16))
        block = st.enter_context(nc.Block())

        def carve(off, shape, dt):
            n = 1
            for s in shape[1:]:
                n *= s
            es = 2 if dt == BF16 else 4
            assert off % 4 == 0 and off + n * es <= ARENA_BYTES, (off, shape)
            a = arena[0:shape[0], off // 2: off // 2 + n * es // 2]
            if dt == F32:
                a = a.bitcast(F32)
            if len(shape) == 3:
                a = a.rearrange("p (a b) -> p a b", a=shape[1])
            elif len(shape) == 4:
                a = a.rearrange("p (a b c) -> p a b c", a=shape[1], b=shape[2])
            return a

        class Mem:
            def __init__(self, base, limit):
                self.off = base
                self.limit = limit

            def alloc(self, shape, dt):
                n = 1
                for s in shape[1:]:
                    n *= s
                es = 2 if dt == BF16 else 4
                a = carve(self.off, shape, dt)
                self.off += (n * es + 63) // 64 * 64
                assert self.off <= self.limit, (self.off, self.limit)
                return a

        KB = 1024

        def sl(start, n, step):
            return slice(start, start + step * (n - 1) + 1, step)

        def fsz(ap):
            n = 1
            for x in ap.shape[1:]:
                n *= x
            return n

        def PE(mms, reads, writes):
            cost = 0.0
            for m in mms:
                n = max(fsz(m[2]), 64)
                c = n / 2.4 + 25.0
                if m[1].dtype == F32:
                    c *= 4
                cost += c

            def fn(e):
                ins = None
                for m in mms:
                    kw = dict(start=m[3], stop=m[4])
                    if len(m) > 5 and m[5]:
                        kw["skip_group_check"] = True
                    ins = e.matmul(m[0], lhsT=m[1], rhs=m[2], **kw)
                return ins
            return P.op("pe", fn, reads, writes, cost=cost)

        def ACT(out, in_, func, reads, writes, **kw):
            return P.op("act", lambda e: e.activation(out=out, in_=in_, func=func, **kw), reads, writes,
                        cost=180.0 + 0.83 * fsz(out))

        def ENG(eng, method, reads, writes, **kw):
            o = kw.get("out", kw.get("ap"))
            n = fsz(o)
            cost = (100.0 + 1.15 * n) if eng == "dve" else (250.0 + 0.6 * n)
            return P.op(eng, lambda e: getattr(e, method)(**kw), reads, writes, cost=cost)

        def DVE(method, reads, writes, **kw):
            return ENG("dve", method, reads, writes, **kw)

        def POOL(method, reads, writes, **kw):
            return ENG("pool", method, reads, writes, **kw)

        def DMA(out, in_, reads, writes, dsem):
            nbytes = out.shape[0] * fsz(out) * 4
            return P.op("sp", lambda e: e.dma_start(out=out, in_=in_), reads, writes, dsem=dsem, cost=120.0, fin=2000.0 + nbytes / 150.0)

        psrot = Rot([Slot(t[:]) for t in psb])
        ptr = Slot(ptr_t[:])

        def PS():
            return psrot.next()

        G = Mem(0, 28 * KB)
        cmat = G.alloc([128, 6, 128], F32)
        identb = G.alloc([128, 128], BF16)
        onesel4 = G.alloc([128, 4, 4], BF16)
        sel4 = G.alloc([4, 4, 128], F32)
        colsT = G.alloc([128, 32], F32)
        hoff = G.alloc([128, 1], F32)
        mhalf4 = G.alloc([128, 4], F32)
        mhalf = mhalf4[:, 0:1]
        statv = G.alloc([128, 64], F32)
        junk = G.alloc([128, 1024], BF16)
        stage = Rot([Slot(G.alloc([128, 1024], F32), P.dma_sem("st%d" % i)) for i in range(2)])
        xts = Rot([Slot(G.alloc([128, 1024], F32), P.dma_sem("xt%d" % i)) for i in range(2)])
        hbs = Rot([Slot(G.alloc([128, 1024], BF16)) for i in range(2)])
        LT = cmat[:, 1, :]
        UT = cmat[:, 2, :]
        CM = cmat[:, 3, :]
        b_const = Buf()
        b_junk = Buf()
        stat_i = [0]

        b_stat = [Buf() for _ in range(64)]

        def stat_col():
            c = stat_i[0] % 64
            stat_i[0] += 1
            return statv[:, c:c + 1], b_stat[c]

        dc = P.dma_sem("const")
        DMA(cmat, cmat_d, [], [b_const], dc)
        DMA(sel4, sel4_d, [], [b_const], dc)
        DMA(colsT, cols_d, [], [b_const], dc)
        DMA(hoff, hoff_d, [], [b_const], dc)
        DVE("tensor_copy", [b_const], [b_const], out=identb, in_=cmat[:, 0, :])
        DVE("tensor_copy", [b_const], [b_const], out=onesel4, in_=cmat[:, 4, 0:16].rearrange("p (a b) -> p a b", a=4))
        POOL("memset", [], [b_const], ap=mhalf4, constant=-0.5)

        def col(i, kc):
            return colsT[:, i * 8 + kc: i * 8 + kc + 1]

        cast_rr = [0]

        def cast_w(dst, src, sc, reads, writes):
            k = (cast_rr[0] % 2) * 2
            cast_rr[0] += 1
            if k == 0:
                POOL("tensor_scalar", reads, writes, out=dst, in0=src, scalar1=sc, scalar2=1.0, op0=ALU.mult, op1=ALU.mult)
            elif k == 1:
                ACT(dst, src, AF.Copy, reads, writes, scale=sc)
            else:
                DVE("tensor_scalar", reads, writes, out=dst, in0=src, scalar1=sc, scalar2=None, op0=ALU.mult)

        def load_w(dst_fn, dram, row0, nk, col0, ncols, scale_i, dst_bufs, stg=None):
            stg = stg or stage
            for kc in range(nk):
                for c0 in range(0, ncols, 1024):
                    cn = min(1024, ncols - c0)
                    s = stg.next()
                    DMA(s.ap[:, 0:cn], dram[row0 + kc * 128: row0 + (kc + 1) * 128, col0 + c0: col0 + c0 + cn],
                        [], [s.buf], s.sem)
                    sc = col(scale_i, kc) if scale_i is not None else 1.0
                    cast_w(dst_fn(kc, c0, cn), s.ap[:, 0:cn], sc, [s.buf, b_const], [dst_bufs[kc]])

        def rstd_of(src_ap, src_bufs, n, scr_ap, scr_buf):
            ss, bss = stat_col()
            ACT(scr_ap, src_ap, AF.Square, src_bufs, [scr_buf, bss], accum_out=ss)
            vv, bvv = stat_col()
            POOL("tensor_scalar", [bss], [bvv], out=vv, in0=ss, scalar1=1.0 / n, scalar2=EPS, op0=ALU.mult, op1=ALU.add)
            rs, brs = stat_col()
            POOL("tensor_tensor", [bvv, b_const], [brs], out=rs, in0=vv, in1=mhalf, op=ALU.pow)
            return rs, brs

        def norm_to_hT(src_ap, src_bufs, hT_out, hT_bufs):
            hb = hbs.next()
            rs, brs = rstd_of(src_ap, src_bufs, 1024, junk, b_junk)
            DVE("tensor_scalar", list(src_bufs) + [brs], [hb.buf], out=hb.ap, in0=src_ap, scalar1=rs, scalar2=None,
                op0=ALU.mult)

            def fn(e):
                ins = None
                for kc in range(8):
                    ins = e.transpose(ptr.ap[:, kc, :], hb.ap[:, kc * 128:(kc + 1) * 128], identb)
                return ins
            P.op("pe", fn, [hb.buf, b_const], [ptr.buf], cost=650.0)
            ACT(hT_out, ptr.ap, AF.Copy, [ptr.buf], hT_bufs)

        def load_x(row0, step=1):
            s = xts.next()
            DMA(s.ap, xe[sl(row0, 128, step), :], [], [s.buf], s.sem)
            return s

        OhT = carve(28 * KB, [128, 4, 2048], BF16)
        b_OhT = Buf()
        ogT = carve(44 * KB, [128, 8, 2048], BF16)
        b_ogT = [Buf() for _ in range(16)]
        mixT = carve(76 * KB, [128, 8, 2048], BF16)
        b_mixT = [Buf() for _ in range(4)]
        x1 = carve(108 * KB, [128, 16, 1024], F32)
        b_x1 = [Buf() for _ in range(16)]
        h2T = carve(28 * KB, [128, 8, 2048], BF16)
        b_h2T = [Buf() for _ in range(16)]

        hTh = carve(44 * KB, [128, 8, 2048], BF16)
        hTo = carve(76 * KB, [128, 8, 2048], BF16)
        b_hTh = [Buf() for _ in range(16)]
        b_hTo = [Buf() for _ in range(16)]
        MA = Mem(108 * KB, 200 * KB)
        NT = MA.alloc([128, 4, 2048], F32)
        ST = MA.alloc([4, 2048], F32)
        biasT = carve(28 * KB, [128, 512], F32)
        expbN = carve(30 * KB, [128, 512], F32)
        expbH = carve(32 * KB, [128, 512], F32)
        wA = [MA.alloc([128, 8, 3, 256], BF16) for _ in range(2)]
        b_wA = [[Buf() for _ in range(8)] for _ in range(2)]
        KTs = Rot([Slot(MA.alloc([128, 2, 512], BF16)) for _ in range(3)])
        QTs = Rot([Slot(MA.alloc([128, 2, 512], BF16)) for _ in range(2)])
        Vs = Rot([Slot(MA.alloc([128, 4, 256], BF16)) for _ in range(3)])
        Efs = Rot([Slot(MA.alloc([128, 512], F32)) for _ in range(2)])
        ETs = Rot([Slot(MA.alloc([128, 512], BF16)) for _ in range(2)])
        b_bias = Buf()
        b_exp = Buf()
        scale_att = 128.0 ** -0.5
        dbias = P.dma_sem("bias")
        b_NTq = [[Buf() for _ in range(4)] for _ in range(2)]
        b_STq = [Buf() for _ in range(4)]
        DVE("memset", [], [b for l in b_NTq for b in l], ap=NT, constant=0.0)
        DVE("memset", [], b_STq, ap=ST, constant=0.0)
        wslot = 0

        def proj_seg(hTs, hbufs, t0, nb, wa, bwa, want_q):
            cols = slice(t0 * 128, (t0 + nb) * 128)
            hb_ = hbufs[t0:t0 + nb]
            kt = KTs.next()
            for hh in range(2):
                ps = PS()
                PE([(ps.ap[:, 0:nb * 128], wa[:, kc, 1, hh * 128:(hh + 1) * 128], hTs[:, kc, cols], kc == 0, kc == 7)
                    for kc in range(8)], hb_ + bwa, [ps.buf])
                ACT(kt.ap[:, hh, 0:nb * 128], ps.ap[:, 0:nb * 128], AF.Copy, [ps.buf], [kt.buf])
            qt = None
            if want_q:
                qt = QTs.next()
                for hh in range(2):
                    ps = PS()
                    PE([(ps.ap[:, 0:nb * 128], wa[:, kc, 0, hh * 128:(hh + 1) * 128], hTs[:, kc, cols], kc == 0, kc == 7)
                        for kc in range(8)], hb_ + bwa, [ps.buf])
                    DVE("tensor_copy", [ps.buf], [qt.buf], out=qt.ap[:, hh, 0:nb * 128], in_=ps.ap[:, 0:nb * 128])
            vs = Vs.next()
            for b in range(nb):
                bc = slice((t0 + b) * 128, (t0 + b + 1) * 128)
                ps = PS()
                PE([(ps.ap[:, 0:256], hTs[:, kc, bc], wa[:, kc, 2, :], kc == 0, kc == 7) for kc in range(8)],
                   [hbufs[t0 + b]] + bwa, [ps.buf])
                if b % 2 == 0:
                    ACT(vs.ap[:, b, :], ps.ap[:, 0:256], AF.Copy, [ps.buf], [vs.buf])
                else:
                    DVE("tensor_copy", [ps.buf], [vs.buf], out=vs.ap[:, b, :], in_=ps.ap[:, 0:256])
            return kt, qt, vs

        def attend(g, hp, prevb, curb, qt, qb, first, nat, quarters):
            blk = [prevb, curb]
            sp_ = PS()
            s4 = sp_.ap.rearrange("p (h j q) -> p h j q", h=2, j=2)
            PE([(s4[:, hh, j, :], blk[j][0].ap[:, hh, blk[j][2] * 128:(blk[j][2] + 1) * 128],
                 qt.ap[:, hh, qb * 128:(qb + 1) * 128], True, True) for hh in range(2) for j in range(2)],
               [prevb[0].buf, curb[0].buf, qt.buf], [sp_.buf])
            ef = Efs.next()
            ACT(ef.ap, sp_.ap, AF.Exp, [sp_.buf], [ef.buf], scale=scale_att)
            et = ETs.next()
            DVE("tensor_tensor", [ef.buf, b_exp], [et.buf], out=et.ap, in0=ef.ap,
                in1=(expbH if first else expbN), op=ALU.mult)
            e4t = et.ap.rearrange("p (h j q) -> p h j q", h=2, j=2)
            np_ = PS()
            mms = []
            for hh in range(2):
                for j in range(2):
                    mms.append((np_.ap[:, hh * 128:(hh + 1) * 128],
                                blk[j][1].ap[:, blk[j][2], hh * 128:(hh + 1) * 128], e4t[:, hh, j, :], j == 0, j == 1))
            k = 0
            for hh in range(2):
                for j in range(2):
                    mms.append((np_.ap[0:4, 256:384], onesel4[:, hp * 2 + hh, :], e4t[:, hh, j, :], k == 0, k == 3))
                    k += 1
            PE(mms, [prevb[1].buf, curb[1].buf, et.buf, b_const], [np_.buf])
            nb_ = [b_NTq[hp][q] for q in quarters]
            DVE("tensor_tensor", [np_.buf] + nb_, nb_, out=NT[:, hp * 2:hp * 2 + 2, nat], in0=NT[:, hp * 2:hp * 2 + 2, nat],
                in1=np_.ap[:, 0:256].rearrange("p (h q) -> p h q", h=2), op=ALU.add)
            sb_ = [b_STq[q] for q in quarters]
            DVE("tensor_tensor", [np_.buf] + sb_, sb_, out=ST[:, nat], in0=ST[:, nat],
                in1=np_.ap[0:4, 256:384], op=ALU.add)

        for g in range(3):
            dil = (1, 4, 16)[g]
            nbk = 16 // dil
            for k in range(dil):
                s = load_x(6144 - 128 * dil + k, dil)
                norm_to_hT(s.ap, [s.buf], hTh[:, :, k * 128:(k + 1) * 128], [b_hTh[k]])
            for k in range(16):
                r, n = divmod(k, nbk)
                s = load_x(6144 + r + dil * 128 * n, dil)
                norm_to_hT(s.ap, [s.buf], hTo[:, :, k * 128:(k + 1) * 128], [b_hTo[k]])
            for hp in range(2):
                DMA(biasT, biasm_d[:, g, hp, :], [], [b_bias], dbias)
                ACT(expbN, biasT, AF.Exp, [b_bias], [b_exp])
                ACT(expbH, biasT, AF.Exp, [b_bias], [b_exp])
                b4 = biasT.rearrange("p (h j q) -> p h j q", h=2, j=2)
                e4 = expbH.rearrange("p (h j q) -> p h j q", h=2, j=2)
                ACT(e4[:, :, 0, :], b4[:, :, 0, :], AF.Exp, [b_bias, b_const], [b_exp], bias=hoff)
                wa = wA[wslot % 2]
                bwa = b_wA[wslot % 2]
                wslot += 1
                base = 3088 + g * 1536
                for kc in range(8):
                    s = stage.next()
                    src = w_in[kc * 128:(kc + 1) * 128, base:base + 1536].rearrange("p (c x) -> p c x", c=3)[:, :, hp * 256:(hp + 1) * 256]
                    sv = s.ap[:, 0:768].rearrange("p (c x) -> p c x", c=3)
                    DMA(sv, src, [], [s.buf], s.sem)
                    POOL("tensor_scalar", [s.buf, b_const], [bwa[kc]], out=wa[:, kc, :, :], in0=sv,
                         scalar1=col(0, kc), scalar2=1.0, op0=ALU.mult, op1=ALU.mult)
                if g < 2:
                    for r in range(dil):
                        kt_p, _, vs_p = proj_seg(hTh, b_hTh, r, 1, wa, bwa, False)
                        prev = (kt_p, vs_p, 0)
                        for n0 in range(0, nbk, 4):
                            nb = min(4, nbk - n0)
                            kt, qt, vs = proj_seg(hTo, b_hTo, r * nbk + n0, nb, wa, bwa, True)
                            for b in range(nb):
                                cur = (kt, vs, b)
                                n = n0 + b
                                tok0 = r + dil * 128 * n
                                quarters = [tok0 // 512] if g == 0 else [n]
                                attend(g, hp, prev, cur, qt, b, (n == 0), sl(tok0, 128, dil), quarters)
                                prev = cur
                else:
                    for q4 in range(4):
                        kt_h, _, vs_h = proj_seg(hTh, b_hTh, q4 * 4, 4, wa, bwa, False)
                        kt, qt, vs = proj_seg(hTo, b_hTo, q4 * 4, 4, wa, bwa, True)
                        for b in range(4):
                            r = q4 * 4 + b
                            attend(g, hp, (kt_h, vs_h, b), (kt, vs, b), qt, b, True, sl(r, 128, 16), [0, 1, 2, 3])
        DVE("reciprocal", b_STq, b_STq, out=ST, in_=ST)
        for hh in range(4):
            for tb in range(4):
                ps = PS()
                PE([(ps.ap, sel4[:, hh, :], ST[:, tb * 512:(tb + 1) * 512], True, True)], b_STq + [b_const], [ps.buf])
                DVE("tensor_tensor", [ps.buf, b_NTq[hh // 2][tb]], [b_OhT], out=OhT[:, hh, tb * 512:(tb + 1) * 512],
                    in0=NT[:, hh, tb * 512:(tb + 1) * 512], in1=ps.ap, op=ALU.mult)
        P.barrier()

        MG = Mem(76 * KB, 200 * KB)
        wG = MG.alloc([128, 8, 3088], BF16)
        b_wGq = [Buf() for _ in range(8)]
        b_wGk = [Buf() for _ in range(8)]
        b_wGv = [Buf() for _ in range(8)]
        b_wGr = [Buf() for _ in range(8)]
        b_wGa = [Buf() for _ in range(8)]
        wa2 = MG.alloc([32, 512], BF16)
        b_wa2 = Buf()
        CM4 = MG.alloc([128, 4, 128], F32)
        NS1 = 3
        NS2 = 3
        hTt = [Slot(MG.alloc([128, 8, 128], BF16)) for _ in range(NS1)]
        haT = [Slot(MG.alloc([32, 128], BF16)) for _ in range(NS1)]
        sps = [Slot(MG.alloc([128, 512], F32)) for _ in range(1)]
        sph = [Slot(MG.alloc([128, 512], BF16)) for _ in range(2)]
        spl = [Slot(MG.alloc([128, 512], BF16)) for _ in range(2)]
        cb = MG.alloc([128, 3, 128], BF16)
        ones16b = MG.alloc([128, 2], BF16)
        b_cb = Buf()
        LTb, UTb, UT128b = cb[:, 0, :], cb[:, 1, :], cb[:, 2, :]
        wex = [Slot(MG.alloc([128, 512], F32)) for _ in range(2)]
        kouts = [Slot(MG.alloc([128, 512], BF16)) for _ in range(NS2)]
        vbs = [Slot(MG.alloc([128, 1024], BF16)) for _ in range(NS2)]
        e1s = [Slot(MG.alloc([128, 4, 128], F32)) for _ in range(NS2)]
        e2s = [Slot(MG.alloc([128, 4, 128], F32)) for _ in range(2)]
        qdA = [Slot(MG.alloc([128, 4, 128], BF16)) for _ in range(NS2)]
        qdB = [Slot(MG.alloc([128, 4, 128], BF16)) for _ in range(NS2)]
        kinT = [Slot(MG.alloc([128, 4, 128], BF16)) for _ in range(2)]
        attnT = [Slot(MG.alloc([128, 4, 128], BF16)) for _ in range(NS2)]
        sil = [Slot(MG.alloc([128, 1024], F32), P.dma_sem("sil%d" % i)) for i in range(NS2)]
        Sst = MG.alloc([128, 4, 256], F32)
        SbA = MG.alloc([128, 4, 256], BF16)
        SbB = MG.alloc([128, 4, 256], BF16)
        ogbs = Rot([Slot(MG.alloc([128, 1024], BF16)) for _ in range(2)])
        for o_ in ogbs.slots:
            o_.bufs = [Buf(), Buf()]
        ssh = MG.alloc([128, 32], F32)
        b_ssh = [Buf() for _ in range(4)]
        junkH = MG.alloc([128, 256], BF16)
        b_junkH = Buf()
        b_S = [Buf() for _ in range(4)]
        b_SbA = [Buf() for _ in range(4)]
        b_SbB = [Buf() for _ in range(4)]

        stgG = Rot(stage.slots + sil)
        load_w(lambda kc, c0, cn: wG[:, kc, 512 + c0:512 + c0 + cn], w_in, 0, 8, 512, 512, 0, b_wGk, stgG)
        load_w(lambda kc, c0, cn: wG[:, kc, 3072 + c0:3072 + c0 + cn], w_in, 0, 8, 3072, 16, 0, b_wGa, stgG)
        load_w(lambda kc, c0, cn: wG[:, kc, 1024 + c0:1024 + c0 + cn], w_in, 0, 8, 1024, 1024, 0, b_wGv, stgG)
        load_w(lambda kc, c0, cn: wG[:, kc, c0:c0 + cn], w_in, 0, 8, 0, 512, 0, b_wGq, stgG)
        load_w(lambda kc, c0, cn: wG[:, kc, 2048 + c0:2048 + c0 + cn], w_in, 0, 8, 2048, 1024, 0, b_wGr, stgG)
        s = stage.next()
        DMA(s.ap[0:32, 0:512], wa2_d, [], [s.buf], s.sem)
        POOL("tensor_copy", [s.buf], [b_wa2], out=wa2, in_=s.ap[0:32, 0:512])
        for hh in range(4):
            DVE("tensor_copy", [b_const], [b_const], out=CM4[:, hh, :], in_=CM)
        DVE("tensor_copy", [b_const], [b_cb], out=cb[:, 0, :], in_=cmat[:, 1, :])
        DVE("tensor_copy", [b_const], [b_cb], out=cb[:, 1, :], in_=cmat[:, 2, :])
        DVE("tensor_copy", [b_const], [b_cb], out=cb[:, 2, :], in_=cmat[:, 5, :])
        POOL("memset", [], [b_cb], ap=ones16b, constant=0.0625)
        for i in range(NS1):
            POOL("memset", [], [haT[i].buf], ap=haT[i].ap, constant=1.0)
        for i in range(NS2):
            POOL("memset", [], [qdA[i].buf], ap=qdA[i].ap, constant=0.0)
            POOL("memset", [], [qdB[i].buf], ap=qdB[i].ap, constant=0.0)
        DVE("memset", [], b_S, ap=Sst, constant=0.0)
        POOL("memset", [], b_SbA, ap=SbA, constant=0.0)

        gl = {}

        def gla_stage1(t):
            own = t >= 48
            i1_ = t % NS1
            i = t % 2
            j = t % NS2
            xs = load_x(t * 128)
            ht = hTt[i1_]
            norm_to_hT(xs.ap, [xs.buf], ht.ap, [ht.buf])
            kps = PS()
            PE([(kps.ap, ht.ap[:, kc, :], wG[:, kc, 512:1024], kc == 0, kc == 7) for kc in range(8)], [ht.buf] + b_wGk, [kps.buf])
            hps = PS()
            PE([(hps.ap[0:16, 0:128], wG[:, kc, 3072:3088], ht.ap[:, kc, :], kc == 0, kc == 7) for kc in range(8)],
               [ht.buf] + b_wGa, [hps.buf])
            ACT(haT[i1_].ap[0:16, :], hps.ap[0:16, 0:128], AF.Copy, [hps.buf], [haT[i1_].buf])
            zps = PS()
            PE([(zps.ap, haT[i1_].ap, wa2, True, True)], [haT[i1_].buf, b_wa2], [zps.buf])
            ACT(sps[0].ap, zps.ap, AF.Exp, [zps.buf], [sps[0].buf], scale=-1.0)
            ACT(sps[0].ap, sps[0].ap, AF.Ln, [sps[0].buf], [sps[0].buf], bias=1.0)
            ACT(sph[i].ap, sps[0].ap, AF.Copy, [sps[0].buf], [sph[i].buf])
            DVE("tensor_tensor", [sps[0].buf, sph[i].buf], [spl[i].buf], out=spl[i].ap, in0=sps[0].ap, in1=sph[i].ap,
                op=ALU.subtract)
            rs_ = [sph[i].buf, spl[i].buf, b_cb]
            vp = [PS(), PS()]
            for h2 in range(2):
                PE([(vp[h2].ap, ht.ap[:, kc, :], wG[:, kc, 1024 + h2 * 512:1536 + h2 * 512], kc == 0, kc == 7) for kc in range(8)],
                   [ht.buf] + b_wGv, [vp[h2].buf])
            ACT(vbs[j].ap[:, 0:512], vp[0].ap, AF.Copy, [vp[0].buf], [vbs[j].buf])
            DVE("tensor_copy", [vp[1].buf], [vbs[j].buf], out=vbs[j].ap[:, 512:1024], in_=vp[1].ap)
            dps = PS()
            Um = UTb if own else UT128b
            PE([(dps.ap, Um, sph[i].ap, True, False), (dps.ap, Um, spl[i].ap, False, True)], rs_, [dps.buf])
            ACT(wex[i].ap, dps.ap, AF.Exp, [dps.buf], [wex[i].buf])
            DVE("tensor_tensor", [kps.buf, wex[i].buf], [kouts[j].buf], out=kouts[j].ap, in0=kps.ap, in1=wex[i].ap, op=ALU.mult)
            nps = PS()
            n4 = nps.ap.rearrange("p (h q) -> p h q", h=4)
            if not own:
                PE([(nps.ap[:, hh:hh + 1], sp_[i].ap[:, hh * 128:(hh + 1) * 128], ones16b[:, 0:1], k == 0, k == 1)
                    for hh in range(4) for k, sp_ in enumerate((sph, spl))], rs_, [nps.buf])
                ACT(e1s[j].ap[:, :, 127], nps.ap[:, 0:4], AF.Exp, [nps.buf], [e1s[j].buf], scale=-1.0)
                return
            PE([(n4[:, hh, :], sp_[i].ap[:, hh * 128:(hh + 1) * 128], LTb, k == 0, k == 1)
                for hh in range(4) for k, sp_ in enumerate((sph, spl))], rs_, [nps.buf])
            ACT(e1s[j].ap, n4, AF.Exp, [nps.buf], [e1s[j].buf], scale=-1.0)
            ACT(e2s[i].ap, n4, AF.Exp, [nps.buf], [e2s[i].buf])
            qps = PS()
            q4 = qps.ap.rearrange("p (h q) -> p h q", h=4)
            PE([(q4[:, hh, :], wG[:, kc, hh * 128:(hh + 1) * 128], ht.ap[:, kc, :], kc == 0, kc == 7)
                for hh in range(4) for kc in range(8)], [ht.buf] + b_wGq, [qps.buf])
            DVE("scalar_tensor_tensor", [qps.buf, e1s[j].buf], [qdA[j].buf], out=qdA[j].ap[:, :, 0:64], in0=q4[:, :, 0:64],
                scalar=128.0 ** -0.5, in1=e1s[j].ap[:, :, 0:64], op0=ALU.mult, op1=ALU.mult)
            DVE("scalar_tensor_tensor", [qps.buf, e1s[j].buf], [qdB[j].buf], out=qdB[j].ap[:, :, 64:128], in0=q4[:, :, 64:128],
                scalar=128.0 ** -0.5, in1=e1s[j].ap[:, :, 64:128], op0=ALU.mult, op1=ALU.mult)
            ktp = PS()
            k4 = ktp.ap.rearrange("p (h q) -> p h q", h=4)
            PE([(k4[:, hh, :], wG[:, kc, 512 + hh * 128:512 + (hh + 1) * 128], ht.ap[:, kc, :], kc == 0, kc == 7)
                for hh in range(4) for kc in range(8)], [ht.buf] + b_wGk, [ktp.buf])
            DVE("tensor_tensor", [ktp.buf, e2s[i].buf], [kinT[i].buf], out=kinT[i].ap, in0=k4, in1=e2s[i].ap, op=ALU.mult)
            aps = PS()
            a4 = aps.ap.rearrange("p (h q) -> p h q", h=4)
            mms = []
            for hh in range(4):
                mms.append((a4[:, hh, 0:64], kinT[i].ap[:, hh, :], qdA[j].ap[:, hh, 0:64], True, True))
                mms.append((a4[:, hh, 64:128], kinT[i].ap[:, hh, :], qdB[j].ap[:, hh, 64:128], True, True))
            PE(mms, [kinT[i].buf, qdA[j].buf, qdB[j].buf], [aps.buf])
            DVE("tensor_tensor", [aps.buf, b_const], [attnT[j].buf], out=attnT[j].ap, in0=a4, in1=CM4, op=ALU.mult)
            rp = [PS(), PS()]
            for h2 in range(2):
                hs = slice(h2 * 512, (h2 + 1) * 512)
                PE([(rp[h2].ap, ht.ap[:, kc, :], wG[:, kc, 2048 + h2 * 512:2560 + h2 * 512], kc == 0, kc == 7) for kc in range(8)],
                   [ht.buf] + b_wGr, [rp[h2].buf])
                ACT(sil[j].ap[:, hs], rp[h2].ap, AF.Exp, [rp[h2].buf], [sil[j].buf], scale=-1.0)
                ACT(sil[j].ap[:, hs], sil[j].ap[:, hs], AF.Ln, [sil[j].buf], [sil[j].buf], bias=1.0)
                ACT(sil[j].ap[:, hs], sil[j].ap[:, hs], AF.Exp, [sil[j].buf], [sil[j].buf], scale=-1.0)
                DVE("tensor_tensor", [rp[h2].buf, sil[j].buf], [sil[j].buf], out=sil[j].ap[:, hs],
                    in0=rp[h2].ap, in1=sil[j].ap[:, hs], op=ALU.mult)

        def gla_stage2(t):
            own = t >= 48
            j = t % NS2
            tt = t - 48
            last_prefix = (t == 47)
            ko = kouts[j]
            vb = vbs[j]
            e1 = e1s[j]
            if own:
                oP = [PS(), PS()]
                for pr in range(2):
                    mms = []
                    for hq in range(2):
                        hh = pr * 2 + hq
                        mms.append((oP[pr].ap[:, hq * 256:(hq + 1) * 256], attnT[j].ap[:, hh, :], vb.ap[:, hh * 256:(hh + 1) * 256],
                                    hq == 0, False, True))
                    for hq in range(2):
                        hh = pr * 2 + hq
                        mms.append((oP[pr].ap[:, hq * 256:(hq + 1) * 256], qdA[j].ap[:, hh, :], SbA[:, hh, :], False, False, True))
                    PE(mms, [attnT[j].buf, vb.buf, qdA[j].buf] + b_SbA[pr * 2:pr * 2 + 2], [oP[pr].buf])
            for c in range(2 if own else 1):
                rows = slice(c * 64, (c + 1) * 64) if own else slice(0, 128)
                dcol = c * 64 + 63 if own else 127
                lastc = (c == 1) or not own
                uP = [PS(), PS()]
                for pr in range(2):
                    PE([(uP[pr].ap[:, hq * 256:(hq + 1) * 256], ko.ap[rows, (pr * 2 + hq) * 128:(pr * 2 + hq + 1) * 128],
                         vb.ap[rows, (pr * 2 + hq) * 256:(pr * 2 + hq + 1) * 256], True, True) for hq in range(2)],
                       [ko.buf, vb.buf], [uP[pr].buf])
                for hh in range(4):
                    pr, hq = hh // 2, hh % 2
                    DVE("scalar_tensor_tensor", [uP[pr].buf, e1.buf, b_S[hh]], [b_S[hh]], out=Sst[:, hh, :], in0=Sst[:, hh, :],
                        scalar=e1.ap[:, hh, dcol:dcol + 1], in1=uP[pr].ap[:, hq * 256:(hq + 1) * 256],
                        op0=ALU.mult, op1=ALU.add)
                    if c == 0 and own:
                        ACT(SbB[:, hh, :], Sst[:, hh, :], AF.Copy, [b_S[hh]], [b_SbB[hh]])
                    if lastc and (own or last_prefix):
                        ACT(SbA[:, hh, :], Sst[:, hh, :], AF.Copy, [b_S[hh]], [b_SbA[hh]])
                if c == 0 and own:
                    for pr in range(2):
                        PE([(oP[pr].ap[:, hq * 256:(hq + 1) * 256], qdB[j].ap[:, pr * 2 + hq, :], SbB[:, pr * 2 + hq, :],
                             False, hq == 1, True) for hq in range(2)],
                           [qdB[j].buf] + b_SbB[pr * 2:pr * 2 + 2], [oP[pr].buf])
            if not own:
                return
            og = ogbs.next()
            for pr in range(2):
                bssh = b_ssh[(t % 2) * 2 + pr]
                sc = (t % 2) * 16 + pr * 8
                for hq in range(2):
                    ACT(junkH, oP[pr].ap[:, hq * 256:(hq + 1) * 256], AF.Square, [oP[pr].buf], [b_junkH, bssh],
                        accum_out=ssh[:, sc + hq:sc + hq + 1])
                POOL("tensor_scalar", [bssh], [bssh], out=ssh[:, sc + 2:sc + 4], in0=ssh[:, sc:sc + 2], scalar1=1.0 / 256, scalar2=EPS,
                     op0=ALU.mult, op1=ALU.add)
                POOL("tensor_tensor", [bssh, b_const], [bssh], out=ssh[:, sc + 2:sc + 4], in0=ssh[:, sc + 2:sc + 4], in1=mhalf4[:, 0:2],
                     op=ALU.pow)
                for hq in range(2):
                    hh = pr * 2 + hq
                    DVE("scalar_tensor_tensor", [oP[pr].buf, bssh, sil[j].buf], [og.bufs[pr]], out=og.ap[:, hh * 256:(hh + 1) * 256],
                        in0=oP[pr].ap[:, hq * 256:(hq + 1) * 256], scalar=ssh[:, sc + 2 + hq:sc + 3 + hq],
                        in1=sil[j].ap[:, hh * 256:(hh + 1) * 256], op0=ALU.mult, op1=ALU.mult)

            def fn(e, og=og):
                ins = None
                for kc in range(8):
                    ins = e.transpose(ptr.ap[:, kc, :], og.ap[:, kc * 128:(kc + 1) * 128], identb)
                return ins
            P.op("pe", fn, og.bufs + [b_const], [ptr.buf], cost=650.0)
            ACT(ogT[:, :, tt * 128:(tt + 1) * 128], ptr.ap, AF.Copy, [ptr.buf], [b_ogT[tt]])

        gla_stage1(0)
        for t in range(64):
            if t + 1 < 64:
                gla_stage1(t + 1)
            gla_stage2(t)
        P.barrier()

        hTo2 = carve(108 * KB, [128, 8, 2048], BF16)
        b_hTo2 = [Buf() for _ in range(16)]
        for t in range(16):
            s = load_x(6144 + t * 128)
            norm_to_hT(s.ap, [s.buf], hTo2[:, :, t * 128:(t + 1) * 128], [b_hTo2[t]])
        MM = Mem(140 * KB, 180 * KB)
        wsets = []
        for _ in range(2):
            wsets.append(dict(
                wog=MM.alloc([128, 8, 256], BF16), woa=MM.alloc([128, 4, 256], BF16),
                wgA=MM.alloc([128, 8, 256], BF16), wgB=MM.alloc([128, 8, 256], BF16),
                b_wog=[Buf() for _ in range(8)], b_woa=[Buf() for _ in range(4)],
                b_wgA=[Buf() for _ in range(8)], b_wgB=[Buf() for _ in range(8)]))
        sgs = Rot([Slot(MM.alloc([128, 512], F32)) for _ in range(4)])
        tms = Rot([Slot(MM.alloc([128, 512], F32)) for _ in range(2)])
        wout = carve(180 * KB, [128, 8, 1024], BF16)
        b_wout = [Buf() for _ in range(8)]
        for fo in range(4):
            ws = wsets[fo % 2]
            wog, woa, wgA, wgB = ws["wog"], ws["woa"], ws["wgA"], ws["wgB"]
            b_wog, b_woa, b_wgA, b_wgB = ws["b_wog"], ws["b_woa"], ws["b_wgA"], ws["b_wgB"]
            load_w(lambda kc, c0, cn: wgA[:, kc, c0:c0 + cn], w_in, 0, 8, 7696 + fo * 256, 256, 0, b_wgA)
            load_w(lambda kc, c0, cn: wgB[:, kc, c0:c0 + cn], w_in, 0, 8, 8720 + fo * 256, 256, 0, b_wgB)
            load_w(lambda kc, c0, cn: wog[:, kc, c0:c0 + cn], wog_d, 0, 8, fo * 256, 256, 3, b_wog)
            load_w(lambda kc, c0, cn: woa[:, kc, c0:c0 + cn], woa_d, 0, 4, fo * 256, 256, None, b_woa)
            if fo == 1:
                load_w(lambda kc, c0, cn: wout[:, kc, c0:c0 + cn], wout_d, 0, 8, 0, 1024, None, b_wout)
            for tb in range(4):
                tk = slice(tb * 512, (tb + 1) * 512)
                for fc in range(2):
                    fs = slice(fc * 128, (fc + 1) * 128)
                    ga = PS()
                    PE([(ga.ap, wgA[:, kc, fs], hTo2[:, kc, tk], kc == 0, kc == 7) for kc in range(8)],
                       b_wgA + b_hTo2[tb * 4:tb * 4 + 4], [ga.buf])
                    sa = sgs.next()
                    ACT(sa.ap, ga.ap, AF.Sigmoid, [ga.buf], [sa.buf])
                    gb = PS()
                    PE([(gb.ap, wgB[:, kc, fs], hTo2[:, kc, tk], kc == 0, kc == 7) for kc in range(8)],
                       b_wgB + b_hTo2[tb * 4:tb * 4 + 4], [gb.buf])
                    sb_ = sgs.next()
                    ACT(sb_.ap, gb.ap, AF.Sigmoid, [gb.buf], [sb_.buf])
                    yg = PS()
                    PE([(yg.ap, wog[:, kc, fs], ogT[:, kc, tk], kc == 0, kc == 7) for kc in range(8)],
                       b_wog + b_ogT[tb * 4:tb * 4 + 4], [yg.buf])
                    t1 = tms.next()
                    DVE("tensor_tensor", [yg.buf, sa.buf], [t1.buf], out=t1.ap, in0=yg.ap, in1=sa.ap, op=ALU.mult)
                    ya = PS()
                    PE([(ya.ap, woa[:, hh, fs], OhT[:, hh, tk], hh == 0, hh == 3) for hh in range(4)],
                       b_woa + [b_OhT], [ya.buf])
                    t2 = tms.next()
                    DVE("tensor_tensor", [ya.buf, sb_.buf], [t2.buf], out=t2.ap, in0=ya.ap, in1=sb_.ap, op=ALU.mult)
                    DVE("tensor_tensor", [t1.buf, t2.buf], [b_mixT[tb]], out=mixT[:, fo * 2 + fc, tk], in0=t1.ap, in1=t2.ap, op=ALU.add)
        P.barrier()

        w1c = [carve(60 * KB, [128, 8, 512], BF16), carve(76 * KB, [128, 8, 512], BF16)]
        w2c = [carve(68 * KB, [128, 4, 1024], BF16), carve(84 * KB, [128, 4, 1024], BF16)]
        b_w1c = [[Buf() for _ in range(8)] for _ in range(2)]
        b_w2c = [[Buf() for _ in range(4)] for _ in range(2)]

        def load_ff(ffg):
            wi = ffg % 2
            w1 = w1c[wi]
            w2 = w2c[wi]
            load_w(lambda kc, c0, cn: w1[:, kc, c0:c0 + cn], w1_d, 0, 8, ffg * 512, 512, 1, b_w1c[wi])
            load_w(lambda kc, c0, cn: w2[:, kc, c0:c0 + cn], w2_d, ffg * 512, 4, 0, 1024, None, b_w2c[wi])
        load_ff(0)
        for t in range(16):
            s = load_x(6144 + t * 128)
            for h2 in range(2):
                ps = PS()
                PE([(ps.ap, mixT[:, kc, t * 128:(t + 1) * 128], wout[:, kc, h2 * 512:(h2 + 1) * 512], kc == 0, kc == 7) for kc in range(8)],
                   b_mixT + b_wout, [ps.buf])
                DVE("tensor_tensor", [ps.buf, s.buf], [b_x1[t]], out=x1[:, t, h2 * 512:(h2 + 1) * 512], in0=ps.ap,
                    in1=s.ap[:, h2 * 512:(h2 + 1) * 512], op=ALU.add)
            norm_to_hT(x1[:, t, :], [b_x1[t]], h2T[:, :, t * 128:(t + 1) * 128], [b_h2T[t]])
        P.barrier()

        MF = Mem(92 * KB, 108 * KB)
        uTs = Rot([Slot(MF.alloc([128, 4, 512], BF16)) for _ in range(2)])
        rls = Rot([Slot(MF.alloc([128, 512], F32)) for _ in range(2)])
        wpg = carve(172 * KB, [128, 8, 1024], BF16)
        wpp = carve(188 * KB, [128, 2, 1024], BF16)
        lnfb = carve(192 * KB, [128, 1024], F32)
        b_wpg = [Buf() for _ in range(8)]
        b_wpp = [Buf() for _ in range(2)]
        b_lnf = Buf()
        for ffg in range(8):
            wi = ffg % 2
            w1 = w1c[wi]
            w2 = w2c[wi]
            if ffg > 0:
                load_ff(ffg)
            if ffg == 2:
                load_w(lambda kc, c0, cn: wpg[:, kc, c0:c0 + cn], wpg_d, 0, 8, 0, 1024, 2, b_wpg)
                load_w(lambda kc, c0, cn: wpp[:, kc, c0:c0 + cn], wpp_d, 0, 2, 0, 1024, None, b_wpp)
                dl = P.dma_sem("lnf")
                DMA(lnfb, lnf_d.partition_broadcast(128), [], [b_lnf], dl)
            for tb in range(4):
                tk = slice(tb * 512, (tb + 1) * 512)
                ut = uTs.next()
                for j in range(4):
                    ps = PS()
                    PE([(ps.ap, w1[:, kc, j * 128:(j + 1) * 128], h2T[:, kc, tk], kc == 0, kc == 7) for kc in range(8)],
                       b_w1c[wi] + b_h2T[tb * 4:tb * 4 + 4], [ps.buf])
                    rl = rls.next()
                    ACT(rl.ap, ps.ap, AF.Relu, [ps.buf], [rl.buf])
                    DVE("tensor_tensor", [rl.buf], [ut.buf], out=ut.ap[:, j, :], in0=rl.ap, in1=rl.ap, op=ALU.mult)
                for tt in range(4):
                    t = tb * 4 + tt
                    for h2 in range(2):
                        ps = PS()
                        PE([(ps.ap, ut.ap[:, j, tt * 128:(tt + 1) * 128], w2[:, j, h2 * 512:(h2 + 1) * 512], j == 0, j == 3) for j in range(4)],
                           [ut.buf] + b_w2c[wi], [ps.buf])
                        DVE("tensor_tensor", [ps.buf, b_x1[t]], [b_x1[t]], out=x1[:, t, h2 * 512:(h2 + 1) * 512],
                            in0=x1[:, t, h2 * 512:(h2 + 1) * 512], in1=ps.ap, op=ALU.add)
        P.barrier()

        MP = Mem(28 * KB, 108 * KB)
        junkP = MP.alloc([128, 1024], BF16)
        b_junkP = Buf()
        h3s = Rot([Slot(MP.alloc([128, 8, 128], BF16)) for _ in range(2)])
        pfs = Rot([Slot(MP.alloc([128, 256], F32), P.dma_sem("pf%d" % i)) for i in range(2)])
        pbs = Rot([Slot(MP.alloc([128, 256], BF16)) for _ in range(2)])
        pTs = Rot([Slot(MP.alloc([128, 2, 128], BF16)) for _ in range(2)])
        sg2 = Rot([Slot(MP.alloc([128, 1024], F32)) for _ in range(2)])
        osb = Rot([Slot(MP.alloc([128, 1024], F32), P.dma_sem("os%d" % i)) for i in range(2)])
        out_toks = []
        for t in range(16):
            h3 = h3s.next()
            norm_to_hT(x1[:, t, :], [b_x1[t]], h3.ap, [h3.buf])
            pf = pfs.next()
            DMA(pf.ap, pin[t * 128:(t + 1) * 128, :], [], [pf.buf], pf.sem)
            pb = pbs.next()
            DVE("tensor_copy", [pf.buf], [pb.buf], out=pb.ap, in_=pf.ap)

            def fn(e, pb=pb):
                ins = None
                for c in range(2):
                    ins = e.transpose(ptr.ap[:, c, :], pb.ap[:, c * 128:(c + 1) * 128], identb)
                return ins
            P.op("pe", fn, [pb.buf, b_const], [ptr.buf], cost=200.0)
            pT = pTs.next()
            ACT(pT.ap, ptr.ap[:, 0:2, :], AF.Copy, [ptr.buf], [pT.buf])
            sg = sg2.next()
            for h2 in range(2):
                hs = slice(h2 * 512, (h2 + 1) * 512)
                gp = PS()
                PE([(gp.ap, h3.ap[:, kc, :], wpg[:, kc, hs], kc == 0, kc == 7) for kc in range(8)], [h3.buf] + b_wpg, [gp.buf])
                ACT(sg.ap[:, hs], gp.ap, AF.Sigmoid, [gp.buf], [sg.buf])
                pp = PS()
                PE([(pp.ap, pT.ap[:, c, :], wpp[:, c, hs], c == 0, c == 1) for c in range(2)], [pT.buf] + b_wpp, [pp.buf])
                DVE("tensor_tensor", [pp.buf, sg.buf], [sg.buf], out=sg.ap[:, hs], in0=sg.ap[:, hs], in1=pp.ap, op=ALU.mult)
            DVE("tensor_tensor", [sg.buf, b_x1[t]], [b_x1[t]], out=x1[:, t, :], in0=x1[:, t, :], in1=sg.ap, op=ALU.add)
            ob = osb.next()
            rs, brs = rstd_of(x1[:, t, :], [b_x1[t]], 1024, junkP, b_junkP)
            DVE("scalar_tensor_tensor", [b_x1[t], brs, b_lnf], [ob.buf], out=ob.ap, in0=x1[:, t, :], scalar=rs, in1=lnfb,
                op0=ALU.mult, op1=ALU.mult)
            out_toks.append(DMA(y[t * 128:(t + 1) * 128, :], ob.ap, [ob.buf], [], ob.sem))
        P.final_wait("sp", out_toks[-2:])
        P.run(block)
    return nc


def _t5_bucket(n):
    max_exact = 16
    nf = np.maximum(n, 1).astype(np.float32)
    large = max_exact + (np.log(nf / max_exact) / np.log(2048 / max_exact) * (32 - max_exact)).astype(np.int32)
    large = np.minimum(large, 31)
    return np.where(n < max_exact, n, large).astype(np.int32)


def _const_mats():
    m = np.arange(128)[:, None]
    t = np.arange(128)[None, :]
    same = (m // 64) == (t // 64)
    cm = np.zeros((128, 6, 128), np.float32)
    cm[:, 0, :] = np.eye(128)
    cm[:, 1, :] = np.where(same & (m <= t), 1.0 / 16, 0.0)
    cm[:, 2, :] = np.where(same & (m > t), -1.0 / 16, 0.0)
    cm[:, 3, :] = np.where(same & (m <= t), 1.0, 0.0)
    cm[:, 5, :] = np.where(m > t, -1.0 / 16, 0.0)
    cm[:, 4, 0:16] = np.eye(4, dtype=np.float32).reshape(16)[None, :]
    sel4 = np.zeros((4, 4, 128), np.float32)
    for hh in range(4):
        sel4[hh, hh, :] = 1.0
    return cm, sel4


def _bias_layout(rel_bias):
    k = np.arange(128)[:, None, None]
    j = np.arange(2)[None, :, None]
    q = np.arange(128)[None, None, :]
    delta = q - k + 128 * (1 - j)
    valid = (delta >= 0) & (delta <= 128)
    out = np.full((128, 3, 2, 2, 2, 128), NEGM, np.float32)
    for g, dil in enumerate((1, 4, 16)):
        bucket = _t5_bucket(np.maximum(delta, 0) * dil)
        for hp in range(2):
            for hh in range(2):
                tab = rel_bias[:, g * 4 + hp * 2 + hh]
                vals = tab[bucket]
                out[:, g, hp, hh] = np.where(valid, vals, NEGM)
    return out.reshape(128, 3, 2, 512)


_PROG = None


def kernel(x, p, ln1, w_in, w_a2, b_a, gla_gn, w_o_gla, w_o_attn, w_out, ln2, w_mlp1, w_mlp2, ln3, w_pp, w_pg,
           rel_bias, ln_f):
    global _PROG
    f = lambda a: np.ascontiguousarray(np.asarray(a, dtype=np.float32))
    x = f(x); p = f(p)
    cm, sel4 = _const_mats()
    cols = np.stack([f(ln1)[0], f(ln2)[0], f(ln3)[0], f(gla_gn)[0]]).reshape(4, 8, 128).transpose(2, 0, 1).reshape(128, 32)
    wa2aug = np.zeros((32, 512), np.float32)
    wa2aug[0:16] = f(w_a2)[0]
    wa2aug[16] = f(b_a)[0]
    shared = {
        "w_in": f(w_in)[0], "w_a2aug": wa2aug, "w_o_gla": f(w_o_gla)[0], "w_o_attn": f(w_o_attn)[0],
        "w_out": f(w_out)[0], "w_mlp1": f(w_mlp1)[0], "w_mlp2": f(w_mlp2)[0], "w_pp": f(w_pp)[0], "w_pg": f(w_pg)[0],
        "cols": np.ascontiguousarray(cols), "ln_f": f(ln_f), "biasm": _bias_layout(f(rel_bias)),
        "cmat": cm, "sel4": sel4,
    }
    in_maps = []
    for c in range(NCORES):
        b, j = c // 4, c % 4
        xe = np.zeros((8192, 1024), np.float32)
        n = SEG * (j + 1)
        xe[8192 - n:] = x[b, 0:n]
        m = dict(shared)
        m["xe"] = xe
        m["p"] = np.ascontiguousarray(p[0, b, j * SEG:(j + 1) * SEG])
        m["hoff"] = np.full((128, 1), NEGM if j == 0 else 0.0, np.float32)
        in_maps.append(m)
    if _PROG is None:
        _PROG = build_program()
    res = run_bass_kernel_spmd(_PROG, in_maps, core_ids=list(range(NCORES)))
    out = np.zeros((2, 8192, 1024), np.float32)
    for c in range(NCORES):
        b, j = c // 4, c % 4
        out[b, j * SEG:(j + 1) * SEG] = res.results[c]["y"]
    return out
```

```python
import contextlib
import numpy as np
import concourse.bass as bass
import concourse.mybir as mybir
from concourse.bass_utils import run_bass_kernel_spmd

F32 = mybir.dt.float32
BF16 = mybir.dt.bfloat16
ALU = mybir.AluOpType
AF = mybir.ActivationFunctionType

SAFE_SAME = True
EPS = 1e-6
NCORES = 8
SEG = 2048
NEGM = -30000.0


class Buf:
    __slots__ = ("w", "r")

    def __init__(self):
        self.w = None
        self.r = []


class Prog:
    ENGS = ("pe", "act", "dve", "pool", "sp")
    WINDOW = 100
    LAT = 300.0
    SLACK = 500.0
    LAT_DMA = 200.0

    def __init__(self, nc, stack):
        self.nc = nc
        self.stack = stack
        self.ops = []
        self.phase = 0
        self.sems = {}
        self.all_dsems = []
        self.final = []
        for e in ("pe", "act", "dve", "pool"):
            self.sems[e] = stack.enter_context(nc.semaphore("s_" + e))

    def dma_sem(self, name):
        s = self.stack.enter_context(self.nc.semaphore("d_" + name))
        d = [s, 0]
        self.all_dsems.append(d)
        return d

    def op(self, eng, fn, reads=(), writes=(), dsem=None, cost=500.0, fin=None):
        idx = len(self.ops)
        deps = set()
        for b in reads:
            if b.w is not None:
                deps.add(b.w)
        for b in writes:
            if b.w is not None:
                deps.add(b.w)
            deps.update(b.r)
        self.ops.append([eng, fn, sorted(deps), dsem, cost, self.phase, cost if fin is None else fin])
        for b in reads:
            b.r.append(idx)
        for b in writes:
            b.w = idx
            b.r = []
        return idx

    def barrier(self):
        self.phase += 1

    def final_wait(self, eng, toks):
        self.final.append((eng, list(toks)))

    def schedule(self):
        ops = self.ops
        n = len(ops)
        succ = [[] for _ in range(n)]
        for i, o in enumerate(ops):
            for d in o[2]:
                if ops[d][5] == o[5]:
                    succ[d].append(i)
        bl = [0.0] * n
        for i in range(n - 1, -1, -1):
            m = 0.0
            for j in succ[i]:
                v = bl[j] + (0.0 if ops[j][0] == ops[i][0] else self.LAT)
                if v > m:
                    m = v
            bl[i] = ops[i][6] + m
        order = {e: [] for e in self.ENGS}
        finish = {}
        tnow = 0.0
        for ph in range(self.phase + 1):
            pend = {e: [] for e in self.ENGS}
            for i, o in enumerate(ops):
                if o[5] == ph:
                    pend[o[0]].append(i)
            tfree = {e: tnow for e in self.ENGS}
            remaining = sum(len(v) for v in pend.values())
            cand = {e: None for e in self.ENGS}
            dirty = set(self.ENGS)
            while remaining:
                for e in list(dirty):
                    cl = []
                    for i in pend[e][:self.WINDOW]:
                        o = ops[i]
                        st = tfree[e]
                        ok = True
                        for d in o[2]:
                            f = finish.get(d)
                            if f is None:
                                ok = False
                                break
                            lat = self.LAT_DMA if ops[d][3] is not None else (0.0 if ops[d][0] == e else self.LAT)
                            if f + lat > st:
                                st = f + lat
                        if ok:
                            cl.append((st, i))
                    if not cl:
                        cand[e] = None
                    else:
                        tmin = min(c[0] for c in cl)
                        lim = tmin + self.SLACK
                        best = None
                        for st, i in cl:
                            if st <= lim:
                                key = (-bl[i], i)
                                if best is None or key < best[0]:
                                    best = (key, st, i)
                        cand[e] = (best[1], best[2])
                dirty.clear()
                pick = None
                for e in self.ENGS:
                    c = cand[e]
                    if c is not None and (pick is None or c < pick[0]):
                        pick = (c, e)
                assert pick is not None, "scheduler stuck"
                (st, i), e = pick
                o = ops[i]
                tfree[e] = st + o[4]
                finish[i] = st + o[6]
                pend[e].remove(i)
                order[e].append(i)
                remaining -= 1
                dirty.update(self.ENGS)
            tnow = max(tfree.values())
            self.phase_ends = getattr(self, 'phase_ends', []) + [tnow]
            for e in self.ENGS:
                order[e].append(None)
        self.est_total = tnow
        return order

    def lower(self):
        order = self.schedule()
        ops = self.ops
        tok = {}
        cnt = {e: 0 for e in ("pe", "act", "dve", "pool")}
        bar_cnt = []
        nph = self.phase + 1
        pos = {e: 0 for e in self.ENGS}
        dcount = {id(d): 0 for d in self.all_dsems}
        bar_state = []
        for ph in range(nph):
            for e in self.ENGS:
                lst = order[e]
                while lst[pos[e]] is not None:
                    i = lst[pos[e]]
                    o = ops[i]
                    if o[3] is not None:
                        dcount[id(o[3])] += 16
                        tok[i] = (o[3][0], dcount[id(o[3])], e, True)
                    else:
                        cnt[e] += 1
                        tok[i] = (self.sems[e], cnt[e], e, False)
                    pos[e] += 1
                pos[e] += 1
            bar_state.append((dict(cnt), dict(dcount)))
        streams = {e: [] for e in self.ENGS}
        for e in self.ENGS:
            waited = {}
            ph = 0
            for i in order[e]:
                if i is None:
                    c, dc = bar_state[ph]
                    waits = []
                    for e2 in ("pe", "act", "dve", "pool"):
                        if e2 != e and c[e2] > waited.get(id(self.sems[e2]), 0):
                            waits.append((self.sems[e2], c[e2]))
                            waited[id(self.sems[e2])] = c[e2]
                    for d in self.all_dsems:
                        v = dc[id(d)]
                        if v > waited.get(id(d[0]), 0):
                            waits.append((d[0], v))
                            waited[id(d[0])] = v
                    if waits and ph < nph - 1:
                        streams[e].append((waits, None, None))
                    ph += 1
                    continue
                o = ops[i]
                waits = {}
                for d in o[2]:
                    s, v, e2, isdma = tok[d]
                    if e2 == e and not isdma:
                        if e in ("pe", "sp") or not SAFE_SAME:
                            continue
                    k = id(s)
                    if waited.get(k, 0) >= v:
                        continue
                    if k not in waits or waits[k][1] < v:
                        waits[k] = (s, v)
                for k, (s, v) in waits.items():
                    waited[k] = v
                t = tok[i]
                streams[e].append((list(waits.values()), o[1], (t[0], 16 if t[3] else 1)))
        for eng, toks in self.final:
            streams[eng].append(([(tok[t][0], tok[t][1]) for t in toks], None, None))
        self.streams = streams

    def run(self, block):
        self.lower()

        def play(name):
            def _f(e):
                for waits, fn, inc in self.streams[name]:
                    for s, v in waits:
                        e.wait_ge(s, v)
                    if fn is None:
                        continue
                    ins = fn(e)
                    if inc is not None:
                        ins.then_inc(inc[0], inc[1])
            return _f
        block.tensor(play("pe"))
        block.scalar(play("act"))
        block.vector(play("dve"))
        block.gpsimd(play("pool"))
        block.sync(play("sp"))


class Slot:
    def __init__(self, ap, sem=None):
        self.ap = ap
        self.buf = Buf()
        self.sem = sem


class Rot:
    def __init__(self, slots):
        self.slots = slots
        self.i = 0

    def next(self):
        s = self.slots[self.i % len(self.slots)]
        self.i += 1
        return s


def build_program():
    nc = bass.Bass("TRN2", target_bir_lowering=False)

    def din(name, shape):
        return nc.dram_tensor(name, shape, F32, kind="ExternalInput").ap()

    xe = din("xe", [8192, 1024])
    pin = din("p", [SEG, 256])
    w_in = din("w_in", [1024, 9744])
    wa2_d = din("w_a2aug", [32, 512])
    wog_d = din("w_o_gla", [1024, 1024])
    woa_d = din("w_o_attn", [512, 1024])
    wout_d = din("w_out", [1024, 1024])
    w1_d = din("w_mlp1", [1024, 4096])
    w2_d = din("w_mlp2", [4096, 1024])
    wpp_d = din("w_pp", [256, 1024])
    wpg_d = din("w_pg", [1024, 1024])
    cols_d = din("cols", [128, 32])
    lnf_d = din("ln_f", [1024])
    biasm_d = din("biasm", [128, 3, 2, 512])
    hoff_d = din("hoff", [128, 1])
    cmat_d = din("cmat", [128, 6, 128])
    sel4_d = din("sel4", [4, 4, 128])
    y = nc.dram_tensor("y", [SEG, 1024], F32, kind="ExternalOutput").ap()

    with contextlib.ExitStack() as st:
        P = Prog(nc, st)
        ARENA_BYTES = 200 * 1024
        arena = st.enter_context(nc.sbuf_tensor("arena", [128, ARENA_BYTES // 2], BF16))
        psb = [st.enter_context(nc.psum_tensor("psb%d" % i, [128, 512], F32)) for i in range(7)]
        ptr_t = st.enter_context(nc.psum_tensor("ptr", [128, 8, 128], BF16))
        block = st.enter_context(nc.Block())

        def carve(off, shape, dt):
            n = 1
            for s in shape[1:]:
                n *= s
            es = 2 if dt == BF16 else 4
            assert off % 4 == 0 and off + n * es <= ARENA_BYTES, (off, shape)
            a = arena[0:shape[0], off // 2: off // 2 + n * es // 2]
            if dt == F32:
                a = a.bitcast(F32)
            if len(shape) == 3:
                a = a.rearrange("p (a b) -> p a b", a=shape[1])
            elif len(shape) == 4:
                a = a.rearrange("p (a b c) -> p a b c", a=shape[1], b=shape[2])
            return a

        class Mem:
            def __init__(self, base, limit):
                self.off = base
                self.limit = limit

            def alloc(self, shape, dt):
                n = 1
                for s in shape[1:]:
                    n *= s
                es = 2 if dt == BF16 else 4
                a = carve(self.off, shape, dt)
                self.off += (n * es + 63) // 64 * 64
                assert self.off <= self.limit, (self.off, self.limit)
                return a

        KB = 1024

        def sl(start, n, step):
            return slice(start, start + step * (n - 1) + 1, step)

        def fsz(ap):
            n = 1
            for x in ap.shape[1:]:
                n *= x
            return n

        def PE(mms, reads, writes):
            cost = 0.0
            for m in mms:
                n = max(fsz(m[2]), 64)
                c = n / 2.4 + 25.0
                if m[1].dtype == F32:
                    c *= 4
                cost += c

            def fn(e):
                ins = None
                for m in mms:
                    kw = dict(start=m[3], stop=m[4])
                    if len(m) > 5 and m[5]:
                        kw["skip_group_check"] = True
                    ins = e.matmul(m[0], lhsT=m[1], rhs=m[2], **kw)
                return ins
            return P.op("pe", fn, reads, writes, cost=cost)

        def ACT(out, in_, func, reads, writes, **kw):
            return P.op("act", lambda e: e.activation(out=out, in_=in_, func=func, **kw), reads, writes,
                        cost=180.0 + 0.83 * fsz(out))

        def ENG(eng, method, reads, writes, **kw):
            o = kw.get("out", kw.get("ap"))
            n = fsz(o)
            cost = (100.0 + 1.15 * n) if eng == "dve" else (250.0 + 0.6 * n)
            return P.op(eng, lambda e: getattr(e, method)(**kw), reads, writes, cost=cost)

        def DVE(method, reads, writes, **kw):
            return ENG("dve", method, reads, writes, **kw)

        def POOL(method, reads, writes, **kw):
            return ENG("pool", method, reads, writes, **kw)

        def DMA(out, in_, reads, writes, dsem):
            nbytes = out.shape[0] * fsz(out) * 4
            return P.op("sp", lambda e: e.dma_start(out=out, in_=in_), reads, writes, dsem=dsem, cost=120.0, fin=2000.0 + nbytes / 150.0)

        psrot = Rot([Slot(t[:]) for t in psb])
        ptr = Slot(ptr_t[:])

        def PS():
            return psrot.next()

        G = Mem(0, 28 * KB)
        cmat = G.alloc([128, 6, 128], F32)
        identb = G.alloc([128, 128], BF16)
        onesel4 = G.alloc([128, 4, 4], BF16)
        sel4 = G.alloc([4, 4, 128], F32)
        colsT = G.alloc([128, 32], F32)
        hoff = G.alloc([128, 1], F32)
        mhalf4 = G.alloc([128, 4], F32)
        mhalf = mhalf4[:, 0:1]
        statv = G.alloc([128, 64], F32)
        junk = G.alloc([128, 1024], BF16)
        stage = Rot([Slot(G.alloc([128, 1024], F32), P.dma_sem("st%d" % i)) for i in range(2)])
        xts = Rot([Slot(G.alloc([128, 1024], F32), P.dma_sem("xt%d" % i)) for i in range(2)])
        hbs = Rot([Slot(G.alloc([128, 1024], BF16)) for i in range(2)])
        LT = cmat[:, 1, :]
        UT = cmat[:, 2, :]
        CM = cmat[:, 3, :]
        b_const = Buf()
        b_junk = Buf()
        stat_i = [0]

        b_stat = [Buf() for _ in range(64)]

        def stat_col():
            c = stat_i[0] % 64
            stat_i[0] += 1
            return statv[:, c:c + 1], b_stat[c]

        dc = P.dma_sem("const")
        DMA(cmat, cmat_d, [], [b_const], dc)
        DMA(sel4, sel4_d, [], [b_const], dc)
        DMA(colsT, cols_d, [], [b_const], dc)
        DMA(hoff, hoff_d, [], [b_const], dc)
        DVE("tensor_copy", [b_const], [b_const], out=identb, in_=cmat[:, 0, :])
        DVE("tensor_copy", [b_const], [b_const], out=onesel4, in_=cmat[:, 4, 0:16].rearrange("p (a b) -> p a b", a=4))
        POOL("memset", [], [b_const], ap=mhalf4, constant=-0.5)

        def col(i, kc):
            return colsT[:, i * 8 + kc: i * 8 + kc + 1]

        cast_rr = [0]

        def cast_w(dst, src, sc, reads, writes):
            k = (cast_rr[0] % 2) * 2
            cast_rr[0] += 1
            if k == 0:
                POOL("tensor_scalar", reads, writes, out=dst, in0=src, scalar1=sc, scalar2=1.0, op0=ALU.mult, op1=ALU.mult)
            elif k == 1:
                ACT(dst, src, AF.Copy, reads, writes, scale=sc)
            else:
                DVE("tensor_scalar", reads, writes, out=dst, in0=src, scalar1=sc, scalar2=None, op0=ALU.mult)

        def load_w(dst_fn, dram, row0, nk, col0, ncols, scale_i, dst_bufs, stg=None):
            stg = stg or stage
            for kc in range(nk):
                for c0 in range(0, ncols, 1024):
                    cn = min(1024, ncols - c0)
                    s = stg.next()
                    DMA(s.ap[:, 0:cn], dram[row0 + kc * 128: row0 + (kc + 1) * 128, col0 + c0: col0 + c0 + cn],
                        [], [s.buf], s.sem)
                    sc = col(scale_i, kc) if scale_i is not None else 1.0
                    cast_w(dst_fn(kc, c0, cn), s.ap[:, 0:cn], sc, [s.buf, b_const], [dst_bufs[kc]])

        def rstd_of(src_ap, src_bufs, n, scr_ap, scr_buf):
            ss, bss = stat_col()
            ACT(scr_ap, src_ap, AF.Square, src_bufs, [scr_buf, bss], accum_out=ss)
            vv, bvv = stat_col()
            POOL("tensor_scalar", [bss], [bvv], out=vv, in0=ss, scalar1=1.0 / n, scalar2=EPS, op0=ALU.mult, op1=ALU.add)
            rs, brs = stat_col()
            POOL("tensor_tensor", [bvv, b_const], [brs], out=rs, in0=vv, in1=mhalf, op=ALU.pow)
            return rs, brs

        def norm_to_hT(src_ap, src_bufs, hT_out, hT_bufs):
            hb = hbs.next()
            rs, brs = rstd_of(src_ap, src_bufs, 1024, junk, b_junk)
            DVE("tensor_scalar", list(src_bufs) + [brs], [hb.buf], out=hb.ap, in0=src_ap, scalar1=rs, scalar2=None,
                op0=ALU.mult)

            def fn(e):
                ins = None
                for kc in range(8):
                    ins = e.transpose(ptr.ap[:, kc, :], hb.ap[:, kc * 128:(kc + 1) * 128], identb)
                return ins
            P.op("pe", fn, [hb.buf, b_const], [ptr.buf], cost=650.0)
            ACT(hT_out, ptr.ap, AF.Copy, [ptr.buf], hT_bufs)

        def load_x(row0, step=1):
            s = xts.next()
            DMA(s.ap, xe[sl(row0, 128, step), :], [], [s.buf], s.sem)
            return s

        OhT = carve(28 * KB, [128, 4, 2048], BF16)
        b_OhT = Buf()
        ogT = carve(44 * KB, [128, 8, 2048], BF16)
        b_ogT = [Buf() for _ in range(16)]
        mixT = carve(76 * KB, [128, 8, 2048], BF16)
        b_mixT = [Buf() for _ in range(4)]
        x1 = carve(108 * KB, [128, 16, 1024], F32)
        b_x1 = [Buf() for _ in range(16)]
        h2T = carve(28 * KB, [128, 8, 2048], BF16)
        b_h2T = [Buf() for _ in range(16)]

        hTh = carve(44 * KB, [128, 8, 2048], BF16)
        hTo = carve(76 * KB, [128, 8, 2048], BF16)
        b_hTh = [Buf() for _ in range(16)]
        b_hTo = [Buf() for _ in range(16)]
        MA = Mem(108 * KB, 200 * KB)
        NT = MA.alloc([128, 4, 2048], F32)
        ST = MA.alloc([4, 2048], F32)
        biasT = carve(28 * KB, [128, 512], F32)
        expbN = carve(30 * KB, [128, 512], F32)
        expbH = carve(32 * KB, [128, 512], F32)
        wA = [MA.alloc([128, 8, 3, 256], BF16) for _ in range(2)]
        b_wA = [[Buf() for _ in range(8)] for _ in range(2)]
        KTs = Rot([Slot(MA.alloc([128, 2, 512], BF16)) for _ in range(3)])
        QTs = Rot([Slot(MA.alloc([128, 2, 512], BF16)) for _ in range(2)])
        Vs = Rot([Slot(MA.alloc([128, 4, 256], BF16)) for _ in range(3)])
        Efs = Rot([Slot(MA.alloc([128, 512], F32)) for _ in range(2)])
        ETs = Rot([Slot(MA.alloc([128, 512], BF16)) for _ in range(2)])
        b_bias = Buf()
        b_exp = Buf()
        scale_att = 128.0 ** -0.5
        dbias = P.dma_sem("bias")
        b_NTq = [[Buf() for _ in range(4)] for _ in range(2)]
        b_STq = [Buf() for _ in range(4)]
        DVE("memset", [], [b for l in b_NTq for b in l], ap=NT, constant=0.0)
        DVE("memset", [], b_STq, ap=ST, constant=0.0)
        wslot = 0

        def proj_seg(hTs, hbufs, t0, nb, wa, bwa, want_q):
            cols = slice(t0 * 128, (t0 + nb) * 128)
            hb_ = hbufs[t0:t0 + nb]
            kt = KTs.next()
            for hh in range(2):
                ps = PS()
                PE([(ps.ap[:, 0:nb * 128], wa[:, kc, 1, hh * 128:(hh + 1) * 128], hTs[:, kc, cols], kc == 0, kc == 7)
                    for kc in range(8)], hb_ + bwa, [ps.buf])
                ACT(kt.ap[:, hh, 0:nb * 128], ps.ap[:, 0:nb * 128], AF.Copy, [ps.buf], [kt.buf])
            qt = None
            if want_q:
                qt = QTs.next()
                for hh in range(2):
                    ps = PS()
                    PE([(ps.ap[:, 0:nb * 128], wa[:, kc, 0, hh * 128:(hh + 1) * 128], hTs[:, kc, cols], kc == 0, kc == 7)
                        for kc in range(8)], hb_ + bwa, [ps.buf])
                    DVE("tensor_copy", [ps.buf], [qt.buf], out=qt.ap[:, hh, 0:nb * 128], in_=ps.ap[:, 0:nb * 128])
            vs = Vs.next()
            for b in range(nb):
                bc = slice((t0 + b) * 128, (t0 + b + 1) * 128)
                ps = PS()
                PE([(ps.ap[:, 0:256], hTs[:, kc, bc], wa[:, kc, 2, :], kc == 0, kc == 7) for kc in range(8)],
                   [hbufs[t0 + b]] + bwa, [ps.buf])
                if b % 2 == 0:
                    ACT(vs.ap[:, b, :], ps.ap[:, 0:256], AF.Copy, [ps.buf], [vs.buf])
                else:
                    DVE("tensor_copy", [ps.buf], [vs.buf], out=vs.ap[:, b, :], in_=ps.ap[:, 0:256])
            return kt, qt, vs

        def attend(g, hp, prevb, curb, qt, qb, first, nat, quarters):
            blk = [prevb, curb]
            sp_ = PS()
            s4 = sp_.ap.rearrange("p (h j q) -> p h j q", h=2, j=2)
            PE([(s4[:, hh, j, :], blk[j][0].ap[:, hh, blk[j][2] * 128:(blk[j][2] + 1) * 128],
                 qt.ap[:, hh, qb * 128:(qb + 1) * 128], True, True) for hh in range(2) for j in range(2)],
               [prevb[0].buf, curb[0].buf, qt.buf], [sp_.buf])
            ef = Efs.next()
            ACT(ef.ap, sp_.ap, AF.Exp, [sp_.buf], [ef.buf], scale=scale_att)
            et = ETs.next()
            DVE("tensor_tensor", [ef.buf, b_exp], [et.buf], out=et.ap, in0=ef.ap,
                in1=(expbH if first else expbN), op=ALU.mult)
            e4t = et.ap.rearrange("p (h j q) -> p h j q", h=2, j=2)
            np_ = PS()
            mms = []
            for hh in range(2):
                for j in range(2):
                    mms.append((np_.ap[:, hh * 128:(hh + 1) * 128],
                                blk[j][1].ap[:, blk[j][2], hh * 128:(hh + 1) * 128], e4t[:, hh, j, :], j == 0, j == 1))
            k = 0
            for hh in range(2):
                for j in range(2):
                    mms.append((np_.ap[0:4, 256:384], onesel4[:, hp * 2 + hh, :], e4t[:, hh, j, :], k == 0, k == 3))
                    k += 1
            PE(mms, [prevb[1].buf, curb[1].buf, et.buf, b_const], [np_.buf])
            nb_ = [b_NTq[hp][q] for q in quarters]
            DVE("tensor_tensor", [np_.buf] + nb_, nb_, out=NT[:, hp * 2:hp * 2 + 2, nat], in0=NT[:, hp * 2:hp * 2 + 2, nat],
                in1=np_.ap[:, 0:256].rearrange("p (h q) -> p h q", h=2), op=ALU.add)
            sb_ = [b_STq[q] for q in quarters]
            DVE("tensor_tensor", [np_.buf] + sb_, sb_, out=ST[:, nat], in0=ST[:, nat],
                in1=np_.ap[0:4, 256:384], op=ALU.add)

        for g in range(3):
            dil = (1, 4, 16)[g]
            nbk = 16 // dil
            for k in range(dil):
                s = load_x(6144 - 128 * dil + k, dil)
                norm_to_hT(s.ap, [s.buf], hTh[:, :, k * 128:(k + 1) * 128], [b_hTh[k]])
            for k in range(16):
                r, n = divmod(k, nbk)
                s = load_x(6144 + r + dil * 128 * n, dil)
                norm_to_hT(s.ap, [s.buf], hTo[:, :, k * 128:(k + 1) * 128], [b_hTo[k]])
            for hp in range(2):
                DMA(biasT, biasm_d[:, g, hp, :], [], [b_bias], dbias)
                ACT(expbN, biasT, AF.Exp, [b_bias], [b_exp])
                ACT(expbH, biasT, AF.Exp, [b_bias], [b_exp])
                b4 = biasT.rearrange("p (h j q) -> p h j q", h=2, j=2)
                e4 = expbH.rearrange("p (h j q) -> p h j q", h=2, j=2)
                ACT(e4[:, :, 0, :], b4[:, :, 0, :], AF.Exp, [b_bias, b_const], [b_exp], bias=hoff)
                wa = wA[wslot % 2]
                bwa = b_wA[wslot % 2]
                wslot += 1
                base = 3088 + g * 1536
                for kc in range(8):
                    s = stage.next()
                    src = w_in[kc * 128:(kc + 1) * 128, base:base + 1536].rearrange("p (c x) -> p c x", c=3)[:, :, hp * 256:(hp + 1) * 256]
                    sv = s.ap[:, 0:768].rearrange("p (c x) -> p c x", c=3)
                    DMA(sv, src, [], [s.buf], s.sem)
                    POOL("tensor_scalar", [s.buf, b_const], [bwa[kc]], out=wa[:, kc, :, :], in0=sv,
                         scalar1=col(0, kc), scalar2=1.0, op0=ALU.mult, op1=ALU.mult)
                if g < 2:
                    for r in range(dil):
                        kt_p, _, vs_p = proj_seg(hTh, b_hTh, r, 1, wa, bwa, False)
                        prev = (kt_p, vs_p, 0)
                        for n0 in range(0, nbk, 4):
                            nb = min(4, nbk - n0)
                            kt, qt, vs = proj_seg(hTo, b_hTo, r * nbk + n0, nb, wa, bwa, True)
                            for b in range(nb):
                                cur = (kt, vs, b)
                                n = n0 + b
                                tok0 = r + dil * 128 * n
                                quarters = [tok0 // 512] if g == 0 else [n]
                                attend(g, hp, prev, cur, qt, b, (n == 0), sl(tok0, 128, dil), quarters)
                                prev = cur
                else:
                    for q4 in range(4):
                        kt_h, _, vs_h = proj_seg(hTh, b_hTh, q4 * 4, 4, wa, bwa, False)
                        kt, qt, vs = proj_seg(hTo, b_hTo, q4 * 4, 4, wa, bwa, True)
                        for b in range(4):
                            r = q4 * 4 + b
                            attend(g, hp, (kt_h, vs_h, b), (kt, vs, b), qt, b, True, sl(r, 128, 16), [0, 1, 2, 3])
        ACT(ST, ST, AF.Ln, b_STq, b_STq)
        ACT(ST, ST, AF.Exp, b_STq, b_STq, scale=-1.0)
        for hh in range(4):
            for tb in range(4):
                ps = PS()
                PE([(ps.ap, sel4[:, hh, :], ST[:, tb * 512:(tb + 1) * 512], True, True)], b_STq + [b_const], [ps.buf])
                DVE("tensor_tensor", [ps.buf, b_NTq[hh // 2][tb]], [b_OhT], out=OhT[:, hh, tb * 512:(tb + 1) * 512],
                    in0=NT[:, hh, tb * 512:(tb + 1) * 512], in1=ps.ap, op=ALU.mult)
        P.barrier()

        MG = Mem(76 * KB, 200 * KB)
        wG = MG.alloc([128, 8, 3088], BF16)
        b_wGq = [Buf() for _ in range(8)]
        b_wGk = [Buf() for _ in range(8)]
        b_wGv = [Buf() for _ in range(8)]
        b_wGr = [Buf() for _ in range(8)]
        b_wGa = [Buf() for _ in range(8)]
        wa2 = MG.alloc([32, 512], BF16)
        b_wa2 = Buf()
        CM4 = MG.alloc([128, 4, 128], F32)
        NS1 = 3
        NS2 = 3
        hTt = [Slot(MG.alloc([128, 8, 128], BF16)) for _ in range(NS1)]
        haT = [Slot(MG.alloc([32, 128], BF16)) for _ in range(NS1)]
        sps = [Slot(MG.alloc([128, 512], F32)) for _ in range(1)]
        sph = [Slot(MG.alloc([128, 512], BF16)) for _ in range(2)]
        spl = [Slot(MG.alloc([128, 512], BF16)) for _ in range(2)]
        cb = MG.alloc([128, 3, 128], BF16)
        ones16b = MG.alloc([128, 2], BF16)
        b_cb = Buf()
        LTb, UTb, UT128b = cb[:, 0, :], cb[:, 1, :], cb[:, 2, :]
        wex = [Slot(MG.alloc([128, 512], F32)) for _ in range(2)]
        kouts = [Slot(MG.alloc([128, 512], BF16)) for _ in range(NS2)]
        vbs = [Slot(MG.alloc([128, 1024], BF16)) for _ in range(NS2)]
        e1s = [Slot(MG.alloc([128, 4, 128], F32)) for _ in range(NS2)]
        e2s = [Slot(MG.alloc([128, 4, 128], F32)) for _ in range(2)]
        qdA = [Slot(MG.alloc([128, 4, 128], BF16)) for _ in range(NS2)]
        qdB = [Slot(MG.alloc([128, 4, 128], BF16)) for _ in range(NS2)]
        kinT = [Slot(MG.alloc([128, 4, 128], BF16)) for _ in range(2)]
        attnT = [Slot(MG.alloc([128, 4, 128], BF16)) for _ in range(NS2)]
        sil = [Slot(MG.alloc([128, 1024], F32), P.dma_sem("sil%d" % i)) for i in range(NS2)]
        Sst = MG.alloc([128, 4, 256], F32)
        SbA = MG.alloc([128, 4, 256], BF16)
        SbB = MG.alloc([128, 4, 256], BF16)
        ogbs = Rot([Slot(MG.alloc([128, 1024], BF16)) for _ in range(2)])
        for o_ in ogbs.slots:
            o_.bufs = [Buf(), Buf()]
        ssh = MG.alloc([128, 32], F32)
        b_ssh = [Buf() for _ in range(4)]
        junkH = MG.alloc([128, 256], BF16)
        b_junkH = Buf()
        b_S = [Buf() for _ in range(4)]
        b_SbA = [Buf() for _ in range(4)]
        b_SbB = [Buf() for _ in range(4)]

        stgG = Rot(stage.slots + sil)
        load_w(lambda kc, c0, cn: wG[:, kc, 512 + c0:512 + c0 + cn], w_in, 0, 8, 512, 512, 0, b_wGk, stgG)
        load_w(lambda kc, c0, cn: wG[:, kc, 3072 + c0:3072 + c0 + cn], w_in, 0, 8, 3072, 16, 0, b_wGa, stgG)
        load_w(lambda kc, c0, cn: wG[:, kc, 1024 + c0:1024 + c0 + cn], w_in, 0, 8, 1024, 1024, 0, b_wGv, stgG)
        load_w(lambda kc, c0, cn: wG[:, kc, c0:c0 + cn], w_in, 0, 8, 0, 512, 0, b_wGq, stgG)
        load_w(lambda kc, c0, cn: wG[:, kc, 2048 + c0:2048 + c0 + cn], w_in, 0, 8, 2048, 1024, 0, b_wGr, stgG)
        s = stage.next()
        DMA(s.ap[0:32, 0:512], wa2_d, [], [s.buf], s.sem)
        POOL("tensor_copy", [s.buf], [b_wa2], out=wa2, in_=s.ap[0:32, 0:512])
        for hh in range(4):
            DVE("tensor_copy", [b_const], [b_const], out=CM4[:, hh, :], in_=CM)
        DVE("tensor_copy", [b_const], [b_cb], out=cb[:, 0, :], in_=cmat[:, 1, :])
        DVE("tensor_copy", [b_const], [b_cb], out=cb[:, 1, :], in_=cmat[:, 2, :])
        DVE("tensor_copy", [b_const], [b_cb], out=cb[:, 2, :], in_=cmat[:, 5, :])
        POOL("memset", [], [b_cb], ap=ones16b, constant=0.0625)
        for i in range(NS1):
            POOL("memset", [], [haT[i].buf], ap=haT[i].ap, constant=1.0)
        for i in range(NS2):
            POOL("memset", [], [qdA[i].buf], ap=qdA[i].ap, constant=0.0)
            POOL("memset", [], [qdB[i].buf], ap=qdB[i].ap, constant=0.0)
        DVE("memset", [], b_S, ap=Sst, constant=0.0)
        POOL("memset", [], b_SbA, ap=SbA, constant=0.0)

        gl = {}

        def gla_stage1(t):
            own = t >= 48
            i1_ = t % NS1
            i = t % 2
            j = t % NS2
            xs = load_x(t * 128)
            ht = hTt[i1_]
            norm_to_hT(xs.ap, [xs.buf], ht.ap, [ht.buf])
            kps = PS()
            PE([(kps.ap, ht.ap[:, kc, :], wG[:, kc, 512:1024], kc == 0, kc == 7) for kc in range(8)], [ht.buf] + b_wGk, [kps.buf])
            hps = PS()
            PE([(hps.ap[0:16, 0:128], wG[:, kc, 3072:3088], ht.ap[:, kc, :], kc == 0, kc == 7) for kc in range(8)],
               [ht.buf] + b_wGa, [hps.buf])
            ACT(haT[i1_].ap[0:16, :], hps.ap[0:16, 0:128], AF.Copy, [hps.buf], [haT[i1_].buf])
            zps = PS()
            PE([(zps.ap, haT[i1_].ap, wa2, True, True)], [haT[i1_].buf, b_wa2], [zps.buf])
            ACT(sps[0].ap, zps.ap, AF.Exp, [zps.buf], [sps[0].buf], scale=-1.0)
            ACT(sps[0].ap, sps[0].ap, AF.Ln, [sps[0].buf], [sps[0].buf], bias=1.0)
            ACT(sph[i].ap, sps[0].ap, AF.Copy, [sps[0].buf], [sph[i].buf])
            DVE("tensor_tensor", [sps[0].buf, sph[i].buf], [spl[i].buf], out=spl[i].ap, in0=sps[0].ap, in1=sph[i].ap,
                op=ALU.subtract)
            rs_ = [sph[i].buf, spl[i].buf, b_cb]
            vp = [PS(), PS()]
            for h2 in range(2):
                PE([(vp[h2].ap, ht.ap[:, kc, :], wG[:, kc, 1024 + h2 * 512:1536 + h2 * 512], kc == 0, kc == 7) for kc in range(8)],
                   [ht.buf] + b_wGv, [vp[h2].buf])
            ACT(vbs[j].ap[:, 0:512], vp[0].ap, AF.Copy, [vp[0].buf], [vbs[j].buf])
            DVE("tensor_copy", [vp[1].buf], [vbs[j].buf], out=vbs[j].ap[:, 512:1024], in_=vp[1].ap)
            dps = PS()
            Um = UTb if own else UT128b
            PE([(dps.ap, Um, sph[i].ap, True, False), (dps.ap, Um, spl[i].ap, False, True)], rs_, [dps.buf])
            ACT(wex[i].ap, dps.ap, AF.Exp, [dps.buf], [wex[i].buf])
            DVE("tensor_tensor", [kps.buf, wex[i].buf], [kouts[j].buf], out=kouts[j].ap, in0=kps.ap, in1=wex[i].ap, op=ALU.mult)
            nps = PS()
            n4 = nps.ap.rearrange("p (h q) -> p h q", h=4)
            if not own:
                PE([(nps.ap[:, hh:hh + 1], sp_[i].ap[:, hh * 128:(hh + 1) * 128], ones16b[:, 0:1], k == 0, k == 1)
                    for hh in range(4) for k, sp_ in enumerate((sph, spl))], rs_, [nps.buf])
                ACT(e1s[j].ap[:, :, 127], nps.ap[:, 0:4], AF.Exp, [nps.buf], [e1s[j].buf], scale=-1.0)
                return
            PE([(n4[:, hh, :], sp_[i].ap[:, hh * 128:(hh + 1) * 128], LTb, k == 0, k == 1)
                for hh in range(4) for k, sp_ in enumerate((sph, spl))], rs_, [nps.buf])
            ACT(e1s[j].ap, n4, AF.Exp, [nps.buf], [e1s[j].buf], scale=-1.0)
            ACT(e2s[i].ap, n4, AF.Exp, [nps.buf], [e2s[i].buf])
            qps = PS()
            q4 = qps.ap.rearrange("p (h q) -> p h q", h=4)
            PE([(q4[:, hh, :], wG[:, kc, hh * 128:(hh + 1) * 128], ht.ap[:, kc, :], kc == 0, kc == 7)
                for hh in range(4) for kc in range(8)], [ht.buf] + b_wGq, [qps.buf])
            DVE("scalar_tensor_tensor", [qps.buf, e1s[j].buf], [qdA[j].buf], out=qdA[j].ap[:, :, 0:64], in0=q4[:, :, 0:64],
                scalar=128.0 ** -0.5, in1=e1s[j].ap[:, :, 0:64], op0=ALU.mult, op1=ALU.mult)
            DVE("scalar_tensor_tensor", [qps.buf, e1s[j].buf], [qdB[j].buf], out=qdB[j].ap[:, :, 64:128], in0=q4[:, :, 64:128],
                scalar=128.0 ** -0.5, in1=e1s[j].ap[:, :, 64:128], op0=ALU.mult, op1=ALU.mult)
            ktp = PS()
            k4 = ktp.ap.rearrange("p (h q) -> p h q", h=4)
            PE([(k4[:, hh, :], wG[:, kc, 512 + hh * 128:512 + (hh + 1) * 128], ht.ap[:, kc, :], kc == 0, kc == 7)
                for hh in range(4) for kc in range(8)], [ht.buf] + b_wGk, [ktp.buf])
            DVE("tensor_tensor", [ktp.buf, e2s[i].buf], [kinT[i].buf], out=kinT[i].ap, in0=k4, in1=e2s[i].ap, op=ALU.mult)
            aps = PS()
            a4 = aps.ap.rearrange("p (h q) -> p h q", h=4)
            mms = []
            for hh in range(4):
                mms.append((a4[:, hh, 0:64], kinT[i].ap[:, hh, :], qdA[j].ap[:, hh, 0:64], True, True))
                mms.append((a4[:, hh, 64:128], kinT[i].ap[:, hh, :], qdB[j].ap[:, hh, 64:128], True, True))
            PE(mms, [kinT[i].buf, qdA[j].buf, qdB[j].buf], [aps.buf])
            DVE("tensor_tensor", [aps.buf, b_const], [attnT[j].buf], out=attnT[j].ap, in0=a4, in1=CM4, op=ALU.mult)
            rp = [PS(), PS()]
            for h2 in range(2):
                hs = slice(h2 * 512, (h2 + 1) * 512)
                PE([(rp[h2].ap, ht.ap[:, kc, :], wG[:, kc, 2048 + h2 * 512:2560 + h2 * 512], kc == 0, kc == 7) for kc in range(8)],
                   [ht.buf] + b_wGr, [rp[h2].buf])
                ACT(sil[j].ap[:, hs], rp[h2].ap, AF.Exp, [rp[h2].buf], [sil[j].buf], scale=-1.0)
                ACT(sil[j].ap[:, hs], sil[j].ap[:, hs], AF.Ln, [sil[j].buf], [sil[j].buf], bias=1.0)
                ACT(sil[j].ap[:, hs], sil[j].ap[:, hs], AF.Exp, [sil[j].buf], [sil[j].buf], scale=-1.0)
                DVE("tensor_tensor", [rp[h2].buf, sil[j].buf], [sil[j].buf], out=sil[j].ap[:, hs],
                    in0=rp[h2].ap, in1=sil[j].ap[:, hs], op=ALU.mult)

        def gla_stage2(t):
            own = t >= 48
            j = t % NS2
            tt = t - 48
            last_prefix = (t == 47)
            ko = kouts[j]
            vb = vbs[j]
            e1 = e1s[j]
            if own:
                oP = [PS(), PS()]
                for pr in range(2):
                    mms = []
                    for hq in range(2):
                        hh = pr * 2 + hq
                        mms.append((oP[pr].ap[:, hq * 256:(hq + 1) * 256], attnT[j].ap[:, hh, :], vb.ap[:, hh * 256:(hh + 1) * 256],
                                    hq == 0, False, True))
                    for hq in range(2):
                        hh = pr * 2 + hq
                        mms.append((oP[pr].ap[:, hq * 256:(hq + 1) * 256], qdA[j].ap[:, hh, :], SbA[:, hh, :], False, False, True))
                    PE(mms, [attnT[j].buf, vb.buf, qdA[j].buf] + b_SbA[pr * 2:pr * 2 + 2], [oP[pr].buf])
            for c in range(2 if own else 1):
                rows = slice(c * 64, (c + 1) * 64) if own else slice(0, 128)
                dcol = c * 64 + 63 if own else 127
                lastc = (c == 1) or not own
                uP = [PS(), PS()]
                for pr in range(2):
                    PE([(uP[pr].ap[:, hq * 256:(hq + 1) * 256], ko.ap[rows, (pr * 2 + hq) * 128:(pr * 2 + hq + 1) * 128],
                         vb.ap[rows, (pr * 2 + hq) * 256:(pr * 2 + hq + 1) * 256], True, True) for hq in range(2)],
                       [ko.buf, vb.buf], [uP[pr].buf])
                for hh in range(4):
                    pr, hq = hh // 2, hh % 2
                    DVE("scalar_tensor_tensor", [uP[pr].buf, e1.buf, b_S[hh]], [b_S[hh]], out=Sst[:, hh, :], in0=Sst[:, hh, :],
                        scalar=e1.ap[:, hh, dcol:dcol + 1], in1=uP[pr].ap[:, hq * 256:(hq + 1) * 256],
                        op0=ALU.mult, op1=ALU.add)
                    if c == 0 and own:
                        ACT(SbB[:, hh, :], Sst[:, hh, :], AF.Copy, [b_S[hh]], [b_SbB[hh]])
                    if lastc and (own or last_prefix):
                        ACT(SbA[:, hh, :], Sst[:, hh, :], AF.Copy, [b_S[hh]], [b_SbA[hh]])
                if c == 0 and own:
                    for pr in range(2):
                        PE([(oP[pr].ap[:, hq * 256:(hq + 1) * 256], qdB[j].ap[:, pr * 2 + hq, :], SbB[:, pr * 2 + hq, :],
                             False, hq == 1, True) for hq in range(2)],
                           [qdB[j].buf] + b_SbB[pr * 2:pr * 2 + 2], [oP[pr].buf])
            if not own:
                return
            og = ogbs.next()
            for pr in range(2):
                bssh = b_ssh[(t % 2) * 2 + pr]
                sc = (t % 2) * 16 + pr * 8
                for hq in range(2):
                    ACT(junkH, oP[pr].ap[:, hq * 256:(hq + 1) * 256], AF.Square, [oP[pr].buf], [b_junkH, bssh],
                        accum_out=ssh[:, sc + hq:sc + hq + 1])
                POOL("tensor_scalar", [bssh], [bssh], out=ssh[:, sc + 2:sc + 4], in0=ssh[:, sc:sc + 2], scalar1=1.0 / 256, scalar2=EPS,
                     op0=ALU.mult, op1=ALU.add)
                POOL("tensor_tensor", [bssh, b_const], [bssh], out=ssh[:, sc + 2:sc + 4], in0=ssh[:, sc + 2:sc + 4], in1=mhalf4[:, 0:2],
                     op=ALU.pow)
                for hq in range(2):
                    hh = pr * 2 + hq
                    DVE("scalar_tensor_tensor", [oP[pr].buf, bssh, sil[j].buf], [og.bufs[pr]], out=og.ap[:, hh * 256:(hh + 1) * 256],
                        in0=oP[pr].ap[:, hq * 256:(hq + 1) * 256], scalar=ssh[:, sc + 2 + hq:sc + 3 + hq],
                        in1=sil[j].ap[:, hh * 256:(hh + 1) * 256], op0=ALU.mult, op1=ALU.mult)

            def fn(e, og=og):
                ins = None
                for kc in range(8):
                    ins = e.transpose(ptr.ap[:, kc, :], og.ap[:, kc * 128:(kc + 1) * 128], identb)
                return ins
            P.op("pe", fn, og.bufs + [b_const], [ptr.buf], cost=650.0)
            ACT(ogT[:, :, tt * 128:(tt + 1) * 128], ptr.ap, AF.Copy, [ptr.buf], [b_ogT[tt]])

        gla_stage1(0)
        for t in range(64):
            if t + 1 < 64:
                gla_stage1(t + 1)
            gla_stage2(t)
        P.barrier()

        hTo2 = carve(108 * KB, [128, 8, 2048], BF16)
        b_hTo2 = [Buf() for _ in range(16)]
        for t in range(16):
            s = load_x(6144 + t * 128)
            norm_to_hT(s.ap, [s.buf], hTo2[:, :, t * 128:(t + 1) * 128], [b_hTo2[t]])
        MM = Mem(140 * KB, 180 * KB)
        wsets = []
        for _ in range(2):
            wsets.append(dict(
                wog=MM.alloc([128, 8, 256], BF16), woa=MM.alloc([128, 4, 256], BF16),
                wgA=MM.alloc([128, 8, 256], BF16), wgB=MM.alloc([128, 8, 256], BF16),
                b_wog=[Buf() for _ in range(8)], b_woa=[Buf() for _ in range(4)],
                b_wgA=[Buf() for _ in range(8)], b_wgB=[Buf() for _ in range(8)]))
        sgs = Rot([Slot(MM.alloc([128, 512], F32)) for _ in range(4)])
        tms = Rot([Slot(MM.alloc([128, 512], F32)) for _ in range(2)])
        wout = carve(180 * KB, [128, 8, 1024], BF16)
        b_wout = [Buf() for _ in range(8)]
        for fo in range(4):
            ws = wsets[fo % 2]
            wog, woa, wgA, wgB = ws["wog"], ws["woa"], ws["wgA"], ws["wgB"]
            b_wog, b_woa, b_wgA, b_wgB = ws["b_wog"], ws["b_woa"], ws["b_wgA"], ws["b_wgB"]
            load_w(lambda kc, c0, cn: wgA[:, kc, c0:c0 + cn], w_in, 0, 8, 7696 + fo * 256, 256, 0, b_wgA)
            load_w(lambda kc, c0, cn: wgB[:, kc, c0:c0 + cn], w_in, 0, 8, 8720 + fo * 256, 256, 0, b_wgB)
            load_w(lambda kc, c0, cn: wog[:, kc, c0:c0 + cn], wog_d, 0, 8, fo * 256, 256, 3, b_wog)
            load_w(lambda kc, c0, cn: woa[:, kc, c0:c0 + cn], woa_d, 0, 4, fo * 256, 256, None, b_woa)
            if fo == 1:
                load_w(lambda kc, c0, cn: wout[:, kc, c0:c0 + cn], wout_d, 0, 8, 0, 1024, None, b_wout)
            for tb in range(4):
                tk = slice(tb * 512, (tb + 1) * 512)
                for fc in range(2):
                    fs = slice(fc * 128, (fc + 1) * 128)
                    ga = PS()
                    PE([(ga.ap, wgA[:, kc, fs], hTo2[:, kc, tk], kc == 0, kc == 7) for kc in range(8)],
                       b_wgA + b_hTo2[tb * 4:tb * 4 + 4], [ga.buf])
                    sa = sgs.next()
                    ACT(sa.ap, ga.ap, AF.Sigmoid, [ga.buf], [sa.buf])
                    gb = PS()
                    PE([(gb.ap, wgB[:, kc, fs], hTo2[:, kc, tk], kc == 0, kc == 7) for kc in range(8)],
                       b_wgB + b_hTo2[tb * 4:tb * 4 + 4], [gb.buf])
                    sb_ = sgs.next()
                    ACT(sb_.ap, gb.ap, AF.Sigmoid, [gb.buf], [sb_.buf])
                    yg = PS()
                    PE([(yg.ap, wog[:, kc, fs], ogT[:, kc, tk], kc == 0, kc == 7) for kc in range(8)],
                       b_wog + b_ogT[tb * 4:tb * 4 + 4], [yg.buf])
                    t1 = tms.next()
                    DVE("tensor_tensor", [yg.buf, sa.buf], [t1.buf], out=t1.ap, in0=yg.ap, in1=sa.ap, op=ALU.mult)
                    ya = PS()
                    PE([(ya.ap, woa[:, hh, fs], OhT[:, hh, tk], hh == 0, hh == 3) for hh in range(4)],
                       b_woa + [b_OhT], [ya.buf])
                    t2 = tms.next()
                    DVE("tensor_tensor", [ya.buf, sb_.buf], [t2.buf], out=t2.ap, in0=ya.ap, in1=sb_.ap, op=ALU.mult)
                    DVE("tensor_tensor", [t1.buf, t2.buf], [b_mixT[tb]], out=mixT[:, fo * 2 + fc, tk], in0=t1.ap, in1=t2.ap, op=ALU.add)
        P.barrier()

        w1c = [carve(60 * KB, [128, 8, 512], BF16), carve(76 * KB, [128, 8, 512], BF16)]
        w2c = [carve(68 * KB, [128, 4, 1024], BF16), carve(84 * KB, [128, 4, 1024], BF16)]
        b_w1c = [[Buf() for _ in range(8)] for _ in range(2)]
        b_w2c = [[Buf() for _ in range(4)] for _ in range(2)]

        def load_ff(ffg):
            wi = ffg % 2
            w1 = w1c[wi]
            w2 = w2c[wi]
            load_w(lambda kc, c0, cn: w1[:, kc, c0:c0 + cn], w1_d, 0, 8, ffg * 512, 512, 1, b_w1c[wi])
            load_w(lambda kc, c0, cn: w2[:, kc, c0:c0 + cn], w2_d, ffg * 512, 4, 0, 1024, None, b_w2c[wi])
        load_ff(0)
        for t in range(16):
            s = load_x(6144 + t * 128)
            for h2 in range(2):
                ps = PS()
                PE([(ps.ap, mixT[:, kc, t * 128:(t + 1) * 128], wout[:, kc, h2 * 512:(h2 + 1) * 512], kc == 0, kc == 7) for kc in range(8)],
                   b_mixT + b_wout, [ps.buf])
                DVE("tensor_tensor", [ps.buf, s.buf], [b_x1[t]], out=x1[:, t, h2 * 512:(h2 + 1) * 512], in0=ps.ap,
                    in1=s.ap[:, h2 * 512:(h2 + 1) * 512], op=ALU.add)
            norm_to_hT(x1[:, t, :], [b_x1[t]], h2T[:, :, t * 128:(t + 1) * 128], [b_h2T[t]])
        P.barrier()

        MF = Mem(92 * KB, 108 * KB)
        uTs = Rot([Slot(MF.alloc([128, 4, 512], BF16)) for _ in range(2)])
        rls = Rot([Slot(MF.alloc([128, 512], F32)) for _ in range(2)])
        wpg = carve(172 * KB, [128, 8, 1024], BF16)
        wpp = carve(188 * KB, [128, 2, 1024], BF16)
        lnfb = carve(192 * KB, [128, 1024], F32)
        b_wpg = [Buf() for _ in range(8)]
        b_wpp = [Buf() for _ in range(2)]
        b_lnf = Buf()
        for ffg in range(8):
            wi = ffg % 2
            w1 = w1c[wi]
            w2 = w2c[wi]
            if ffg > 0:
                load_ff(ffg)
            if ffg == 2:
                load_w(lambda kc, c0, cn: wpg[:, kc, c0:c0 + cn], wpg_d, 0, 8, 0, 1024, 2, b_wpg)
                load_w(lambda kc, c0, cn: wpp[:, kc, c0:c0 + cn], wpp_d, 0, 2, 0, 1024, None, b_wpp)
                dl = P.dma_sem("lnf")
                DMA(lnfb, lnf_d.partition_broadcast(128), [], [b_lnf], dl)
            for tb in range(4):
                tk = slice(tb * 512, (tb + 1) * 512)
                ut = uTs.next()
                for j in range(4):
                    ps = PS()
                    PE([(ps.ap, w1[:, kc, j * 128:(j + 1) * 128], h2T[:, kc, tk], kc == 0, kc == 7) for kc in range(8)],
                       b_w1c[wi] + b_h2T[tb * 4:tb * 4 + 4], [ps.buf])
                    rl = rls.next()
                    ACT(rl.ap, ps.ap, AF.Relu, [ps.buf], [rl.buf])
                    DVE("tensor_tensor", [rl.buf], [ut.buf], out=ut.ap[:, j, :], in0=rl.ap, in1=rl.ap, op=ALU.mult)
                for tt in range(4):
                    t = tb * 4 + tt
                    for h2 in range(2):
                        ps = PS()
                        PE([(ps.ap, ut.ap[:, j, tt * 128:(tt + 1) * 128], w2[:, j, h2 * 512:(h2 + 1) * 512], j == 0, j == 3) for j in range(4)],
                           [ut.buf] + b_w2c[wi], [ps.buf])
                        DVE("tensor_tensor", [ps.buf, b_x1[t]], [b_x1[t]], out=x1[:, t, h2 * 512:(h2 + 1) * 512],
                            in0=x1[:, t, h2 * 512:(h2 + 1) * 512], in1=ps.ap, op=ALU.add)
        P.barrier()

        MP = Mem(28 * KB, 108 * KB)
        junkP = MP.alloc([128, 1024], BF16)
        b_junkP = Buf()
        h3s = Rot([Slot(MP.alloc([128, 8, 128], BF16)) for _ in range(2)])
        pfs = Rot([Slot(MP.alloc([128, 256], F32), P.dma_sem("pf%d" % i)) for i in range(2)])
        pbs = Rot([Slot(MP.alloc([128, 256], BF16)) for _ in range(2)])
        pTs = Rot([Slot(MP.alloc([128, 2, 128], BF16)) for _ in range(2)])
        sg2 = Rot([Slot(MP.alloc([128, 1024], F32)) for _ in range(2)])
        osb = Rot([Slot(MP.alloc([128, 1024], F32), P.dma_sem("os%d" % i)) for i in range(2)])
        out_toks = []
        for t in range(16):
            h3 = h3s.next()
            norm_to_hT(x1[:, t, :], [b_x1[t]], h3.ap, [h3.buf])
            pf = pfs.next()
            DMA(pf.ap, pin[t * 128:(t + 1) * 128, :], [], [pf.buf], pf.sem)
            pb = pbs.next()
            DVE("tensor_copy", [pf.buf], [pb.buf], out=pb.ap, in_=pf.ap)

            def fn(e, pb=pb):
                ins = None
                for c in range(2):
                    ins = e.transpose(ptr.ap[:, c, :], pb.ap[:, c * 128:(c + 1) * 128], identb)
                return ins
            P.op("pe", fn, [pb.buf, b_const], [ptr.buf], cost=200.0)
            pT = pTs.next()
            ACT(pT.ap, ptr.ap[:, 0:2, :], AF.Copy, [ptr.buf], [pT.buf])
            sg = sg2.next()
            for h2 in range(2):
                hs = slice(h2 * 512, (h2 + 1) * 512)
                gp = PS()
                PE([(gp.ap, h3.ap[:, kc, :], wpg[:, kc, hs], kc == 0, kc == 7) for kc in range(8)], [h3.buf] + b_wpg, [gp.buf])
                ACT(sg.ap[:, hs], gp.ap, AF.Sigmoid, [gp.buf], [sg.buf])
                pp = PS()
                PE([(pp.ap, pT.ap[:, c, :], wpp[:, c, hs], c == 0, c == 1) for c in range(2)], [pT.buf] + b_wpp, [pp.buf])
                DVE("tensor_tensor", [pp.buf, sg.buf], [sg.buf], out=sg.ap[:, hs], in0=sg.ap[:, hs], in1=pp.ap, op=ALU.mult)
            DVE("tensor_tensor", [sg.buf, b_x1[t]], [b_x1[t]], out=x1[:, t, :], in0=x1[:, t, :], in1=sg.ap, op=ALU.add)
            ob = osb.next()
            rs, brs = rstd_of(x1[:, t, :], [b_x1[t]], 1024, junkP, b_junkP)
            DVE("scalar_tensor_tensor", [b_x1[t], brs, b_lnf], [ob.buf], out=ob.ap, in0=x1[:, t, :], scalar=rs, in1=lnfb,
                op0=ALU.mult, op1=ALU.mult)
            out_toks.append(DMA(y[t * 128:(t + 1) * 128, :], ob.ap, [ob.buf], [], ob.sem))
        P.final_wait("sp", out_toks[-2:])
        P.run(block)
    return nc


def _t5_bucket(n):
    max_exact = 16
    nf = np.maximum(n, 1).astype(np.float32)
    large = max_exact + (np.log(nf / max_exact) / np.log(2048 / max_exact) * (32 - max_exact)).astype(np.int32)
    large = np.minimum(large, 31)
    return np.where(n < max_exact, n, large).astype(np.int32)


def _const_mats():
    m = np.arange(128)[:, None]
    t = np.arange(128)[None, :]
    same = (m // 64) == (t // 64)
    cm = np.zeros((128, 6, 128), np.float32)
    cm[:, 0, :] = np.eye(128)
    cm[:, 1, :] = np.where(same & (m <= t), 1.0 / 16, 0.0)
    cm[:, 2, :] = np.where(same & (m > t), -1.0 / 16, 0.0)
    cm[:, 3, :] = np.where(same & (m <= t), 1.0, 0.0)
    cm[:, 5, :] = np.where(m > t, -1.0 / 16, 0.0)
    cm[:, 4, 0:16] = np.eye(4, dtype=np.float32).reshape(16)[None, :]
    sel4 = np.zeros((4, 4, 128), np.float32)
    for hh in range(4):
        sel4[hh, hh, :] = 1.0
    return cm, sel4


def _bias_layout(rel_bias):
    k = np.arange(128)[:, None, None]
    j = np.arange(2)[None, :, None]
    q = np.arange(128)[None, None, :]
    delta = q - k + 128 * (1 - j)
    valid = (delta >= 0) & (delta <= 128)
    out = np.full((128, 3, 2, 2, 2, 128), NEGM, np.float32)
    for g, dil in enumerate((1, 4, 16)):
        bucket = _t5_bucket(np.maximum(delta, 0) * dil)
        for hp in range(2):
            for hh in range(2):
                tab = rel_bias[:, g * 4 + hp * 2 + hh]
                vals = tab[bucket]
                out[:, g, hp, hh] = np.where(valid, vals, NEGM)
    return out.reshape(128, 3, 2, 512)


_PROG = None


def kernel(x, p, ln1, w_in, w_a2, b_a, gla_gn, w_o_gla, w_o_attn, w_out, ln2, w_mlp1, w_mlp2, ln3, w_pp, w_pg,
           rel_bias, ln_f):
    global _PROG
    f = lambda a: np.ascontiguousarray(np.asarray(a, dtype=np.float32))
    x = f(x); p = f(p)
    cm, sel4 = _const_mats()
    cols = np.stack([f(ln1)[0], f(ln2)[0], f(ln3)[0], f(gla_gn)[0]]).reshape(4, 8, 128).transpose(2, 0, 1).reshape(128, 32)
    wa2aug = np.zeros((32, 512), np.float32)
    wa2aug[0:16] = f(w_a2)[0]
    wa2aug[16] = f(b_a)[0]
    shared = {
        "w_in": f(w_in)[0], "w_a2aug": wa2aug, "w_o_gla": f(w_o_gla)[0], "w_o_attn": f(w_o_attn)[0],
        "w_out": f(w_out)[0], "w_mlp1": f(w_mlp1)[0], "w_mlp2": f(w_mlp2)[0], "w_pp": f(w_pp)[0], "w_pg": f(w_pg)[0],
        "cols": np.ascontiguousarray(cols), "ln_f": f(ln_f), "biasm": _bias_layout(f(rel_bias)),
        "cmat": cm, "sel4": sel4,
    }
    in_maps = []
    for c in range(NCORES):
        b, j = c // 4, c % 4
        xe = np.zeros((8192, 1024), np.float32)
        n = SEG * (j + 1)
        xe[8192 - n:] = x[b, 0:n]
        m = dict(shared)
        m["xe"] = xe
        m["p"] = np.ascontiguousarray(p[0, b, j * SEG:(j + 1) * SEG])
        m["hoff"] = np.full((128, 1), NEGM if j == 0 else 0.0, np.float32)
        in_maps.append(m)
    if _PROG is None:
        _PROG = build_program()
    res = run_bass_kernel_spmd(_PROG, in_maps, core_ids=list(range(NCORES)))
    out = np.zeros((2, 8192, 1024), np.float32)
    for c in range(NCORES):
        b, j = c // 4, c % 4
        out[b, j * SEG:(j + 1) * SEG] = res.results[c]["y"]
    return out
```

```python
import contextlib
import numpy as np
import concourse.bass as bass
import concourse.mybir as mybir
from concourse.bass_utils import run_bass_kernel_spmd

F32 = mybir.dt.float32
BF16 = mybir.dt.bfloat16
ALU = mybir.AluOpType
AF = mybir.ActivationFunctionType

SAFE_SAME = True
EPS = 1e-6
NCORES = 8
SEG = 2048
NEGM = -30000.0


class Buf:
    __slots__ = ("w", "r")

    def __init__(self):
        self.w = None
        self.r = []


class Prog:
    ENGS = ("pe", "act", "dve", "pool", "sp")
    WINDOW = 100
    LAT = 300.0
    SLACK = 500.0
    LAT_DMA = 200.0

    def __init__(self, nc, stack):
        self.nc = nc
        self.stack = stack
        self.ops = []
        self.phase = 0
        self.sems = {}
        self.all_dsems = []
        self.final = []
        for e in ("pe", "act", "dve", "pool"):
            self.sems[e] = stack.enter_context(nc.semaphore("s_" + e))

    def dma_sem(self, name):
        s = self.stack.enter_context(self.nc.semaphore("d_" + name))
        d = [s, 0]
        self.all_dsems.append(d)
        return d

    def op(self, eng, fn, reads=(), writes=(), dsem=None, cost=500.0, fin=None):
        idx = len(self.ops)
        deps = set()
        for b in reads:
            if b.w is not None:
                deps.add(b.w)
        for b in writes:
            if b.w is not None:
                deps.add(b.w)
            deps.update(b.r)
        self.ops.append([eng, fn, sorted(deps), dsem, cost, self.phase, cost if fin is None else fin])
        for b in reads:
            b.r.append(idx)
        for b in writes:
            b.w = idx
            b.r = []
        return idx

    def barrier(self):
        self.phase += 1

    def final_wait(self, eng, toks):
        self.final.append((eng, list(toks)))

    def schedule(self):
        ops = self.ops
        n = len(ops)
        succ = [[] for _ in range(n)]
        for i, o in enumerate(ops):
            for d in o[2]:
                if ops[d][5] == o[5]:
                    succ[d].append(i)
        bl = [0.0] * n
        for i in range(n - 1, -1, -1):
            m = 0.0
            for j in succ[i]:
                v = bl[j] + (0.0 if ops[j][0] == ops[i][0] else self.LAT)
                if v > m:
                    m = v
            bl[i] = ops[i][6] + m
        order = {e: [] for e in self.ENGS}
        finish = {}
        tnow = 0.0
        for ph in range(self.phase + 1):
            pend = {e: [] for e in self.ENGS}
            for i, o in enumerate(ops):
                if o[5] == ph:
                    pend[o[0]].append(i)
            tfree = {e: tnow for e in self.ENGS}
            remaining = sum(len(v) for v in pend.values())
            cand = {e: None for e in self.ENGS}
            dirty = set(self.ENGS)
            while remaining:
                for e in list(dirty):
                    cl = []
                    for i in pend[e][:self.WINDOW]:
                        o = ops[i]
                        st = tfree[e]
                        ok = True
                        for d in o[2]:
                            f = finish.get(d)
                            if f is None:
                                ok = False
                                break
                            lat = self.LAT_DMA if ops[d][3] is not None else (0.0 if ops[d][0] == e else self.LAT)
                            if f + lat > st:
                                st = f + lat
                        if ok:
                            cl.append((st, i))
                    if not cl:
                        cand[e] = None
                    else:
                        tmin = min(c[0] for c in cl)
                        lim = tmin + self.SLACK
                        best = None
                        for st, i in cl:
                            if st <= lim:
                                key = (-bl[i], i)
                                if best is None or key < best[0]:
                                    best = (key, st, i)
                        cand[e] = (best[1], best[2])
                dirty.clear()
                pick = None
                for e in self.ENGS:
                    c = cand[e]
                    if c is not None and (pick is None or c < pick[0]):
                        pick = (c, e)
                assert pick is not None, "scheduler stuck"
                (st, i), e = pick
                o = ops[i]
                tfree[e] = st + o[4]
                finish[i] = st + o[6]
                pend[e].remove(i)
                order[e].append(i)
                remaining -= 1
                dirty.update(self.ENGS)
            tnow = max(tfree.values())
            self.phase_ends = getattr(self, 'phase_ends', []) + [tnow]
            for e in self.ENGS:
                order[e].append(None)
        self.est_total = tnow
        return order

    def lower(self):
        order = self.schedule()
        ops = self.ops
        tok = {}
        cnt = {e: 0 for e in ("pe", "act", "dve", "pool")}
        bar_cnt = []
        nph = self.phase + 1
        pos = {e: 0 for e in self.ENGS}
        dcount = {id(d): 0 for d in self.all_dsems}
        bar_state = []
        for ph in range(nph):
            for e in self.ENGS:
                lst = order[e]
                while lst[pos[e]] is not None:
                    i = lst[pos[e]]
                    o = ops[i]
                    if o[3] is not None:
                        dcount[id(o[3])] += 16
                        tok[i] = (o[3][0], dcount[id(o[3])], e, True)
                    else:
                        cnt[e] += 1
                        tok[i] = (self.sems[e], cnt[e], e, False)
                    pos[e] += 1
                pos[e] += 1
            bar_state.append((dict(cnt), dict(dcount)))
        streams = {e: [] for e in self.ENGS}
        for e in self.ENGS:
            waited = {}
            ph = 0
            for i in order[e]:
                if i is None:
                    c, dc = bar_state[ph]
                    waits = []
                    for e2 in ("pe", "act", "dve", "pool"):
                        if e2 != e and c[e2] > waited.get(id(self.sems[e2]), 0):
                            waits.append((self.sems[e2], c[e2]))
                            waited[id(self.sems[e2])] = c[e2]
                    for d in self.all_dsems:
                        v = dc[id(d)]
                        if v > waited.get(id(d[0]), 0):
                            waits.append((d[0], v))
                            waited[id(d[0])] = v
                    if waits and ph < nph - 1:
                        streams[e].append((waits, None, None))
                    ph += 1
                    continue
                o = ops[i]
                waits = {}
                for d in o[2]:
                    s, v, e2, isdma = tok[d]
                    if e2 == e and not isdma:
                        if e in ("pe", "sp") or not SAFE_SAME:
                            continue
                    k = id(s)
                    if waited.get(k, 0) >= v:
                        continue
                    if k not in waits or waits[k][1] < v:
                        waits[k] = (s, v)
                for k, (s, v) in waits.items():
                    waited[k] = v
                t = tok[i]
                streams[e].append((list(waits.values()), o[1], (t[0], 16 if t[3] else 1)))
        for eng, toks in self.final:
            streams[eng].append(([(tok[t][0], tok[t][1]) for t in toks], None, None))
        self.streams = streams

    def run(self, block):
        self.lower()

        def play(name):
            def _f(e):
                for waits, fn, inc in self.streams[name]:
                    for s, v in waits:
                        e.wait_ge(s, v)
                    if fn is None:
                        continue
                    ins = fn(e)
                    if inc is not None:
                        ins.then_inc(inc[0], inc[1])
            return _f
        block.tensor(play("pe"))
        block.scalar(play("act"))
        block.vector(play("dve"))
        block.gpsimd(play("pool"))
        block.sync(play("sp"))


class Slot:
    def __init__(self, ap, sem=None):
        self.ap = ap
        self.buf = Buf()
        self.sem = sem


class Rot:
    def __init__(self, slots):
        self.slots = slots
        self.i = 0

    def next(self):
        s = self.slots[self.i % len(self.slots)]
        self.i += 1
        return s


def build_program():
    nc = bass.Bass("TRN2", target_bir_lowering=False)

    def din(name, shape):
        return nc.dram_tensor(name, shape, F32, kind="ExternalInput").ap()

    xe = din("xe", [8192, 1024])
    pin = din("p", [SEG, 256])
    w_in = din("w_in", [1024, 9744])
    wa2_d = din("w_a2aug", [32, 512])
    wog_d = din("w_o_gla", [1024, 1024])
    woa_d = din("w_o_attn", [512, 1024])
    wout_d = din("w_out", [1024, 1024])
    w1_d = din("w_mlp1", [1024, 4096])
    w2_d = din("w_mlp2", [4096, 1024])
    wpp_d = din("w_pp", [256, 1024])
    wpg_d = din("w_pg", [1024, 1024])
    cols_d = din("cols", [128, 32])
    lnf_d = din("ln_f", [1024])
    biasm_d = din("biasm", [128, 3, 2, 512])
    hoff_d = din("hoff", [128, 1])
    cmat_d = din("cmat", [128, 6, 128])
    sel4_d = din("sel4", [4, 4, 128])
    y = nc.dram_tensor("y", [SEG, 1024], F32, kind="ExternalOutput").ap()

    with contextlib.ExitStack() as st:
        P = Prog(nc, st)
        ARENA_BYTES = 204 * 1024
        arena = st.enter_context(nc.sbuf_tensor("arena", [128, ARENA_BYTES // 2], BF16))
        psb = [st.enter_context(nc.psum_tensor("psb%d" % i, [128, 512], F32)) for i in range(7)]
        ptr_t = st.enter_context(nc.psum_tensor("ptr", [128, 8, 128], BF16))
        block = st.enter_context(nc.Block())

        def carve(off, shape, dt):
            n = 1
            for s in shape[1:]:
                n *= s
            es = 2 if dt == BF16 else 4
            assert off % 4 == 0 and off + n * es <= ARENA_BYTES, (off, shape)
            a = arena[0:shape[0], off // 2: off // 2 + n * es // 2]
            if dt == F32:
                a = a.bitcast(F32)
            if len(shape) == 3:
                a = a.rearrange("p (a b) -> p a b", a=shape[1])
            elif len(shape) == 4:
                a = a.rearrange("p (a b c) -> p a b c", a=shape[1], b=shape[2])
            return a

        class Mem:
            def __init__(self, base, limit):
                self.off = base
                self.limit = limit

            def alloc(self, shape, dt):
                n = 1
                for s in shape[1:]:
                    n *= s
                es = 2 if dt == BF16 else 4
                a = carve(self.off, shape, dt)
                self.off += (n * es + 63) // 64 * 64
                assert self.off <= self.limit, (self.off, self.limit)
                return a

        KB = 1024

        def sl(start, n, step):
            return slice(start, start + step * (n - 1) + 1, step)

        def fsz(ap):
            n = 1
            for x in ap.shape[1:]:
                n *= x
            return n

        def PE(mms, reads, writes):
            cost = 0.0
            for m in mms:
                n = max(fsz(m[2]), 64)
                c = n / 2.4 + 25.0
                if m[1].dtype == F32:
                    c *= 4
                cost += c

            def fn(e):
                ins = None
                for m in mms:
                    kw = dict(start=m[3], stop=m[4])
                    if len(m) > 5 and m[5]:
                        kw["skip_group_check"] = True
                    ins = e.matmul(m[0], lhsT=m[1], rhs=m[2], **kw)
                return ins
            return P.op("pe", fn, reads, writes, cost=cost)

        def ACT(out, in_, func, reads, writes, **kw):
            return P.op("act", lambda e: e.activation(out=out, in_=in_, func=func, **kw), reads, writes,
                        cost=180.0 + 0.83 * fsz(out))

        def ENG(eng, method, reads, writes, **kw):
            o = kw.get("out", kw.get("ap"))
            n = fsz(o)
            cost = (100.0 + 1.15 * n) if eng == "dve" else (250.0 + 0.6 * n)
            return P.op(eng, lambda e: getattr(e, method)(**kw), reads, writes, cost=cost)

        def DVE(method, reads, writes, **kw):
            return ENG("dve", method, reads, writes, **kw)

        def POOL(method, reads, writes, **kw):
            return ENG("pool", method, reads, writes, **kw)

        def DMA(out, in_, reads, writes, dsem):
            nbytes = out.shape[0] * fsz(out) * 4
            return P.op("sp", lambda e: e.dma_start(out=out, in_=in_), reads, writes, dsem=dsem, cost=120.0, fin=2000.0 + nbytes / 150.0)

        psrot = Rot([Slot(t[:]) for t in psb])
        ptr = Slot(ptr_t[:])

        def PS():
            return psrot.next()

        G = Mem(0, 32 * KB)
        cmat = G.alloc([128, 6, 128], F32)
        identb = G.alloc([128, 128], BF16)
        onesel4 = G.alloc([128, 4, 4], BF16)
        sel4 = G.alloc([4, 4, 128], F32)
        colsT = G.alloc([128, 32], F32)
        hoff = G.alloc([128, 1], F32)
        mhalf4 = G.alloc([128, 4], F32)
        mhalf = mhalf4[:, 0:1]
        statv = G.alloc([128, 64], F32)
        junk = G.alloc([128, 1024], BF16)
        stage = Rot([Slot(G.alloc([128, 1024], F32), P.dma_sem("st%d" % i)) for i in range(2)])
        xts = Rot([Slot(G.alloc([128, 1024], F32), P.dma_sem("xt%d" % i)) for i in range(3)])
        hbs = Rot([Slot(G.alloc([128, 1024], BF16)) for i in range(2)])
        LT = cmat[:, 1, :]
        UT = cmat[:, 2, :]
        CM = cmat[:, 3, :]
        b_const = Buf()
        b_junk = Buf()
        stat_i = [0]

        b_stat = [Buf() for _ in range(64)]

        def stat_col():
            c = stat_i[0] % 64
            stat_i[0] += 1
            return statv[:, c:c + 1], b_stat[c]

        dc = P.dma_sem("const")
        DMA(cmat, cmat_d, [], [b_const], dc)
        DMA(sel4, sel4_d, [], [b_const], dc)
        DMA(colsT, cols_d, [], [b_const], dc)
        DMA(hoff, hoff_d, [], [b_const], dc)
        DVE("tensor_copy", [b_const], [b_const], out=identb, in_=cmat[:, 0, :])
        DVE("tensor_copy", [b_const], [b_const], out=onesel4, in_=cmat[:, 4, 0:16].rearrange("p (a b) -> p a b", a=4))
        POOL("memset", [], [b_const], ap=mhalf4, constant=-0.5)

        def col(i, kc):
            return colsT[:, i * 8 + kc: i * 8 + kc + 1]

        cast_rr = [0]

        def cast_w(dst, src, sc, reads, writes):
            k = (cast_rr[0] % 2) * 2
            cast_rr[0] += 1
            if k == 0:
                POOL("tensor_scalar", reads, writes, out=dst, in0=src, scalar1=sc, scalar2=1.0, op0=ALU.mult, op1=ALU.mult)
            elif k == 1:
                ACT(dst, src, AF.Copy, reads, writes, scale=sc)
            else:
                DVE("tensor_scalar", reads, writes, out=dst, in0=src, scalar1=sc, scalar2=None, op0=ALU.mult)

        def load_w(dst_fn, dram, row0, nk, col0, ncols, scale_i, dst_bufs, stg=None):
            stg = stg or stage
            for kc in range(nk):
                for c0 in range(0, ncols, 1024):
                    cn = min(1024, ncols - c0)
                    s = stg.next()
                    DMA(s.ap[:, 0:cn], dram[row0 + kc * 128: row0 + (kc + 1) * 128, col0 + c0: col0 + c0 + cn],
                        [], [s.buf], s.sem)
                    sc = col(scale_i, kc) if scale_i is not None else 1.0
                    cast_w(dst_fn(kc, c0, cn), s.ap[:, 0:cn], sc, [s.buf, b_const], [dst_bufs[kc]])

        def rstd_of(src_ap, src_bufs, n, scr_ap, scr_buf):
            ss, bss = stat_col()
            ACT(scr_ap, src_ap, AF.Square, src_bufs, [scr_buf, bss], accum_out=ss)
            vv, bvv = stat_col()
            POOL("tensor_scalar", [bss], [bvv], out=vv, in0=ss, scalar1=1.0 / n, scalar2=EPS, op0=ALU.mult, op1=ALU.add)
            rs, brs = stat_col()
            POOL("tensor_tensor", [bvv, b_const], [brs], out=rs, in0=vv, in1=mhalf, op=ALU.pow)
            return rs, brs

        def norm_to_hT(src_ap, src_bufs, hT_out, hT_bufs):
            hb = hbs.next()
            rs, brs = rstd_of(src_ap, src_bufs, 1024, junk, b_junk)
            DVE("tensor_scalar", list(src_bufs) + [brs], [hb.buf], out=hb.ap, in0=src_ap, scalar1=rs, scalar2=None,
                op0=ALU.mult)

            def fn(e):
                ins = None
                for kc in range(8):
                    ins = e.transpose(ptr.ap[:, kc, :], hb.ap[:, kc * 128:(kc + 1) * 128], identb)
                return ins
            P.op("pe", fn, [hb.buf, b_const], [ptr.buf], cost=650.0)
            ACT(hT_out, ptr.ap, AF.Copy, [ptr.buf], hT_bufs)

        def load_x(row0, step=1):
            s = xts.next()
            DMA(s.ap, xe[sl(row0, 128, step), :], [], [s.buf], s.sem)
            return s

        OhT = carve(32 * KB, [128, 4, 2048], BF16)
        b_OhT = Buf()
        ogT = carve(48 * KB, [128, 8, 2048], BF16)
        b_ogT = [Buf() for _ in range(16)]
        mixT = carve(80 * KB, [128, 8, 2048], BF16)
        b_mixT = [Buf() for _ in range(4)]
        x1 = carve(112 * KB, [128, 16, 1024], F32)
        b_x1 = [Buf() for _ in range(16)]
        h2T = carve(32 * KB, [128, 8, 2048], BF16)
        b_h2T = [Buf() for _ in range(16)]

        hTh = carve(48 * KB, [128, 8, 2048], BF16)
        hTo = carve(80 * KB, [128, 8, 2048], BF16)
        b_hTh = [Buf() for _ in range(16)]
        b_hTo = [Buf() for _ in range(16)]
        MA = Mem(112 * KB, 204 * KB)
        NT = MA.alloc([128, 4, 2048], F32)
        ST = MA.alloc([4, 2048], F32)
        biasT = carve(32 * KB, [128, 512], F32)
        expbN = carve(34 * KB, [128, 512], F32)
        expbH = carve(36 * KB, [128, 512], F32)
        wA = [MA.alloc([128, 8, 3, 256], BF16) for _ in range(2)]
        b_wA = [[Buf() for _ in range(8)] for _ in range(2)]
        KTs = Rot([Slot(MA.alloc([128, 2, 512], BF16)) for _ in range(3)])
        QTs = Rot([Slot(MA.alloc([128, 2, 512], BF16)) for _ in range(2)])
        Vs = Rot([Slot(MA.alloc([128, 4, 256], BF16)) for _ in range(3)])
        Efs = Rot([Slot(MA.alloc([128, 512], F32)) for _ in range(2)])
        ETs = Rot([Slot(MA.alloc([128, 512], BF16)) for _ in range(2)])
        b_bias = Buf()
        b_exp = Buf()
        scale_att = 128.0 ** -0.5
        dbias = P.dma_sem("bias")
        b_NTq = [[Buf() for _ in range(4)] for _ in range(2)]
        b_STq = [Buf() for _ in range(4)]
        DVE("memset", [], [b for l in b_NTq for b in l], ap=NT, constant=0.0)
        DVE("memset", [], b_STq, ap=ST, constant=0.0)
        wslot = 0

        def proj_seg(hTs, hbufs, t0, nb, wa, bwa, want_q):
            cols = slice(t0 * 128, (t0 + nb) * 128)
            hb_ = hbufs[t0:t0 + nb]
            kt = KTs.next()
            for hh in range(2):
                ps = PS()
                PE([(ps.ap[:, 0:nb * 128], wa[:, kc, 1, hh * 128:(hh + 1) * 128], hTs[:, kc, cols], kc == 0, kc == 7)
                    for kc in range(8)], hb_ + bwa, [ps.buf])
                ACT(kt.ap[:, hh, 0:nb * 128], ps.ap[:, 0:nb * 128], AF.Copy, [ps.buf], [kt.buf])
            qt = None
            if want_q:
                qt = QTs.next()
                for hh in range(2):
                    ps = PS()
                    PE([(ps.ap[:, 0:nb * 128], wa[:, kc, 0, hh * 128:(hh + 1) * 128], hTs[:, kc, cols], kc == 0, kc == 7)
                        for kc in range(8)], hb_ + bwa, [ps.buf])
                    DVE("tensor_copy", [ps.buf], [qt.buf], out=qt.ap[:, hh, 0:nb * 128], in_=ps.ap[:, 0:nb * 128])
            vs = Vs.next()
            for b in range(nb):
                bc = slice((t0 + b) * 128, (t0 + b + 1) * 128)
                ps = PS()
                PE([(ps.ap[:, 0:256], hTs[:, kc, bc], wa[:, kc, 2, :], kc == 0, kc == 7) for kc in range(8)],
                   [hbufs[t0 + b]] + bwa, [ps.buf])
                if b % 2 == 0:
                    ACT(vs.ap[:, b, :], ps.ap[:, 0:256], AF.Copy, [ps.buf], [vs.buf])
                else:
                    DVE("tensor_copy", [ps.buf], [vs.buf], out=vs.ap[:, b, :], in_=ps.ap[:, 0:256])
            return kt, qt, vs

        def attend(g, hp, prevb, curb, qt, qb, first, nat, quarters):
            blk = [prevb, curb]
            sp_ = PS()
            s4 = sp_.ap.rearrange("p (h j q) -> p h j q", h=2, j=2)
            PE([(s4[:, hh, j, :], blk[j][0].ap[:, hh, blk[j][2] * 128:(blk[j][2] + 1) * 128],
                 qt.ap[:, hh, qb * 128:(qb + 1) * 128], True, True) for hh in range(2) for j in range(2)],
               [prevb[0].buf, curb[0].buf, qt.buf], [sp_.buf])
            ef = Efs.next()
            ACT(ef.ap, sp_.ap, AF.Exp, [sp_.buf], [ef.buf], scale=scale_att)
            et = ETs.next()
            DVE("tensor_tensor", [ef.buf, b_exp], [et.buf], out=et.ap, in0=ef.ap,
                in1=(expbH if first else expbN), op=ALU.mult)
            e4t = et.ap.rearrange("p (h j q) -> p h j q", h=2, j=2)
            np_ = PS()
            mms = []
            for hh in range(2):
                for j in range(2):
                    mms.append((np_.ap[:, hh * 128:(hh + 1) * 128],
                                blk[j][1].ap[:, blk[j][2], hh * 128:(hh + 1) * 128], e4t[:, hh, j, :], j == 0, j == 1))
            k = 0
            for hh in range(2):
                for j in range(2):
                    mms.append((np_.ap[0:4, 256:384], onesel4[:, hp * 2 + hh, :], e4t[:, hh, j, :], k == 0, k == 3))
                    k += 1
            PE(mms, [prevb[1].buf, curb[1].buf, et.buf, b_const], [np_.buf])
            nb_ = [b_NTq[hp][q] for q in quarters]
            DVE("tensor_tensor", [np_.buf] + nb_, nb_, out=NT[:, hp * 2:hp * 2 + 2, nat], in0=NT[:, hp * 2:hp * 2 + 2, nat],
                in1=np_.ap[:, 0:256].rearrange("p (h q) -> p h q", h=2), op=ALU.add)
            sb_ = [b_STq[q] for q in quarters]
            DVE("tensor_tensor", [np_.buf] + sb_, sb_, out=ST[:, nat], in0=ST[:, nat],
                in1=np_.ap[0:4, 256:384], op=ALU.add)

        for g in range(3):
            dil = (1, 4, 16)[g]
            nbk = 16 // dil
            for k in range(dil):
                s = load_x(6144 - 128 * dil + k, dil)
                norm_to_hT(s.ap, [s.buf], hTh[:, :, k * 128:(k + 1) * 128], [b_hTh[k]])
            for k in range(16):
                r, n = divmod(k, nbk)
                s = load_x(6144 + r + dil * 128 * n, dil)
                norm_to_hT(s.ap, [s.buf], hTo[:, :, k * 128:(k + 1) * 128], [b_hTo[k]])
            for hp in range(2):
                DMA(biasT, biasm_d[:, g, hp, :], [], [b_bias], dbias)
                ACT(expbN, biasT, AF.Exp, [b_bias], [b_exp])
                ACT(expbH, biasT, AF.Exp, [b_bias], [b_exp])
                b4 = biasT.rearrange("p (h j q) -> p h j q", h=2, j=2)
                e4 = expbH.rearrange("p (h j q) -> p h j q", h=2, j=2)
                ACT(e4[:, :, 0, :], b4[:, :, 0, :], AF.Exp, [b_bias, b_const], [b_exp], bias=hoff)
                wa = wA[wslot % 2]
                bwa = b_wA[wslot % 2]
                wslot += 1
                base = 3088 + g * 1536
                for kc in range(8):
                    s = stage.next()
                    src = w_in[kc * 128:(kc + 1) * 128, base:base + 1536].rearrange("p (c x) -> p c x", c=3)[:, :, hp * 256:(hp + 1) * 256]
                    sv = s.ap[:, 0:768].rearrange("p (c x) -> p c x", c=3)
                    DMA(sv, src, [], [s.buf], s.sem)
                    POOL("tensor_scalar", [s.buf, b_const], [bwa[kc]], out=wa[:, kc, :, :], in0=sv,
                         scalar1=col(0, kc), scalar2=1.0, op0=ALU.mult, op1=ALU.mult)
                if g < 2:
                    for r in range(dil):
                        kt_p, _, vs_p = proj_seg(hTh, b_hTh, r, 1, wa, bwa, False)
                        prev = (kt_p, vs_p, 0)
                        for n0 in range(0, nbk, 4):
                            nb = min(4, nbk - n0)
                            kt, qt, vs = proj_seg(hTo, b_hTo, r * nbk + n0, nb, wa, bwa, True)
                            for b in range(nb):
                                cur = (kt, vs, b)
                                n = n0 + b
                                tok0 = r + dil * 128 * n
                                quarters = [tok0 // 512] if g == 0 else [n]
                                attend(g, hp, prev, cur, qt, b, (n == 0), sl(tok0, 128, dil), quarters)
                                prev = cur
                else:
                    for q4 in range(4):
                        kt_h, _, vs_h = proj_seg(hTh, b_hTh, q4 * 4, 4, wa, bwa, False)
                        kt, qt, vs = proj_seg(hTo, b_hTo, q4 * 4, 4, wa, bwa, True)
                        for b in range(4):
                            r = q4 * 4 + b
                            attend(g, hp, (kt_h, vs_h, b), (kt, vs, b), qt, b, True, sl(r, 128, 16), [0, 1, 2, 3])
        DVE("reciprocal", b_STq, b_STq, out=ST, in_=ST)
        for hh in range(4):
            for tb in range(4):
                ps = PS()
                PE([(ps.ap, sel4[:, hh, :], ST[:, tb * 512:(tb + 1) * 512], True, True)], b_STq + [b_const], [ps.buf])
                DVE("tensor_tensor", [ps.buf, b_NTq[hh // 2][tb]], [b_OhT], out=OhT[:, hh, tb * 512:(tb + 1) * 512],
                    in0=NT[:, hh, tb * 512:(tb + 1) * 512], in1=ps.ap, op=ALU.mult)
        P.barrier()

        MG = Mem(80 * KB, 204 * KB)
        wG = MG.alloc([128, 8, 3088], BF16)
        b_wGq = [Buf() for _ in range(8)]
        b_wGk = [Buf() for _ in range(8)]
        b_wGv = [Buf() for _ in range(8)]
        b_wGr = [Buf() for _ in range(8)]
        b_wGa = [Buf() for _ in range(8)]
        wa2 = MG.alloc([32, 512], BF16)
        b_wa2 = Buf()
        CM4 = MG.alloc([128, 4, 128], F32)
        NS1 = 3
        NS2 = 3
        hTt = [Slot(MG.alloc([128, 8, 128], BF16)) for _ in range(NS1)]
        haT = [Slot(MG.alloc([32, 128], BF16)) for _ in range(NS1)]
        sps = [Slot(MG.alloc([128, 512], F32)) for _ in range(1)]
        sph = [Slot(MG.alloc([128, 512], BF16)) for _ in range(2)]
        spl = [Slot(MG.alloc([128, 512], BF16)) for _ in range(2)]
        cb = MG.alloc([128, 3, 128], BF16)
        ones16b = MG.alloc([128, 2], BF16)
        b_cb = Buf()
        LTb, UTb, UT128b = cb[:, 0, :], cb[:, 1, :], cb[:, 2, :]
        wex = [Slot(MG.alloc([128, 512], F32)) for _ in range(2)]
        kouts = [Slot(MG.alloc([128, 512], BF16)) for _ in range(NS2)]
        vbs = [Slot(MG.alloc([128, 1024], BF16)) for _ in range(NS2)]
        e1s = [Slot(MG.alloc([128, 4, 128], F32)) for _ in range(NS2)]
        e2s = [Slot(MG.alloc([128, 4, 128], F32)) for _ in range(2)]
        qdA = [Slot(MG.alloc([128, 4, 128], BF16)) for _ in range(NS2)]
        qdB = [Slot(MG.alloc([128, 4, 128], BF16)) for _ in range(NS2)]
        kinT = [Slot(MG.alloc([128, 4, 128], BF16)) for _ in range(2)]
        attnT = [Slot(MG.alloc([128, 4, 128], BF16)) for _ in range(NS2)]
        sil = [Slot(MG.alloc([128, 1024], F32), P.dma_sem("sil%d" % i)) for i in range(NS2)]
        Sst = MG.alloc([128, 4, 256], F32)
        SbA = MG.alloc([128, 4, 256], BF16)
        SbB = MG.alloc([128, 4, 256], BF16)
        ogbs = Rot([Slot(MG.alloc([128, 1024], BF16)) for _ in range(2)])
        for o_ in ogbs.slots:
            o_.bufs = [Buf(), Buf()]
        ssh = MG.alloc([128, 32], F32)
        b_ssh = [Buf() for _ in range(4)]
        junkH = MG.alloc([128, 256], BF16)
        b_junkH = Buf()
        b_S = [Buf() for _ in range(4)]
        b_SbA = [Buf() for _ in range(4)]
        b_SbB = [Buf() for _ in range(4)]

        stgG = Rot(stage.slots + sil)
        load_w(lambda kc, c0, cn: wG[:, kc, 512 + c0:512 + c0 + cn], w_in, 0, 8, 512, 512, 0, b_wGk, stgG)
        load_w(lambda kc, c0, cn: wG[:, kc, 3072 + c0:3072 + c0 + cn], w_in, 0, 8, 3072, 16, 0, b_wGa, stgG)
        load_w(lambda kc, c0, cn: wG[:, kc, 1024 + c0:1024 + c0 + cn], w_in, 0, 8, 1024, 1024, 0, b_wGv, stgG)
        load_w(lambda kc, c0, cn: wG[:, kc, c0:c0 + cn], w_in, 0, 8, 0, 512, 0, b_wGq, stgG)
        load_w(lambda kc, c0, cn: wG[:, kc, 2048 + c0:2048 + c0 + cn], w_in, 0, 8, 2048, 1024, 0, b_wGr, stgG)
        s = stage.next()
        DMA(s.ap[0:32, 0:512], wa2_d, [], [s.buf], s.sem)
        POOL("tensor_copy", [s.buf], [b_wa2], out=wa2, in_=s.ap[0:32, 0:512])
        for hh in range(4):
            DVE("tensor_copy", [b_const], [b_const], out=CM4[:, hh, :], in_=CM)
        DVE("tensor_copy", [b_const], [b_cb], out=cb[:, 0, :], in_=cmat[:, 1, :])
        DVE("tensor_copy", [b_const], [b_cb], out=cb[:, 1, :], in_=cmat[:, 2, :])
        DVE("tensor_copy", [b_const], [b_cb], out=cb[:, 2, :], in_=cmat[:, 5, :])
        POOL("memset", [], [b_cb], ap=ones16b, constant=0.0625)
        for i in range(NS1):
            POOL("memset", [], [haT[i].buf], ap=haT[i].ap, constant=1.0)
        for i in range(NS2):
            POOL("memset", [], [qdA[i].buf], ap=qdA[i].ap, constant=0.0)
            POOL("memset", [], [qdB[i].buf], ap=qdB[i].ap, constant=0.0)
        DVE("memset", [], b_S, ap=Sst, constant=0.0)
        POOL("memset", [], b_SbA, ap=SbA, constant=0.0)

        gl = {}

        def gla_stage1(t):
            own = t >= 48
            i1_ = t % NS1
            i = t % 2
            j = t % NS2
            xs = load_x(t * 128)
            ht = hTt[i1_]
            norm_to_hT(xs.ap, [xs.buf], ht.ap, [ht.buf])
            kps = PS()
            PE([(kps.ap, ht.ap[:, kc, :], wG[:, kc, 512:1024], kc == 0, kc == 7) for kc in range(8)], [ht.buf] + b_wGk, [kps.buf])
            hps = PS()
            PE([(hps.ap[0:16, 0:128], wG[:, kc, 3072:3088], ht.ap[:, kc, :], kc == 0, kc == 7) for kc in range(8)],
               [ht.buf] + b_wGa, [hps.buf])
            ACT(haT[i1_].ap[0:16, :], hps.ap[0:16, 0:128], AF.Copy, [hps.buf], [haT[i1_].buf])
            zps = PS()
            PE([(zps.ap, haT[i1_].ap, wa2, True, True)], [haT[i1_].buf, b_wa2], [zps.buf])
            ACT(sps[0].ap, zps.ap, AF.Exp, [zps.buf], [sps[0].buf], scale=-1.0)
            ACT(sps[0].ap, sps[0].ap, AF.Ln, [sps[0].buf], [sps[0].buf], bias=1.0)
            ACT(sph[i].ap, sps[0].ap, AF.Copy, [sps[0].buf], [sph[i].buf])
            DVE("tensor_tensor", [sps[0].buf, sph[i].buf], [spl[i].buf], out=spl[i].ap, in0=sps[0].ap, in1=sph[i].ap,
                op=ALU.subtract)
            rs_ = [sph[i].buf, spl[i].buf, b_cb]
            vp = [PS(), PS()]
            for h2 in range(2):
                PE([(vp[h2].ap, ht.ap[:, kc, :], wG[:, kc, 1024 + h2 * 512:1536 + h2 * 512], kc == 0, kc == 7) for kc in range(8)],
                   [ht.buf] + b_wGv, [vp[h2].buf])
            ACT(vbs[j].ap[:, 0:512], vp[0].ap, AF.Copy, [vp[0].buf], [vbs[j].buf])
            DVE("tensor_copy", [vp[1].buf], [vbs[j].buf], out=vbs[j].ap[:, 512:1024], in_=vp[1].ap)
            dps = PS()
            Um = UTb if own else UT128b
            PE([(dps.ap, Um, sph[i].ap, True, False), (dps.ap, Um, spl[i].ap, False, True)], rs_, [dps.buf])
            ACT(wex[i].ap, dps.ap, AF.Exp, [dps.buf], [wex[i].buf])
            DVE("tensor_tensor", [kps.buf, wex[i].buf], [kouts[j].buf], out=kouts[j].ap, in0=kps.ap, in1=wex[i].ap, op=ALU.mult)
            nps = PS()
            n4 = nps.ap.rearrange("p (h q) -> p h q", h=4)
            if not own:
                PE([(nps.ap[:, hh:hh + 1], sp_[i].ap[:, hh * 128:(hh + 1) * 128], ones16b[:, 0:1], k == 0, k == 1)
                    for hh in range(4) for k, sp_ in enumerate((sph, spl))], rs_, [nps.buf])
                ACT(e1s[j].ap[:, :, 127], nps.ap[:, 0:4], AF.Exp, [nps.buf], [e1s[j].buf], scale=-1.0)
                return
            PE([(n4[:, hh, :], sp_[i].ap[:, hh * 128:(hh + 1) * 128], LTb, k == 0, k == 1)
                for hh in range(4) for k, sp_ in enumerate((sph, spl))], rs_, [nps.buf])
            ACT(e1s[j].ap, n4, AF.Exp, [nps.buf], [e1s[j].buf], scale=-1.0)
            ACT(e2s[i].ap, n4, AF.Exp, [nps.buf], [e2s[i].buf])
            qps = PS()
            q4 = qps.ap.rearrange("p (h q) -> p h q", h=4)
            PE([(q4[:, hh, :], wG[:, kc, hh * 128:(hh + 1) * 128], ht.ap[:, kc, :], kc == 0, kc == 7)
                for hh in range(4) for kc in range(8)], [ht.buf] + b_wGq, [qps.buf])
            DVE("scalar_tensor_tensor", [qps.buf, e1s[j].buf], [qdA[j].buf], out=qdA[j].ap[:, :, 0:64], in0=q4[:, :, 0:64],
                scalar=128.0 ** -0.5, in1=e1s[j].ap[:, :, 0:64], op0=ALU.mult, op1=ALU.mult)
            DVE("scalar_tensor_tensor", [qps.buf, e1s[j].buf], [qdB[j].buf], out=qdB[j].ap[:, :, 64:128], in0=q4[:, :, 64:128],
                scalar=128.0 ** -0.5, in1=e1s[j].ap[:, :, 64:128], op0=ALU.mult, op1=ALU.mult)
            ktp = PS()
            k4 = ktp.ap.rearrange("p (h q) -> p h q", h=4)
            PE([(k4[:, hh, :], wG[:, kc, 512 + hh * 128:512 + (hh + 1) * 128], ht.ap[:, kc, :], kc == 0, kc == 7)
                for hh in range(4) for kc in range(8)], [ht.buf] + b_wGk, [ktp.buf])
            DVE("tensor_tensor", [ktp.buf, e2s[i].buf], [kinT[i].buf], out=kinT[i].ap, in0=k4, in1=e2s[i].ap, op=ALU.mult)
            aps = PS()
            a4 = aps.ap.rearrange("p (h q) -> p h q", h=4)
            mms = []
            for hh in range(4):
                mms.append((a4[:, hh, 0:64], kinT[i].ap[:, hh, :], qdA[j].ap[:, hh, 0:64], True, True))
                mms.append((a4[:, hh, 64:128], kinT[i].ap[:, hh, :], qdB[j].ap[:, hh, 64:128], True, True))
            PE(mms, [kinT[i].buf, qdA[j].buf, qdB[j].buf], [aps.buf])
            DVE("tensor_tensor", [aps.buf, b_const], [attnT[j].buf], out=attnT[j].ap, in0=a4, in1=CM4, op=ALU.mult)
            rp = [PS(), PS()]
            for h2 in range(2):
                hs = slice(h2 * 512, (h2 + 1) * 512)
                PE([(rp[h2].ap, ht.ap[:, kc, :], wG[:, kc, 2048 + h2 * 512:2560 + h2 * 512], kc == 0, kc == 7) for kc in range(8)],
                   [ht.buf] + b_wGr, [rp[h2].buf])
                ACT(sil[j].ap[:, hs], rp[h2].ap, AF.Exp, [rp[h2].buf], [sil[j].buf], scale=-1.0)
                ACT(sil[j].ap[:, hs], sil[j].ap[:, hs], AF.Ln, [sil[j].buf], [sil[j].buf], bias=1.0)
                ACT(sil[j].ap[:, hs], sil[j].ap[:, hs], AF.Exp, [sil[j].buf], [sil[j].buf], scale=-1.0)
                DVE("tensor_tensor", [rp[h2].buf, sil[j].buf], [sil[j].buf], out=sil[j].ap[:, hs],
                    in0=rp[h2].ap, in1=sil[j].ap[:, hs], op=ALU.mult)

        def gla_stage2(t):
            own = t >= 48
            j = t % NS2
            tt = t - 48
            last_prefix = (t == 47)
            ko = kouts[j]
            vb = vbs[j]
            e1 = e1s[j]
            if own:
                oP = [PS(), PS()]
                for pr in range(2):
                    mms = []
                    for hq in range(2):
                        hh = pr * 2 + hq
                        mms.append((oP[pr].ap[:, hq * 256:(hq + 1) * 256], attnT[j].ap[:, hh, :], vb.ap[:, hh * 256:(hh + 1) * 256],
                                    hq == 0, False, True))
                    for hq in range(2):
                        hh = pr * 2 + hq
                        mms.append((oP[pr].ap[:, hq * 256:(hq + 1) * 256], qdA[j].ap[:, hh, :], SbA[:, hh, :], False, False, True))
                    PE(mms, [attnT[j].buf, vb.buf, qdA[j].buf] + b_SbA[pr * 2:pr * 2 + 2], [oP[pr].buf])
            for c in range(2 if own else 1):
                rows = slice(c * 64, (c + 1) * 64) if own else slice(0, 128)
                dcol = c * 64 + 63 if own else 127
                lastc = (c == 1) or not own
                uP = [PS(), PS()]
                for pr in range(2):
                    PE([(uP[pr].ap[:, hq * 256:(hq + 1) * 256], ko.ap[rows, (pr * 2 + hq) * 128:(pr * 2 + hq + 1) * 128],
                         vb.ap[rows, (pr * 2 + hq) * 256:(pr * 2 + hq + 1) * 256], True, True) for hq in range(2)],
                       [ko.buf, vb.buf], [uP[pr].buf])
                for hh in range(4):
                    pr, hq = hh // 2, hh % 2
                    DVE("scalar_tensor_tensor", [uP[pr].buf, e1.buf, b_S[hh]], [b_S[hh]], out=Sst[:, hh, :], in0=Sst[:, hh, :],
                        scalar=e1.ap[:, hh, dcol:dcol + 1], in1=uP[pr].ap[:, hq * 256:(hq + 1) * 256],
                        op0=ALU.mult, op1=ALU.add)
                    if c == 0 and own:
                        ACT(SbB[:, hh, :], Sst[:, hh, :], AF.Copy, [b_S[hh]], [b_SbB[hh]])
                    if lastc and (own or last_prefix):
                        ACT(SbA[:, hh, :], Sst[:, hh, :], AF.Copy, [b_S[hh]], [b_SbA[hh]])
                if c == 0 and own:
                    for pr in range(2):
                        PE([(oP[pr].ap[:, hq * 256:(hq + 1) * 256], qdB[j].ap[:, pr * 2 + hq, :], SbB[:, pr * 2 + hq, :],
                             False, hq == 1, True) for hq in range(2)],
                           [qdB[j].buf] + b_SbB[pr * 2:pr * 2 + 2], [oP[pr].buf])
            if not own:
                return
            og = ogbs.next()
            for pr in range(2):
                bssh = b_ssh[(t % 2) * 2 + pr]
                sc = (t % 2) * 16 + pr * 8
                for hq in range(2):
                    ACT(junkH, oP[pr].ap[:, hq * 256:(hq + 1) * 256], AF.Square, [oP[pr].buf], [b_junkH, bssh],
                        accum_out=ssh[:, sc + hq:sc + hq + 1])
                POOL("tensor_scalar", [bssh], [bssh], out=ssh[:, sc + 2:sc + 4], in0=ssh[:, sc:sc + 2], scalar1=1.0 / 256, scalar2=EPS,
                     op0=ALU.mult, op1=ALU.add)
                POOL("tensor_tensor", [bssh, b_const], [bssh], out=ssh[:, sc + 2:sc + 4], in0=ssh[:, sc + 2:sc + 4], in1=mhalf4[:, 0:2],
                     op=ALU.pow)
                for hq in range(2):
                    hh = pr * 2 + hq
                    DVE("scalar_tensor_tensor", [oP[pr].buf, bssh, sil[j].buf], [og.bufs[pr]], out=og.ap[:, hh * 256:(hh + 1) * 256],
                        in0=oP[pr].ap[:, hq * 256:(hq + 1) * 256], scalar=ssh[:, sc + 2 + hq:sc + 3 + hq],
                        in1=sil[j].ap[:, hh * 256:(hh + 1) * 256], op0=ALU.mult, op1=ALU.mult)

            def fn(e, og=og):
                ins = None
                for kc in range(8):
                    ins = e.transpose(ptr.ap[:, kc, :], og.ap[:, kc * 128:(kc + 1) * 128], identb)
                return ins
            P.op("pe", fn, og.bufs + [b_const], [ptr.buf], cost=650.0)
            ACT(ogT[:, :, tt * 128:(tt + 1) * 128], ptr.ap, AF.Copy, [ptr.buf], [b_ogT[tt]])

        gla_stage1(0)
        for t in range(64):
            if t + 1 < 64:
                gla_stage1(t + 1)
            gla_stage2(t)
        P.barrier()

        hTo2 = carve(112 * KB, [128, 8, 2048], BF16)
        b_hTo2 = [Buf() for _ in range(16)]
        for t in range(16):
            s = load_x(6144 + t * 128)
            norm_to_hT(s.ap, [s.buf], hTo2[:, :, t * 128:(t + 1) * 128], [b_hTo2[t]])
        MM = Mem(144 * KB, 184 * KB)
        wsets = []
        for _ in range(2):
            wsets.append(dict(
                wog=MM.alloc([128, 8, 256], BF16), woa=MM.alloc([128, 4, 256], BF16),
                wgA=MM.alloc([128, 8, 256], BF16), wgB=MM.alloc([128, 8, 256], BF16),
                b_wog=[Buf() for _ in range(8)], b_woa=[Buf() for _ in range(4)],
                b_wgA=[Buf() for _ in range(8)], b_wgB=[Buf() for _ in range(8)]))
        sgs = Rot([Slot(MM.alloc([128, 512], F32)) for _ in range(4)])
        tms = Rot([Slot(MM.alloc([128, 512], F32)) for _ in range(2)])
        wout = carve(184 * KB, [128, 8, 1024], BF16)
        b_wout = [Buf() for _ in range(8)]
        for fo in range(4):
            ws = wsets[fo % 2]
            wog, woa, wgA, wgB = ws["wog"], ws["woa"], ws["wgA"], ws["wgB"]
            b_wog, b_woa, b_wgA, b_wgB = ws["b_wog"], ws["b_woa"], ws["b_wgA"], ws["b_wgB"]
            load_w(lambda kc, c0, cn: wgA[:, kc, c0:c0 + cn], w_in, 0, 8, 7696 + fo * 256, 256, 0, b_wgA)
            load_w(lambda kc, c0, cn: wgB[:, kc, c0:c0 + cn], w_in, 0, 8, 8720 + fo * 256, 256, 0, b_wgB)
            load_w(lambda kc, c0, cn: wog[:, kc, c0:c0 + cn], wog_d, 0, 8, fo * 256, 256, 3, b_wog)
            load_w(lambda kc, c0, cn: woa[:, kc, c0:c0 + cn], woa_d, 0, 4, fo * 256, 256, None, b_woa)
            if fo == 1:
                load_w(lambda kc, c0, cn: wout[:, kc, c0:c0 + cn], wout_d, 0, 8, 0, 1024, None, b_wout)
            for tb in range(4):
                tk = slice(tb * 512, (tb + 1) * 512)
                for fc in range(2):
                    fs = slice(fc * 128, (fc + 1) * 128)
                    ga = PS()
                    PE([(ga.ap, wgA[:, kc, fs], hTo2[:, kc, tk], kc == 0, kc == 7) for kc in range(8)],
                       b_wgA + b_hTo2[tb * 4:tb * 4 + 4], [ga.buf])
                    sa = sgs.next()
                    ACT(sa.ap, ga.ap, AF.Sigmoid, [ga.buf], [sa.buf])
                    gb = PS()
                    PE([(gb.ap, wgB[:, kc, fs], hTo2[:, kc, tk], kc == 0, kc == 7) for kc in range(8)],
                       b_wgB + b_hTo2[tb * 4:tb * 4 + 4], [gb.buf])
                    sb_ = sgs.next()
                    ACT(sb_.ap, gb.ap, AF.Sigmoid, [gb.buf], [sb_.buf])
                    yg = PS()
                    PE([(yg.ap, wog[:, kc, fs], ogT[:, kc, tk], kc == 0, kc == 7) for kc in range(8)],
                       b_wog + b_ogT[tb * 4:tb * 4 + 4], [yg.buf])
                    t1 = tms.next()
                    DVE("tensor_tensor", [yg.buf, sa.buf], [t1.buf], out=t1.ap, in0=yg.ap, in1=sa.ap, op=ALU.mult)
                    ya = PS()
                    PE([(ya.ap, woa[:, hh, fs], OhT[:, hh, tk], hh == 0, hh == 3) for hh in range(4)],
                       b_woa + [b_OhT], [ya.buf])
                    t2 = tms.next()
                    DVE("tensor_tensor", [ya.buf, sb_.buf], [t2.buf], out=t2.ap, in0=ya.ap, in1=sb_.ap, op=ALU.mult)
                    DVE("tensor_tensor", [t1.buf, t2.buf], [b_mixT[tb]], out=mixT[:, fo * 2 + fc, tk], in0=t1.ap, in1=t2.ap, op=ALU.add)
        P.barrier()

        w1c = [carve(64 * KB, [128, 8, 512], BF16), carve(80 * KB, [128, 8, 512], BF16)]
        w2c = [carve(72 * KB, [128, 4, 1024], BF16), carve(88 * KB, [128, 4, 1024], BF16)]
        b_w1c = [[Buf() for _ in range(8)] for _ in range(2)]
        b_w2c = [[Buf() for _ in range(4)] for _ in range(2)]

        def load_ff(ffg):
            wi = ffg % 2
            w1 = w1c[wi]
            w2 = w2c[wi]
            load_w(lambda kc, c0, cn: w1[:, kc, c0:c0 + cn], w1_d, 0, 8, ffg * 512, 512, 1, b_w1c[wi])
            load_w(lambda kc, c0, cn: w2[:, kc, c0:c0 + cn], w2_d, ffg * 512, 4, 0, 1024, None, b_w2c[wi])
        load_ff(0)
        for t in range(16):
            s = load_x(6144 + t * 128)
            for h2 in range(2):
                ps = PS()
                PE([(ps.ap, mixT[:, kc, t * 128:(t + 1) * 128], wout[:, kc, h2 * 512:(h2 + 1) * 512], kc == 0, kc == 7) for kc in range(8)],
                   b_mixT + b_wout, [ps.buf])
                DVE("tensor_tensor", [ps.buf, s.buf], [b_x1[t]], out=x1[:, t, h2 * 512:(h2 + 1) * 512], in0=ps.ap,
                    in1=s.ap[:, h2 * 512:(h2 + 1) * 512], op=ALU.add)
            norm_to_hT(x1[:, t, :], [b_x1[t]], h2T[:, :, t * 128:(t + 1) * 128], [b_h2T[t]])
        P.barrier()

        MF = Mem(96 * KB, 112 * KB)
        uTs = Rot([Slot(MF.alloc([128, 4, 512], BF16)) for _ in range(2)])
        rls = Rot([Slot(MF.alloc([128, 512], F32)) for _ in range(2)])
        wpg = carve(176 * KB, [128, 8, 1024], BF16)
        wpp = carve(192 * KB, [128, 2, 1024], BF16)
        lnfb = carve(196 * KB, [128, 1024], F32)
        b_wpg = [Buf() for _ in range(8)]
        b_wpp = [Buf() for _ in range(2)]
        b_lnf = Buf()
        for ffg in range(8):
            wi = ffg % 2
            w1 = w1c[wi]
            w2 = w2c[wi]
            if ffg > 0:
                load_ff(ffg)
            if ffg == 2:
                load_w(lambda kc, c0, cn: wpg[:, kc, c0:c0 + cn], wpg_d, 0, 8, 0, 1024, 2, b_wpg)
                load_w(lambda kc, c0, cn: wpp[:, kc, c0:c0 + cn], wpp_d, 0, 2, 0, 1024, None, b_wpp)
                dl = P.dma_sem("lnf")
                DMA(lnfb, lnf_d.partition_broadcast(128), [], [b_lnf], dl)
            for tb in range(4):
                tk = slice(tb * 512, (tb + 1) * 512)
                ut = uTs.next()
                for j in range(4):
                    ps = PS()
                    PE([(ps.ap, w1[:, kc, j * 128:(j + 1) * 128], h2T[:, kc, tk], kc == 0, kc == 7) for kc in range(8)],
                       b_w1c[wi] + b_h2T[tb * 4:tb * 4 + 4], [ps.buf])
                    rl = rls.next()
                    ACT(rl.ap, ps.ap, AF.Relu, [ps.buf], [rl.buf])
                    DVE("tensor_tensor", [rl.buf], [ut.buf], out=ut.ap[:, j, :], in0=rl.ap, in1=rl.ap, op=ALU.mult)
                for tt in range(4):
                    t = tb * 4 + tt
                    for h2 in range(2):
                        ps = PS()
                        PE([(ps.ap, ut.ap[:, j, tt * 128:(tt + 1) * 128], w2[:, j, h2 * 512:(h2 + 1) * 512], j == 0, j == 3) for j in range(4)],
                           [ut.buf] + b_w2c[wi], [ps.buf])
                        DVE("tensor_tensor", [ps.buf, b_x1[t]], [b_x1[t]], out=x1[:, t, h2 * 512:(h2 + 1) * 512],
                            in0=x1[:, t, h2 * 512:(h2 + 1) * 512], in1=ps.ap, op=ALU.add)
        P.barrier()

        MP = Mem(32 * KB, 112 * KB)
        junkP = MP.alloc([128, 1024], BF16)
        b_junkP = Buf()
        h3s = Rot([Slot(MP.alloc([128, 8, 128], BF16)) for _ in range(2)])
        pfs = Rot([Slot(MP.alloc([128, 256], F32), P.dma_sem("pf%d" % i)) for i in range(2)])
        pbs = Rot([Slot(MP.alloc([128, 256], BF16)) for _ in range(2)])
        pTs = Rot([Slot(MP.alloc([128, 2, 128], BF16)) for _ in range(2)])
        sg2 = Rot([Slot(MP.alloc([128, 1024], F32)) for _ in range(2)])
        osb = Rot([Slot(MP.alloc([128, 1024], F32), P.dma_sem("os%d" % i)) for i in range(2)])
        out_toks = []
        for t in range(16):
            h3 = h3s.next()
            norm_to_hT(x1[:, t, :], [b_x1[t]], h3.ap, [h3.buf])
            pf = pfs.next()
            DMA(pf.ap, pin[t * 128:(t + 1) * 128, :], [], [pf.buf], pf.sem)
            pb = pbs.next()
            DVE("tensor_copy", [pf.buf], [pb.buf], out=pb.ap, in_=pf.ap)

            def fn(e, pb=pb):
                ins = None
                for c in range(2):
                    ins = e.transpose(ptr.ap[:, c, :], pb.ap[:, c * 128:(c + 1) * 128], identb)
                return ins
            P.op("pe", fn, [pb.buf, b_const], [ptr.buf], cost=200.0)
            pT = pTs.next()
            ACT(pT.ap, ptr.ap[:, 0:2, :], AF.Copy, [ptr.buf], [pT.buf])
            sg = sg2.next()
            for h2 in range(2):
                hs = slice(h2 * 512, (h2 + 1) * 512)
                gp = PS()
                PE([(gp.ap, h3.ap[:, kc, :], wpg[:, kc, hs], kc == 0, kc == 7) for kc in range(8)], [h3.buf] + b_wpg, [gp.buf])
                ACT(sg.ap[:, hs], gp.ap, AF.Sigmoid, [gp.buf], [sg.buf])
                pp = PS()
                PE([(pp.ap, pT.ap[:, c, :], wpp[:, c, hs], c == 0, c == 1) for c in range(2)], [pT.buf] + b_wpp, [pp.buf])
                DVE("tensor_tensor", [pp.buf, sg.buf], [sg.buf], out=sg.ap[:, hs], in0=sg.ap[:, hs], in1=pp.ap, op=ALU.mult)
            DVE("tensor_tensor", [sg.buf, b_x1[t]], [b_x1[t]], out=x1[:, t, :], in0=x1[:, t, :], in1=sg.ap, op=ALU.add)
            ob = osb.next()
            rs, brs = rstd_of(x1[:, t, :], [b_x1[t]], 1024, junkP, b_junkP)
            DVE("scalar_tensor_tensor", [b_x1[t], brs, b_lnf], [ob.buf], out=ob.ap, in0=x1[:, t, :], scalar=rs, in1=lnfb,
                op0=ALU.mult, op1=ALU.mult)
            out_toks.append(DMA(y[t * 128:(t + 1) * 128, :], ob.ap, [ob.buf], [], ob.sem))
        P.final_wait("sp", out_toks[-2:])
        P.run(block)
    return nc


def _t5_bucket(n):
    max_exact = 16
    nf = np.maximum(n, 1).astype(np.float32)
    large = max_exact + (np.log(nf / max_exact) / np.log(2048 / max_exact) * (32 - max_exact)).astype(np.int32)
    large = np.minimum(large, 31)
    return np.where(n < max_exact, n, large).astype(np.int32)


def _const_mats():
    m = np.arange(128)[:, None]
    t = np.arange(128)[None, :]
    same = (m // 64) == (t // 64)
    cm = np.zeros((128, 6, 128), np.float32)
    cm[:, 0, :] = np.eye(128)
    cm[:, 1, :] = np.where(same & (m <= t), 1.0 / 16, 0.0)
    cm[:, 2, :] = np.where(same & (m > t), -1.0 / 16, 0.0)
    cm[:, 3, :] = np.where(same & (m <= t), 1.0, 0.0)
    cm[:, 5, :] = np.where(m > t, -1.0 / 16, 0.0)
    cm[:, 4, 0:16] = np.eye(4, dtype=np.float32).reshape(16)[None, :]
    sel4 = np.zeros((4, 4, 128), np.float32)
    for hh in range(4):
        sel4[hh, hh, :] = 1.0
    return cm, sel4


def _bias_layout(rel_bias):
    k = np.arange(128)[:, None, None]
    j = np.arange(2)[None, :, None]
    q = np.arange(128)[None, None, :]
    delta = q - k + 128 * (1 - j)
    valid = (delta >= 0) & (delta <= 128)
    out = np.full((128, 3, 2, 2, 2, 128), NEGM, np.float32)
    for g, dil in enumerate((1, 4, 16)):
        bucket = _t5_bucket(np.maximum(delta, 0) * dil)
        for hp in range(2):
            for hh in range(2):
                tab = rel_bias[:, g * 4 + hp * 2 + hh]
                vals = tab[bucket]
                out[:, g, hp, hh] = np.where(valid, vals, NEGM)
    return out.reshape(128, 3, 2, 512)


_PROG = None


def kernel(x, p, ln1, w_in, w_a2, b_a, gla_gn, w_o_gla, w_o_attn, w_out, ln2, w_mlp1, w_mlp2, ln3, w_pp, w_pg,
           rel_bias, ln_f):
    global _PROG
    f = lambda a: np.ascontiguousarray(np.asarray(a, dtype=np.float32))
    x = f(x); p = f(p)
    cm, sel4 = _const_mats()
    cols = np.stack([f(ln1)[0], f(ln2)[0], f(ln3)[0], f(gla_gn)[0]]).reshape(4, 8, 128).transpose(2, 0, 1).reshape(128, 32)
    wa2aug = np.zeros((32, 512), np.float32)
    wa2aug[0:16] = f(w_a2)[0]
    wa2aug[16] = f(b_a)[0]
    shared = {
        "w_in": f(w_in)[0], "w_a2aug": wa2aug, "w_o_gla": f(w_o_gla)[0], "w_o_attn": f(w_o_attn)[0],
        "w_out": f(w_out)[0], "w_mlp1": f(w_mlp1)[0], "w_mlp2": f(w_mlp2)[0], "w_pp": f(w_pp)[0], "w_pg": f(w_pg)[0],
        "cols": np.ascontiguousarray(cols), "ln_f": f(ln_f), "biasm": _bias_layout(f(rel_bias)),
        "cmat": cm, "sel4": sel4,
    }
    in_maps = []
    for c in range(NCORES):
        b, j = c // 4, c % 4
        xe = np.zeros((8192, 1024), np.float32)
        n = SEG * (j + 1)
        xe[8192 - n:] = x[b, 0:n]
        m = dict(shared)
        m["xe"] = xe
        m["p"] = np.ascontiguousarray(p[0, b, j * SEG:(j + 1) * SEG])
        m["hoff"] = np.full((128, 1), NEGM if j == 0 else 0.0, np.float32)
        in_maps.append(m)
    if _PROG is None:
        _PROG = build_program()
    res = run_bass_kernel_spmd(_PROG, in_maps, core_ids=list(range(NCORES)))
    out = np.zeros((2, 8192, 1024), np.float32)
    for c in range(NCORES):
        b, j = c // 4, c % 4
        out[b, j * SEG:(j + 1) * SEG] = res.results[c]["y"]
    return out
```
